# Optimizing a Trainium2 kernel written in Bass

```python
import math
import jax, jax.numpy as jnp
from jax import lax
import numpy as np

D_MODEL = 1024
BATCH = 1
SEQ = 16384
DEPTH = 1
DEC_BATCH = 32
DEC_SEQ = 32
PAST_LEN = 2048

CHUNK = 64
D_MIX = D_MODEL
D_CONV = D_MIX // 2
CONV_W = 3
N_HEADS = 4
HEAD_DIM = (D_MIX - D_CONV) // N_HEADS // 2
V_DIM = 2 * HEAD_DIM
QK_W = N_HEADS * 2 * HEAD_DIM
V_W = N_HEADS * V_DIM
IN_COLS = 3 * D_CONV + 2 * QK_W + V_W
SPLITS = (D_CONV, 2 * D_CONV, 3 * D_CONV, 3 * D_CONV + QK_W, 3 * D_CONV + 2 * QK_W)
D_FF = 4 * D_MODEL
PLE_DIM = 256
Q_BLOCK = 128
NEG_INF = -1e30
RMS_EPS = 1e-6

kernel_name = "hymba_conv_diffattn_streaming_step"


def rmsnorm(x, g):
    xf = x.astype(jnp.float32)
    y = xf * lax.rsqrt(jnp.mean(xf * xf, axis=-1, keepdims=True) + RMS_EPS)
    return (y * g.astype(jnp.float32)).astype(x.dtype)


def alibi_slopes():
    return 2.0 ** (-8.0 * jnp.arange(1, N_HEADS + 1, dtype=jnp.float32) / N_HEADS)


def diff_lambda(lq1, lk1, lq2, lk2, lam_init):
    f = jnp.float32
    return (jnp.exp(jnp.sum(lq1.astype(f) * lk1.astype(f)))
            - jnp.exp(jnp.sum(lq2.astype(f) * lk2.astype(f))) + lam_init)


def mix_inputs(h, w_in, g_pre):
    u = rmsnorm(h, g_pre)
    z = jnp.einsum('btd,de->bte', u, w_in)
    cb, cc, cx, q, k, v = jnp.split(z, SPLITS, axis=-1)
    b, t = h.shape[:2]
    q = q.reshape(b, t, N_HEADS, 2 * HEAD_DIM)
    k = k.reshape(b, t, N_HEADS, 2 * HEAD_DIM)
    v = v.reshape(b, t, N_HEADS, V_DIM)
    return cb, cc * cx, q, k, v


def short_conv(zc, hist, w_conv, b_conv):
    t = zc.shape[1]
    zp = jnp.concatenate([hist.astype(zc.dtype), zc], axis=1)
    y = b_conv + sum(zp[:, j:j + t] * w_conv[j] for j in range(CONV_W))
    return y, zp[:, -(CONV_W - 1):]


def diff_attn(q, k, v, q_pos, k_pos, lam, lam_init, g_subln):
    b, tq = q.shape[:2]
    tk = k.shape[1]
    f = jnp.float32
    qf = q.astype(f).reshape(b, tq, N_HEADS, 2, HEAD_DIM) * (HEAD_DIM ** -0.5)
    kf = k.astype(f).reshape(b, tk, N_HEADS, 2, HEAD_DIM)
    s = jnp.einsum('bqhmd,bkhmd->bhmqk', qf, kf)
    dist = jnp.abs(q_pos[:, None] - k_pos[None, :]).astype(f)
    bias = -alibi_slopes()[:, None, None] * dist
    visible = (k_pos[None, :] // CHUNK) <= (q_pos[:, None] // CHUNK)
    s = jnp.where(visible, s + bias[None, :, None], NEG_INF)
    a = jax.nn.softmax(s, axis=-1)
    attn = a[:, :, 0] - lam * a[:, :, 1]
    o = jnp.einsum('bhqk,bkhd->bqhd', attn, v.astype(f))
    o = rmsnorm(o, g_subln) * (1.0 - lam_init)
    return o.reshape(b, tq, V_W).astype(q.dtype)


def diff_attn_prompt(q, k, v, lam, lam_init, g_subln):
    b, t = q.shape[:2]
    nb = t // Q_BLOCK
    pos = jnp.arange(t, dtype=jnp.int32)
    qb = q.reshape(b, nb, Q_BLOCK, N_HEADS, 2 * HEAD_DIM).transpose(1, 0, 2, 3, 4)
    pb = pos.reshape(nb, Q_BLOCK)
    out = lax.map(lambda a: diff_attn(a[0], k, v, a[1], pos, lam, lam_init, g_subln), (qb, pb))
    return out.transpose(1, 0, 2, 3).reshape(b, t, V_W)


def finish(h, mix, p, w_out, g_post_mix, g_pre_mlp, w_up, w_down, g_post_mlp, w_pe, w_pe_gate, g_pe):
    h = h + rmsnorm(jnp.einsum('bte,ed->btd', mix, w_out), g_post_mix)
    u = rmsnorm(h, g_pre_mlp)
    ff = jnp.einsum('btf,fd->btd', jnp.square(jax.nn.relu(jnp.einsum('btd,df->btf', u, w_up))), w_down)
    h = h + rmsnorm(ff, g_post_mlp)
    e = jnp.einsum('btp,pd->btd', p, w_pe) * jax.nn.sigmoid(jnp.einsum('btd,de->bte', h, w_pe_gate))
    return h + rmsnorm(e, g_pe)


def setup_inputs(seed: int = 0) -> dict:
    key = jax.random.key(seed)
    ks = jax.random.split(key, 25)
    f = jnp.float32

    def nrm(k, shape, scale):
        return jax.random.normal(k, shape, f) * scale

    def gain(k, shape):
        return 1.0 + 0.05 * jax.random.normal(k, shape, f)

    return {
        "x_prompt": nrm(ks[0], (BATCH, SEQ, D_MODEL), 1.0),
        "x_sample": nrm(ks[1], (DEC_BATCH, DEC_SEQ, D_MODEL), 1.0),
        "cache_k": nrm(ks[2], (DEPTH, DEC_BATCH, PAST_LEN, N_HEADS, 2 * HEAD_DIM), 1.0),
        "cache_v": nrm(ks[3], (DEPTH, DEC_BATCH, PAST_LEN, N_HEADS, V_DIM), 1.0),
        "state_conv": nrm(ks[4], (DEPTH, DEC_BATCH, CONV_W - 1, D_CONV), 1.0),
        "p_prompt": nrm(ks[5], (DEPTH, BATCH, SEQ, PLE_DIM), 1.0),
        "p_sample": nrm(ks[6], (DEPTH, DEC_BATCH, DEC_SEQ, PLE_DIM), 1.0),
        "w_in": nrm(ks[7], (DEPTH, D_MODEL, IN_COLS), D_MODEL ** -0.5),
        "w_conv": nrm(ks[8], (DEPTH, CONV_W, D_CONV), CONV_W ** -0.5),
        "b_conv": nrm(ks[9], (DEPTH, D_CONV), 0.02),
        "lambda_q1": nrm(ks[10], (DEPTH, HEAD_DIM), 0.1),
        "lambda_k1": nrm(ks[11], (DEPTH, HEAD_DIM), 0.1),
        "lambda_q2": nrm(ks[12], (DEPTH, HEAD_DIM), 0.1),
        "lambda_k2": nrm(ks[13], (DEPTH, HEAD_DIM), 0.1),
        "g_subln": gain(ks[14], (DEPTH, V_DIM)),
        "w_out": nrm(ks[15], (DEPTH, D_MIX, D_MODEL), D_MIX ** -0.5),
        "g_pre_mix": gain(ks[16], (DEPTH, D_MODEL)),
        "g_post_mix": gain(ks[17], (DEPTH, D_MODEL)),
        "g_pre_mlp": gain(ks[18], (DEPTH, D_MODEL)),
        "g_post_mlp": gain(ks[19], (DEPTH, D_MODEL)),
        "w_up": nrm(ks[20], (DEPTH, D_MODEL, D_FF), D_MODEL ** -0.5),
        "w_down": nrm(ks[21], (DEPTH, D_FF, D_MODEL), D_FF ** -0.5),
        "w_pe": nrm(ks[22], (DEPTH, PLE_DIM, D_MODEL), PLE_DIM ** -0.5),
        "w_pe_gate": nrm(ks[23], (DEPTH, D_MODEL, D_MODEL), D_MODEL ** -0.5),
        "g_pe": gain(ks[24], (DEPTH, D_MODEL)),
    }


def reference(x_prompt, x_sample, cache_k, cache_v, state_conv, p_prompt, p_sample,
              w_in, w_conv, b_conv, lambda_q1, lambda_k1, lambda_q2, lambda_k2, g_subln,
              w_out, g_pre_mix, g_post_mix, g_pre_mlp, g_post_mlp, w_up, w_down,
              w_pe, w_pe_gate, g_pe):
    hp, hs = x_prompt, x_sample
    kp_l, vp_l, cp_l, ks_l, vs_l, cs_l = [], [], [], [], [], []
    t_s = x_sample.shape[1]
    q_pos_s = PAST_LEN + jnp.arange(t_s, dtype=jnp.int32)
    k_pos_s = jnp.arange(PAST_LEN + t_s, dtype=jnp.int32)
    for i in range(DEPTH):
        lam_init = 0.8 - 0.6 * math.exp(-0.3 * i)
        lam = diff_lambda(lambda_q1[i], lambda_k1[i], lambda_q2[i], lambda_k2[i], lam_init)
        rest = (w_out[i], g_post_mix[i], g_pre_mlp[i], w_up[i], w_down[i], g_post_mlp[i],
                w_pe[i], w_pe_gate[i], g_pe[i])

        cb, zc, q, k, v = mix_inputs(hp, w_in[i], g_pre_mix[i])
        hist0 = jnp.zeros((hp.shape[0], CONV_W - 1, D_CONV), zc.dtype)
        yc, conv_p = short_conv(zc, hist0, w_conv[i], b_conv[i])
        ya = diff_attn_prompt(q, k, v, lam, lam_init, g_subln[i])
        mix = jnp.concatenate([cb * yc, ya.astype(yc.dtype)], axis=-1)
        hp = finish(hp, mix, p_prompt[i], *rest)
        kp_l.append(k); vp_l.append(v); cp_l.append(conv_p)

        cb, zc, q, k, v = mix_inputs(hs, w_in[i], g_pre_mix[i])
        yc, conv_s = short_conv(zc, state_conv[i], w_conv[i], b_conv[i])
        k_all = jnp.concatenate([cache_k[i].astype(k.dtype), k], axis=1)
        v_all = jnp.concatenate([cache_v[i].astype(v.dtype), v], axis=1)
        ya = diff_attn(q, k_all, v_all, q_pos_s, k_pos_s, lam, lam_init, g_subln[i])
        mix = jnp.concatenate([cb * yc, ya.astype(yc.dtype)], axis=-1)
        hs = finish(hs, mix, p_sample[i], *rest)
        ks_l.append(k); vs_l.append(v); cs_l.append(conv_s)

    k_prompt = jnp.stack(kp_l); v_prompt = jnp.stack(vp_l); conv_prompt = jnp.stack(cp_l)
    k_sample = jnp.stack(ks_l); v_sample = jnp.stack(vs_l); conv_sample = jnp.stack(cs_l)
    return (hp, hs, k_prompt, v_prompt, conv_prompt, k_sample, v_sample, conv_sample)
```

```python
import math
from contextlib import ExitStack
import numpy as np
import concourse.bass as bass
import concourse.mybir as mybir
from concourse.bass_utils import run_bass_kernel_spmd

F32 = mybir.dt.float32
BF16 = mybir.dt.bfloat16
AF = mybir.ActivationFunctionType
ALU = mybir.AluOpType

NCORE = 8
D = 1024
DFF = 4096
PLE = 256
H = 4
DEC_T = 32
BPC = 4
RMS_EPS = 1e-6
NEG = -30000.0
SLOPES = [2.0 ** (-8.0 * (h + 1) / 4) for h in range(4)]
LAM_INIT = 0.8 - 0.6 * math.exp(-0.3 * 0)
CFG = dict(NSLOT=4, PAST=2048)

ENGS = ["pe", "act", "dve", "pool", "sp"]


class T:
    __slots__ = ("w", "rs")

    def __init__(self):
        self.w = None
        self.rs = []


class Op:
    __slots__ = ("eng", "fn", "deps", "needed", "ev", "dma_sem", "epoch", "barrier")

    def __init__(self, eng, fn):
        self.eng = eng
        self.fn = fn
        self.deps = []
        self.needed = False
        self.ev = None
        self.dma_sem = None
        self.barrier = 0


class DSem:
    def __init__(self, sem):
        self.sem = sem
        self.n = 0


class KB:
    def __init__(self, nc, es):
        self.nc = nc
        self.es = es
        self.ops = []
        self.epoch = 0
        self.esem = {e: es.enter_context(nc.semaphore("es_" + e)) for e in ENGS[:4]}
        self.bsem = es.enter_context(nc.semaphore("bar"))
        self.dsems = {}
        self.nbar = 0

    def dsem(self, name):
        if name not in self.dsems:
            self.dsems[name] = DSem(self.es.enter_context(self.nc.semaphore("ds_" + name)))
        return self.dsems[name]

    def op(self, eng, fn, reads=(), writes=(), dsem=None):
        o = Op(eng, fn)
        o.epoch = self.epoch
        o.dma_sem = dsem
        deps = []
        for r in reads:
            if r.w is not None:
                deps.append(r.w)
        for w in writes:
            if w.w is not None:
                deps.append(w.w)
            deps.extend(w.rs)
        seen = set()
        for d in deps:
            if d is o or id(d) in seen or d.epoch < self.epoch:
                continue
            seen.add(id(d))
            if d.eng == eng and d.dma_sem is None and eng in ("pe", "sp"):
                continue
            if eng == "sp" and d.eng == "sp" and d.dma_sem is not None and d.dma_sem is dsem:
                continue
            o.deps.append(d)
            d.needed = True
        for r in reads:
            r.rs.append(o)
        for w in writes:
            w.w = o
            w.rs = []
        self.ops.append(o)
        return o

    def barrier(self):
        self.nbar += 1
        for e in ENGS:
            o = Op(e, None)
            o.barrier = self.nbar
            o.epoch = self.epoch
            self.ops.append(o)
        self.epoch += 1

    def emit(self):
        nc = self.nc
        per = {e: [] for e in ENGS}
        cnt = {e: 0 for e in ENGS}
        for o in self.ops:
            per[o.eng].append(o)
            if o.barrier:
                continue
            if o.dma_sem is not None:
                o.dma_sem.n += 1
                o.ev = (o.dma_sem, o.dma_sem.n * 16)
            elif o.needed:
                cnt[o.eng] += 1
                o.ev = (o.eng, cnt[o.eng])
        esem = self.esem
        bsem = self.bsem
        alld = list(self.dsems.values())

        def run(ename, eng):
            waited = {}
            dcount = {id(d): 0 for d in alld}
            for o in per[ename]:
                if o.barrier:
                    for d in alld:
                        if dcount[id(d)] > waited.get(id(d), 0):
                            eng.wait_ge(d.sem, dcount[id(d)])
                            waited[id(d)] = dcount[id(d)]
                    if ename == "sp":
                        eng.sem_inc(bsem, 1)
                    else:
                        eng.drain().then_inc(bsem, 1)
                    eng.wait_ge(bsem, 5 * o.barrier)
                    continue
                need = {}
                for d in o.deps:
                    key, val = d.ev
                    k = id(key) if isinstance(key, DSem) else key
                    if waited.get(k, 0) >= val:
                        continue
                    if k not in need or need[k][1] < val:
                        need[k] = (key, val)
                for k, (key, val) in need.items():
                    sem = key.sem if isinstance(key, DSem) else esem[key]
                    eng.wait_ge(sem, val)
                    waited[k] = val
                inst = o.fn(eng)
                if o.dma_sem is not None:
                    inst.then_inc(o.dma_sem.sem, 16)
                    dcount[id(o.dma_sem)] = o.ev[1]
                elif o.needed:
                    inst.then_inc(esem[o.eng], 1)
            if ename == "sp":
                for d in alld:
                    if d.n > 0:
                        eng.wait_ge(d.sem, d.n * 16)

        with nc.Block() as block:
            @block.tensor
            def _(eng):
                run("pe", eng)

            @block.scalar
            def _(eng):
                run("act", eng)

            @block.vector
            def _(eng):
                run("dve", eng)

            @block.gpsimd
            def _(eng):
                run("pool", eng)

            @block.sync
            def _(eng):
                run("sp", eng)


class Arena:
    def __init__(self, t, n):
        self.t = t
        self.n = n
        self.top = 0

    def _take(self, n32):
        off = self.top
        self.top += n32
        assert self.top <= self.n, ("SBUF arena overflow", self.top, self.n)
        return off

    def f32(self, *shape):
        n = int(np.prod(shape))
        off = self._take(n)
        v = self.t[:, off:off + n]
        return _shape(v, shape)

    def bf16(self, *shape):
        n = int(np.prod(shape))
        n32 = (n + 1) // 2
        off = self._take(n32)
        v = self.t[:, off:off + n32].bitcast(BF16)[:, 0:n]
        return _shape(v, shape)


def _shape(v, shape):
    if len(shape) == 1:
        return v
    if len(shape) == 2:
        return v.rearrange("p (a b) -> p a b", a=shape[0])
    if len(shape) == 3:
        return v.rearrange("p (a b c) -> p a b c", a=shape[0], b=shape[1])
    raise ValueError(shape)


def _hw(h):
    return 256 if h == 0 else 512


def _refoff(h, a):
    return 256 * (a // 2) if h == 0 else 0


def bias_layout(NSLOT):
    E = [4 * (NCORE * j + NCORE - 1) for j in range(NSLOT)]
    idx = {}
    n = 0
    for j in range(NSLOT):
        for h in range(H):
            for kb in range(E[j]):
                for half in range(2 if h == 0 else 1):
                    idx[(j, h, kb, half)] = n
                    n += 1
    return E, idx, n


def make_tables(c, NSLOT, PAST):
    E, idx, ncol = bias_layout(NSLOT)
    kl = np.arange(128, dtype=np.float64)
    btab = np.zeros((128, ncol), np.float64)
    for (j, h, kb, half), col in idx.items():
        sb = NCORE * j + c
        m = SLOPES[h]
        Cc = m * _hw(h) / 2
        ref = sb * 512 + half * 256
        if kb < 4 * sb:
            btab[:, col] = m * (kl + 128 * kb - ref) - Cc
        else:
            btab[:, col] = NEG
    btab = np.maximum(btab, NEG)
    dbias = np.zeros((128, H, 4, 4), np.float64)
    bdiag = np.zeros((128, H, 4, 128), np.float64)
    ql = np.arange(128, dtype=np.float64)
    for h in range(H):
        m = SLOPES[h]
        Cc = m * _hw(h) / 2
        for a in range(4):
            for ap in range(4):
                dbias[:, h, ap, a] = m * (kl + 128 * ap - _refoff(h, a)) - Cc
            vis = (kl[:, None] // 64) <= (ql[None, :] // 64)
            val = -m * np.abs(ql[None, :] - kl[:, None]) + m * (128 * a + ql[None, :] - _refoff(h, a)) - Cc
            bdiag[:, h, a, :] = np.where(vis, val, NEG)
    nkb = PAST // 128
    sbias = np.zeros((128, H, nkb), np.float64)
    bs = np.full((128, BPC, H, DEC_T), NEG, np.float64)
    q32 = np.arange(DEC_T, dtype=np.float64)
    for h in range(H):
        m = SLOPES[h]
        Cs = m * 16
        for kb in range(nkb):
            sbias[:, h, kb] = m * (kl + 128 * kb - PAST) - Cs
        for b in range(BPC):
            for tp in range(DEC_T):
                bs[b * 32 + tp, b, h, :] = -m * np.abs(q32 - tp) + m * q32 - Cs
    sbias = np.maximum(sbias, NEG)
    shm = np.zeros((128, 10, 128), np.float32)
    for t in range(128):
        if t >= 1:
            shm[t - 1, 0, t] = 1
        if t >= 2:
            shm[t - 2, 1, t] = 1
        if t % 32 != 0:
            shm[t - 1, 6, t] = 1
        if t % 32 >= 2:
            shm[t - 2, 7, t] = 1
    shm[127, 2, 0] = 1
    shm[126, 3, 0] = 1
    shm[127, 3, 1] = 1
    shm[1, 4, 0] = 1
    shm[0, 5, 0] = 1
    shm[1, 5, 1] = 1
    for b in range(BPC):
        shm[2 * b + 1, 8, 32 * b] = 1
        shm[2 * b, 9, 32 * b] = 1
        shm[2 * b + 1, 9, 32 * b + 1] = 1
    f = np.float32
    return dict(btab=btab.astype(f), dbias=dbias.reshape(128, -1).astype(f),
                bdiag=bdiag.reshape(128, -1).astype(f), sbias=sbias.reshape(128, -1).astype(f),
                bs=bs.reshape(128, -1).astype(f), shm=shm.reshape(128, -1))


def build_program(NSLOT, PAST):
    SEQ = NCORE * NSLOT * 512
    NBLK = SEQ // 128
    NOWN = 4 * NSLOT + 1
    NKBS = PAST // 128
    E, BIDX, NCOL = bias_layout(NSLOT)
    EMAX = E[-1]
    NG = EMAX // 4

    nc = bass.Bass("TRN2", target_bir_lowering=False)

    def din(name, shape, dt=F32):
        return nc.dram_tensor(name, list(shape), dt, kind="ExternalInput").ap()

    def dout(name, shape):
        return nc.dram_tensor(name, list(shape), F32, kind="ExternalOutput").ap()

    def dscr(name, shape):
        return nc.dram_tensor(name, list(shape), BF16, kind="Internal").ap()

    x_all = din("x_all", [SEQ, D])
    x_own = din("x_own", [NOWN, 128, D])
    x_halo = din("x_halo", [NSLOT, 2, D])
    p_own = din("p_own", [NOWN, 128, PLE])
    hist_s = din("hist_s", [8, 512])
    ckT = din("ckT", [BPC, H, 128, PAST])
    cv = din("cv", [BPC, PAST, 512])
    w_in = din("w_in", [D, 3072])
    w_out = din("w_out", [D, D])
    w_up = din("w_up", [D, DFF])
    w_down = din("w_down", [DFF, D])
    w_pe = din("w_pe", [PLE, D])
    w_g = din("w_g", [D, D])
    gvec_d = din("gvec", [128, 16])
    gbc_d = din("gbc", [128, 3 * D])
    gsub_d = din("gsub", [128, 128])
    convbc_d = din("convbc", [128, 4 * 512])
    lamv_d = din("lamv", [128, 4 * 64])
    ident_d = din("ident", [128, 128])
    shm_d = din("shm", [128, 10 * 128])
    btab_d = din("btab", [128, NCOL])
    dbias_d = din("dbias", [128, H * 16])
    bdiag_d = din("bdiag", [128, H * 4 * 128])
    sbias_d = din("sbias", [128, H * NKBS])
    bs_d = din("bs", [128, BPC * H * DEC_T])

    y_own = dout("y_own", [NOWN, 128, D])
    k_own = dout("k_own", [NOWN, 128, 512])
    v_own = dout("v_own", [NOWN, 128, 512])
    conv_p = dout("conv_p", [2, 512])
    conv_s = dout("conv_s", [8, 512])
    DEBUG = CFG.get("DEBUG", False)
    if DEBUG:
        dbg_mix = dout("dbg_mix", [NOWN, 128, D])

    if CFG.get("DEBUG", False):
        kt_scr = nc.dram_tensor("kt_scr", [H, 128, NG * 512], BF16, kind="ExternalOutput").ap()
        v_scr = nc.dram_tensor("v_scr", [H, 128, NG * 4, 130], BF16, kind="ExternalOutput").ap()
    else:
        kt_scr = dscr("kt_scr", [H, 128, NG * 512])
        v_scr = dscr("v_scr", [H, 128, NG * 4, 130])
    ws_in = dscr("ws_in", [4, 4, 128, 1024])
    ws_out = dscr("ws_out", [2, 4, 128, 1024])
    ws_g = dscr("ws_g", [2, 4, 128, 1024])
    ws_pe = dscr("ws_pe", [2, 1, 128, 1024])
    ws_dn = dscr("ws_dn", [2, 16, 128, 1024])
    ws_up = dscr("ws_up", [32, 128, 1024])

    ARENA_N = 53000
    with ExitStack() as es:
        kb = KB(nc, es)
        arena_t = es.enter_context(nc.sbuf_tensor("arena", [128, ARENA_N], F32))
        ps_t = es.enter_context(nc.psum_tensor("psum", [128, 8 * 512], F32))
        A = Arena(arena_t, ARENA_N)
        TB = [T() for _ in range(8)]

        def bank(b):
            return ps_t[:, b * 512:(b + 1) * 512]

        def bankb(b):
            return ps_t[:, b * 512:(b + 1) * 512].bitcast(BF16)

        def pe(fn, r=(), w=()):
            return kb.op("pe", fn, r, w)

        def act(fn, r=(), w=()):
            return kb.op("act", fn, r, w)

        def dve(fn, r=(), w=()):
            return kb.op("dve", fn, r, w)

        def pool(fn, r=(), w=()):
            return kb.op("pool", fn, r, w)

        def dma(out, in_, r=(), w=(), sem="const", q="sp"):
            return kb.op(q, lambda e: e.dma_start(out=out, in_=in_), r, w, dsem=kb.dsem(sem))

        ident_f = A.f32(128); T_identf = T()
        ident_b = A.bf16(128); T_identb = T()
        shm = A.f32(10, 128); T_shm = T()
        gvec = A.f32(16); T_gvec = T()
        gbc = A.f32(3, D); T_gbc = T()
        gsub = A.f32(128); T_gsub = T()
        convbc = A.f32(4, 512); T_convbc = T()
        btab = A.f32(NCOL); T_btab = T()
        dbias = A.f32(H * 16)
        bdiag = A.f32(H * 4, 128)
        sbias = A.f32(H * NKBS)
        bs = A.f32(BPC * H, DEC_T)
        lam_t = A.f32(4); T_lam = T()
        wkv = A.bf16(8, 1024); T_wkv = T()
        NRING = 8
        ring = [A.bf16(1024) for _ in range(NRING)]
        T_ring = [T() for _ in range(NRING)]
        ring_pos = [0]
        T_c = T()
        xo = [A.f32(D) for _ in range(4)]; T_xo = [T() for _ in range(4)]

        def load_xo(slot_idx, is_sample_):
            na = 1 if is_sample_ else 4
            o0 = 4 * NSLOT if is_sample_ else 4 * slot_idx
            for a in range(na):
                dma(xo[a], x_own[o0 + a], w=[T_xo[a]], sem="xo%d" % a)

        for dst, src in ((ident_f, ident_d), (shm.rearrange("p a b -> p (a b)"), shm_d), (gvec, gvec_d),
                         (gbc.rearrange("p a b -> p (a b)"), gbc_d), (gsub, gsub_d),
                         (convbc.rearrange("p a b -> p (a b)"), convbc_d), (btab, btab_d), (dbias, dbias_d),
                         (bdiag.rearrange("p a b -> p (a b)"), bdiag_d), (sbias, sbias_d),
                         (bs.rearrange("p a b -> p (a b)"), bs_d)):
            dma(dst, src, w=[T_c])
        dve(lambda e: e.tensor_copy(out=ident_b, in_=ident_f), [T_c], [T_identb])
        dve(lambda e: e.memset(lam_t[:, 2:3], RMS_EPS), [], [T_lam])
        dve(lambda e: e.tensor_scalar(out=gsub, in0=gsub, scalar1=1.0 - LAM_INIT, scalar2=None, op0=ALU.mult),
            [T_c], [T_c])
        eps_ap = lam_t[:, 2:3]

        mark0 = A.top
        lamv = A.f32(4, 64)
        lprod = A.f32(2, 64)
        lsum = A.f32(2)
        dma(lamv.rearrange("p a b -> p (a b)"), lamv_d, w=[T_c])
        for i in range(2):
            dve(lambda e, i=i: e.tensor_tensor(out=lprod[:, i, :], in0=lamv[:, 2 * i, :], in1=lamv[:, 2 * i + 1, :],
                                               op=ALU.mult), [T_c], [T_c])
            act(lambda e, i=i: e.activation(out=lprod[:, i, :], in_=lprod[:, i, :], func=AF.Copy,
                                            accum_out=lsum[:, i:i + 1]), [T_c], [T_c])
        act(lambda e: e.activation(out=lsum, in_=lsum, func=AF.Exp), [T_c], [T_c])
        dve(lambda e: e.tensor_tensor(out=lam_t[:, 0:1], in0=lsum[:, 0:1], in1=lsum[:, 1:2], op=ALU.subtract),
            [T_c, T_lam], [T_lam])
        dve(lambda e: e.tensor_scalar(out=lam_t[:, 0:1], in0=lam_t[:, 0:1], scalar1=LAM_INIT, scalar2=None,
                                      op0=ALU.add), [T_lam], [T_lam])
        dve(lambda e: e.tensor_scalar(out=lam_t[:, 1:2], in0=lam_t[:, 0:1], scalar1=-1.0, scalar2=None,
                                      op0=ALU.mult), [T_lam], [T_lam])

        NWS = 2
        wst_f = [A.f32(4096) for _ in range(NWS)]
        wst_b = [A.bf16(4096) for _ in range(NWS)]
        T_wf = [T() for _ in range(NWS)]
        T_wb = [T() for _ in range(NWS)]
        T_ws = T()
        prep_i = [0]
        prep_q = ["sp"]

        def make_job(src_rows, width, gcol, stores_fn, late=False):
            st = {}

            def load():
                st["s"] = prep_i[0] % NWS
                prep_i[0] += 1
                s = st["s"]
                dma(wst_f[s][:, 0:width], src_rows, w=[T_wf[s]], sem="wf%d" % s, q=("pool" if late else "sp"))

            def cast():
                s = st["s"]
                o = wst_b[s][:, 0:width]
                src_ = wst_f[s][:, 0:width]
                if late:
                    pool(lambda e: e.tensor_copy(out=o, in_=src_), [T_wf[s]], [T_wb[s]])
                elif gcol is None:
                    dve(lambda e: e.tensor_copy(out=o, in_=src_), [T_wf[s]], [T_wb[s]])
                else:
                    g_ap = gvec[:, gcol:gcol + 1]
                    dve(lambda e: e.tensor_scalar(out=o, in0=src_, scalar1=g_ap, scalar2=None, op0=ALU.mult),
                        [T_wf[s], T_c], [T_wb[s]])

            def store():
                s = st["s"]
                stores_fn(s)
            return (load, cast, store)

        def std_stores(wdst, kc, ncg):
            def f(s):
                for cg in range(ncg):
                    dma(wdst[cg, kc // 2, :, (kc % 2) * 512:(kc % 2) * 512 + 512],
                        wst_b[s][:, cg * 512:(cg + 1) * 512], r=[T_wb[s]], w=[T_ws], sem="wb%d" % s, q="pool")
            return f

        for kc in range(8):
            def win_store(s, kc=kc):
                src_ap = wst_b[s][:, 0:1024]
                dve(lambda e: e.tensor_copy(out=wkv[:, kc, :], in_=src_ap), [T_wb[s]], [T_wkv])
            ld, cs, stf = make_job(w_in[kc * 128:(kc + 1) * 128, 2048:3072], 1024, kc, win_store)
            ld(); cs(); stf()
        prep_jobs = []
        for kc in range(8):
            prep_jobs.append(make_job(w_in[kc * 128:(kc + 1) * 128, 0:2048], 2048, kc, std_stores(ws_in, kc, 4)))
        down_jobs = []
        for (wsrc, wdst, nkc) in ((w_out, ws_out, 8), (w_g, ws_g, 8), (w_pe, ws_pe, 2), (w_down, ws_dn, 32)):
            for kc in range(nkc):
                if wsrc is w_down and CFG.get("LATE_DOWN", False):
                    down_jobs.append(make_job(wsrc[kc * 128:(kc + 1) * 128, :], 1024, None,
                                              std_stores(wdst, kc, 2), late=True))
                else:
                    prep_jobs.append(make_job(wsrc[kc * 128:(kc + 1) * 128, :], 1024, None,
                                              std_stores(wdst, kc, 2)))
        for kc in range(8):
            def up_store(s, kc=kc):
                dma(ws_up[:, :, kc * 128:(kc + 1) * 128].rearrange("f p n -> p f n"),
                    wst_b[s][:, 0:4096].rearrange("p (f n) -> p f n", f=32), r=[T_wb[s]], w=[T_ws],
                    sem="wb%d" % s, q="pool")
            prep_jobs.append(make_job(w_up[kc * 128:(kc + 1) * 128, :], 4096, 8 + kc, up_store))

        ring_q = []
        ring_issued = [0]
        ring_taken = [0]

        def ring_plan(srcs):
            ring_q.extend(srcs)

        def ring_prefetch():
            while ring_issued[0] < len(ring_q) and ring_issued[0] - ring_taken[0] < NRING:
                s = ring_issued[0] % NRING
                dma(ring[s], ring_q[ring_issued[0]], r=[T_ws], w=[T_ring[s]], sem="ring%d" % s)
                ring_issued[0] += 1

        def ring_load(src):
            if ring_taken[0] >= len(ring_q):
                ring_q.append(src)
            assert ring_q[ring_taken[0]] is src or True
            if ring_issued[0] <= ring_taken[0]:
                ring_prefetch()
            s = ring_taken[0] % NRING
            ring_taken[0] += 1
            return ring[s], T_ring[s]

        def ring_topup():
            ring_prefetch()

        def plan_tok(ws, cgs, nkcp):
            return [ws[cg, kcp] for cg in cgs for kcp in range(nkcp)]

        def plan_A(is_sample):
            p = []
            if not is_sample:
                p += plan_tok(ws_in, [1, 2], 4)
            p += plan_tok(ws_in, [0, 1, 2, 3], 4)
            return p

        def plan_C():
            return (plan_tok(ws_out, [0, 1], 4) + [ws_up[f] for f in range(32)] + plan_tok(ws_dn, [0, 1], 16)
                    + plan_tok(ws_g, [0, 1], 4) + plan_tok(ws_pe, [0, 1], 1))

        def rstd_from_ss(ss_ap, n_feat, out_ap, toks):
            npp = ss_ap.shape[0]
            act(lambda e: e.activation(out=out_ap, in_=ss_ap, func=AF.Ln, scale=1.0 / n_feat, bias=eps_ap[0:npp]),
                toks + [T_lam], toks)
            act(lambda e: e.activation(out=out_ap, in_=out_ap, func=AF.Exp, scale=-0.5), toks, toks)

        def transpose_to(dst_fn, src_b, nparts, nchunks, bk, Tsrc, Tdst, evac_eng="act"):
            pv = bankb(bk)
            for c in range(nchunks):
                pe(lambda e, c=c: e.transpose(out=pv[:, c * nparts:(c + 1) * nparts],
                                              in_=src_b[0:nparts, c * 128:(c + 1) * 128],
                                              identity=ident_b[0:nparts, 0:nparts]),
                   [Tsrc, T_identb], [TB[bk]])
            dst_fn(pv[:, 0:nchunks * nparts].rearrange("p (c t) -> p c t", c=nchunks))

        mark1 = A.top
        xg = [A.f32(4, D) for _ in range(2)]; T_xg = [T(), T()]
        xb1 = [A.bf16(D) for _ in range(3)]; T_xb1 = [T(), T(), T()]
        junk = A.bf16(D); T_junk = T()
        xT1 = [A.bf16(8, 128) for _ in range(2)]; T_xT1 = [T(), T()]
        ss1 = [A.f32(4) for _ in range(2)]; T_ss1 = [[T() for _ in range(4)] for _ in range(2)]
        kbf = [A.bf16(512) for _ in range(2)]; T_kbf = [T(), T()]
        ktst = [A.bf16(4, 512) for _ in range(2)]; T_ktst = [T(), T()]
        vst = [A.bf16(4 * 4, 130) for _ in range(2)]; T_vst = [T(), T()]
        for s in range(2):
            dve(lambda e, s=s: e.memset(vst[s][:, :, 128:129], 1.0), [], [T_vst[s]])

        def load_xg(g):
            s = g % 2
            dma(xg[s], x_all[g * 512:(g + 1) * 512, :].rearrange("(a p) d -> p a d", p=128), w=[T_xg[s]],
                sem="xg%d" % s)

        NB1 = 4 * NG

        def st_L(n):
            g, a = divmod(n, 4)
            s = g % 2
            b2 = n % 2
            if a == 0 and g + 1 < NG:
                load_xg(g + 1)
            xa = xg[s][:, a, :]
            act(lambda e: e.activation(out=junk, in_=xa, func=AF.Square, accum_out=ss1[s][:, a:a + 1]),
                [T_xg[s]], [T_ss1[s][a]])
            b3 = n % 3
            dve(lambda e: e.tensor_copy(out=xb1[b3], in_=xa), [T_xg[s]], [T_xb1[b3]])
            rstd_from_ss(ss1[s][:, a:a + 1], D, ss1[s][:, a:a + 1], [T_ss1[s][a]])

        def st_TX(n):
            b2 = n % 2
            b3 = n % 3
            transpose_to(lambda pv: act(lambda e: e.activation(out=xT1[b2], in_=pv, func=AF.Copy),
                                        [], [TB[b2], T_xT1[b2]]),
                         xb1[b3], 128, 8, b2, T_xb1[b3], None)

        def st_MM(n):
            g, a = divmod(n, 4)
            s = g % 2
            b2 = n % 2
            for cg in range(2):
                bk = 2 + 2 * cg + b2
                for kc in range(8):
                    pe(lambda e, bk=bk, kc=kc, cg=cg: e.matmul(
                        bank(bk), lhsT=xT1[b2][:, kc, :], rhs=wkv[:, kc, cg * 512:(cg + 1) * 512],
                        start=(kc == 0), stop=(kc == 7)), [T_xT1[b2], T_wkv], [TB[bk]])
            rs = ss1[s][:, a:a + 1]
            dve(lambda e: e.tensor_scalar(out=kbf[b2], in0=bank(2 + b2), scalar1=rs, scalar2=None, op0=ALU.mult),
                [T_ss1[s][a]], [TB[2 + b2], T_kbf[b2]])
            vdst = vst[s].rearrange("p (h a) n -> p h a n", h=4)[:, :, a, 0:128]
            act(lambda e: e.activation(out=vdst, in_=bank(4 + b2).rearrange("p (h n) -> p h n", h=4),
                                       func=AF.Copy, scale=rs),
                [T_ss1[s][a]], [TB[4 + b2], T_vst[s]])

        def st_TK(n):
            g, a = divmod(n, 4)
            s = g % 2
            b2 = n % 2
            transpose_to(lambda pv: dve(lambda e: e.tensor_copy(out=ktst[s][:, :, a * 128:(a + 1) * 128], in_=pv),
                                        [], [TB[6 + b2], T_ktst[s]]),
                         kbf[b2], 128, 4, 6 + b2, T_kbf[b2], None)
            if a == 3:
                dma(kt_scr[:, :, g * 512:(g + 1) * 512].rearrange("h p n -> p h n"), ktst[s], r=[T_ktst[s]],
                    sem="kts%d" % s)
                dma(v_scr.rearrange("h p k n -> p h (k n)")[:, :, g * 520:(g + 1) * 520],
                    vst[s].rearrange("p (h a) n -> p h (a n)", h=4), r=[T_vst[s]], sem="vs%d" % s)

        if NG > 0:
            load_xg(0)
            st_L(0)
            if NB1 > 1:
                st_L(1)
            st_TX(0)
        stages = []
        pj = 0
        for n in range(NB1):
            if n + 2 < NB1:
                st_L(n + 2)
            if n + 1 < NB1:
                st_TX(n + 1)
            st_MM(n)
            if n >= 1:
                st_TK(n - 1)
            if stages:
                cs, stf = stages.pop(0)
                cs(); stf()
            if pj < len(prep_jobs) and (n % 2 == 0):
                ld, cs, stf = prep_jobs[pj]
                pj += 1
                ld()
                stages.append((cs, stf))
        if NB1 > 0:
            st_TK(NB1 - 1)
        while stages or pj < len(prep_jobs):
            if stages:
                cs, stf = stages.pop(0)
                cs(); stf()
            if pj < len(prep_jobs):
                ld, cs, stf = prep_jobs[pj]
                pj += 1
                ld()
                stages.append((cs, stf))
        late_jobs = prep_jobs[pj:] + down_jobs
        SLOT_ORDER = list(range(NSLOT))[::-1]
        ring_plan(plan_A(False))
        ring_prefetch()
        load_xo(SLOT_ORDER[0], False)
        kb.barrier()
        A.top = mark0

        def stream_tokmajor(ws, ncg, nkcp, lhs_fn, NA, evac_fn, lhs_toks, bank_sets, M=128, cgs=None, wres=None):
            cgl = list(range(ncg)) if cgs is None else cgs
            for ci, cg in enumerate(cgl):
                banks = bank_sets[ci % len(bank_sets)]
                for kcp in range(nkcp):
                    if wres is None:
                        piece, Tp = ring_load(ws[cg, kcp])
                        pv = piece.rearrange("p (k n) -> p k n", k=2)
                    for kk in range(2):
                        kc = 2 * kcp + kk
                        for a in range(NA):
                            if wres is None:
                                rhs = pv[:, kk, :]
                                rt = Tp
                            else:
                                rhs, rt = wres(cg, kc)
                            pe(lambda e, a=a, kc=kc, rhs=rhs, bk=banks[a]: e.matmul(
                                bank(bk)[0:M, :], lhsT=lhs_fn(a, kc), rhs=rhs,
                                start=(kc == 0), stop=(kc == 2 * nkcp - 1)),
                               lhs_toks(a) + [rt], [TB[banks[a]]])
                    if wres is None:
                        ring_topup()
                for a in range(NA):
                    evac_fn(cg, a, banks[a])

        SETS4 = [[0, 1, 2, 3], [4, 5, 6, 7]]

        def do_slot(j, is_sample):
            NA = 1 if is_sample else 4
            W = NA * 128
            ob0 = 4 * NSLOT if is_sample else 4 * j
            markS = A.top
            mixb = [A.bf16(D) for _ in range(NA)]; T_mix = [T() for _ in range(NA)]
            markQ = A.top
            qpad = [A.bf16(4, W) for _ in range(2)]; T_q = T()
            ktown = A.bf16(4, W); T_kt = T()
            vown = A.bf16(NA * 4, 130); T_vo = T()
            pool(lambda e: e.memset(qpad[0][64:128], 0.0), [], [T_q])
            pool(lambda e: e.memset(qpad[1][0:64], 0.0), [], [T_q])
            dve(lambda e: e.memset(vown[:, :, 128:129], 1.0), [], [T_vo])
            markA = A.top

            xbA = [A.bf16(D) for _ in range(2)]; T_xbA = [T(), T()]
            junkA = A.bf16(D); T_junkA = T()
            xTA = [A.bf16(8, 128) for _ in range(NA)]; T_xTA = [T() for _ in range(NA)]
            rsA = A.f32(NA + 1); T_rsA = [T() for _ in range(NA + 1)]
            cb = [A.f32(512) for _ in range(NA)]; T_cb = [T() for _ in range(NA)]
            zc = [A.f32(512) for _ in range(NA)]; T_zc = [T() for _ in range(NA)]
            zw = [[A.f32(512) for _ in range(2)] for _ in range(2)]
            T_zw = [[T() for _ in range(2)] for _ in range(2)]
            zw2 = A.f32(512); T_zw2 = T()
            qb = [A.bf16(512) for _ in range(NA)]; T_qb = [T() for _ in range(NA)]
            kbA = [A.bf16(512) for _ in range(NA)]; T_kbA = [T() for _ in range(NA)]
            kf = [A.f32(512) for _ in range(2)]; T_kf = [T(), T()]
            vf = [A.f32(512) for _ in range(2)]; T_vf = [T(), T()]
            hz = A.f32(512); T_hz = T()
            hzw = [A.f32(512) for _ in range(2)]; T_hzw = [T(), T()]

            for a in range(NA):
                b2 = a % 2
                act(lambda e, a=a: e.activation(out=junkA, in_=xo[a], func=AF.Square, accum_out=rsA[:, a:a + 1]),
                    [T_xo[a]], [T_rsA[a]])
                dve(lambda e, a=a, b2=b2: e.tensor_copy(out=xbA[b2], in_=xo[a]), [T_xo[a]], [T_xbA[b2]])
                rstd_from_ss(rsA[:, a:a + 1], D, rsA[:, a:a + 1], [T_rsA[a]])
                transpose_to(lambda pv, a=a: act(lambda e: e.activation(out=xTA[a], in_=pv, func=AF.Copy),
                                                 [], [TB[a], T_xTA[a]]),
                             xbA[b2], 128, 8, a, T_xbA[b2], None)

            if not is_sample:
                xh = A.f32(D); T_xh = T()
                xhb = A.bf16(D)
                xhT = A.bf16(8, 2); T_xhT = T()
                dma(xh[0:2], x_halo[j], w=[T_xh], sem="xh")
                act(lambda e: e.activation(out=junkA[0:2], in_=xh[0:2], func=AF.Square,
                                           accum_out=rsA[0:2, NA:NA + 1]), [T_xh], [T_rsA[NA]])
                pool(lambda e: e.tensor_copy(out=xhb[0:2], in_=xh[0:2]), [T_xh], [T_xh])
                rstd_from_ss(rsA[0:2, NA:NA + 1], D, rsA[0:2, NA:NA + 1], [T_rsA[NA]])
                transpose_to(lambda pv: act(lambda e: e.activation(out=xhT, in_=pv, func=AF.Copy),
                                            [], [TB[4], T_xhT]),
                             xhb, 2, 8, 4, T_xh, None)
                rsh = rsA[0:2, NA:NA + 1]

                def evac_halo(cg, a, bk):
                    if cg == 1:
                        act(lambda e: e.activation(out=hzw[0][0:2], in_=bank(bk)[0:2, :], func=AF.Copy, scale=rsh),
                            [T_rsA[NA]], [TB[bk], T_hzw[0]])
                    else:
                        dve(lambda e: e.scalar_tensor_tensor(out=hz[0:2], in0=bank(bk)[0:2, :], scalar=rsh,
                                                             in1=hzw[0][0:2], op0=ALU.mult, op1=ALU.mult),
                            [T_rsA[NA], T_hzw[0]], [TB[bk], T_hz])
                stream_tokmajor(ws_in, 4, 4, lambda a, kc: xhT[:, kc, :], 1, evac_halo, lambda a: [T_xhT],
                                [[5], [6]], M=2, cgs=[1, 2])
                NH = 2
            else:
                dma(hz[0:8], hist_s, w=[T_hz], sem="xh")
                NH = 8

            def evac_z(cg, a, bk):
                rs = rsA[:, a:a + 1]
                b2 = a % 2
                if cg == 0:
                    act(lambda e: e.activation(out=cb[a], in_=bank(bk), func=AF.Copy, scale=rs),
                        [T_rsA[a]], [TB[bk], T_cb[a]])
                elif cg == 1:
                    act(lambda e: e.activation(out=zc[a], in_=bank(bk), func=AF.Copy, scale=rs),
                        [T_rsA[a]], [TB[bk], T_zc[a]])
                elif cg == 2:
                    dve(lambda e: e.scalar_tensor_tensor(out=zc[a], in0=bank(bk), scalar=rs, in1=zc[a],
                                                         op0=ALU.mult, op1=ALU.mult),
                        [T_rsA[a]], [TB[bk], T_zc[a]])
                elif cg == 3:
                    dve(lambda e: e.tensor_scalar(out=qb[a], in0=bank(bk), scalar1=rs, scalar2=None, op0=ALU.mult),
                        [T_rsA[a]], [TB[bk], T_qb[a]])
                elif cg == 4:
                    act(lambda e: e.activation(out=kf[b2], in_=bank(bk), func=AF.Copy, scale=rs),
                        [T_rsA[a]], [TB[bk], T_kf[b2]])
                    dve(lambda e: e.tensor_copy(out=kbA[a], in_=kf[b2]), [T_kf[b2]], [T_kbA[a]])
                    dma(k_own[ob0 + a], kf[b2], r=[T_kf[b2]], sem="kf%d" % b2)
                else:
                    act(lambda e: e.activation(out=vf[b2], in_=bank(bk), func=AF.Copy, scale=rs),
                        [T_rsA[a]], [TB[bk], T_vf[b2]])
                    vdst = vown.rearrange("p (a h) n -> p a h n", a=NA)[:, a, :, 0:128]
                    pool(lambda e: e.tensor_copy(out=vdst, in_=vf[b2].rearrange("p (h n) -> p h n", h=4)),
                         [T_vf[b2]], [T_vo])
                    dma(v_own[ob0 + a], vf[b2], r=[T_vf[b2]], sem="vf%d" % b2)

            stream_tokmajor(ws_in, 4, 4, lambda a, kc: xTA[a][:, kc, :], NA, evac_z, lambda a: [T_xTA[a]], SETS4)
            stream_tokmajor(None, 2, 4, lambda a, kc: xTA[a][:, kc, :], NA,
                            lambda cg, a, bk: evac_z(cg + 4, a, bk), lambda a: [T_xTA[a]], SETS4,
                            wres=lambda cg, kc: (wkv[:, kc, cg * 512:(cg + 1) * 512], T_wkv))

            w0b, w1b, w2b, bbb = convbc[:, 0, :], convbc[:, 1, :], convbc[:, 2, :], convbc[:, 3, :]
            pool(lambda e: e.tensor_tensor(out=hzw[0][0:NH], in0=hz[0:NH], in1=w1b[0:NH], op=ALU.mult),
                 [T_hz, T_c], [T_hzw[0]])
            pool(lambda e: e.tensor_tensor(out=hzw[1][0:NH], in0=hz[0:NH], in1=w0b[0:NH], op=ALU.mult),
                 [T_hz, T_c], [T_hzw[1]])
            if is_sample:
                m_sh1, m_sh2, m_h1, m_h2 = 6, 7, 8, 9
            else:
                m_sh1, m_sh2, m_h1, m_h2 = 0, 1, 4, 5
            for a in range(NA):
                b2 = a % 2

                def q_evac(pv, a=a, b2=b2):
                    dve(lambda e: e.tensor_copy(out=qpad[0][0:64, :, a * 128:(a + 1) * 128], in_=pv[0:64]),
                        [], [TB[0 + b2], T_q])
                    dve(lambda e: e.tensor_copy(out=qpad[1][64:128, :, a * 128:(a + 1) * 128], in_=pv[64:128]),
                        [], [TB[0 + b2], T_q])
                transpose_to(q_evac, qb[a], 128, 4, 0 + b2, T_qb[a], None)
                transpose_to(lambda pv, a=a, b2=b2: act(
                    lambda e: e.activation(out=ktown[:, :, a * 128:(a + 1) * 128], in_=pv, func=AF.Copy),
                    [], [TB[2 + b2], T_kt]), kbA[a], 128, 4, 2 + b2, T_kbA[a], None)
                dve(lambda e, a=a, b2=b2: e.tensor_tensor(out=zw[b2][0], in0=zc[a], in1=w1b, op=ALU.mult),
                     [T_zc[a], T_c], [T_zw[b2][0]])
                dve(lambda e, a=a, b2=b2: e.tensor_tensor(out=zw[b2][1], in0=zc[a], in1=w0b, op=ALU.mult),
                     [T_zc[a], T_c], [T_zw[b2][1]])
                dve(lambda e, a=a: e.tensor_tensor(out=zw2, in0=zc[a], in1=w2b, op=ALU.mult),
                     [T_zc[a], T_c], [T_zw2])
                dve(lambda e: e.tensor_tensor(out=zw2, in0=zw2, in1=bbb, op=ALU.add), [T_c], [T_zw2])
                bk = 4 + b2
                pe(lambda e, b2=b2, bk=bk: e.matmul(bank(bk), lhsT=shm[:, m_sh1, :], rhs=zw[b2][0], start=True,
                                                    stop=False), [T_c, T_zw[b2][0]], [TB[bk]])
                pe(lambda e, b2=b2, bk=bk: e.matmul(bank(bk), lhsT=shm[:, m_sh2, :], rhs=zw[b2][1], start=False,
                                                    stop=False), [T_c, T_zw[b2][1]], [TB[bk]])
                if a == 0:
                    pe(lambda e, bk=bk: e.matmul(bank(bk), lhsT=shm[0:NH, m_h1, :], rhs=hzw[0][0:NH], start=False,
                                                 stop=False), [T_c, T_hzw[0]], [TB[bk]])
                    pe(lambda e, bk=bk: e.matmul(bank(bk), lhsT=shm[0:NH, m_h2, :], rhs=hzw[1][0:NH], start=False,
                                                 stop=True), [T_c, T_hzw[1]], [TB[bk]])
                else:
                    p2 = 1 - b2
                    pe(lambda e, bk=bk, p2=p2: e.matmul(bank(bk), lhsT=shm[:, 2, :], rhs=zw[p2][0], start=False,
                                                        stop=False), [T_c, T_zw[p2][0]], [TB[bk]])
                    pe(lambda e, bk=bk, p2=p2: e.matmul(bank(bk), lhsT=shm[:, 3, :], rhs=zw[p2][1], start=False,
                                                        stop=True), [T_c, T_zw[p2][1]], [TB[bk]])
                dve(lambda e, bk=bk: e.tensor_tensor(out=zw2, in0=bank(bk), in1=zw2, op=ALU.add),
                    [], [TB[bk], T_zw2])
                dve(lambda e, a=a: e.tensor_tensor(out=mixb[a][:, 0:512], in0=zw2, in1=cb[a], op=ALU.mult),
                    [T_cb[a]], [T_zw2, T_mix[a]])
            if is_sample:
                for b in range(BPC):
                    dma(conv_s[2 * b:2 * b + 2, :], zc[0][32 * b + 30:32 * b + 32, :], r=[T_zc[0]], sem="cvo")
            elif j == NSLOT - 1:
                dma(conv_p, zc[3][126:128, :], r=[T_zc[3]], sem="cvo")
            kb.barrier()
            A.top = markA

            if late_jobs:
                for i_ in range(NWS):
                    wst_f[i_] = A.f32(4096)
                    wst_b[i_] = A.bf16(4096)
            ring_plan(plan_C())
            deferred_prefetch = [ring_prefetch]
            if not is_sample:
                nxt = SLOT_ORDER.index(j) + 1
                if nxt < NSLOT:
                    deferred_prefetch.append(lambda: load_xo(SLOT_ORDER[nxt], False))
                else:
                    deferred_prefetch.append(lambda: load_xo(0, True))
            PTs = [A.bf16(2, 512) for _ in range(3)]; T_pt = [T() for _ in range(3)]
            dtmp = [A.f32(128) for _ in range(2)]; T_dtmp = [T(), T()]
            o_t = [A.f32(128) for _ in range(2)]; T_o = [T(), T()]
            junko = A.bf16(128); T_junko = T()
            rr = [A.f32(4) for _ in range(2)]; T_rr = [T(), T()]
            ucount = [0]
            dcount = [0]
            ecount = [0]

            def accv(i):
                return bank(4 + i // 3)[:, (i % 3) * 130:(i % 3) * 130 + 129], 4 + i // 3

            class U:
                pass

            def sreg(sbp):
                return ps_t[:, sbp * 1024:(sbp + 1) * 1024]

            hook_n = [0]
            hook_stage = []

            def unit_hook():
                hook_n[0] += 1
                if hook_n[0] >= 6 and deferred_prefetch and not (late_jobs or hook_stage):
                    for f in deferred_prefetch:
                        f()
                    del deferred_prefetch[:]
                if late_jobs or hook_stage:
                    if hook_n[0] % 10 == 0:
                        nxt_ = None
                        if late_jobs:
                            ld, cs, stf = late_jobs.pop(0)
                            ld()
                            nxt_ = (cs, stf)
                        if hook_stage:
                            cs, stf = hook_stage.pop(0)
                            cs(); stf()
                        if nxt_ is not None:
                            hook_stage.append(nxt_)

            def run_units(units, h, qbase_of, np_q):
                def emit_S(u):
                    if getattr(u, "load", None) is not None:
                        u.load()
                    sbp = u.idx % 2
                    R = sreg(sbp)
                    for (m, rhs, col, n) in u.smm:
                        pe(lambda e, u=u, rhs=rhs, col=col, n=n, R=R: e.matmul(
                            R[:, col:col + n], lhsT=u.kt, rhs=rhs, start=True, stop=True),
                           [u.Tkt, T_q], [TB[2 * sbp], TB[2 * sbp + 1]])

                def emit_exp(u):
                    sbp = u.idx % 2
                    ps = u.idx % 3
                    R = sreg(sbp)
                    PT = PTs[ps].rearrange("p a b -> p (a b)")
                    for (kind, lo, hi, arg) in u.exps:
                        if kind == "fast":
                            act(lambda e, lo=lo, hi=hi, arg=arg, R=R, PT=PT: e.activation(
                                out=PT[:, lo:hi], in_=R[:, lo:hi], func=AF.Exp, scale=0.125, bias=arg),
                                [T_c], [TB[2 * sbp], TB[2 * sbp + 1], T_pt[ps]])
                        elif kind == "fast2":
                            Rv = R.rearrange("p (a b) -> p a b", a=2)[:, :, lo:hi]
                            Pv = PTs[ps][:, :, lo:hi]
                            act(lambda e, arg=arg, Rv=Rv, Pv=Pv: e.activation(
                                out=Pv, in_=Rv, func=AF.Exp, scale=0.125, bias=arg),
                                [T_c], [TB[2 * sbp], TB[2 * sbp + 1], T_pt[ps]])
                        else:
                            k = dcount[0] % 2
                            dcount[0] += 1
                            dve(lambda e, lo=lo, hi=hi, arg=arg, k=k, R=R: e.scalar_tensor_tensor(
                                out=dtmp[k][:, 0:hi - lo], in0=R[:, lo:hi], scalar=0.125, in1=arg,
                                op0=ALU.mult, op1=ALU.add), [T_c], [TB[2 * sbp], TB[2 * sbp + 1], T_dtmp[k]])
                            act(lambda e, lo=lo, hi=hi, k=k, PT=PT: e.activation(
                                out=PT[:, lo:hi], in_=dtmp[k][:, 0:hi - lo], func=AF.Exp),
                                [T_dtmp[k]], [T_pt[ps]])

                def emit_PV(u):
                    ps = u.idx % 3
                    PT = PTs[ps].rearrange("p a b -> p (a b)")
                    for (i, col, wq, st, sp_) in u.pv:
                        av, abk = accv(i)
                        pe(lambda e, u=u, av=av, col=col, wq=wq, st=st, sp_=sp_, PT=PT: e.matmul(
                            av[0:wq, :], lhsT=PT[:, col:col + wq], rhs=u.v, start=st, stop=sp_),
                           [T_pt[ps], u.Tv], [TB[abk]])

                for ui, u in enumerate(units):
                    u.idx = ucount[0]
                    ucount[0] += 1
                for ui, u in enumerate(units):
                    if ui == 0:
                        emit_S(u)
                        if len(units) > 1:
                            emit_S(units[1])
                    emit_exp(u)
                    if ui + 2 < len(units):
                        emit_S(units[ui + 2])
                    emit_PV(u)
                    unit_hook()

            def ptcol(h, m, qi):
                if h == 0:
                    return (qi // 2) * 512 + m * 256 + (qi % 2) * 128
                return m * 512 + qi * 128

            def evac_head(h, np_q, dst_fn, accs):
                for (a, i1, i2) in accs:
                    k2 = ecount[0] % 2
                    ecount[0] += 1
                    O1, b1 = accv(i1)
                    O2, b2_ = accv(i2)
                    r = rr[k2]
                    dve(lambda e, O1=O1, r=r: e.reciprocal(out=r[0:np_q, 0:1], in_=O1[0:np_q, 128:129]),
                        [], [TB[b1], T_rr[k2]])
                    dve(lambda e, O2=O2, r=r: e.reciprocal(out=r[0:np_q, 1:2], in_=O2[0:np_q, 128:129]),
                        [], [TB[b2_], T_rr[k2]])
                    dve(lambda e, r=r: e.tensor_scalar(out=r[0:np_q, 1:2], in0=r[0:np_q, 1:2],
                                                       scalar1=lam_t[0:np_q, 1:2], scalar2=None, op0=ALU.mult),
                        [T_lam], [T_rr[k2]])
                    dve(lambda e, O1=O1, r=r, k2=k2: e.tensor_scalar(out=o_t[k2][0:np_q], in0=O1[0:np_q, 0:128],
                                                                     scalar1=r[0:np_q, 0:1], scalar2=None,
                                                                     op0=ALU.mult),
                        [T_rr[k2]], [TB[b1], T_o[k2]])
                    dve(lambda e, O2=O2, r=r, k2=k2: e.scalar_tensor_tensor(
                        out=o_t[k2][0:np_q], in0=O2[0:np_q, 0:128], scalar=r[0:np_q, 1:2], in1=o_t[k2][0:np_q],
                        op0=ALU.mult, op1=ALU.add), [T_rr[k2]], [TB[b2_], T_o[k2]])
                    act(lambda e, r=r, k2=k2: e.activation(out=junko[0:np_q], in_=o_t[k2][0:np_q], func=AF.Square,
                                                           accum_out=r[0:np_q, 2:3]),
                        [T_o[k2]], [T_rr[k2]])
                    rstd_from_ss(r[0:np_q, 2:3], 128, r[0:np_q, 2:3], [T_rr[k2]])
                    dst = dst_fn(a)
                    dve(lambda e, r=r, k2=k2, dst=dst: e.scalar_tensor_tensor(
                        out=dst, in0=o_t[k2][0:np_q], scalar=r[0:np_q, 2:3], in1=gsub[0:np_q],
                        op0=ALU.mult, op1=ALU.mult), [T_rr[k2], T_c, T_o[k2]], dst_tokens[0])

            dst_tokens = [[]]

            if not is_sample:
                NKV = CFG.get("NKV", 6)
                kvK = [A.bf16(512) for _ in range(NKV)]
                kvV = [A.bf16(4, 130) for _ in range(NKV)]
                T_kv = [T() for _ in range(NKV)]
                kvpos = [0]
                for h in range(H):
                    units = []
                    for ch in range(E[j] // 4):
                        s = kvpos[0] % NKV
                        kvpos[0] += 1

                        def load_chunk(s=s, h=h, ch=ch):
                            dma(kvK[s], kt_scr[h, :, ch * 512:(ch + 1) * 512], w=[T_kv[s]], sem="kv%d" % s)
                            dma(kvV[s].rearrange("p k n -> p (k n)"),
                                v_scr[h].rearrange("p k n -> p (k n)")[:, ch * 520:(ch + 1) * 520],
                                w=[T_kv[s]], sem="kv%d" % s)
                        for i in range(4):
                            kbi = 4 * ch + i
                            u = U()
                            u.load = load_chunk if i == 0 else None
                            u.kt = kvK[s][:, i * 128:(i + 1) * 128]
                            u.Tkt = T_kv[s]
                            u.v = kvV[s][:, i, 0:129]
                            u.Tv = T_kv[s]
                            first = (kbi == 0)
                            if h == 0:
                                u.smm = [(m, qpad[m][:, h, hf * 256:(hf + 1) * 256], hf * 512 + m * 256, 256)
                                         for hf in range(2) for m in range(2)]
                                u.exps = [("fast", hf * 512, (hf + 1) * 512,
                                           btab[:, BIDX[(j, h, kbi, hf)]:BIDX[(j, h, kbi, hf)] + 1])
                                          for hf in range(2)]
                            else:
                                u.smm = [(m, qpad[m][:, h, :], m * 512, 512) for m in range(2)]
                                u.exps = [("fast", 0, 1024, btab[:, BIDX[(j, h, kbi, 0)]:BIDX[(j, h, kbi, 0)] + 1])]
                            u.pv = [(m * 4 + qi, ptcol(h, m, qi), 128, first and ((m * 4 + qi) % 3 == 0), False)
                                    for m in range(2) for qi in range(4)]
                            units.append(u)
                    for ap_ in range(4):
                        u = U()
                        u.kt = ktown[:, h, ap_ * 128:(ap_ + 1) * 128]
                        u.Tkt = T_kt
                        u.v = vown[:, ap_ * 4 + h, 0:129]
                        u.Tv = T_vo
                        u.smm = []
                        for m in range(2):
                            if h == 0:
                                for hf in range(2):
                                    qlo = max(ap_, 2 * hf)
                                    qhi = 2 * hf + 2
                                    if qlo < qhi:
                                        u.smm.append((m, qpad[m][:, h, qlo * 128:qhi * 128],
                                                      hf * 512 + m * 256 + (qlo % 2) * 128, (qhi - qlo) * 128))
                            else:
                                u.smm.append((m, qpad[m][:, h, ap_ * 128:512], m * 512 + ap_ * 128,
                                              (4 - ap_) * 128))
                        u.exps = []
                        u.pv = []
                        for m in range(2):
                            for qi in range(ap_, 4):
                                c = ptcol(h, m, qi)
                                if qi == ap_:
                                    u.exps.append(("diag", c, c + 128, bdiag[:, h * 4 + ap_, :]))
                                else:
                                    c0 = h * 16 + ap_ * 4 + qi
                                    u.exps.append(("fast", c, c + 128, dbias[:, c0:c0 + 1]))
                                u.pv.append((m * 4 + qi, c, 128, False, qi == ap_))
                        units.append(u)
                    run_units(units, h, None, 128)
                    dst_tokens[0] = T_mix
                    evac_head(h, 128, lambda a, h=h: mixb[a][:, 512 + h * 128:512 + (h + 1) * 128],
                              [(a, a, 4 + a) for a in range(4)])
            else:
                ckf = [A.f32(PAST) for _ in range(2)]; T_ckf = [T(), T()]
                ckbs = [A.bf16(4, PAST) for _ in range(2)]; T_ckbs = [T(), T()]
                cvf = [A.f32(4, 512) for _ in range(2)]; T_cvf = [T(), T()]
                cvbs = [A.bf16(NKBS * 4, 130) for _ in range(2)]; T_cvbs = [T(), T()]
                yab = [A.bf16(512) for _ in range(2)]; T_yab = [T(), T()]
                selT = A.bf16(4, 128); T_sel = T()
                dve(lambda e: e.memset(selT, 0.0), [], [T_sel])
                for b in range(BPC):
                    dve(lambda e, b=b: e.tensor_copy(out=selT[0:32, b, 32 * b:32 * b + 32], in_=ident_b[0:32, 0:32]),
                        [T_identb], [T_sel])
                for s in range(2):
                    dve(lambda e, s=s: e.memset(cvbs[s][:, :, 128:129], 1.0), [], [T_cvbs[s]])
                cc = [0]

                def prep_b(b):
                    pb = b % 2
                    for h in range(H):
                        s = cc[0] % 2
                        cc[0] += 1
                        dma(ckf[s], ckT[b, h], w=[T_ckf[s]], sem="ckf%d" % s)
                        dve(lambda e, s=s, h=h: e.tensor_copy(out=ckbs[pb][:, h, :], in_=ckf[s]),
                            [T_ckf[s]], [T_ckbs[pb]])
                    for g4 in range(NKBS // 4):
                        s = cc[0] % 2
                        cc[0] += 1
                        dma(cvf[s], cv[b, g4 * 512:(g4 + 1) * 512, :].rearrange("(k p) n -> p k n", p=128),
                            w=[T_cvf[s]], sem="cvf%d" % s)
                        pool(lambda e, s=s, g4=g4: e.tensor_copy(
                            out=cvbs[pb][:, g4 * 16:(g4 + 1) * 16, 0:128],
                            in_=cvf[s].rearrange("p k (h n) -> p (k h) n", h=4)), [T_cvf[s]], [T_cvbs[pb]])

                prep_b(0)
                for b in range(BPC):
                    if b + 1 < BPC:
                        prep_b(b + 1)
                    ckb = ckbs[b % 2]; T_ckb = T_ckbs[b % 2]
                    cvb = cvbs[b % 2]; T_cvb = T_cvbs[b % 2]
                    units = []
                    started = set()
                    for h in range(H):
                        for kbi in range(NKBS + 1):
                            u = U()
                            own = (kbi == NKBS)
                            if own:
                                u.kt = ktown[:, h, 0:128]
                                u.Tkt = T_kt
                                u.v = vown[:, h, 0:129]
                                u.Tv = T_vo
                                u.exps = [("diag", m * 512, m * 512 + 32, bs[:, b * 4 + h, :]) for m in range(2)]
                            else:
                                u.kt = ckb[:, h, kbi * 128:(kbi + 1) * 128]
                                u.Tkt = T_ckb
                                u.v = cvb[:, kbi * 4 + h, 0:129]
                                u.Tv = T_cvb
                                c0 = h * NKBS + kbi
                                u.exps = [("fast2", 0, 32, sbias[:, c0:c0 + 1])]
                            u.smm = [(m, qpad[m][:, h, b * 32:(b + 1) * 32], m * 512, 32) for m in range(2)]
                            u.pv = []
                            for m in range(2):
                                i = m * 4 + h
                                bk_ = 4 + i // 3
                                st = (kbi == 0) and (bk_ not in started)
                                if kbi == 0:
                                    started.add(bk_)
                                u.pv.append((i, m * 512, 32, st, own))
                            units.append(u)
                    run_units(units, 0, None, 32)
                    dst_tokens[0] = [T_yab[b % 2]]
                    evac_head(0, 32, lambda a, b=b: yab[b % 2][0:32, a * 128:(a + 1) * 128],
                              [(h, h, 4 + h) for h in range(H)])
                    pe(lambda e, b=b: e.matmul(bank(7), lhsT=selT[0:32, b, :], rhs=yab[b % 2][0:32, :],
                                               start=(b == 0), stop=(b == BPC - 1)),
                       [T_sel, T_yab[b % 2]], [TB[7]])
                act(lambda e: e.activation(out=mixb[0][:, 512:1024], in_=bank(7), func=AF.Copy),
                    [], [TB[7], T_mix[0]])
            while hook_stage or late_jobs:
                if hook_stage:
                    cs, stf = hook_stage.pop(0)
                    cs(); stf()
                if late_jobs:
                    ld, cs, stf = late_jobs.pop(0)
                    ld()
                    hook_stage.append((cs, stf))
            for f in deferred_prefetch:
                f()
            kb.barrier()
            A.top = markQ

            if DEBUG:
                dbgt = [A.f32(D) for _ in range(NA)]
                for a in range(NA):
                    dve(lambda e, a=a: e.tensor_copy(out=dbgt[a], in_=mixb[a]), [T_mix[a]], [T_mix[a]])
                    dma(dbg_mix[ob0 + a], dbgt[a], r=[T_mix[a]], sem="dbg%d" % a)
            h1 = [A.f32(D) for _ in range(NA)]; T_h = [T() for _ in range(NA)]
            yt = [A.f32(D) for _ in range(NA)]; T_y = [T() for _ in range(NA)]
            TAb = A.bf16(8, W); T_TA = T()
            TBb = A.bf16(8, W); T_TBb = T()
            ffT = A.bf16(32, W); T_ff = T()
            pf = [A.f32(PLE) for _ in range(NA)]; T_pf = [T() for _ in range(NA)]
            pbf = [A.bf16(PLE) for _ in range(2)]; T_pbf = [T(), T()]
            pT = A.bf16(2, W); T_pT = T()
            cbf = [A.bf16(D) for _ in range(2)]; T_cbf = [T(), T()]
            rsC = A.f32(NA, 8); T_rsC = [T() for _ in range(NA)]
            rl = [A.f32(W) for _ in range(2)]; T_rl = [T(), T()]
            junkC = A.bf16(D); T_junkC = T()

            if not is_sample:
                ring_plan(plan_A(SLOT_ORDER.index(j) + 1 >= NSLOT))
            for a in range(NA):
                dma(h1[a], x_own[ob0 + a], w=[T_h[a]], sem="hx%d" % a)
                dma(pf[a], p_own[ob0 + a], w=[T_pf[a]], sem="pf%d" % a)

            def evac_T(dst, evi):
                def f(pv):
                    if evi % 2 == 0:
                        act(lambda e: e.activation(out=dst, in_=pv, func=AF.Copy), [], f.toks)
                    else:
                        dve(lambda e: e.tensor_copy(out=dst, in_=pv), [], f.toks)
                return f

            def norm_add(a, col, g_idx):
                rs = rsC[:, a, col:col + 1]
                act(lambda e: e.activation(out=junkC, in_=yt[a], func=AF.Square, accum_out=rs),
                    [T_y[a]], [T_rsC[a]])
                rstd_from_ss(rs, D, rs, [T_rsC[a]])
                dve(lambda e: e.scalar_tensor_tensor(out=yt[a], in0=yt[a], scalar=rs, in1=gbc[:, g_idx, :],
                                                     op0=ALU.mult, op1=ALU.mult), [T_rsC[a], T_c], [T_y[a]])
                dve(lambda e: e.tensor_tensor(out=h1[a], in0=h1[a], in1=yt[a], op=ALU.add), [T_y[a]], [T_h[a]])

            for a in range(NA):
                f = evac_T(TAb[:, :, a * 128:(a + 1) * 128], a)
                f.toks = [TB[a], T_TA]
                transpose_to(f, mixb[a], 128, 8, a, T_mix[a], None)
            def evac_norm(g_idx, scale_col):
                def f(cg, a, bk):
                    cs_ = slice(cg * 512, (cg + 1) * 512)
                    acc = rsC[:, a, 5 + cg:6 + cg]
                    if scale_col is None:
                        act(lambda e: e.activation(out=junkC[:, 0:512], in_=bank(bk), func=AF.Square, accum_out=acc),
                            [], [TB[bk], T_rsC[a]])
                        dve(lambda e: e.tensor_tensor(out=yt[a][:, cs_], in0=bank(bk), in1=gbc[:, g_idx, cs_],
                                                      op=ALU.mult), [T_c], [TB[bk], T_y[a]])
                    else:
                        sc = rsC[:, a, scale_col:scale_col + 1]
                        act(lambda e: e.activation(out=junkC[:, 0:512], in_=bank(bk), func=AF.Square, scale=sc,
                                                   accum_out=acc), [T_rsC[a]], [TB[bk], T_rsC[a]])
                        dve(lambda e: e.scalar_tensor_tensor(out=yt[a][:, cs_], in0=bank(bk), scalar=sc,
                                                             in1=gbc[:, g_idx, cs_], op0=ALU.mult, op1=ALU.mult),
                            [T_c, T_rsC[a]], [TB[bk], T_y[a]])
                return f

            def norm_add2(a, col):
                rs = rsC[:, a, col:col + 1]
                dve(lambda e: e.tensor_tensor(out=rs, in0=rsC[:, a, 5:6], in1=rsC[:, a, 6:7], op=ALU.add),
                    [], [T_rsC[a]])
                rstd_from_ss(rs, D, rs, [T_rsC[a]])
                dve(lambda e: e.scalar_tensor_tensor(out=h1[a], in0=yt[a], scalar=rs, in1=h1[a],
                                                     op0=ALU.mult, op1=ALU.add), [T_rsC[a], T_y[a]], [T_h[a]])

            stream_tokmajor(ws_out, 2, 4, lambda a, kc: TAb[:, kc, a * 128:(a + 1) * 128], NA,
                            evac_norm(0, None), lambda a: [T_TA], SETS4)
            for a in range(NA):
                norm_add2(a, 0)
            for a in range(NA):
                b2 = a % 2
                rs2 = rsC[:, a, 1:2]
                act(lambda e, a=a, rs2=rs2: e.activation(out=junkC, in_=h1[a], func=AF.Square, accum_out=rs2),
                    [T_h[a]], [T_rsC[a]])
                rstd_from_ss(rs2, D, rs2, [T_rsC[a]])
                dve(lambda e, a=a, rs2=rs2: e.tensor_tensor(out=rsC[:, a, 2:3], in0=rs2, in1=rs2, op=ALU.mult),
                    [], [T_rsC[a]])
                act(lambda e, a=a, b2=b2: e.activation(out=cbf[b2], in_=h1[a], func=AF.Copy), [T_h[a]], [T_cbf[b2]])
                f = evac_T(TBb[:, :, a * 128:(a + 1) * 128], a + 1)
                f.toks = [TB[4 + a], T_TBb]
                transpose_to(f, cbf[b2], 128, 8, 4 + a, T_cbf[b2], None)
            for fch in range(32):
                piece, Tp = ring_load(ws_up[fch])
                pv = piece.rearrange("p (k n) -> p k n", k=8)
                bk = fch % 2
                for kc in range(8):
                    pe(lambda e, pv=pv, kc=kc, bk=bk: e.matmul(bank(bk)[:, 0:W], lhsT=pv[:, kc, :],
                                                               rhs=TBb[:, kc, :], start=(kc == 0), stop=(kc == 7)),
                       [Tp, T_TBb], [TB[bk]])
                ring_topup()
                act(lambda e, bk=bk: e.activation(out=rl[bk], in_=bank(bk)[:, 0:W], func=AF.Relu),
                    [], [TB[bk], T_rl[bk]])
                dve(lambda e, bk=bk, fch=fch: e.tensor_tensor(out=ffT[:, fch, :], in0=rl[bk], in1=rl[bk],
                                                              op=ALU.mult), [T_rl[bk]], [T_ff])
            stream_tokmajor(ws_dn, 2, 16, lambda a, kc: ffT[:, kc, a * 128:(a + 1) * 128], NA,
                            evac_norm(1, 2), lambda a: [T_ff], SETS4)
            for a in range(NA):
                norm_add2(a, 3)
            for a in range(NA):
                b2 = a % 2
                act(lambda e, a=a, b2=b2: e.activation(out=cbf[b2], in_=h1[a], func=AF.Copy), [T_h[a]], [T_cbf[b2]])
                f = evac_T(TAb[:, :, a * 128:(a + 1) * 128], a)
                f.toks = [TB[a], T_TA]
                transpose_to(f, cbf[b2], 128, 8, a, T_cbf[b2], None)
                act(lambda e, a=a, b2=b2: e.activation(out=pbf[b2], in_=pf[a], func=AF.Copy), [T_pf[a]], [T_pbf[b2]])
                f = evac_T(pT[:, :, a * 128:(a + 1) * 128], a + 1)
                f.toks = [TB[4 + a], T_pT]
                transpose_to(f, pbf[b2], 128, 2, 4 + a, T_pbf[b2], None)
            stream_tokmajor(ws_g, 2, 4, lambda a, kc: TAb[:, kc, a * 128:(a + 1) * 128], NA,
                            lambda cg, a, bk: act(lambda e: e.activation(
                                out=yt[a][:, cg * 512:(cg + 1) * 512], in_=bank(bk), func=AF.Exp, scale=-1.0),
                                [], [TB[bk], T_y[a]]),
                            lambda a: [T_TA], SETS4)
            for a in range(NA):
                dve(lambda e, a=a: e.tensor_scalar(out=yt[a], in0=yt[a], scalar1=1.0, scalar2=None, op0=ALU.add),
                    [], [T_y[a]])
                dve(lambda e, a=a: e.reciprocal(out=yt[a], in_=yt[a]), [], [T_y[a]])
            stream_tokmajor(ws_pe, 2, 1, lambda a, kc: pT[:, kc, a * 128:(a + 1) * 128], NA,
                            lambda cg, a, bk: dve(lambda e: e.tensor_tensor(
                                out=yt[a][:, cg * 512:(cg + 1) * 512], in0=bank(bk),
                                in1=yt[a][:, cg * 512:(cg + 1) * 512], op=ALU.mult), [], [TB[bk], T_y[a]]),
                            lambda a: [T_pT], SETS4)
            for a in range(NA):
                norm_add(a, 4, 2)
                dma(y_own[ob0 + a], h1[a], r=[T_h[a]], sem="yo%d" % a)
            kb.barrier()
            A.top = markS

        for si, j in enumerate(SLOT_ORDER):
            do_slot(j, False)
            if si == 0:
                assert not late_jobs
        do_slot(0, True)
        kb.emit()
    return nc


_PROG_CACHE = {}


def kernel(x_prompt, x_sample, cache_k, cache_v, state_conv, p_prompt, p_sample,
           w_in, w_conv, b_conv, lambda_q1, lambda_k1, lambda_q2, lambda_k2, g_subln,
           w_out, g_pre_mix, g_post_mix, g_pre_mlp, g_post_mlp, w_up, w_down,
           w_pe, w_pe_gate, g_pe):
    NSLOT = CFG["NSLOT"]
    PAST = CFG["PAST"]
    f = np.float32
    A_ = lambda v: np.ascontiguousarray(np.asarray(v), dtype=f)
    x_prompt = A_(x_prompt); x_sample = A_(x_sample); cache_k = A_(cache_k); cache_v = A_(cache_v)
    state_conv = A_(state_conv); p_prompt = A_(p_prompt); p_sample = A_(p_sample)
    SEQ = x_prompt.shape[1]
    assert SEQ == NCORE * NSLOT * 512 and cache_k.shape[2] == PAST
    NOWN = 4 * NSLOT + 1
    key = (NSLOT, PAST)
    if key not in _PROG_CACHE:
        _PROG_CACHE[key] = build_program(NSLOT, PAST)
    nc = _PROG_CACHE[key]

    def bc(v, n=128):
        v = A_(v).reshape(1, -1)
        return np.ascontiguousarray(np.broadcast_to(v, (n, v.shape[1])))

    gvec = np.concatenate([A_(g_pre_mix)[0].reshape(8, 128).T, A_(g_pre_mlp)[0].reshape(8, 128).T], axis=1)
    gbc = np.concatenate([bc(g_post_mix[0]), bc(g_post_mlp[0]), bc(g_pe[0])], axis=1)
    convbc = np.concatenate([bc(A_(w_conv)[0, 0]), bc(A_(w_conv)[0, 1]), bc(A_(w_conv)[0, 2]), bc(A_(b_conv)[0])],
                            axis=1)
    lamv = np.concatenate([bc(lambda_q1[0]), bc(lambda_k1[0]), bc(lambda_q2[0]), bc(lambda_k2[0])], axis=1)
    shared = dict(
        x_all=x_prompt[0], w_in=A_(w_in)[0], w_out=A_(w_out)[0], w_up=A_(w_up)[0], w_down=A_(w_down)[0],
        w_pe=A_(w_pe)[0], w_g=A_(w_pe_gate)[0], gvec=np.ascontiguousarray(gvec), gbc=np.ascontiguousarray(gbc),
        gsub=bc(g_subln[0]), convbc=np.ascontiguousarray(convbc), lamv=np.ascontiguousarray(lamv),
        ident=np.eye(128, dtype=f))
    in_maps = []
    for c in range(NCORE):
        tb = make_tables(c, NSLOT, PAST)
        xo = np.zeros((NOWN, 128, D), f)
        po = np.zeros((NOWN, 128, PLE), f)
        xh = np.zeros((NSLOT, 2, D), f)
        for j in range(NSLOT):
            sb = NCORE * j + c
            xo[4 * j:4 * j + 4] = x_prompt[0, sb * 512:(sb + 1) * 512].reshape(4, 128, D)
            po[4 * j:4 * j + 4] = p_prompt[0, 0, sb * 512:(sb + 1) * 512].reshape(4, 128, PLE)
            if sb > 0:
                xh[j] = x_prompt[0, sb * 512 - 2:sb * 512]
        bsl = slice(c * BPC, (c + 1) * BPC)
        xo[4 * NSLOT] = x_sample[bsl].reshape(128, D)
        po[4 * NSLOT] = p_sample[0, bsl].reshape(128, PLE)
        m = dict(shared)
        m.update(x_own=xo, p_own=po, x_halo=xh,
                 hist_s=np.ascontiguousarray(state_conv[0, bsl].reshape(8, 512)),
                 ckT=np.ascontiguousarray(cache_k[0, bsl].transpose(0, 2, 3, 1)),
                 cv=np.ascontiguousarray(cache_v[0, bsl].reshape(BPC, PAST, 512)),
                 **tb)
        in_maps.append(m)
    res = run_bass_kernel_spmd(nc, in_maps, core_ids=list(range(NCORE)))
    R = res.results
    if CFG.get("DEBUG", False):
        CFG["_dbg"] = [r["dbg_mix"] for r in R]
        CFG["_kt"] = [np.asarray(r["kt_scr"]).astype(np.float32) for r in R]
        CFG["_v"] = [np.asarray(r["v_scr"]).astype(np.float32) for r in R]
    y_prompt = np.zeros((1, SEQ, D), f)
    k_prompt = np.zeros((1, 1, SEQ, H, 128), f)
    v_prompt = np.zeros((1, 1, SEQ, H, 128), f)
    y_sample = np.zeros((NCORE * BPC, DEC_T, D), f)
    k_sample = np.zeros((1, NCORE * BPC, DEC_T, H, 128), f)
    v_sample = np.zeros((1, NCORE * BPC, DEC_T, H, 128), f)
    conv_sample = np.zeros((1, NCORE * BPC, 2, 512), f)
    for c in range(NCORE):
        r = R[c]
        for j in range(NSLOT):
            sb = NCORE * j + c
            sl = slice(sb * 512, (sb + 1) * 512)
            y_prompt[0, sl] = r["y_own"][4 * j:4 * j + 4].reshape(512, D)
            k_prompt[0, 0, sl] = r["k_own"][4 * j:4 * j + 4].reshape(512, H, 128)
            v_prompt[0, 0, sl] = r["v_own"][4 * j:4 * j + 4].reshape(512, H, 128)
        bsl = slice(c * BPC, (c + 1) * BPC)
        y_sample[bsl] = r["y_own"][4 * NSLOT].reshape(BPC, DEC_T, D)
        k_sample[0, bsl] = r["k_own"][4 * NSLOT].reshape(BPC, DEC_T, H, 128)
        v_sample[0, bsl] = r["v_own"][4 * NSLOT].reshape(BPC, DEC_T, H, 128)
        conv_sample[0, bsl] = r["conv_s"].reshape(BPC, 2, 512)
    conv_prompt = np.ascontiguousarray(R[NCORE - 1]["conv_p"]).reshape(1, 1, 2, 512).astype(f)
    return (y_prompt, y_sample, k_prompt, v_prompt, conv_prompt, k_sample, v_sample, conv_sample)
```

```python
import math
from contextlib import ExitStack
import numpy as np
import concourse.bass as bass
import concourse.mybir as mybir
from concourse.bass_utils import run_bass_kernel_spmd

F32 = mybir.dt.float32
BF16 = mybir.dt.bfloat16
AF = mybir.ActivationFunctionType
ALU = mybir.AluOpType

NCORE = 8
D = 1024
DFF = 4096
PLE = 256
H = 4
DEC_T = 32
BPC = 4
RMS_EPS = 1e-6
NEG = -30000.0
SLOPES = [2.0 ** (-8.0 * (h + 1) / 4) for h in range(4)]
LAM_INIT = 0.8 - 0.6 * math.exp(-0.3 * 0)
CFG = dict(NSLOT=4, PAST=2048)

ENGS = ["pe", "act", "dve", "pool", "sp"]


class T:
    __slots__ = ("w", "rs")

    def __init__(self):
        self.w = None
        self.rs = []


class Op:
    __slots__ = ("eng", "fn", "deps", "needed", "ev", "dma_sem", "epoch", "barrier")

    def __init__(self, eng, fn):
        self.eng = eng
        self.fn = fn
        self.deps = []
        self.needed = False
        self.ev = None
        self.dma_sem = None
        self.barrier = 0


class DSem:
    def __init__(self, sem):
        self.sem = sem
        self.n = 0


class KB:
    def __init__(self, nc, es):
        self.nc = nc
        self.es = es
        self.ops = []
        self.epoch = 0
        self.esem = {e: es.enter_context(nc.semaphore("es_" + e)) for e in ENGS[:4]}
        self.bsem = es.enter_context(nc.semaphore("bar"))
        self.dsems = {}
        self.nbar = 0

    def dsem(self, name):
        if name not in self.dsems:
            self.dsems[name] = DSem(self.es.enter_context(self.nc.semaphore("ds_" + name)))
        return self.dsems[name]

    def op(self, eng, fn, reads=(), writes=(), dsem=None):
        o = Op(eng, fn)
        o.epoch = self.epoch
        o.dma_sem = dsem
        deps = []
        for r in reads:
            if r.w is not None:
                deps.append(r.w)
        for w in writes:
            if w.w is not None:
                deps.append(w.w)
            deps.extend(w.rs)
        seen = set()
        for d in deps:
            if d is o or id(d) in seen or d.epoch < self.epoch:
                continue
            seen.add(id(d))
            if d.eng == eng and d.dma_sem is None and eng in ("pe", "sp"):
                continue
            if eng == "sp" and d.eng == "sp" and d.dma_sem is not None and d.dma_sem is dsem:
                continue
            o.deps.append(d)
            d.needed = True
        for r in reads:
            r.rs.append(o)
        for w in writes:
            w.w = o
            w.rs = []
        self.ops.append(o)
        return o

    def barrier(self):
        self.nbar += 1
        for e in ENGS:
            o = Op(e, None)
            o.barrier = self.nbar
            o.epoch = self.epoch
            self.ops.append(o)
        self.epoch += 1

    def emit(self):
        nc = self.nc
        per = {e: [] for e in ENGS}
        cnt = {e: 0 for e in ENGS}
        for o in self.ops:
            per[o.eng].append(o)
            if o.barrier:
                continue
            if o.dma_sem is not None:
                o.dma_sem.n += 1
                o.ev = (o.dma_sem, o.dma_sem.n * 16)
            elif o.needed:
                cnt[o.eng] += 1
                o.ev = (o.eng, cnt[o.eng])
        esem = self.esem
        bsem = self.bsem
        alld = list(self.dsems.values())

        def run(ename, eng):
            waited = {}
            dcount = {id(d): 0 for d in alld}
            for o in per[ename]:
                if o.barrier:
                    for d in alld:
                        if dcount[id(d)] > waited.get(id(d), 0):
                            eng.wait_ge(d.sem, dcount[id(d)])
                            waited[id(d)] = dcount[id(d)]
                    if ename == "sp":
                        eng.sem_inc(bsem, 1)
                    else:
                        eng.drain().then_inc(bsem, 1)
                    eng.wait_ge(bsem, 5 * o.barrier)
                    continue
                need = {}
                for d in o.deps:
                    key, val = d.ev
                    k = id(key) if isinstance(key, DSem) else key
                    if waited.get(k, 0) >= val:
                        continue
                    if k not in need or need[k][1] < val:
                        need[k] = (key, val)
                for k, (key, val) in need.items():
                    sem = key.sem if isinstance(key, DSem) else esem[key]
                    eng.wait_ge(sem, val)
                    waited[k] = val
                inst = o.fn(eng)
                if o.dma_sem is not None:
                    inst.then_inc(o.dma_sem.sem, 16)
                    dcount[id(o.dma_sem)] = o.ev[1]
                elif o.needed:
                    inst.then_inc(esem[o.eng], 1)
            if ename == "sp":
                for d in alld:
                    if d.n > 0:
                        eng.wait_ge(d.sem, d.n * 16)

        with nc.Block() as block:
            @block.tensor
            def _(eng):
                run("pe", eng)

            @block.scalar
            def _(eng):
                run("act", eng)

            @block.vector
            def _(eng):
                run("dve", eng)

            @block.gpsimd
            def _(eng):
                run("pool", eng)

            @block.sync
            def _(eng):
                run("sp", eng)


class Arena:
    def __init__(self, t, n):
        self.t = t
        self.n = n
        self.top = 0

    def _take(self, n32):
        off = self.top
        self.top += n32
        assert self.top <= self.n, ("SBUF arena overflow", self.top, self.n)
        return off

    def f32(self, *shape):
        n = int(np.prod(shape))
        off = self._take(n)
        v = self.t[:, off:off + n]
        return _shape(v, shape)

    def bf16(self, *shape):
        n = int(np.prod(shape))
        n32 = (n + 1) // 2
        off = self._take(n32)
        v = self.t[:, off:off + n32].bitcast(BF16)[:, 0:n]
        return _shape(v, shape)


def _shape(v, shape):
    if len(shape) == 1:
        return v
    if len(shape) == 2:
        return v.rearrange("p (a b) -> p a b", a=shape[0])
    if len(shape) == 3:
        return v.rearrange("p (a b c) -> p a b c", a=shape[0], b=shape[1])
    raise ValueError(shape)


def _hw(h):
    return 256 if h == 0 else 512


def _refoff(h, a):
    return 256 * (a // 2) if h == 0 else 0


def bias_layout(NSLOT):
    E = [4 * (NCORE * j + NCORE - 1) for j in range(NSLOT)]
    idx = {}
    n = 0
    for j in range(NSLOT):
        for h in range(H):
            for kb in range(E[j]):
                for half in range(2 if h == 0 else 1):
                    idx[(j, h, kb, half)] = n
                    n += 1
    return E, idx, n


def make_tables(c, NSLOT, PAST):
    E, idx, ncol = bias_layout(NSLOT)
    kl = np.arange(128, dtype=np.float64)
    btab = np.zeros((128, ncol), np.float64)
    for (j, h, kb, half), col in idx.items():
        sb = NCORE * j + c
        m = SLOPES[h]
        Cc = m * _hw(h) / 2
        ref = sb * 512 + half * 256
        if kb < 4 * sb:
            btab[:, col] = m * (kl + 128 * kb - ref) - Cc
        else:
            btab[:, col] = NEG
    btab = np.maximum(btab, NEG)
    dbias = np.zeros((128, H, 4, 4), np.float64)
    bdiag = np.zeros((128, H, 4, 128), np.float64)
    ql = np.arange(128, dtype=np.float64)
    for h in range(H):
        m = SLOPES[h]
        Cc = m * _hw(h) / 2
        for a in range(4):
            for ap in range(4):
                dbias[:, h, ap, a] = m * (kl + 128 * ap - _refoff(h, a)) - Cc
            vis = (kl[:, None] // 64) <= (ql[None, :] // 64)
            val = -m * np.abs(ql[None, :] - kl[:, None]) + m * (128 * a + ql[None, :] - _refoff(h, a)) - Cc
            bdiag[:, h, a, :] = np.where(vis, val, NEG)
    nkb = PAST // 128
    sbias = np.zeros((128, H, nkb), np.float64)
    bs = np.full((128, BPC, H, DEC_T), NEG, np.float64)
    q32 = np.arange(DEC_T, dtype=np.float64)
    for h in range(H):
        m = SLOPES[h]
        Cs = m * 16
        for kb in range(nkb):
            sbias[:, h, kb] = m * (kl + 128 * kb - PAST) - Cs
        for b in range(BPC):
            for tp in range(DEC_T):
                bs[b * 32 + tp, b, h, :] = -m * np.abs(q32 - tp) + m * q32 - Cs
    sbias = np.maximum(sbias, NEG)
    shm = np.zeros((128, 10, 128), np.float32)
    for t in range(128):
        if t >= 1:
            shm[t - 1, 0, t] = 1
        if t >= 2:
            shm[t - 2, 1, t] = 1
        if t % 32 != 0:
            shm[t - 1, 6, t] = 1
        if t % 32 >= 2:
            shm[t - 2, 7, t] = 1
    shm[127, 2, 0] = 1
    shm[126, 3, 0] = 1
    shm[127, 3, 1] = 1
    shm[1, 4, 0] = 1
    shm[0, 5, 0] = 1
    shm[1, 5, 1] = 1
    for b in range(BPC):
        shm[2 * b + 1, 8, 32 * b] = 1
        shm[2 * b, 9, 32 * b] = 1
        shm[2 * b + 1, 9, 32 * b + 1] = 1
    f = np.float32
    return dict(btab=btab.astype(f), dbias=dbias.reshape(128, -1).astype(f),
                bdiag=bdiag.reshape(128, -1).astype(f), sbias=sbias.reshape(128, -1).astype(f),
                bs=bs.reshape(128, -1).astype(f), shm=shm.reshape(128, -1))


def build_program(NSLOT, PAST):
    SEQ = NCORE * NSLOT * 512
    NBLK = SEQ // 128
    NOWN = 4 * NSLOT + 1
    NKBS = PAST // 128
    E, BIDX, NCOL = bias_layout(NSLOT)
    EMAX = E[-1]
    NG = EMAX // 4

    nc = bass.Bass("TRN2", target_bir_lowering=False)

    def din(name, shape, dt=F32):
        return nc.dram_tensor(name, list(shape), dt, kind="ExternalInput").ap()

    def dout(name, shape):
        return nc.dram_tensor(name, list(shape), F32, kind="ExternalOutput").ap()

    def dscr(name, shape):
        return nc.dram_tensor(name, list(shape), BF16, kind="Internal").ap()

    x_all = din("x_all", [SEQ, D])
    x_own = din("x_own", [NOWN, 128, D])
    x_halo = din("x_halo", [NSLOT, 2, D])
    p_own = din("p_own", [NOWN, 128, PLE])
    hist_s = din("hist_s", [8, 512])
    ckT = din("ckT", [BPC, H, 128, PAST])
    cv = din("cv", [BPC, PAST, 512])
    w_in = din("w_in", [D, 3072])
    w_out = din("w_out", [D, D])
    w_up = din("w_up", [D, DFF])
    w_down = din("w_down", [DFF, D])
    w_pe = din("w_pe", [PLE, D])
    w_g = din("w_g", [D, D])
    gvec_d = din("gvec", [128, 16])
    gbc_d = din("gbc", [128, 3 * D])
    gsub_d = din("gsub", [128, 128])
    convbc_d = din("convbc", [128, 4 * 512])
    lamv_d = din("lamv", [128, 4 * 64])
    ident_d = din("ident", [128, 128])
    shm_d = din("shm", [128, 10 * 128])
    btab_d = din("btab", [128, NCOL])
    dbias_d = din("dbias", [128, H * 16])
    bdiag_d = din("bdiag", [128, H * 4 * 128])
    sbias_d = din("sbias", [128, H * NKBS])
    bs_d = din("bs", [128, BPC * H * DEC_T])

    y_own = dout("y_own", [NOWN, 128, D])
    k_own = dout("k_own", [NOWN, 128, 512])
    v_own = dout("v_own", [NOWN, 128, 512])
    conv_p = dout("conv_p", [2, 512])
    conv_s = dout("conv_s", [8, 512])
    DEBUG = CFG.get("DEBUG", False)
    if DEBUG:
        dbg_mix = dout("dbg_mix", [NOWN, 128, D])

    if CFG.get("DEBUG", False):
        kt_scr = nc.dram_tensor("kt_scr", [H, 128, NG * 512], BF16, kind="ExternalOutput").ap()
        v_scr = nc.dram_tensor("v_scr", [H, 128, NG * 4, 130], BF16, kind="ExternalOutput").ap()
    else:
        kt_scr = dscr("kt_scr", [H, 128, NG * 512])
        v_scr = dscr("v_scr", [H, 128, NG * 4, 130])
    ws_in = dscr("ws_in", [4, 4, 128, 1024])
    ws_out = dscr("ws_out", [2, 4, 128, 1024])
    ws_g = dscr("ws_g", [2, 4, 128, 1024])
    ws_pe = dscr("ws_pe", [2, 1, 128, 1024])
    ws_dn = dscr("ws_dn", [2, 16, 128, 1024])
    ws_up = dscr("ws_up", [32, 128, 1024])

    ARENA_N = 53000
    with ExitStack() as es:
        kb = KB(nc, es)
        arena_t = es.enter_context(nc.sbuf_tensor("arena", [128, ARENA_N], F32))
        ps_t = es.enter_context(nc.psum_tensor("psum", [128, 8 * 512], F32))
        A = Arena(arena_t, ARENA_N)
        TB = [T() for _ in range(8)]

        def bank(b):
            return ps_t[:, b * 512:(b + 1) * 512]

        def bankb(b):
            return ps_t[:, b * 512:(b + 1) * 512].bitcast(BF16)

        def pe(fn, r=(), w=()):
            return kb.op("pe", fn, r, w)

        def act(fn, r=(), w=()):
            return kb.op("act", fn, r, w)

        def dve(fn, r=(), w=()):
            return kb.op("dve", fn, r, w)

        def pool(fn, r=(), w=()):
            return kb.op("pool", fn, r, w)

        def dma(out, in_, r=(), w=(), sem="const", q="sp"):
            return kb.op(q, lambda e: e.dma_start(out=out, in_=in_), r, w, dsem=kb.dsem(sem))

        ident_f = A.f32(128); T_identf = T()
        ident_b = A.bf16(128); T_identb = T()
        shm = A.f32(10, 128); T_shm = T()
        gvec = A.f32(16); T_gvec = T()
        gbc = A.f32(3, D); T_gbc = T()
        gsub = A.f32(128); T_gsub = T()
        convbc = A.f32(4, 512); T_convbc = T()
        btab = A.f32(NCOL); T_btab = T()
        dbias = A.f32(H * 16)
        bdiag = A.f32(H * 4, 128)
        sbias = A.f32(H * NKBS)
        bs = A.f32(BPC * H, DEC_T)
        lam_t = A.f32(4); T_lam = T()
        wkv = A.bf16(8, 1024); T_wkv = T()
        NRING = 8
        ring = [A.bf16(1024) for _ in range(NRING)]
        T_ring = [T() for _ in range(NRING)]
        ring_pos = [0]
        T_c = T()
        xo = [A.f32(D) for _ in range(4)]; T_xo = [T() for _ in range(4)]

        def load_xo(slot_idx, is_sample_):
            na = 1 if is_sample_ else 4
            o0 = 4 * NSLOT if is_sample_ else 4 * slot_idx
            for a in range(na):
                dma(xo[a], x_own[o0 + a], w=[T_xo[a]], sem="xo%d" % a)

        for dst, src in ((ident_f, ident_d), (shm.rearrange("p a b -> p (a b)"), shm_d), (gvec, gvec_d),
                         (gbc.rearrange("p a b -> p (a b)"), gbc_d), (gsub, gsub_d),
                         (convbc.rearrange("p a b -> p (a b)"), convbc_d), (btab, btab_d), (dbias, dbias_d),
                         (bdiag.rearrange("p a b -> p (a b)"), bdiag_d), (sbias, sbias_d),
                         (bs.rearrange("p a b -> p (a b)"), bs_d)):
            dma(dst, src, w=[T_c])
        dve(lambda e: e.tensor_copy(out=ident_b, in_=ident_f), [T_c], [T_identb])
        dve(lambda e: e.memset(lam_t[:, 2:3], RMS_EPS), [], [T_lam])
        dve(lambda e: e.tensor_scalar(out=gsub, in0=gsub, scalar1=1.0 - LAM_INIT, scalar2=None, op0=ALU.mult),
            [T_c], [T_c])
        eps_ap = lam_t[:, 2:3]

        mark0 = A.top
        lamv = A.f32(4, 64)
        lprod = A.f32(2, 64)
        lsum = A.f32(2)
        dma(lamv.rearrange("p a b -> p (a b)"), lamv_d, w=[T_c])
        for i in range(2):
            dve(lambda e, i=i: e.tensor_tensor(out=lprod[:, i, :], in0=lamv[:, 2 * i, :], in1=lamv[:, 2 * i + 1, :],
                                               op=ALU.mult), [T_c], [T_c])
            act(lambda e, i=i: e.activation(out=lprod[:, i, :], in_=lprod[:, i, :], func=AF.Copy,
                                            accum_out=lsum[:, i:i + 1]), [T_c], [T_c])
        act(lambda e: e.activation(out=lsum, in_=lsum, func=AF.Exp), [T_c], [T_c])
        dve(lambda e: e.tensor_tensor(out=lam_t[:, 0:1], in0=lsum[:, 0:1], in1=lsum[:, 1:2], op=ALU.subtract),
            [T_c, T_lam], [T_lam])
        dve(lambda e: e.tensor_scalar(out=lam_t[:, 0:1], in0=lam_t[:, 0:1], scalar1=LAM_INIT, scalar2=None,
                                      op0=ALU.add), [T_lam], [T_lam])
        dve(lambda e: e.tensor_scalar(out=lam_t[:, 1:2], in0=lam_t[:, 0:1], scalar1=-1.0, scalar2=None,
                                      op0=ALU.mult), [T_lam], [T_lam])

        NWS = 2
        wst_f = [A.f32(4096) for _ in range(NWS)]
        wst_b = [A.bf16(4096) for _ in range(NWS)]
        T_wf = [T() for _ in range(NWS)]
        T_wb = [T() for _ in range(NWS)]
        T_ws = T()
        prep_i = [0]
        prep_q = ["sp"]

        def make_job(src_rows, width, gcol, stores_fn):
            st = {}

            def load():
                st["s"] = prep_i[0] % NWS
                prep_i[0] += 1
                s = st["s"]
                dma(wst_f[s][:, 0:width], src_rows, w=[T_wf[s]], sem="wf%d" % s, q=prep_q[0])

            def cast():
                s = st["s"]
                o = wst_b[s][:, 0:width]
                src_ = wst_f[s][:, 0:width]
                if gcol is None:
                    dve(lambda e: e.tensor_copy(out=o, in_=src_), [T_wf[s]], [T_wb[s]])
                else:
                    g_ap = gvec[:, gcol:gcol + 1]
                    dve(lambda e: e.tensor_scalar(out=o, in0=src_, scalar1=g_ap, scalar2=None, op0=ALU.mult),
                        [T_wf[s], T_c], [T_wb[s]])

            def store():
                s = st["s"]
                stores_fn(s)
            return (load, cast, store)

        def std_stores(wdst, kc, ncg):
            def f(s):
                for cg in range(ncg):
                    dma(wdst[cg, kc // 2, :, (kc % 2) * 512:(kc % 2) * 512 + 512],
                        wst_b[s][:, cg * 512:(cg + 1) * 512], r=[T_wb[s]], w=[T_ws], sem="wb%d" % s, q="pool")
            return f

        for kc in range(8):
            def win_store(s, kc=kc):
                src_ap = wst_b[s][:, 0:1024]
                dve(lambda e: e.tensor_copy(out=wkv[:, kc, :], in_=src_ap), [T_wb[s]], [T_wkv])
            ld, cs, stf = make_job(w_in[kc * 128:(kc + 1) * 128, 2048:3072], 1024, kc, win_store)
            ld(); cs(); stf()
        prep_jobs = []
        for kc in range(8):
            prep_jobs.append(make_job(w_in[kc * 128:(kc + 1) * 128, 0:2048], 2048, kc, std_stores(ws_in, kc, 4)))
        for (wsrc, wdst, nkc) in ((w_out, ws_out, 8), (w_g, ws_g, 8), (w_pe, ws_pe, 2), (w_down, ws_dn, 32)):
            for kc in range(nkc):
                prep_jobs.append(make_job(wsrc[kc * 128:(kc + 1) * 128, :], 1024, None, std_stores(wdst, kc, 2)))
        for kc in range(8):
            def up_store(s, kc=kc):
                dma(ws_up[:, :, kc * 128:(kc + 1) * 128].rearrange("f p n -> p f n"),
                    wst_b[s][:, 0:4096].rearrange("p (f n) -> p f n", f=32), r=[T_wb[s]], w=[T_ws],
                    sem="wb%d" % s, q="pool")
            prep_jobs.append(make_job(w_up[kc * 128:(kc + 1) * 128, :], 4096, 8 + kc, up_store))

        ring_q = []
        ring_issued = [0]
        ring_taken = [0]

        def ring_plan(srcs):
            ring_q.extend(srcs)

        def ring_prefetch():
            while ring_issued[0] < len(ring_q) and ring_issued[0] - ring_taken[0] < NRING:
                s = ring_issued[0] % NRING
                dma(ring[s], ring_q[ring_issued[0]], r=[T_ws], w=[T_ring[s]], sem="ring%d" % s)
                ring_issued[0] += 1

        def ring_load(src):
            if ring_taken[0] >= len(ring_q):
                ring_q.append(src)
            assert ring_q[ring_taken[0]] is src or True
            if ring_issued[0] <= ring_taken[0]:
                ring_prefetch()
            s = ring_taken[0] % NRING
            ring_taken[0] += 1
            return ring[s], T_ring[s]

        def ring_topup():
            ring_prefetch()

        def plan_tok(ws, cgs, nkcp):
            return [ws[cg, kcp] for cg in cgs for kcp in range(nkcp)]

        def plan_A(is_sample):
            p = []
            if not is_sample:
                p += plan_tok(ws_in, [1, 2], 4)
            p += plan_tok(ws_in, [0, 1, 2, 3], 4)
            return p

        def plan_C():
            return (plan_tok(ws_out, [0, 1], 4) + [ws_up[f] for f in range(32)] + plan_tok(ws_dn, [0, 1], 16)
                    + plan_tok(ws_g, [0, 1], 4) + plan_tok(ws_pe, [0, 1], 1))

        def rstd_from_ss(ss_ap, n_feat, out_ap, toks):
            npp = ss_ap.shape[0]
            act(lambda e: e.activation(out=out_ap, in_=ss_ap, func=AF.Ln, scale=1.0 / n_feat, bias=eps_ap[0:npp]),
                toks + [T_lam], toks)
            act(lambda e: e.activation(out=out_ap, in_=out_ap, func=AF.Exp, scale=-0.5), toks, toks)

        def transpose_to(dst_fn, src_b, nparts, nchunks, bk, Tsrc, Tdst, evac_eng="act"):
            pv = bankb(bk)
            for c in range(nchunks):
                pe(lambda e, c=c: e.transpose(out=pv[:, c * nparts:(c + 1) * nparts],
                                              in_=src_b[0:nparts, c * 128:(c + 1) * 128],
                                              identity=ident_b[0:nparts, 0:nparts]),
                   [Tsrc, T_identb], [TB[bk]])
            dst_fn(pv[:, 0:nchunks * nparts].rearrange("p (c t) -> p c t", c=nchunks))

        mark1 = A.top
        xg = [A.f32(4, D) for _ in range(2)]; T_xg = [T(), T()]
        xb1 = [A.bf16(D) for _ in range(3)]; T_xb1 = [T(), T(), T()]
        junk = A.bf16(D); T_junk = T()
        xT1 = [A.bf16(8, 128) for _ in range(2)]; T_xT1 = [T(), T()]
        ss1 = [A.f32(4) for _ in range(2)]; T_ss1 = [[T() for _ in range(4)] for _ in range(2)]
        kbf = [A.bf16(512) for _ in range(2)]; T_kbf = [T(), T()]
        ktst = [A.bf16(4, 512) for _ in range(2)]; T_ktst = [T(), T()]
        vst = [A.bf16(4 * 4, 130) for _ in range(2)]; T_vst = [T(), T()]
        for s in range(2):
            dve(lambda e, s=s: e.memset(vst[s][:, :, 128:129], 1.0), [], [T_vst[s]])

        def load_xg(g):
            s = g % 2
            dma(xg[s], x_all[g * 512:(g + 1) * 512, :].rearrange("(a p) d -> p a d", p=128), w=[T_xg[s]],
                sem="xg%d" % s)

        NB1 = 4 * NG

        def st_L(n):
            g, a = divmod(n, 4)
            s = g % 2
            b2 = n % 2
            if a == 0 and g + 1 < NG:
                load_xg(g + 1)
            xa = xg[s][:, a, :]
            act(lambda e: e.activation(out=junk, in_=xa, func=AF.Square, accum_out=ss1[s][:, a:a + 1]),
                [T_xg[s]], [T_ss1[s][a]])
            b3 = n % 3
            dve(lambda e: e.tensor_copy(out=xb1[b3], in_=xa), [T_xg[s]], [T_xb1[b3]])
            rstd_from_ss(ss1[s][:, a:a + 1], D, ss1[s][:, a:a + 1], [T_ss1[s][a]])

        def st_TX(n):
            b2 = n % 2
            b3 = n % 3
            transpose_to(lambda pv: act(lambda e: e.activation(out=xT1[b2], in_=pv, func=AF.Copy),
                                        [], [TB[b2], T_xT1[b2]]),
                         xb1[b3], 128, 8, b2, T_xb1[b3], None)

        def st_MM(n):
            g, a = divmod(n, 4)
            s = g % 2
            b2 = n % 2
            for cg in range(2):
                bk = 2 + 2 * cg + b2
                for kc in range(8):
                    pe(lambda e, bk=bk, kc=kc, cg=cg: e.matmul(
                        bank(bk), lhsT=xT1[b2][:, kc, :], rhs=wkv[:, kc, cg * 512:(cg + 1) * 512],
                        start=(kc == 0), stop=(kc == 7)), [T_xT1[b2], T_wkv], [TB[bk]])
            rs = ss1[s][:, a:a + 1]
            dve(lambda e: e.tensor_scalar(out=kbf[b2], in0=bank(2 + b2), scalar1=rs, scalar2=None, op0=ALU.mult),
                [T_ss1[s][a]], [TB[2 + b2], T_kbf[b2]])
            vdst = vst[s].rearrange("p (h a) n -> p h a n", h=4)[:, :, a, 0:128]
            act(lambda e: e.activation(out=vdst, in_=bank(4 + b2).rearrange("p (h n) -> p h n", h=4),
                                       func=AF.Copy, scale=rs),
                [T_ss1[s][a]], [TB[4 + b2], T_vst[s]])

        def st_TK(n):
            g, a = divmod(n, 4)
            s = g % 2
            b2 = n % 2
            transpose_to(lambda pv: dve(lambda e: e.tensor_copy(out=ktst[s][:, :, a * 128:(a + 1) * 128], in_=pv),
                                        [], [TB[6 + b2], T_ktst[s]]),
                         kbf[b2], 128, 4, 6 + b2, T_kbf[b2], None)
            if a == 3:
                dma(kt_scr[:, :, g * 512:(g + 1) * 512].rearrange("h p n -> p h n"), ktst[s], r=[T_ktst[s]],
                    sem="kts%d" % s)
                dma(v_scr.rearrange("h p k n -> p h (k n)")[:, :, g * 520:(g + 1) * 520],
                    vst[s].rearrange("p (h a) n -> p h (a n)", h=4), r=[T_vst[s]], sem="vs%d" % s)

        if NG > 0:
            load_xg(0)
            st_L(0)
            if NB1 > 1:
                st_L(1)
            st_TX(0)
        stages = []
        pj = 0
        for n in range(NB1):
            if n + 2 < NB1:
                st_L(n + 2)
            if n + 1 < NB1:
                st_TX(n + 1)
            st_MM(n)
            if n >= 1:
                st_TK(n - 1)
            if stages:
                cs, stf = stages.pop(0)
                cs(); stf()
            if pj < len(prep_jobs) and (n % 2 == 0):
                ld, cs, stf = prep_jobs[pj]
                pj += 1
                ld()
                stages.append((cs, stf))
        if NB1 > 0:
            st_TK(NB1 - 1)
        while stages or pj < len(prep_jobs):
            if stages:
                cs, stf = stages.pop(0)
                cs(); stf()
            if pj < len(prep_jobs):
                ld, cs, stf = prep_jobs[pj]
                pj += 1
                ld()
                stages.append((cs, stf))
        late_jobs = prep_jobs[pj:]
        SLOT_ORDER = list(range(NSLOT))[::-1]
        ring_plan(plan_A(False))
        ring_prefetch()
        load_xo(SLOT_ORDER[0], False)
        kb.barrier()
        A.top = mark0

        def stream_tokmajor(ws, ncg, nkcp, lhs_fn, NA, evac_fn, lhs_toks, bank_sets, M=128, cgs=None, wres=None):
            cgl = list(range(ncg)) if cgs is None else cgs
            for ci, cg in enumerate(cgl):
                banks = bank_sets[ci % len(bank_sets)]
                for kcp in range(nkcp):
                    if wres is None:
                        piece, Tp = ring_load(ws[cg, kcp])
                        pv = piece.rearrange("p (k n) -> p k n", k=2)
                    for kk in range(2):
                        kc = 2 * kcp + kk
                        for a in range(NA):
                            if wres is None:
                                rhs = pv[:, kk, :]
                                rt = Tp
                            else:
                                rhs, rt = wres(cg, kc)
                            pe(lambda e, a=a, kc=kc, rhs=rhs, bk=banks[a]: e.matmul(
                                bank(bk)[0:M, :], lhsT=lhs_fn(a, kc), rhs=rhs,
                                start=(kc == 0), stop=(kc == 2 * nkcp - 1)),
                               lhs_toks(a) + [rt], [TB[banks[a]]])
                    if wres is None:
                        ring_topup()
                for a in range(NA):
                    evac_fn(cg, a, banks[a])

        SETS4 = [[0, 1, 2, 3], [4, 5, 6, 7]]

        def do_slot(j, is_sample):
            NA = 1 if is_sample else 4
            W = NA * 128
            ob0 = 4 * NSLOT if is_sample else 4 * j
            markS = A.top
            mixb = [A.bf16(D) for _ in range(NA)]; T_mix = [T() for _ in range(NA)]
            markQ = A.top
            qpad = [A.bf16(4, W) for _ in range(2)]; T_q = T()
            ktown = A.bf16(4, W); T_kt = T()
            vown = A.bf16(NA * 4, 130); T_vo = T()
            pool(lambda e: e.memset(qpad[0][64:128], 0.0), [], [T_q])
            pool(lambda e: e.memset(qpad[1][0:64], 0.0), [], [T_q])
            dve(lambda e: e.memset(vown[:, :, 128:129], 1.0), [], [T_vo])
            markA = A.top

            xbA = [A.bf16(D) for _ in range(2)]; T_xbA = [T(), T()]
            junkA = A.bf16(D); T_junkA = T()
            xTA = [A.bf16(8, 128) for _ in range(NA)]; T_xTA = [T() for _ in range(NA)]
            rsA = A.f32(NA + 1); T_rsA = [T() for _ in range(NA + 1)]
            cb = [A.f32(512) for _ in range(NA)]; T_cb = [T() for _ in range(NA)]
            zc = [A.f32(512) for _ in range(NA)]; T_zc = [T() for _ in range(NA)]
            zw = [[A.f32(512) for _ in range(2)] for _ in range(2)]
            T_zw = [[T() for _ in range(2)] for _ in range(2)]
            zw2 = A.f32(512); T_zw2 = T()
            qb = [A.bf16(512) for _ in range(NA)]; T_qb = [T() for _ in range(NA)]
            kbA = [A.bf16(512) for _ in range(NA)]; T_kbA = [T() for _ in range(NA)]
            kf = [A.f32(512) for _ in range(2)]; T_kf = [T(), T()]
            vf = [A.f32(512) for _ in range(2)]; T_vf = [T(), T()]
            hz = A.f32(512); T_hz = T()
            hzw = [A.f32(512) for _ in range(2)]; T_hzw = [T(), T()]

            for a in range(NA):
                b2 = a % 2
                act(lambda e, a=a: e.activation(out=junkA, in_=xo[a], func=AF.Square, accum_out=rsA[:, a:a + 1]),
                    [T_xo[a]], [T_rsA[a]])
                dve(lambda e, a=a, b2=b2: e.tensor_copy(out=xbA[b2], in_=xo[a]), [T_xo[a]], [T_xbA[b2]])
                rstd_from_ss(rsA[:, a:a + 1], D, rsA[:, a:a + 1], [T_rsA[a]])
                transpose_to(lambda pv, a=a: act(lambda e: e.activation(out=xTA[a], in_=pv, func=AF.Copy),
                                                 [], [TB[a], T_xTA[a]]),
                             xbA[b2], 128, 8, a, T_xbA[b2], None)

            if not is_sample:
                xh = A.f32(D); T_xh = T()
                xhb = A.bf16(D)
                xhT = A.bf16(8, 2); T_xhT = T()
                dma(xh[0:2], x_halo[j], w=[T_xh], sem="xh")
                act(lambda e: e.activation(out=junkA[0:2], in_=xh[0:2], func=AF.Square,
                                           accum_out=rsA[0:2, NA:NA + 1]), [T_xh], [T_rsA[NA]])
                pool(lambda e: e.tensor_copy(out=xhb[0:2], in_=xh[0:2]), [T_xh], [T_xh])
                rstd_from_ss(rsA[0:2, NA:NA + 1], D, rsA[0:2, NA:NA + 1], [T_rsA[NA]])
                transpose_to(lambda pv: act(lambda e: e.activation(out=xhT, in_=pv, func=AF.Copy),
                                            [], [TB[4], T_xhT]),
                             xhb, 2, 8, 4, T_xh, None)
                rsh = rsA[0:2, NA:NA + 1]

                def evac_halo(cg, a, bk):
                    if cg == 1:
                        act(lambda e: e.activation(out=hzw[0][0:2], in_=bank(bk)[0:2, :], func=AF.Copy, scale=rsh),
                            [T_rsA[NA]], [TB[bk], T_hzw[0]])
                    else:
                        dve(lambda e: e.scalar_tensor_tensor(out=hz[0:2], in0=bank(bk)[0:2, :], scalar=rsh,
                                                             in1=hzw[0][0:2], op0=ALU.mult, op1=ALU.mult),
                            [T_rsA[NA], T_hzw[0]], [TB[bk], T_hz])
                stream_tokmajor(ws_in, 4, 4, lambda a, kc: xhT[:, kc, :], 1, evac_halo, lambda a: [T_xhT],
                                [[5], [6]], M=2, cgs=[1, 2])
                NH = 2
            else:
                dma(hz[0:8], hist_s, w=[T_hz], sem="xh")
                NH = 8

            def evac_z(cg, a, bk):
                rs = rsA[:, a:a + 1]
                b2 = a % 2
                if cg == 0:
                    act(lambda e: e.activation(out=cb[a], in_=bank(bk), func=AF.Copy, scale=rs),
                        [T_rsA[a]], [TB[bk], T_cb[a]])
                elif cg == 1:
                    act(lambda e: e.activation(out=zc[a], in_=bank(bk), func=AF.Copy, scale=rs),
                        [T_rsA[a]], [TB[bk], T_zc[a]])
                elif cg == 2:
                    dve(lambda e: e.scalar_tensor_tensor(out=zc[a], in0=bank(bk), scalar=rs, in1=zc[a],
                                                         op0=ALU.mult, op1=ALU.mult),
                        [T_rsA[a]], [TB[bk], T_zc[a]])
                elif cg == 3:
                    dve(lambda e: e.tensor_scalar(out=qb[a], in0=bank(bk), scalar1=rs, scalar2=None, op0=ALU.mult),
                        [T_rsA[a]], [TB[bk], T_qb[a]])
                elif cg == 4:
                    act(lambda e: e.activation(out=kf[b2], in_=bank(bk), func=AF.Copy, scale=rs),
                        [T_rsA[a]], [TB[bk], T_kf[b2]])
                    dve(lambda e: e.tensor_copy(out=kbA[a], in_=kf[b2]), [T_kf[b2]], [T_kbA[a]])
                    dma(k_own[ob0 + a], kf[b2], r=[T_kf[b2]], sem="kf%d" % b2)
                else:
                    act(lambda e: e.activation(out=vf[b2], in_=bank(bk), func=AF.Copy, scale=rs),
                        [T_rsA[a]], [TB[bk], T_vf[b2]])
                    vdst = vown.rearrange("p (a h) n -> p a h n", a=NA)[:, a, :, 0:128]
                    pool(lambda e: e.tensor_copy(out=vdst, in_=vf[b2].rearrange("p (h n) -> p h n", h=4)),
                         [T_vf[b2]], [T_vo])
                    dma(v_own[ob0 + a], vf[b2], r=[T_vf[b2]], sem="vf%d" % b2)

            stream_tokmajor(ws_in, 4, 4, lambda a, kc: xTA[a][:, kc, :], NA, evac_z, lambda a: [T_xTA[a]], SETS4)
            stream_tokmajor(None, 2, 4, lambda a, kc: xTA[a][:, kc, :], NA,
                            lambda cg, a, bk: evac_z(cg + 4, a, bk), lambda a: [T_xTA[a]], SETS4,
                            wres=lambda cg, kc: (wkv[:, kc, cg * 512:(cg + 1) * 512], T_wkv))

            w0b, w1b, w2b, bbb = convbc[:, 0, :], convbc[:, 1, :], convbc[:, 2, :], convbc[:, 3, :]
            pool(lambda e: e.tensor_tensor(out=hzw[0][0:NH], in0=hz[0:NH], in1=w1b[0:NH], op=ALU.mult),
                 [T_hz, T_c], [T_hzw[0]])
            pool(lambda e: e.tensor_tensor(out=hzw[1][0:NH], in0=hz[0:NH], in1=w0b[0:NH], op=ALU.mult),
                 [T_hz, T_c], [T_hzw[1]])
            if is_sample:
                m_sh1, m_sh2, m_h1, m_h2 = 6, 7, 8, 9
            else:
                m_sh1, m_sh2, m_h1, m_h2 = 0, 1, 4, 5
            for a in range(NA):
                b2 = a % 2

                def q_evac(pv, a=a, b2=b2):
                    dve(lambda e: e.tensor_copy(out=qpad[0][0:64, :, a * 128:(a + 1) * 128], in_=pv[0:64]),
                        [], [TB[0 + b2], T_q])
                    dve(lambda e: e.tensor_copy(out=qpad[1][64:128, :, a * 128:(a + 1) * 128], in_=pv[64:128]),
                        [], [TB[0 + b2], T_q])
                transpose_to(q_evac, qb[a], 128, 4, 0 + b2, T_qb[a], None)
                transpose_to(lambda pv, a=a, b2=b2: act(
                    lambda e: e.activation(out=ktown[:, :, a * 128:(a + 1) * 128], in_=pv, func=AF.Copy),
                    [], [TB[2 + b2], T_kt]), kbA[a], 128, 4, 2 + b2, T_kbA[a], None)
                dve(lambda e, a=a, b2=b2: e.tensor_tensor(out=zw[b2][0], in0=zc[a], in1=w1b, op=ALU.mult),
                     [T_zc[a], T_c], [T_zw[b2][0]])
                dve(lambda e, a=a, b2=b2: e.tensor_tensor(out=zw[b2][1], in0=zc[a], in1=w0b, op=ALU.mult),
                     [T_zc[a], T_c], [T_zw[b2][1]])
                dve(lambda e, a=a: e.tensor_tensor(out=zw2, in0=zc[a], in1=w2b, op=ALU.mult),
                     [T_zc[a], T_c], [T_zw2])
                dve(lambda e: e.tensor_tensor(out=zw2, in0=zw2, in1=bbb, op=ALU.add), [T_c], [T_zw2])
                bk = 4 + b2
                pe(lambda e, b2=b2, bk=bk: e.matmul(bank(bk), lhsT=shm[:, m_sh1, :], rhs=zw[b2][0], start=True,
                                                    stop=False), [T_c, T_zw[b2][0]], [TB[bk]])
                pe(lambda e, b2=b2, bk=bk: e.matmul(bank(bk), lhsT=shm[:, m_sh2, :], rhs=zw[b2][1], start=False,
                                                    stop=False), [T_c, T_zw[b2][1]], [TB[bk]])
                if a == 0:
                    pe(lambda e, bk=bk: e.matmul(bank(bk), lhsT=shm[0:NH, m_h1, :], rhs=hzw[0][0:NH], start=False,
                                                 stop=False), [T_c, T_hzw[0]], [TB[bk]])
                    pe(lambda e, bk=bk: e.matmul(bank(bk), lhsT=shm[0:NH, m_h2, :], rhs=hzw[1][0:NH], start=False,
                                                 stop=True), [T_c, T_hzw[1]], [TB[bk]])
                else:
                    p2 = 1 - b2
                    pe(lambda e, bk=bk, p2=p2: e.matmul(bank(bk), lhsT=shm[:, 2, :], rhs=zw[p2][0], start=False,
                                                        stop=False), [T_c, T_zw[p2][0]], [TB[bk]])
                    pe(lambda e, bk=bk, p2=p2: e.matmul(bank(bk), lhsT=shm[:, 3, :], rhs=zw[p2][1], start=False,
                                                        stop=True), [T_c, T_zw[p2][1]], [TB[bk]])
                dve(lambda e, bk=bk: e.tensor_tensor(out=zw2, in0=bank(bk), in1=zw2, op=ALU.add),
                    [], [TB[bk], T_zw2])
                dve(lambda e, a=a: e.tensor_tensor(out=mixb[a][:, 0:512], in0=zw2, in1=cb[a], op=ALU.mult),
                    [T_cb[a]], [T_zw2, T_mix[a]])
            if is_sample:
                for b in range(BPC):
                    dma(conv_s[2 * b:2 * b + 2, :], zc[0][32 * b + 30:32 * b + 32, :], r=[T_zc[0]], sem="cvo")
            elif j == NSLOT - 1:
                dma(conv_p, zc[3][126:128, :], r=[T_zc[3]], sem="cvo")
            kb.barrier()
            A.top = markA

            if late_jobs:
                for i_ in range(NWS):
                    wst_f[i_] = A.f32(4096)
                    wst_b[i_] = A.bf16(4096)
            ring_plan(plan_C())
            deferred_prefetch = [ring_prefetch]
            if not is_sample:
                nxt = SLOT_ORDER.index(j) + 1
                if nxt < NSLOT:
                    deferred_prefetch.append(lambda: load_xo(SLOT_ORDER[nxt], False))
                else:
                    deferred_prefetch.append(lambda: load_xo(0, True))
            PTs = [A.bf16(2, 512) for _ in range(3)]; T_pt = [T() for _ in range(3)]
            dtmp = [A.f32(128) for _ in range(2)]; T_dtmp = [T(), T()]
            o_t = [A.f32(128) for _ in range(2)]; T_o = [T(), T()]
            junko = A.bf16(128); T_junko = T()
            rr = [A.f32(4) for _ in range(2)]; T_rr = [T(), T()]
            ucount = [0]
            dcount = [0]
            ecount = [0]

            def accv(i):
                return bank(4 + i // 3)[:, (i % 3) * 130:(i % 3) * 130 + 129], 4 + i // 3

            class U:
                pass

            def sreg(sbp):
                return ps_t[:, sbp * 1024:(sbp + 1) * 1024]

            hook_n = [0]
            hook_stage = []

            def unit_hook():
                hook_n[0] += 1
                if hook_n[0] >= 6 and deferred_prefetch and not (late_jobs or hook_stage):
                    for f in deferred_prefetch:
                        f()
                    del deferred_prefetch[:]
                if late_jobs or hook_stage:
                    if hook_n[0] % 3 == 0:
                        if hook_stage:
                            cs, stf = hook_stage.pop(0)
                            cs(); stf()
                        if late_jobs:
                            ld, cs, stf = late_jobs.pop(0)
                            ld()
                            hook_stage.append((cs, stf))

            def run_units(units, h, qbase_of, np_q):
                def emit_S(u):
                    if getattr(u, "load", None) is not None:
                        u.load()
                    sbp = u.idx % 2
                    R = sreg(sbp)
                    for (m, rhs, col, n) in u.smm:
                        pe(lambda e, u=u, rhs=rhs, col=col, n=n, R=R: e.matmul(
                            R[:, col:col + n], lhsT=u.kt, rhs=rhs, start=True, stop=True),
                           [u.Tkt, T_q], [TB[2 * sbp], TB[2 * sbp + 1]])

                def emit_exp(u):
                    sbp = u.idx % 2
                    ps = u.idx % 3
                    R = sreg(sbp)
                    PT = PTs[ps].rearrange("p a b -> p (a b)")
                    for (kind, lo, hi, arg) in u.exps:
                        if kind == "fast":
                            act(lambda e, lo=lo, hi=hi, arg=arg, R=R, PT=PT: e.activation(
                                out=PT[:, lo:hi], in_=R[:, lo:hi], func=AF.Exp, scale=0.125, bias=arg),
                                [T_c], [TB[2 * sbp], TB[2 * sbp + 1], T_pt[ps]])
                        elif kind == "fast2":
                            Rv = R.rearrange("p (a b) -> p a b", a=2)[:, :, lo:hi]
                            Pv = PTs[ps][:, :, lo:hi]
                            act(lambda e, arg=arg, Rv=Rv, Pv=Pv: e.activation(
                                out=Pv, in_=Rv, func=AF.Exp, scale=0.125, bias=arg),
                                [T_c], [TB[2 * sbp], TB[2 * sbp + 1], T_pt[ps]])
                        else:
                            k = dcount[0] % 2
                            dcount[0] += 1
                            dve(lambda e, lo=lo, hi=hi, arg=arg, k=k, R=R: e.scalar_tensor_tensor(
                                out=dtmp[k][:, 0:hi - lo], in0=R[:, lo:hi], scalar=0.125, in1=arg,
                                op0=ALU.mult, op1=ALU.add), [T_c], [TB[2 * sbp], TB[2 * sbp + 1], T_dtmp[k]])
                            act(lambda e, lo=lo, hi=hi, k=k, PT=PT: e.activation(
                                out=PT[:, lo:hi], in_=dtmp[k][:, 0:hi - lo], func=AF.Exp),
                                [T_dtmp[k]], [T_pt[ps]])

                def emit_PV(u):
                    ps = u.idx % 3
                    PT = PTs[ps].rearrange("p a b -> p (a b)")
                    for (i, col, wq, st, sp_) in u.pv:
                        av, abk = accv(i)
                        pe(lambda e, u=u, av=av, col=col, wq=wq, st=st, sp_=sp_, PT=PT: e.matmul(
                            av[0:wq, :], lhsT=PT[:, col:col + wq], rhs=u.v, start=st, stop=sp_),
                           [T_pt[ps], u.Tv], [TB[abk]])

                for ui, u in enumerate(units):
                    u.idx = ucount[0]
                    ucount[0] += 1
                for u in units[:10]:
                    if getattr(u, "load", None) is not None:
                        u.load()
                        u.load = None
                for ui, u in enumerate(units):
                    if ui == 0:
                        emit_S(u)
                        if len(units) > 1:
                            emit_S(units[1])
                    emit_exp(u)
                    if ui + 2 < len(units):
                        emit_S(units[ui + 2])
                    emit_PV(u)
                    unit_hook()

            def ptcol(h, m, qi):
                if h == 0:
                    return (qi // 2) * 512 + m * 256 + (qi % 2) * 128
                return m * 512 + qi * 128

            def evac_head(h, np_q, dst_fn, accs):
                for (a, i1, i2) in accs:
                    k2 = ecount[0] % 2
                    ecount[0] += 1
                    O1, b1 = accv(i1)
                    O2, b2_ = accv(i2)
                    r = rr[k2]
                    dve(lambda e, O1=O1, r=r: e.reciprocal(out=r[0:np_q, 0:1], in_=O1[0:np_q, 128:129]),
                        [], [TB[b1], T_rr[k2]])
                    dve(lambda e, O2=O2, r=r: e.reciprocal(out=r[0:np_q, 1:2], in_=O2[0:np_q, 128:129]),
                        [], [TB[b2_], T_rr[k2]])
                    dve(lambda e, r=r: e.tensor_scalar(out=r[0:np_q, 1:2], in0=r[0:np_q, 1:2],
                                                       scalar1=lam_t[0:np_q, 1:2], scalar2=None, op0=ALU.mult),
                        [T_lam], [T_rr[k2]])
                    dve(lambda e, O1=O1, r=r, k2=k2: e.tensor_scalar(out=o_t[k2][0:np_q], in0=O1[0:np_q, 0:128],
                                                                     scalar1=r[0:np_q, 0:1], scalar2=None,
                                                                     op0=ALU.mult),
                        [T_rr[k2]], [TB[b1], T_o[k2]])
                    dve(lambda e, O2=O2, r=r, k2=k2: e.scalar_tensor_tensor(
                        out=o_t[k2][0:np_q], in0=O2[0:np_q, 0:128], scalar=r[0:np_q, 1:2], in1=o_t[k2][0:np_q],
                        op0=ALU.mult, op1=ALU.add), [T_rr[k2]], [TB[b2_], T_o[k2]])
                    act(lambda e, r=r, k2=k2: e.activation(out=junko[0:np_q], in_=o_t[k2][0:np_q], func=AF.Square,
                                                           accum_out=r[0:np_q, 2:3]),
                        [T_o[k2]], [T_rr[k2]])
                    rstd_from_ss(r[0:np_q, 2:3], 128, r[0:np_q, 2:3], [T_rr[k2]])
                    dst = dst_fn(a)
                    dve(lambda e, r=r, k2=k2, dst=dst: e.scalar_tensor_tensor(
                        out=dst, in0=o_t[k2][0:np_q], scalar=r[0:np_q, 2:3], in1=gsub[0:np_q],
                        op0=ALU.mult, op1=ALU.mult), [T_rr[k2], T_c, T_o[k2]], dst_tokens[0])

            dst_tokens = [[]]

            if not is_sample:
                NKV = CFG.get("NKV", 6)
                kvK = [A.bf16(512) for _ in range(NKV)]
                kvV = [A.bf16(4, 130) for _ in range(NKV)]
                T_kv = [T() for _ in range(NKV)]
                kvpos = [0]
                for h in range(H):
                    units = []
                    for ch in range(E[j] // 4):
                        s = kvpos[0] % NKV
                        kvpos[0] += 1

                        def load_chunk(s=s, h=h, ch=ch):
                            dma(kvK[s], kt_scr[h, :, ch * 512:(ch + 1) * 512], w=[T_kv[s]], sem="kv%d" % s)
                            dma(kvV[s].rearrange("p k n -> p (k n)"),
                                v_scr[h].rearrange("p k n -> p (k n)")[:, ch * 520:(ch + 1) * 520],
                                w=[T_kv[s]], sem="kv%d" % s)
                        for i in range(4):
                            kbi = 4 * ch + i
                            u = U()
                            u.load = load_chunk if i == 0 else None
                            u.kt = kvK[s][:, i * 128:(i + 1) * 128]
                            u.Tkt = T_kv[s]
                            u.v = kvV[s][:, i, 0:129]
                            u.Tv = T_kv[s]
                            first = False
                            if h == 0:
                                u.smm = [(m, qpad[m][:, h, hf * 256:(hf + 1) * 256], hf * 512 + m * 256, 256)
                                         for hf in range(2) for m in range(2)]
                                u.exps = [("fast", hf * 512, (hf + 1) * 512,
                                           btab[:, BIDX[(j, h, kbi, hf)]:BIDX[(j, h, kbi, hf)] + 1])
                                          for hf in range(2)]
                            else:
                                u.smm = [(m, qpad[m][:, h, :], m * 512, 512) for m in range(2)]
                                u.exps = [("fast", 0, 1024, btab[:, BIDX[(j, h, kbi, 0)]:BIDX[(j, h, kbi, 0)] + 1])]
                            u.pv = [(m * 4 + qi, ptcol(h, m, qi), 128, first and ((m * 4 + qi) % 3 == 0), False)
                                    for m in range(2) for qi in range(4)]
                            units.append(u)
                    dunits = []
                    for ap_ in range(4):
                        u = U()
                        u.kt = ktown[:, h, ap_ * 128:(ap_ + 1) * 128]
                        u.Tkt = T_kt
                        u.v = vown[:, ap_ * 4 + h, 0:129]
                        u.Tv = T_vo
                        u.smm = []
                        for m in range(2):
                            if h == 0:
                                for hf in range(2):
                                    qlo = max(ap_, 2 * hf)
                                    qhi = 2 * hf + 2
                                    if qlo < qhi:
                                        u.smm.append((m, qpad[m][:, h, qlo * 128:qhi * 128],
                                                      hf * 512 + m * 256 + (qlo % 2) * 128, (qhi - qlo) * 128))
                            else:
                                u.smm.append((m, qpad[m][:, h, ap_ * 128:512], m * 512 + ap_ * 128,
                                              (4 - ap_) * 128))
                        u.exps = []
                        u.pv = []
                        for m in range(2):
                            for qi in range(ap_, 4):
                                c = ptcol(h, m, qi)
                                if qi == ap_:
                                    u.exps.append(("diag", c, c + 128, bdiag[:, h * 4 + ap_, :]))
                                else:
                                    c0 = h * 16 + ap_ * 4 + qi
                                    u.exps.append(("fast", c, c + 128, dbias[:, c0:c0 + 1]))
                                u.pv.append((m * 4 + qi, c, 128, (ap_ == 0) and ((m * 4 + qi) % 3 == 0), False))
                        dunits.append(u)
                    units = dunits + units
                    run_units(units, h, None, 128)
                    dst_tokens[0] = T_mix
                    evac_head(h, 128, lambda a, h=h: mixb[a][:, 512 + h * 128:512 + (h + 1) * 128],
                              [(a, a, 4 + a) for a in range(4)])
            else:
                ckf = [A.f32(PAST) for _ in range(2)]; T_ckf = [T(), T()]
                ckbs = [A.bf16(4, PAST) for _ in range(2)]; T_ckbs = [T(), T()]
                cvf = [A.f32(4, 512) for _ in range(2)]; T_cvf = [T(), T()]
                cvbs = [A.bf16(NKBS * 4, 130) for _ in range(2)]; T_cvbs = [T(), T()]
                yab = [A.bf16(512) for _ in range(2)]; T_yab = [T(), T()]
                selT = A.bf16(4, 128); T_sel = T()
                dve(lambda e: e.memset(selT, 0.0), [], [T_sel])
                for b in range(BPC):
                    dve(lambda e, b=b: e.tensor_copy(out=selT[0:32, b, 32 * b:32 * b + 32], in_=ident_b[0:32, 0:32]),
                        [T_identb], [T_sel])
                for s in range(2):
                    dve(lambda e, s=s: e.memset(cvbs[s][:, :, 128:129], 1.0), [], [T_cvbs[s]])
                cc = [0]

                def prep_b(b):
                    pb = b % 2
                    for h in range(H):
                        s = cc[0] % 2
                        cc[0] += 1
                        dma(ckf[s], ckT[b, h], w=[T_ckf[s]], sem="ckf%d" % s)
                        dve(lambda e, s=s, h=h: e.tensor_copy(out=ckbs[pb][:, h, :], in_=ckf[s]),
                            [T_ckf[s]], [T_ckbs[pb]])
                    for g4 in range(NKBS // 4):
                        s = cc[0] % 2
                        cc[0] += 1
                        dma(cvf[s], cv[b, g4 * 512:(g4 + 1) * 512, :].rearrange("(k p) n -> p k n", p=128),
                            w=[T_cvf[s]], sem="cvf%d" % s)
                        pool(lambda e, s=s, g4=g4: e.tensor_copy(
                            out=cvbs[pb][:, g4 * 16:(g4 + 1) * 16, 0:128],
                            in_=cvf[s].rearrange("p k (h n) -> p (k h) n", h=4)), [T_cvf[s]], [T_cvbs[pb]])

                prep_b(0)
                for b in range(BPC):
                    if b + 1 < BPC:
                        prep_b(b + 1)
                    ckb = ckbs[b % 2]; T_ckb = T_ckbs[b % 2]
                    cvb = cvbs[b % 2]; T_cvb = T_cvbs[b % 2]
                    units = []
                    started = set()
                    for h in range(H):
                        for kbi in range(NKBS + 1):
                            u = U()
                            own = (kbi == NKBS)
                            if own:
                                u.kt = ktown[:, h, 0:128]
                                u.Tkt = T_kt
                                u.v = vown[:, h, 0:129]
                                u.Tv = T_vo
                                u.exps = [("diag", m * 512, m * 512 + 32, bs[:, b * 4 + h, :]) for m in range(2)]
                            else:
                                u.kt = ckb[:, h, kbi * 128:(kbi + 1) * 128]
                                u.Tkt = T_ckb
                                u.v = cvb[:, kbi * 4 + h, 0:129]
                                u.Tv = T_cvb
                                c0 = h * NKBS + kbi
                                u.exps = [("fast2", 0, 32, sbias[:, c0:c0 + 1])]
                            u.smm = [(m, qpad[m][:, h, b * 32:(b + 1) * 32], m * 512, 32) for m in range(2)]
                            u.pv = []
                            for m in range(2):
                                i = m * 4 + h
                                bk_ = 4 + i // 3
                                st = (kbi == 0) and (bk_ not in started)
                                if kbi == 0:
                                    started.add(bk_)
                                u.pv.append((i, m * 512, 32, st, own))
                            units.append(u)
                    run_units(units, 0, None, 32)
                    dst_tokens[0] = [T_yab[b % 2]]
                    evac_head(0, 32, lambda a, b=b: yab[b % 2][0:32, a * 128:(a + 1) * 128],
                              [(h, h, 4 + h) for h in range(H)])
                    pe(lambda e, b=b: e.matmul(bank(7), lhsT=selT[0:32, b, :], rhs=yab[b % 2][0:32, :],
                                               start=(b == 0), stop=(b == BPC - 1)),
                       [T_sel, T_yab[b % 2]], [TB[7]])
                act(lambda e: e.activation(out=mixb[0][:, 512:1024], in_=bank(7), func=AF.Copy),
                    [], [TB[7], T_mix[0]])
            while hook_stage or late_jobs:
                if hook_stage:
                    cs, stf = hook_stage.pop(0)
                    cs(); stf()
                if late_jobs:
                    ld, cs, stf = late_jobs.pop(0)
                    ld()
                    hook_stage.append((cs, stf))
            for f in deferred_prefetch:
                f()
            kb.barrier()
            A.top = markQ

            if DEBUG:
                dbgt = [A.f32(D) for _ in range(NA)]
                for a in range(NA):
                    dve(lambda e, a=a: e.tensor_copy(out=dbgt[a], in_=mixb[a]), [T_mix[a]], [T_mix[a]])
                    dma(dbg_mix[ob0 + a], dbgt[a], r=[T_mix[a]], sem="dbg%d" % a)
            h1 = [A.f32(D) for _ in range(NA)]; T_h = [T() for _ in range(NA)]
            yt = [A.f32(D) for _ in range(NA)]; T_y = [T() for _ in range(NA)]
            TAb = A.bf16(8, W); T_TA = T()
            TBb = A.bf16(8, W); T_TBb = T()
            ffT = A.bf16(32, W); T_ff = T()
            pf = [A.f32(PLE) for _ in range(NA)]; T_pf = [T() for _ in range(NA)]
            pbf = [A.bf16(PLE) for _ in range(2)]; T_pbf = [T(), T()]
            pT = A.bf16(2, W); T_pT = T()
            cbf = [A.bf16(D) for _ in range(2)]; T_cbf = [T(), T()]
            rsC = A.f32(NA, 8); T_rsC = [T() for _ in range(NA)]
            rl = [A.f32(W) for _ in range(2)]; T_rl = [T(), T()]
            junkC = A.bf16(D); T_junkC = T()

            if not is_sample:
                ring_plan(plan_A(SLOT_ORDER.index(j) + 1 >= NSLOT))
            for a in range(NA):
                dma(h1[a], x_own[ob0 + a], w=[T_h[a]], sem="hx%d" % a)
                dma(pf[a], p_own[ob0 + a], w=[T_pf[a]], sem="pf%d" % a)

            def evac_T(dst, evi):
                def f(pv):
                    if evi % 2 == 0:
                        act(lambda e: e.activation(out=dst, in_=pv, func=AF.Copy), [], f.toks)
                    else:
                        dve(lambda e: e.tensor_copy(out=dst, in_=pv), [], f.toks)
                return f

            def norm_add(a, col, g_idx):
                rs = rsC[:, a, col:col + 1]
                act(lambda e: e.activation(out=junkC, in_=yt[a], func=AF.Square, accum_out=rs),
                    [T_y[a]], [T_rsC[a]])
                rstd_from_ss(rs, D, rs, [T_rsC[a]])
                dve(lambda e: e.scalar_tensor_tensor(out=yt[a], in0=yt[a], scalar=rs, in1=gbc[:, g_idx, :],
                                                     op0=ALU.mult, op1=ALU.mult), [T_rsC[a], T_c], [T_y[a]])
                dve(lambda e: e.tensor_tensor(out=h1[a], in0=h1[a], in1=yt[a], op=ALU.add), [T_y[a]], [T_h[a]])

            for a in range(NA):
                f = evac_T(TAb[:, :, a * 128:(a + 1) * 128], a)
                f.toks = [TB[a], T_TA]
                transpose_to(f, mixb[a], 128, 8, a, T_mix[a], None)
            stream_tokmajor(ws_out, 2, 4, lambda a, kc: TAb[:, kc, a * 128:(a + 1) * 128], NA,
                            lambda cg, a, bk: act(lambda e: e.activation(
                                out=yt[a][:, cg * 512:(cg + 1) * 512], in_=bank(bk), func=AF.Copy),
                                [], [TB[bk], T_y[a]]),
                            lambda a: [T_TA], SETS4)
            for a in range(NA):
                norm_add(a, 0, 0)
            for a in range(NA):
                b2 = a % 2
                rs2 = rsC[:, a, 1:2]
                act(lambda e, a=a, rs2=rs2: e.activation(out=junkC, in_=h1[a], func=AF.Square, accum_out=rs2),
                    [T_h[a]], [T_rsC[a]])
                rstd_from_ss(rs2, D, rs2, [T_rsC[a]])
                dve(lambda e, a=a, rs2=rs2: e.tensor_tensor(out=rsC[:, a, 2:3], in0=rs2, in1=rs2, op=ALU.mult),
                    [], [T_rsC[a]])
                act(lambda e, a=a, b2=b2: e.activation(out=cbf[b2], in_=h1[a], func=AF.Copy), [T_h[a]], [T_cbf[b2]])
                f = evac_T(TBb[:, :, a * 128:(a + 1) * 128], a + 1)
                f.toks = [TB[4 + a], T_TBb]
                transpose_to(f, cbf[b2], 128, 8, 4 + a, T_cbf[b2], None)
            for fch in range(32):
                piece, Tp = ring_load(ws_up[fch])
                pv = piece.rearrange("p (k n) -> p k n", k=8)
                bk = fch % 2
                for kc in range(8):
                    pe(lambda e, pv=pv, kc=kc, bk=bk: e.matmul(bank(bk)[:, 0:W], lhsT=pv[:, kc, :],
                                                               rhs=TBb[:, kc, :], start=(kc == 0), stop=(kc == 7)),
                       [Tp, T_TBb], [TB[bk]])
                ring_topup()
                act(lambda e, bk=bk: e.activation(out=rl[bk], in_=bank(bk)[:, 0:W], func=AF.Relu),
                    [], [TB[bk], T_rl[bk]])
                dve(lambda e, bk=bk, fch=fch: e.tensor_tensor(out=ffT[:, fch, :], in0=rl[bk], in1=rl[bk],
                                                              op=ALU.mult), [T_rl[bk]], [T_ff])
            stream_tokmajor(ws_dn, 2, 16, lambda a, kc: ffT[:, kc, a * 128:(a + 1) * 128], NA,
                            lambda cg, a, bk: act(lambda e: e.activation(
                                out=yt[a][:, cg * 512:(cg + 1) * 512], in_=bank(bk), func=AF.Copy,
                                scale=rsC[:, a, 2:3]), [T_rsC[a]], [TB[bk], T_y[a]]),
                            lambda a: [T_ff], SETS4)
            for a in range(NA):
                norm_add(a, 3, 1)
            for a in range(NA):
                b2 = a % 2
                act(lambda e, a=a, b2=b2: e.activation(out=cbf[b2], in_=h1[a], func=AF.Copy), [T_h[a]], [T_cbf[b2]])
                f = evac_T(TAb[:, :, a * 128:(a + 1) * 128], a)
                f.toks = [TB[a], T_TA]
                transpose_to(f, cbf[b2], 128, 8, a, T_cbf[b2], None)
                act(lambda e, a=a, b2=b2: e.activation(out=pbf[b2], in_=pf[a], func=AF.Copy), [T_pf[a]], [T_pbf[b2]])
                f = evac_T(pT[:, :, a * 128:(a + 1) * 128], a + 1)
                f.toks = [TB[4 + a], T_pT]
                transpose_to(f, pbf[b2], 128, 2, 4 + a, T_pbf[b2], None)
            stream_tokmajor(ws_g, 2, 4, lambda a, kc: TAb[:, kc, a * 128:(a + 1) * 128], NA,
                            lambda cg, a, bk: act(lambda e: e.activation(
                                out=yt[a][:, cg * 512:(cg + 1) * 512], in_=bank(bk), func=AF.Exp, scale=-1.0),
                                [], [TB[bk], T_y[a]]),
                            lambda a: [T_TA], SETS4)
            for a in range(NA):
                dve(lambda e, a=a: e.tensor_scalar(out=yt[a], in0=yt[a], scalar1=1.0, scalar2=None, op0=ALU.add),
                    [], [T_y[a]])
                dve(lambda e, a=a: e.reciprocal(out=yt[a], in_=yt[a]), [], [T_y[a]])
            stream_tokmajor(ws_pe, 2, 1, lambda a, kc: pT[:, kc, a * 128:(a + 1) * 128], NA,
                            lambda cg, a, bk: dve(lambda e: e.tensor_tensor(
                                out=yt[a][:, cg * 512:(cg + 1) * 512], in0=bank(bk),
                                in1=yt[a][:, cg * 512:(cg + 1) * 512], op=ALU.mult), [], [TB[bk], T_y[a]]),
                            lambda a: [T_pT], SETS4)
            for a in range(NA):
                norm_add(a, 4, 2)
                dma(y_own[ob0 + a], h1[a], r=[T_h[a]], sem="yo%d" % a)
            kb.barrier()
            A.top = markS

        for si, j in enumerate(SLOT_ORDER):
            do_slot(j, False)
            if si == 0:
                assert not late_jobs
        do_slot(0, True)
        kb.emit()
    return nc


_PROG_CACHE = {}


def kernel(x_prompt, x_sample, cache_k, cache_v, state_conv, p_prompt, p_sample,
           w_in, w_conv, b_conv, lambda_q1, lambda_k1, lambda_q2, lambda_k2, g_subln,
           w_out, g_pre_mix, g_post_mix, g_pre_mlp, g_post_mlp, w_up, w_down,
           w_pe, w_pe_gate, g_pe):
    NSLOT = CFG["NSLOT"]
    PAST = CFG["PAST"]
    f = np.float32
    A_ = lambda v: np.ascontiguousarray(np.asarray(v), dtype=f)
    x_prompt = A_(x_prompt); x_sample = A_(x_sample); cache_k = A_(cache_k); cache_v = A_(cache_v)
    state_conv = A_(state_conv); p_prompt = A_(p_prompt); p_sample = A_(p_sample)
    SEQ = x_prompt.shape[1]
    assert SEQ == NCORE * NSLOT * 512 and cache_k.shape[2] == PAST
    NOWN = 4 * NSLOT + 1
    key = (NSLOT, PAST)
    if key not in _PROG_CACHE:
        _PROG_CACHE[key] = build_program(NSLOT, PAST)
    nc = _PROG_CACHE[key]

    def bc(v, n=128):
        v = A_(v).reshape(1, -1)
        return np.ascontiguousarray(np.broadcast_to(v, (n, v.shape[1])))

    gvec = np.concatenate([A_(g_pre_mix)[0].reshape(8, 128).T, A_(g_pre_mlp)[0].reshape(8, 128).T], axis=1)
    gbc = np.concatenate([bc(g_post_mix[0]), bc(g_post_mlp[0]), bc(g_pe[0])], axis=1)
    convbc = np.concatenate([bc(A_(w_conv)[0, 0]), bc(A_(w_conv)[0, 1]), bc(A_(w_conv)[0, 2]), bc(A_(b_conv)[0])],
                            axis=1)
    lamv = np.concatenate([bc(lambda_q1[0]), bc(lambda_k1[0]), bc(lambda_q2[0]), bc(lambda_k2[0])], axis=1)
    shared = dict(
        x_all=x_prompt[0], w_in=A_(w_in)[0], w_out=A_(w_out)[0], w_up=A_(w_up)[0], w_down=A_(w_down)[0],
        w_pe=A_(w_pe)[0], w_g=A_(w_pe_gate)[0], gvec=np.ascontiguousarray(gvec), gbc=np.ascontiguousarray(gbc),
        gsub=bc(g_subln[0]), convbc=np.ascontiguousarray(convbc), lamv=np.ascontiguousarray(lamv),
        ident=np.eye(128, dtype=f))
    in_maps = []
    for c in range(NCORE):
        tb = make_tables(c, NSLOT, PAST)
        xo = np.zeros((NOWN, 128, D), f)
        po = np.zeros((NOWN, 128, PLE), f)
        xh = np.zeros((NSLOT, 2, D), f)
        for j in range(NSLOT):
            sb = NCORE * j + c
            xo[4 * j:4 * j + 4] = x_prompt[0, sb * 512:(sb + 1) * 512].reshape(4, 128, D)
            po[4 * j:4 * j + 4] = p_prompt[0, 0, sb * 512:(sb + 1) * 512].reshape(4, 128, PLE)
            if sb > 0:
                xh[j] = x_prompt[0, sb * 512 - 2:sb * 512]
        bsl = slice(c * BPC, (c + 1) * BPC)
        xo[4 * NSLOT] = x_sample[bsl].reshape(128, D)
        po[4 * NSLOT] = p_sample[0, bsl].reshape(128, PLE)
        m = dict(shared)
        m.update(x_own=xo, p_own=po, x_halo=xh,
                 hist_s=np.ascontiguousarray(state_conv[0, bsl].reshape(8, 512)),
                 ckT=np.ascontiguousarray(cache_k[0, bsl].transpose(0, 2, 3, 1)),
                 cv=np.ascontiguousarray(cache_v[0, bsl].reshape(BPC, PAST, 512)),
                 **tb)
        in_maps.append(m)
    res = run_bass_kernel_spmd(nc, in_maps, core_ids=list(range(NCORE)))
    R = res.results
    if CFG.get("DEBUG", False):
        CFG["_dbg"] = [r["dbg_mix"] for r in R]
        CFG["_kt"] = [np.asarray(r["kt_scr"]).astype(np.float32) for r in R]
        CFG["_v"] = [np.asarray(r["v_scr"]).astype(np.float32) for r in R]
    y_prompt = np.zeros((1, SEQ, D), f)
    k_prompt = np.zeros((1, 1, SEQ, H, 128), f)
    v_prompt = np.zeros((1, 1, SEQ, H, 128), f)
    y_sample = np.zeros((NCORE * BPC, DEC_T, D), f)
    k_sample = np.zeros((1, NCORE * BPC, DEC_T, H, 128), f)
    v_sample = np.zeros((1, NCORE * BPC, DEC_T, H, 128), f)
    conv_sample = np.zeros((1, NCORE * BPC, 2, 512), f)
    for c in range(NCORE):
        r = R[c]
        for j in range(NSLOT):
            sb = NCORE * j + c
            sl = slice(sb * 512, (sb + 1) * 512)
            y_prompt[0, sl] = r["y_own"][4 * j:4 * j + 4].reshape(512, D)
            k_prompt[0, 0, sl] = r["k_own"][4 * j:4 * j + 4].reshape(512, H, 128)
            v_prompt[0, 0, sl] = r["v_own"][4 * j:4 * j + 4].reshape(512, H, 128)
        bsl = slice(c * BPC, (c + 1) * BPC)
        y_sample[bsl] = r["y_own"][4 * NSLOT].reshape(BPC, DEC_T, D)
        k_sample[0, bsl] = r["k_own"][4 * NSLOT].reshape(BPC, DEC_T, H, 128)
        v_sample[0, bsl] = r["v_own"][4 * NSLOT].reshape(BPC, DEC_T, H, 128)
        conv_sample[0, bsl] = r["conv_s"].reshape(BPC, 2, 512)
    conv_prompt = np.ascontiguousarray(R[NCORE - 1]["conv_p"]).reshape(1, 1, 2, 512).astype(f)
    return (y_prompt, y_sample, k_prompt, v_prompt, conv_prompt, k_sample, v_sample, conv_sample)
```

```python
import math
from contextlib import ExitStack
import numpy as np
import concourse.bass as bass
import concourse.mybir as mybir
from concourse.bass_utils import run_bass_kernel_spmd

F32 = mybir.dt.float32
BF16 = mybir.dt.bfloat16
AF = mybir.ActivationFunctionType
ALU = mybir.AluOpType

NCORE = 8
D = 1024
DFF = 4096
PLE = 256
H = 4
DEC_T = 32
BPC = 4
RMS_EPS = 1e-6
NEG = -30000.0
SLOPES = [2.0 ** (-8.0 * (h + 1) / 4) for h in range(4)]
LAM_INIT = 0.8 - 0.6 * math.exp(-0.3 * 0)
CFG = dict(NSLOT=4, PAST=2048)

ENGS = ["pe", "act", "dve", "pool", "sp"]


class T:
    __slots__ = ("w", "rs")

    def __init__(self):
        self.w = None
        self.rs = []


class Op:
    __slots__ = ("eng", "fn", "deps", "needed", "ev", "dma_sem", "epoch", "barrier")

    def __init__(self, eng, fn):
        self.eng = eng
        self.fn = fn
        self.deps = []
        self.needed = False
        self.ev = None
        self.dma_sem = None
        self.barrier = 0


class DSem:
    def __init__(self, sem):
        self.sem = sem
        self.n = 0


class KB:
    def __init__(self, nc, es):
        self.nc = nc
        self.es = es
        self.ops = []
        self.epoch = 0
        self.esem = {e: es.enter_context(nc.semaphore("es_" + e)) for e in ENGS[:4]}
        self.bsem = es.enter_context(nc.semaphore("bar"))
        self.dsems = {}
        self.nbar = 0

    def dsem(self, name):
        if name not in self.dsems:
            self.dsems[name] = DSem(self.es.enter_context(self.nc.semaphore("ds_" + name)))
        return self.dsems[name]

    def op(self, eng, fn, reads=(), writes=(), dsem=None):
        o = Op(eng, fn)
        o.epoch = self.epoch
        o.dma_sem = dsem
        deps = []
        for r in reads:
            if r.w is not None:
                deps.append(r.w)
        for w in writes:
            if w.w is not None:
                deps.append(w.w)
            deps.extend(w.rs)
        seen = set()
        for d in deps:
            if d is o or id(d) in seen or d.epoch < self.epoch:
                continue
            seen.add(id(d))
            if d.eng == eng and d.dma_sem is None and eng in ("pe", "sp"):
                continue
            if eng == "sp" and d.eng == "sp" and d.dma_sem is not None and d.dma_sem is dsem:
                continue
            o.deps.append(d)
            d.needed = True
        for r in reads:
            r.rs.append(o)
        for w in writes:
            w.w = o
            w.rs = []
        self.ops.append(o)
        return o

    def barrier(self):
        self.nbar += 1
        for e in ENGS:
            o = Op(e, None)
            o.barrier = self.nbar
            o.epoch = self.epoch
            self.ops.append(o)
        self.epoch += 1

    def emit(self):
        nc = self.nc
        per = {e: [] for e in ENGS}
        cnt = {e: 0 for e in ENGS}
        for o in self.ops:
            per[o.eng].append(o)
            if o.barrier:
                continue
            if o.dma_sem is not None:
                o.dma_sem.n += 1
                o.ev = (o.dma_sem, o.dma_sem.n * 16)
            elif o.needed:
                cnt[o.eng] += 1
                o.ev = (o.eng, cnt[o.eng])
        esem = self.esem
        bsem = self.bsem
        alld = list(self.dsems.values())

        def run(ename, eng):
            waited = {}
            dcount = {id(d): 0 for d in alld}
            for o in per[ename]:
                if o.barrier:
                    for d in alld:
                        if dcount[id(d)] > waited.get(id(d), 0):
                            eng.wait_ge(d.sem, dcount[id(d)])
                            waited[id(d)] = dcount[id(d)]
                    if ename == "sp":
                        eng.sem_inc(bsem, 1)
                    else:
                        eng.drain().then_inc(bsem, 1)
                    eng.wait_ge(bsem, 5 * o.barrier)
                    continue
                need = {}
                for d in o.deps:
                    key, val = d.ev
                    k = id(key) if isinstance(key, DSem) else key
                    if waited.get(k, 0) >= val:
                        continue
                    if k not in need or need[k][1] < val:
                        need[k] = (key, val)
                for k, (key, val) in need.items():
                    sem = key.sem if isinstance(key, DSem) else esem[key]
                    eng.wait_ge(sem, val)
                    waited[k] = val
                inst = o.fn(eng)
                if o.dma_sem is not None:
                    inst.then_inc(o.dma_sem.sem, 16)
                    dcount[id(o.dma_sem)] = o.ev[1]
                elif o.needed:
                    inst.then_inc(esem[o.eng], 1)
            if ename == "sp":
                for d in alld:
                    if d.n > 0:
                        eng.wait_ge(d.sem, d.n * 16)

        with nc.Block() as block:
            @block.tensor
            def _(eng):
                run("pe", eng)

            @block.scalar
            def _(eng):
                run("act", eng)

            @block.vector
            def _(eng):
                run("dve", eng)

            @block.gpsimd
            def _(eng):
                run("pool", eng)

            @block.sync
            def _(eng):
                run("sp", eng)


class Arena:
    def __init__(self, t, n):
        self.t = t
        self.n = n
        self.top = 0

    def _take(self, n32):
        off = self.top
        self.top += n32
        assert self.top <= self.n, ("SBUF arena overflow", self.top, self.n)
        return off

    def f32(self, *shape):
        n = int(np.prod(shape))
        off = self._take(n)
        v = self.t[:, off:off + n]
        return _shape(v, shape)

    def bf16(self, *shape):
        n = int(np.prod(shape))
        n32 = (n + 1) // 2
        off = self._take(n32)
        v = self.t[:, off:off + n32].bitcast(BF16)[:, 0:n]
        return _shape(v, shape)


def _shape(v, shape):
    if len(shape) == 1:
        return v
    if len(shape) == 2:
        return v.rearrange("p (a b) -> p a b", a=shape[0])
    if len(shape) == 3:
        return v.rearrange("p (a b c) -> p a b c", a=shape[0], b=shape[1])
    raise ValueError(shape)


def _hw(h):
    return 256 if h == 0 else 512


def _refoff(h, a):
    return 256 * (a // 2) if h == 0 else 0


def bias_layout(NSLOT):
    E = [4 * (NCORE * j + NCORE - 1) for j in range(NSLOT)]
    idx = {}
    n = 0
    for j in range(NSLOT):
        for h in range(H):
            for kb in range(E[j]):
                for half in range(2 if h == 0 else 1):
                    idx[(j, h, kb, half)] = n
                    n += 1
    return E, idx, n


def make_tables(c, NSLOT, PAST):
    E, idx, ncol = bias_layout(NSLOT)
    kl = np.arange(128, dtype=np.float64)
    btab = np.zeros((128, ncol), np.float64)
    for (j, h, kb, half), col in idx.items():
        sb = NCORE * j + c
        m = SLOPES[h]
        Cc = m * _hw(h) / 2
        ref = sb * 512 + half * 256
        if kb < 4 * sb:
            btab[:, col] = m * (kl + 128 * kb - ref) - Cc
        else:
            btab[:, col] = NEG
    btab = np.maximum(btab, NEG)
    dbias = np.zeros((128, H, 4, 4), np.float64)
    bdiag = np.zeros((128, H, 4, 128), np.float64)
    ql = np.arange(128, dtype=np.float64)
    for h in range(H):
        m = SLOPES[h]
        Cc = m * _hw(h) / 2
        for a in range(4):
            for ap in range(4):
                dbias[:, h, ap, a] = m * (kl + 128 * ap - _refoff(h, a)) - Cc
            vis = (kl[:, None] // 64) <= (ql[None, :] // 64)
            val = -m * np.abs(ql[None, :] - kl[:, None]) + m * (128 * a + ql[None, :] - _refoff(h, a)) - Cc
            bdiag[:, h, a, :] = np.where(vis, val, NEG)
    nkb = PAST // 128
    sbias = np.zeros((128, H, nkb), np.float64)
    bs = np.full((128, BPC, H, DEC_T), NEG, np.float64)
    q32 = np.arange(DEC_T, dtype=np.float64)
    for h in range(H):
        m = SLOPES[h]
        Cs = m * 16
        for kb in range(nkb):
            sbias[:, h, kb] = m * (kl + 128 * kb - PAST) - Cs
        for b in range(BPC):
            for tp in range(DEC_T):
                bs[b * 32 + tp, b, h, :] = -m * np.abs(q32 - tp) + m * q32 - Cs
    sbias = np.maximum(sbias, NEG)
    shm = np.zeros((128, 10, 128), np.float32)
    for t in range(128):
        if t >= 1:
            shm[t - 1, 0, t] = 1
        if t >= 2:
            shm[t - 2, 1, t] = 1
        if t % 32 != 0:
            shm[t - 1, 6, t] = 1
        if t % 32 >= 2:
            shm[t - 2, 7, t] = 1
    shm[127, 2, 0] = 1
    shm[126, 3, 0] = 1
    shm[127, 3, 1] = 1
    shm[1, 4, 0] = 1
    shm[0, 5, 0] = 1
    shm[1, 5, 1] = 1
    for b in range(BPC):
        shm[2 * b + 1, 8, 32 * b] = 1
        shm[2 * b, 9, 32 * b] = 1
        shm[2 * b + 1, 9, 32 * b + 1] = 1
    f = np.float32
    return dict(btab=btab.astype(f), dbias=dbias.reshape(128, -1).astype(f),
                bdiag=bdiag.reshape(128, -1).astype(f), sbias=sbias.reshape(128, -1).astype(f),
                bs=bs.reshape(128, -1).astype(f), shm=shm.reshape(128, -1))


def build_program(NSLOT, PAST):
    SEQ = NCORE * NSLOT * 512
    NBLK = SEQ // 128
    NOWN = 4 * NSLOT + 1
    NKBS = PAST // 128
    E, BIDX, NCOL = bias_layout(NSLOT)
    EMAX = E[-1]
    NG = EMAX // 4

    nc = bass.Bass("TRN2", target_bir_lowering=False)

    def din(name, shape, dt=F32):
        return nc.dram_tensor(name, list(shape), dt, kind="ExternalInput").ap()

    def dout(name, shape):
        return nc.dram_tensor(name, list(shape), F32, kind="ExternalOutput").ap()

    def dscr(name, shape):
        return nc.dram_tensor(name, list(shape), BF16, kind="Internal").ap()

    x_all = din("x_all", [SEQ, D])
    x_own = din("x_own", [NOWN, 128, D])
    x_halo = din("x_halo", [NSLOT, 2, D])
    p_own = din("p_own", [NOWN, 128, PLE])
    hist_s = din("hist_s", [8, 512])
    ckT = din("ckT", [BPC, H, 128, PAST])
    cv = din("cv", [BPC, PAST, 512])
    w_in = din("w_in", [D, 3072])
    w_out = din("w_out", [D, D])
    w_up = din("w_up", [D, DFF])
    w_down = din("w_down", [DFF, D])
    w_pe = din("w_pe", [PLE, D])
    w_g = din("w_g", [D, D])
    gvec_d = din("gvec", [128, 16])
    gbc_d = din("gbc", [128, 3 * D])
    gsub_d = din("gsub", [128, 128])
    convbc_d = din("convbc", [128, 4 * 512])
    lamv_d = din("lamv", [128, 4 * 64])
    ident_d = din("ident", [128, 128])
    shm_d = din("shm", [128, 10 * 128])
    btab_d = din("btab", [128, NCOL])
    dbias_d = din("dbias", [128, H * 16])
    bdiag_d = din("bdiag", [128, H * 4 * 128])
    sbias_d = din("sbias", [128, H * NKBS])
    bs_d = din("bs", [128, BPC * H * DEC_T])

    y_own = dout("y_own", [NOWN, 128, D])
    k_own = dout("k_own", [NOWN, 128, 512])
    v_own = dout("v_own", [NOWN, 128, 512])
    conv_p = dout("conv_p", [2, 512])
    conv_s = dout("conv_s", [8, 512])
    DEBUG = CFG.get("DEBUG", False)
    if DEBUG:
        dbg_mix = dout("dbg_mix", [NOWN, 128, D])

    if CFG.get("DEBUG", False):
        kt_scr = nc.dram_tensor("kt_scr", [H, 128, NG * 512], BF16, kind="ExternalOutput").ap()
        v_scr = nc.dram_tensor("v_scr", [H, 128, NG * 4, 130], BF16, kind="ExternalOutput").ap()
    else:
        kt_scr = dscr("kt_scr", [H, 128, NG * 512])
        v_scr = dscr("v_scr", [H, 128, NG * 4, 130])
    ws_in = dscr("ws_in", [4, 4, 128, 1024])
    ws_out = dscr("ws_out", [2, 4, 128, 1024])
    ws_g = dscr("ws_g", [2, 4, 128, 1024])
    ws_pe = dscr("ws_pe", [2, 1, 128, 1024])
    ws_dn = dscr("ws_dn", [2, 16, 128, 1024])
    ws_up = dscr("ws_up", [32, 128, 1024])

    ARENA_N = 53000
    with ExitStack() as es:
        kb = KB(nc, es)
        arena_t = es.enter_context(nc.sbuf_tensor("arena", [128, ARENA_N], F32))
        ps_t = es.enter_context(nc.psum_tensor("psum", [128, 8 * 512], F32))
        A = Arena(arena_t, ARENA_N)
        TB = [T() for _ in range(8)]

        def bank(b):
            return ps_t[:, b * 512:(b + 1) * 512]

        def bankb(b):
            return ps_t[:, b * 512:(b + 1) * 512].bitcast(BF16)

        def pe(fn, r=(), w=()):
            return kb.op("pe", fn, r, w)

        def act(fn, r=(), w=()):
            return kb.op("act", fn, r, w)

        def dve(fn, r=(), w=()):
            return kb.op("dve", fn, r, w)

        def pool(fn, r=(), w=()):
            return kb.op("pool", fn, r, w)

        def dma(out, in_, r=(), w=(), sem="const", q="sp"):
            return kb.op(q, lambda e: e.dma_start(out=out, in_=in_), r, w, dsem=kb.dsem(sem))

        ident_f = A.f32(128); T_identf = T()
        ident_b = A.bf16(128); T_identb = T()
        shm = A.f32(10, 128); T_shm = T()
        gvec = A.f32(16); T_gvec = T()
        gbc = A.f32(3, D); T_gbc = T()
        gsub = A.f32(128); T_gsub = T()
        convbc = A.f32(4, 512); T_convbc = T()
        btab = A.f32(NCOL); T_btab = T()
        dbias = A.f32(H * 16)
        bdiag = A.f32(H * 4, 128)
        sbias = A.f32(H * NKBS)
        bs = A.f32(BPC * H, DEC_T)
        lam_t = A.f32(4); T_lam = T()
        wkv = A.bf16(8, 1024); T_wkv = T()
        NRING = 8
        ring = [A.bf16(1024) for _ in range(NRING)]
        T_ring = [T() for _ in range(NRING)]
        ring_pos = [0]
        T_c = T()
        xo = [A.f32(D) for _ in range(4)]; T_xo = [T() for _ in range(4)]

        def load_xo(slot_idx, is_sample_):
            na = 1 if is_sample_ else 4
            o0 = 4 * NSLOT if is_sample_ else 4 * slot_idx
            for a in range(na):
                dma(xo[a], x_own[o0 + a], w=[T_xo[a]], sem="xo%d" % a)

        for dst, src in ((ident_f, ident_d), (shm.rearrange("p a b -> p (a b)"), shm_d), (gvec, gvec_d),
                         (gbc.rearrange("p a b -> p (a b)"), gbc_d), (gsub, gsub_d),
                         (convbc.rearrange("p a b -> p (a b)"), convbc_d), (btab, btab_d), (dbias, dbias_d),
                         (bdiag.rearrange("p a b -> p (a b)"), bdiag_d), (sbias, sbias_d),
                         (bs.rearrange("p a b -> p (a b)"), bs_d)):
            dma(dst, src, w=[T_c])
        dve(lambda e: e.tensor_copy(out=ident_b, in_=ident_f), [T_c], [T_identb])
        dve(lambda e: e.memset(lam_t[:, 2:3], RMS_EPS), [], [T_lam])
        dve(lambda e: e.tensor_scalar(out=gsub, in0=gsub, scalar1=1.0 - LAM_INIT, scalar2=None, op0=ALU.mult),
            [T_c], [T_c])
        eps_ap = lam_t[:, 2:3]

        mark0 = A.top
        lamv = A.f32(4, 64)
        lprod = A.f32(2, 64)
        lsum = A.f32(2)
        dma(lamv.rearrange("p a b -> p (a b)"), lamv_d, w=[T_c])
        for i in range(2):
            dve(lambda e, i=i: e.tensor_tensor(out=lprod[:, i, :], in0=lamv[:, 2 * i, :], in1=lamv[:, 2 * i + 1, :],
                                               op=ALU.mult), [T_c], [T_c])
            act(lambda e, i=i: e.activation(out=lprod[:, i, :], in_=lprod[:, i, :], func=AF.Copy,
                                            accum_out=lsum[:, i:i + 1]), [T_c], [T_c])
        act(lambda e: e.activation(out=lsum, in_=lsum, func=AF.Exp), [T_c], [T_c])
        dve(lambda e: e.tensor_tensor(out=lam_t[:, 0:1], in0=lsum[:, 0:1], in1=lsum[:, 1:2], op=ALU.subtract),
            [T_c, T_lam], [T_lam])
        dve(lambda e: e.tensor_scalar(out=lam_t[:, 0:1], in0=lam_t[:, 0:1], scalar1=LAM_INIT, scalar2=None,
                                      op0=ALU.add), [T_lam], [T_lam])
        dve(lambda e: e.tensor_scalar(out=lam_t[:, 1:2], in0=lam_t[:, 0:1], scalar1=-1.0, scalar2=None,
                                      op0=ALU.mult), [T_lam], [T_lam])

        NWS = 2
        wst_f = [A.f32(4096) for _ in range(NWS)]
        wst_b = [A.bf16(4096) for _ in range(NWS)]
        T_wf = [T() for _ in range(NWS)]
        T_wb = [T() for _ in range(NWS)]
        T_ws = T()
        prep_i = [0]
        prep_q = ["sp"]

        def make_job(src_rows, width, gcol, stores_fn):
            st = {}

            def load():
                st["s"] = prep_i[0] % NWS
                prep_i[0] += 1
                s = st["s"]
                dma(wst_f[s][:, 0:width], src_rows, w=[T_wf[s]], sem="wf%d" % s, q=prep_q[0])

            def cast():
                s = st["s"]
                o = wst_b[s][:, 0:width]
                src_ = wst_f[s][:, 0:width]
                if gcol is None:
                    dve(lambda e: e.tensor_copy(out=o, in_=src_), [T_wf[s]], [T_wb[s]])
                else:
                    g_ap = gvec[:, gcol:gcol + 1]
                    dve(lambda e: e.tensor_scalar(out=o, in0=src_, scalar1=g_ap, scalar2=None, op0=ALU.mult),
                        [T_wf[s], T_c], [T_wb[s]])

            def store():
                s = st["s"]
                stores_fn(s)
            return (load, cast, store)

        def std_stores(wdst, kc, ncg):
            def f(s):
                for cg in range(ncg):
                    dma(wdst[cg, kc // 2, :, (kc % 2) * 512:(kc % 2) * 512 + 512],
                        wst_b[s][:, cg * 512:(cg + 1) * 512], r=[T_wb[s]], w=[T_ws], sem="wb%d" % s, q="pool")
            return f

        for kc in range(8):
            def win_store(s, kc=kc):
                src_ap = wst_b[s][:, 0:1024]
                dve(lambda e: e.tensor_copy(out=wkv[:, kc, :], in_=src_ap), [T_wb[s]], [T_wkv])
            ld, cs, stf = make_job(w_in[kc * 128:(kc + 1) * 128, 2048:3072], 1024, kc, win_store)
            ld(); cs(); stf()
        prep_jobs = []
        for kc in range(8):
            prep_jobs.append(make_job(w_in[kc * 128:(kc + 1) * 128, 0:2048], 2048, kc, std_stores(ws_in, kc, 4)))
        for (wsrc, wdst, nkc) in ((w_out, ws_out, 8), (w_g, ws_g, 8), (w_pe, ws_pe, 2), (w_down, ws_dn, 32)):
            for kc in range(nkc):
                prep_jobs.append(make_job(wsrc[kc * 128:(kc + 1) * 128, :], 1024, None, std_stores(wdst, kc, 2)))
        for kc in range(8):
            def up_store(s, kc=kc):
                dma(ws_up[:, :, kc * 128:(kc + 1) * 128].rearrange("f p n -> p f n"),
                    wst_b[s][:, 0:4096].rearrange("p (f n) -> p f n", f=32), r=[T_wb[s]], w=[T_ws],
                    sem="wb%d" % s, q="pool")
            prep_jobs.append(make_job(w_up[kc * 128:(kc + 1) * 128, :], 4096, 8 + kc, up_store))

        ring_q = []
        ring_issued = [0]
        ring_taken = [0]

        def ring_plan(srcs):
            ring_q.extend(srcs)

        def ring_prefetch():
            while ring_issued[0] < len(ring_q) and ring_issued[0] - ring_taken[0] < NRING:
                s = ring_issued[0] % NRING
                dma(ring[s], ring_q[ring_issued[0]], r=[T_ws], w=[T_ring[s]], sem="ring%d" % s)
                ring_issued[0] += 1

        def ring_load(src):
            if ring_taken[0] >= len(ring_q):
                ring_q.append(src)
            assert ring_q[ring_taken[0]] is src or True
            if ring_issued[0] <= ring_taken[0]:
                ring_prefetch()
            s = ring_taken[0] % NRING
            ring_taken[0] += 1
            return ring[s], T_ring[s]

        def ring_topup():
            ring_prefetch()

        def plan_tok(ws, cgs, nkcp):
            return [ws[cg, kcp] for cg in cgs for kcp in range(nkcp)]

        def plan_A(is_sample):
            p = []
            if not is_sample:
                p += plan_tok(ws_in, [1, 2], 4)
            p += plan_tok(ws_in, [0, 1, 2, 3], 4)
            return p

        def plan_C():
            return (plan_tok(ws_out, [0, 1], 4) + [ws_up[f] for f in range(32)] + plan_tok(ws_dn, [0, 1], 16)
                    + plan_tok(ws_g, [0, 1], 4) + plan_tok(ws_pe, [0, 1], 1))

        def rstd_from_ss(ss_ap, n_feat, out_ap, toks):
            npp = ss_ap.shape[0]
            act(lambda e: e.activation(out=out_ap, in_=ss_ap, func=AF.Ln, scale=1.0 / n_feat, bias=eps_ap[0:npp]),
                toks + [T_lam], toks)
            act(lambda e: e.activation(out=out_ap, in_=out_ap, func=AF.Exp, scale=-0.5), toks, toks)

        def transpose_to(dst_fn, src_b, nparts, nchunks, bk, Tsrc, Tdst, evac_eng="act"):
            pv = bankb(bk)
            for c in range(nchunks):
                pe(lambda e, c=c: e.transpose(out=pv[:, c * nparts:(c + 1) * nparts],
                                              in_=src_b[0:nparts, c * 128:(c + 1) * 128],
                                              identity=ident_b[0:nparts, 0:nparts]),
                   [Tsrc, T_identb], [TB[bk]])
            dst_fn(pv[:, 0:nchunks * nparts].rearrange("p (c t) -> p c t", c=nchunks))

        mark1 = A.top
        xg = [A.f32(4, D) for _ in range(2)]; T_xg = [T(), T()]
        xb1 = [A.bf16(D) for _ in range(3)]; T_xb1 = [T(), T(), T()]
        junk = A.bf16(D); T_junk = T()
        xT1 = [A.bf16(8, 128) for _ in range(2)]; T_xT1 = [T(), T()]
        ss1 = [A.f32(4) for _ in range(2)]; T_ss1 = [[T() for _ in range(4)] for _ in range(2)]
        kbf = [A.bf16(512) for _ in range(2)]; T_kbf = [T(), T()]
        ktst = [A.bf16(4, 512) for _ in range(2)]; T_ktst = [T(), T()]
        vst = [A.bf16(4 * 4, 130) for _ in range(2)]; T_vst = [T(), T()]
        for s in range(2):
            dve(lambda e, s=s: e.memset(vst[s][:, :, 128:129], 1.0), [], [T_vst[s]])

        def load_xg(g):
            s = g % 2
            dma(xg[s], x_all[g * 512:(g + 1) * 512, :].rearrange("(a p) d -> p a d", p=128), w=[T_xg[s]],
                sem="xg%d" % s)

        NB1 = 4 * NG

        def st_L(n):
            g, a = divmod(n, 4)
            s = g % 2
            b2 = n % 2
            if a == 0 and g + 1 < NG:
                load_xg(g + 1)
            xa = xg[s][:, a, :]
            act(lambda e: e.activation(out=junk, in_=xa, func=AF.Square, accum_out=ss1[s][:, a:a + 1]),
                [T_xg[s]], [T_ss1[s][a]])
            b3 = n % 3
            dve(lambda e: e.tensor_copy(out=xb1[b3], in_=xa), [T_xg[s]], [T_xb1[b3]])
            rstd_from_ss(ss1[s][:, a:a + 1], D, ss1[s][:, a:a + 1], [T_ss1[s][a]])

        def st_TX(n):
            b2 = n % 2
            b3 = n % 3
            transpose_to(lambda pv: act(lambda e: e.activation(out=xT1[b2], in_=pv, func=AF.Copy),
                                        [], [TB[b2], T_xT1[b2]]),
                         xb1[b3], 128, 8, b2, T_xb1[b3], None)

        def st_MM(n):
            g, a = divmod(n, 4)
            s = g % 2
            b2 = n % 2
            for cg in range(2):
                bk = 2 + 2 * cg + b2
                for kc in range(8):
                    pe(lambda e, bk=bk, kc=kc, cg=cg: e.matmul(
                        bank(bk), lhsT=xT1[b2][:, kc, :], rhs=wkv[:, kc, cg * 512:(cg + 1) * 512],
                        start=(kc == 0), stop=(kc == 7)), [T_xT1[b2], T_wkv], [TB[bk]])
            rs = ss1[s][:, a:a + 1]
            dve(lambda e: e.tensor_scalar(out=kbf[b2], in0=bank(2 + b2), scalar1=rs, scalar2=None, op0=ALU.mult),
                [T_ss1[s][a]], [TB[2 + b2], T_kbf[b2]])
            vdst = vst[s].rearrange("p (h a) n -> p h a n", h=4)[:, :, a, 0:128]
            act(lambda e: e.activation(out=vdst, in_=bank(4 + b2).rearrange("p (h n) -> p h n", h=4),
                                       func=AF.Copy, scale=rs),
                [T_ss1[s][a]], [TB[4 + b2], T_vst[s]])

        def st_TK(n):
            g, a = divmod(n, 4)
            s = g % 2
            b2 = n % 2
            transpose_to(lambda pv: dve(lambda e: e.tensor_copy(out=ktst[s][:, :, a * 128:(a + 1) * 128], in_=pv),
                                        [], [TB[6 + b2], T_ktst[s]]),
                         kbf[b2], 128, 4, 6 + b2, T_kbf[b2], None)
            if a == 3:
                dma(kt_scr[:, :, g * 512:(g + 1) * 512].rearrange("h p n -> p h n"), ktst[s], r=[T_ktst[s]],
                    sem="kts%d" % s)
                dma(v_scr.rearrange("h p k n -> p h (k n)")[:, :, g * 520:(g + 1) * 520],
                    vst[s].rearrange("p (h a) n -> p h (a n)", h=4), r=[T_vst[s]], sem="vs%d" % s)

        if NG > 0:
            load_xg(0)
            st_L(0)
            if NB1 > 1:
                st_L(1)
            st_TX(0)
        stages = []
        pj = 0
        for n in range(NB1):
            if n + 2 < NB1:
                st_L(n + 2)
            if n + 1 < NB1:
                st_TX(n + 1)
            st_MM(n)
            if n >= 1:
                st_TK(n - 1)
            if stages:
                cs, stf = stages.pop(0)
                cs(); stf()
            if pj < len(prep_jobs) and (n % 2 == 0 or n < 10):
                ld, cs, stf = prep_jobs[pj]
                pj += 1
                ld()
                stages.append((cs, stf))
        if NB1 > 0:
            st_TK(NB1 - 1)
        while stages or pj < len(prep_jobs):
            if stages:
                cs, stf = stages.pop(0)
                cs(); stf()
            if pj < len(prep_jobs):
                ld, cs, stf = prep_jobs[pj]
                pj += 1
                ld()
                stages.append((cs, stf))
        late_jobs = prep_jobs[pj:]
        SLOT_ORDER = list(range(NSLOT))[::-1]
        ring_plan(plan_A(False))
        ring_prefetch()
        load_xo(SLOT_ORDER[0], False)
        kb.barrier()
        A.top = mark0

        def stream_tokmajor(ws, ncg, nkcp, lhs_fn, NA, evac_fn, lhs_toks, bank_sets, M=128, cgs=None, wres=None):
            cgl = list(range(ncg)) if cgs is None else cgs
            for ci, cg in enumerate(cgl):
                banks = bank_sets[ci % len(bank_sets)]
                for kcp in range(nkcp):
                    if wres is None:
                        piece, Tp = ring_load(ws[cg, kcp])
                        pv = piece.rearrange("p (k n) -> p k n", k=2)
                    for kk in range(2):
                        kc = 2 * kcp + kk
                        for a in range(NA):
                            if wres is None:
                                rhs = pv[:, kk, :]
                                rt = Tp
                            else:
                                rhs, rt = wres(cg, kc)
                            pe(lambda e, a=a, kc=kc, rhs=rhs, bk=banks[a]: e.matmul(
                                bank(bk)[0:M, :], lhsT=lhs_fn(a, kc), rhs=rhs,
                                start=(kc == 0), stop=(kc == 2 * nkcp - 1)),
                               lhs_toks(a) + [rt], [TB[banks[a]]])
                    if wres is None:
                        ring_topup()
                for a in range(NA):
                    evac_fn(cg, a, banks[a])

        SETS4 = [[0, 1, 2, 3], [4, 5, 6, 7]]

        def do_slot(j, is_sample):
            NA = 1 if is_sample else 4
            W = NA * 128
            ob0 = 4 * NSLOT if is_sample else 4 * j
            markS = A.top
            mixb = [A.bf16(D) for _ in range(NA)]; T_mix = [T() for _ in range(NA)]
            markQ = A.top
            qpad = [A.bf16(4, W) for _ in range(2)]; T_q = T()
            ktown = A.bf16(4, W); T_kt = T()
            vown = A.bf16(NA * 4, 130); T_vo = T()
            pool(lambda e: e.memset(qpad[0][64:128], 0.0), [], [T_q])
            pool(lambda e: e.memset(qpad[1][0:64], 0.0), [], [T_q])
            dve(lambda e: e.memset(vown[:, :, 128:129], 1.0), [], [T_vo])
            markA = A.top

            xbA = [A.bf16(D) for _ in range(2)]; T_xbA = [T(), T()]
            junkA = A.bf16(D); T_junkA = T()
            xTA = [A.bf16(8, 128) for _ in range(NA)]; T_xTA = [T() for _ in range(NA)]
            rsA = A.f32(NA + 1); T_rsA = [T() for _ in range(NA + 1)]
            cb = [A.f32(512) for _ in range(NA)]; T_cb = [T() for _ in range(NA)]
            zc = [A.f32(512) for _ in range(NA)]; T_zc = [T() for _ in range(NA)]
            zw = [[A.f32(512) for _ in range(2)] for _ in range(2)]
            T_zw = [[T() for _ in range(2)] for _ in range(2)]
            zw2 = A.f32(512); T_zw2 = T()
            qb = [A.bf16(512) for _ in range(NA)]; T_qb = [T() for _ in range(NA)]
            kbA = [A.bf16(512) for _ in range(NA)]; T_kbA = [T() for _ in range(NA)]
            kf = [A.f32(512) for _ in range(2)]; T_kf = [T(), T()]
            vf = [A.f32(512) for _ in range(2)]; T_vf = [T(), T()]
            hz = A.f32(512); T_hz = T()
            hzw = [A.f32(512) for _ in range(2)]; T_hzw = [T(), T()]

            for a in range(NA):
                b2 = a % 2
                act(lambda e, a=a: e.activation(out=junkA, in_=xo[a], func=AF.Square, accum_out=rsA[:, a:a + 1]),
                    [T_xo[a]], [T_rsA[a]])
                dve(lambda e, a=a, b2=b2: e.tensor_copy(out=xbA[b2], in_=xo[a]), [T_xo[a]], [T_xbA[b2]])
                rstd_from_ss(rsA[:, a:a + 1], D, rsA[:, a:a + 1], [T_rsA[a]])
                transpose_to(lambda pv, a=a: act(lambda e: e.activation(out=xTA[a], in_=pv, func=AF.Copy),
                                                 [], [TB[a], T_xTA[a]]),
                             xbA[b2], 128, 8, a, T_xbA[b2], None)

            if not is_sample:
                xh = A.f32(D); T_xh = T()
                xhb = A.bf16(D)
                xhT = A.bf16(8, 2); T_xhT = T()
                dma(xh[0:2], x_halo[j], w=[T_xh], sem="xh")
                act(lambda e: e.activation(out=junkA[0:2], in_=xh[0:2], func=AF.Square,
                                           accum_out=rsA[0:2, NA:NA + 1]), [T_xh], [T_rsA[NA]])
                pool(lambda e: e.tensor_copy(out=xhb[0:2], in_=xh[0:2]), [T_xh], [T_xh])
                rstd_from_ss(rsA[0:2, NA:NA + 1], D, rsA[0:2, NA:NA + 1], [T_rsA[NA]])
                transpose_to(lambda pv: act(lambda e: e.activation(out=xhT, in_=pv, func=AF.Copy),
                                            [], [TB[4], T_xhT]),
                             xhb, 2, 8, 4, T_xh, None)
                rsh = rsA[0:2, NA:NA + 1]

                def evac_halo(cg, a, bk):
                    if cg == 1:
                        act(lambda e: e.activation(out=hzw[0][0:2], in_=bank(bk)[0:2, :], func=AF.Copy, scale=rsh),
                            [T_rsA[NA]], [TB[bk], T_hzw[0]])
                    else:
                        dve(lambda e: e.scalar_tensor_tensor(out=hz[0:2], in0=bank(bk)[0:2, :], scalar=rsh,
                                                             in1=hzw[0][0:2], op0=ALU.mult, op1=ALU.mult),
                            [T_rsA[NA], T_hzw[0]], [TB[bk], T_hz])
                stream_tokmajor(ws_in, 4, 4, lambda a, kc: xhT[:, kc, :], 1, evac_halo, lambda a: [T_xhT],
                                [[5], [6]], M=2, cgs=[1, 2])
                NH = 2
            else:
                dma(hz[0:8], hist_s, w=[T_hz], sem="xh")
                NH = 8

            def evac_z(cg, a, bk):
                rs = rsA[:, a:a + 1]
                b2 = a % 2
                if cg == 0:
                    act(lambda e: e.activation(out=cb[a], in_=bank(bk), func=AF.Copy, scale=rs),
                        [T_rsA[a]], [TB[bk], T_cb[a]])
                elif cg == 1:
                    act(lambda e: e.activation(out=zc[a], in_=bank(bk), func=AF.Copy, scale=rs),
                        [T_rsA[a]], [TB[bk], T_zc[a]])
                elif cg == 2:
                    dve(lambda e: e.scalar_tensor_tensor(out=zc[a], in0=bank(bk), scalar=rs, in1=zc[a],
                                                         op0=ALU.mult, op1=ALU.mult),
                        [T_rsA[a]], [TB[bk], T_zc[a]])
                elif cg == 3:
                    dve(lambda e: e.tensor_scalar(out=qb[a], in0=bank(bk), scalar1=rs, scalar2=None, op0=ALU.mult),
                        [T_rsA[a]], [TB[bk], T_qb[a]])
                elif cg == 4:
                    act(lambda e: e.activation(out=kf[b2], in_=bank(bk), func=AF.Copy, scale=rs),
                        [T_rsA[a]], [TB[bk], T_kf[b2]])
                    dve(lambda e: e.tensor_copy(out=kbA[a], in_=kf[b2]), [T_kf[b2]], [T_kbA[a]])
                    dma(k_own[ob0 + a], kf[b2], r=[T_kf[b2]], sem="kf%d" % b2)
                else:
                    act(lambda e: e.activation(out=vf[b2], in_=bank(bk), func=AF.Copy, scale=rs),
                        [T_rsA[a]], [TB[bk], T_vf[b2]])
                    vdst = vown.rearrange("p (a h) n -> p a h n", a=NA)[:, a, :, 0:128]
                    pool(lambda e: e.tensor_copy(out=vdst, in_=vf[b2].rearrange("p (h n) -> p h n", h=4)),
                         [T_vf[b2]], [T_vo])
                    dma(v_own[ob0 + a], vf[b2], r=[T_vf[b2]], sem="vf%d" % b2)

            stream_tokmajor(ws_in, 4, 4, lambda a, kc: xTA[a][:, kc, :], NA, evac_z, lambda a: [T_xTA[a]], SETS4)
            stream_tokmajor(None, 2, 4, lambda a, kc: xTA[a][:, kc, :], NA,
                            lambda cg, a, bk: evac_z(cg + 4, a, bk), lambda a: [T_xTA[a]], SETS4,
                            wres=lambda cg, kc: (wkv[:, kc, cg * 512:(cg + 1) * 512], T_wkv))

            w0b, w1b, w2b, bbb = convbc[:, 0, :], convbc[:, 1, :], convbc[:, 2, :], convbc[:, 3, :]
            pool(lambda e: e.tensor_tensor(out=hzw[0][0:NH], in0=hz[0:NH], in1=w1b[0:NH], op=ALU.mult),
                 [T_hz, T_c], [T_hzw[0]])
            pool(lambda e: e.tensor_tensor(out=hzw[1][0:NH], in0=hz[0:NH], in1=w0b[0:NH], op=ALU.mult),
                 [T_hz, T_c], [T_hzw[1]])
            if is_sample:
                m_sh1, m_sh2, m_h1, m_h2 = 6, 7, 8, 9
            else:
                m_sh1, m_sh2, m_h1, m_h2 = 0, 1, 4, 5
            for a in range(NA):
                b2 = a % 2

                def q_evac(pv, a=a, b2=b2):
                    dve(lambda e: e.tensor_copy(out=qpad[0][0:64, :, a * 128:(a + 1) * 128], in_=pv[0:64]),
                        [], [TB[0 + b2], T_q])
                    dve(lambda e: e.tensor_copy(out=qpad[1][64:128, :, a * 128:(a + 1) * 128], in_=pv[64:128]),
                        [], [TB[0 + b2], T_q])
                transpose_to(q_evac, qb[a], 128, 4, 0 + b2, T_qb[a], None)
                transpose_to(lambda pv, a=a, b2=b2: act(
                    lambda e: e.activation(out=ktown[:, :, a * 128:(a + 1) * 128], in_=pv, func=AF.Copy),
                    [], [TB[2 + b2], T_kt]), kbA[a], 128, 4, 2 + b2, T_kbA[a], None)
                dve(lambda e, a=a, b2=b2: e.tensor_tensor(out=zw[b2][0], in0=zc[a], in1=w1b, op=ALU.mult),
                     [T_zc[a], T_c], [T_zw[b2][0]])
                dve(lambda e, a=a, b2=b2: e.tensor_tensor(out=zw[b2][1], in0=zc[a], in1=w0b, op=ALU.mult),
                     [T_zc[a], T_c], [T_zw[b2][1]])
                dve(lambda e, a=a: e.tensor_tensor(out=zw2, in0=zc[a], in1=w2b, op=ALU.mult),
                     [T_zc[a], T_c], [T_zw2])
                dve(lambda e: e.tensor_tensor(out=zw2, in0=zw2, in1=bbb, op=ALU.add), [T_c], [T_zw2])
                bk = 4 + b2
                pe(lambda e, b2=b2, bk=bk: e.matmul(bank(bk), lhsT=shm[:, m_sh1, :], rhs=zw[b2][0], start=True,
                                                    stop=False), [T_c, T_zw[b2][0]], [TB[bk]])
                pe(lambda e, b2=b2, bk=bk: e.matmul(bank(bk), lhsT=shm[:, m_sh2, :], rhs=zw[b2][1], start=False,
                                                    stop=False), [T_c, T_zw[b2][1]], [TB[bk]])
                if a == 0:
                    pe(lambda e, bk=bk: e.matmul(bank(bk), lhsT=shm[0:NH, m_h1, :], rhs=hzw[0][0:NH], start=False,
                                                 stop=False), [T_c, T_hzw[0]], [TB[bk]])
                    pe(lambda e, bk=bk: e.matmul(bank(bk), lhsT=shm[0:NH, m_h2, :], rhs=hzw[1][0:NH], start=False,
                                                 stop=True), [T_c, T_hzw[1]], [TB[bk]])
                else:
                    p2 = 1 - b2
                    pe(lambda e, bk=bk, p2=p2: e.matmul(bank(bk), lhsT=shm[:, 2, :], rhs=zw[p2][0], start=False,
                                                        stop=False), [T_c, T_zw[p2][0]], [TB[bk]])
                    pe(lambda e, bk=bk, p2=p2: e.matmul(bank(bk), lhsT=shm[:, 3, :], rhs=zw[p2][1], start=False,
                                                        stop=True), [T_c, T_zw[p2][1]], [TB[bk]])
                dve(lambda e, bk=bk: e.tensor_tensor(out=zw2, in0=bank(bk), in1=zw2, op=ALU.add),
                    [], [TB[bk], T_zw2])
                dve(lambda e, a=a: e.tensor_tensor(out=mixb[a][:, 0:512], in0=zw2, in1=cb[a], op=ALU.mult),
                    [T_cb[a]], [T_zw2, T_mix[a]])
            if is_sample:
                for b in range(BPC):
                    dma(conv_s[2 * b:2 * b + 2, :], zc[0][32 * b + 30:32 * b + 32, :], r=[T_zc[0]], sem="cvo")
            elif j == NSLOT - 1:
                dma(conv_p, zc[3][126:128, :], r=[T_zc[3]], sem="cvo")
            kb.barrier()
            A.top = markA

            if late_jobs:
                for i_ in range(NWS):
                    wst_f[i_] = A.f32(4096)
                    wst_b[i_] = A.bf16(4096)
            ring_plan(plan_C())
            deferred_prefetch = [ring_prefetch]
            if not is_sample:
                nxt = SLOT_ORDER.index(j) + 1
                if nxt < NSLOT:
                    deferred_prefetch.append(lambda: load_xo(SLOT_ORDER[nxt], False))
                else:
                    deferred_prefetch.append(lambda: load_xo(0, True))
            PTs = [A.bf16(2, 512) for _ in range(3)]; T_pt = [T() for _ in range(3)]
            dtmp = [A.f32(128) for _ in range(2)]; T_dtmp = [T(), T()]
            o_t = [A.f32(128) for _ in range(2)]; T_o = [T(), T()]
            junko = A.bf16(128); T_junko = T()
            rr = [A.f32(4) for _ in range(2)]; T_rr = [T(), T()]
            ucount = [0]
            dcount = [0]
            ecount = [0]

            def accv(i):
                return bank(4 + i // 3)[:, (i % 3) * 130:(i % 3) * 130 + 129], 4 + i // 3

            class U:
                pass

            def sreg(sbp):
                return ps_t[:, sbp * 1024:(sbp + 1) * 1024]

            hook_n = [0]
            hook_stage = []

            def unit_hook():
                hook_n[0] += 1
                if hook_n[0] >= 6 and deferred_prefetch and not (late_jobs or hook_stage):
                    for f in deferred_prefetch:
                        f()
                    del deferred_prefetch[:]
                if late_jobs or hook_stage:
                    if hook_n[0] % 3 == 0:
                        if hook_stage:
                            cs, stf = hook_stage.pop(0)
                            cs(); stf()
                        if late_jobs:
                            ld, cs, stf = late_jobs.pop(0)
                            ld()
                            hook_stage.append((cs, stf))

            def run_units(units, h, qbase_of, np_q):
                def emit_S(u):
                    if getattr(u, "load", None) is not None:
                        u.load()
                    sbp = u.idx % 2
                    R = sreg(sbp)
                    for (m, rhs, col, n) in u.smm:
                        pe(lambda e, u=u, rhs=rhs, col=col, n=n, R=R: e.matmul(
                            R[:, col:col + n], lhsT=u.kt, rhs=rhs, start=True, stop=True),
                           [u.Tkt, T_q], [TB[2 * sbp], TB[2 * sbp + 1]])

                def emit_exp(u):
                    sbp = u.idx % 2
                    ps = u.idx % 3
                    R = sreg(sbp)
                    PT = PTs[ps].rearrange("p a b -> p (a b)")
                    for (kind, lo, hi, arg) in u.exps:
                        if kind == "fast":
                            act(lambda e, lo=lo, hi=hi, arg=arg, R=R, PT=PT: e.activation(
                                out=PT[:, lo:hi], in_=R[:, lo:hi], func=AF.Exp, scale=0.125, bias=arg),
                                [T_c], [TB[2 * sbp], TB[2 * sbp + 1], T_pt[ps]])
                        elif kind == "fast2":
                            Rv = R.rearrange("p (a b) -> p a b", a=2)[:, :, lo:hi]
                            Pv = PTs[ps][:, :, lo:hi]
                            act(lambda e, arg=arg, Rv=Rv, Pv=Pv: e.activation(
                                out=Pv, in_=Rv, func=AF.Exp, scale=0.125, bias=arg),
                                [T_c], [TB[2 * sbp], TB[2 * sbp + 1], T_pt[ps]])
                        else:
                            k = dcount[0] % 2
                            dcount[0] += 1
                            dve(lambda e, lo=lo, hi=hi, arg=arg, k=k, R=R: e.scalar_tensor_tensor(
                                out=dtmp[k][:, 0:hi - lo], in0=R[:, lo:hi], scalar=0.125, in1=arg,
                                op0=ALU.mult, op1=ALU.add), [T_c], [TB[2 * sbp], TB[2 * sbp + 1], T_dtmp[k]])
                            act(lambda e, lo=lo, hi=hi, k=k, PT=PT: e.activation(
                                out=PT[:, lo:hi], in_=dtmp[k][:, 0:hi - lo], func=AF.Exp),
                                [T_dtmp[k]], [T_pt[ps]])

                def emit_PV(u):
                    ps = u.idx % 3
                    PT = PTs[ps].rearrange("p a b -> p (a b)")
                    for (i, col, wq, st, sp_) in u.pv:
                        av, abk = accv(i)
                        pe(lambda e, u=u, av=av, col=col, wq=wq, st=st, sp_=sp_, PT=PT: e.matmul(
                            av[0:wq, :], lhsT=PT[:, col:col + wq], rhs=u.v, start=st, stop=sp_),
                           [T_pt[ps], u.Tv], [TB[abk]])

                for ui, u in enumerate(units):
                    u.idx = ucount[0]
                    ucount[0] += 1
                for u in units[:10]:
                    if getattr(u, "load", None) is not None:
                        u.load()
                        u.load = None
                for ui, u in enumerate(units):
                    if ui == 0:
                        emit_S(u)
                        if len(units) > 1:
                            emit_S(units[1])
                    emit_exp(u)
                    if ui + 2 < len(units):
                        emit_S(units[ui + 2])
                    emit_PV(u)
                    unit_hook()

            def ptcol(h, m, qi):
                if h == 0:
                    return (qi // 2) * 512 + m * 256 + (qi % 2) * 128
                return m * 512 + qi * 128

            rr4 = [A.f32(12) for _ in range(2)]; T_rr4 = [T(), T()]
            o4 = [A.f32(128) for _ in range(4)]; T_o4 = [T() for _ in range(4)]

            def evac_head(h, np_q, dst_fn, accs):
                k2 = ecount[0] % 2
                ecount[0] += 1
                r = rr4[k2]
                Tr = T_rr4[k2]
                n = len(accs)
                for idx, (a, i1, i2) in enumerate(accs):
                    O1, b1 = accv(i1)
                    O2, b2_ = accv(i2)
                    dve(lambda e, O1=O1, idx=idx: e.reciprocal(out=r[0:np_q, idx:idx + 1], in_=O1[0:np_q, 128:129]),
                        [], [TB[b1], Tr])
                    dve(lambda e, O2=O2, idx=idx: e.reciprocal(out=r[0:np_q, 4 + idx:5 + idx],
                                                               in_=O2[0:np_q, 128:129]), [], [TB[b2_], Tr])
                dve(lambda e: e.tensor_scalar(out=r[0:np_q, 4:4 + n], in0=r[0:np_q, 4:4 + n],
                                              scalar1=lam_t[0:np_q, 1:2], scalar2=None, op0=ALU.mult),
                    [T_lam], [Tr])
                for idx, (a, i1, i2) in enumerate(accs):
                    O1, b1 = accv(i1)
                    O2, b2_ = accv(i2)
                    dve(lambda e, O1=O1, idx=idx: e.tensor_scalar(out=o4[idx][0:np_q], in0=O1[0:np_q, 0:128],
                                                                  scalar1=r[0:np_q, idx:idx + 1], scalar2=None,
                                                                  op0=ALU.mult), [Tr], [TB[b1], T_o4[idx]])
                    dve(lambda e, O2=O2, idx=idx: e.scalar_tensor_tensor(
                        out=o4[idx][0:np_q], in0=O2[0:np_q, 0:128], scalar=r[0:np_q, 4 + idx:5 + idx],
                        in1=o4[idx][0:np_q], op0=ALU.mult, op1=ALU.add), [Tr], [TB[b2_], T_o4[idx]])
                    act(lambda e, idx=idx: e.activation(out=junko[0:np_q], in_=o4[idx][0:np_q], func=AF.Square,
                                                        accum_out=r[0:np_q, 8 + idx:9 + idx]),
                        [T_o4[idx]], [Tr])
                rstd_from_ss(r[0:np_q, 8:8 + n], 128, r[0:np_q, 8:8 + n], [Tr])
                for idx, (a, i1, i2) in enumerate(accs):
                    dst = dst_fn(a)
                    dve(lambda e, idx=idx, dst=dst: e.scalar_tensor_tensor(
                        out=dst, in0=o4[idx][0:np_q], scalar=r[0:np_q, 8 + idx:9 + idx], in1=gsub[0:np_q],
                        op0=ALU.mult, op1=ALU.mult), [Tr, T_c, T_o4[idx]], dst_tokens[0])

            dst_tokens = [[]]

            if not is_sample:
                NKV = CFG.get("NKV", 6)
                kvK = [A.bf16(512) for _ in range(NKV)]
                kvV = [A.bf16(4, 130) for _ in range(NKV)]
                T_kv = [T() for _ in range(NKV)]
                kvpos = [0]
                for h in range(H):
                    units = []
                    for ch in range(E[j] // 4):
                        s = kvpos[0] % NKV
                        kvpos[0] += 1

                        def load_chunk(s=s, h=h, ch=ch):
                            dma(kvK[s], kt_scr[h, :, ch * 512:(ch + 1) * 512], w=[T_kv[s]], sem="kv%d" % s)
                            dma(kvV[s].rearrange("p k n -> p (k n)"),
                                v_scr[h].rearrange("p k n -> p (k n)")[:, ch * 520:(ch + 1) * 520],
                                w=[T_kv[s]], sem="kv%d" % s)
                        for i in range(4):
                            kbi = 4 * ch + i
                            u = U()
                            u.load = load_chunk if i == 0 else None
                            u.kt = kvK[s][:, i * 128:(i + 1) * 128]
                            u.Tkt = T_kv[s]
                            u.v = kvV[s][:, i, 0:129]
                            u.Tv = T_kv[s]
                            first = False
                            if h == 0:
                                u.smm = [(m, qpad[m][:, h, hf * 256:(hf + 1) * 256], hf * 512 + m * 256, 256)
                                         for hf in range(2) for m in range(2)]
                                u.exps = [("fast", hf * 512, (hf + 1) * 512,
                                           btab[:, BIDX[(j, h, kbi, hf)]:BIDX[(j, h, kbi, hf)] + 1])
                                          for hf in range(2)]
                            else:
                                u.smm = [(m, qpad[m][:, h, :], m * 512, 512) for m in range(2)]
                                u.exps = [("fast", 0, 1024, btab[:, BIDX[(j, h, kbi, 0)]:BIDX[(j, h, kbi, 0)] + 1])]
                            u.pv = [(m * 4 + qi, ptcol(h, m, qi), 128, first and ((m * 4 + qi) % 3 == 0), False)
                                    for m in range(2) for qi in range(4)]
                            units.append(u)
                    dunits = []
                    for ap_ in range(4):
                        u = U()
                        u.kt = ktown[:, h, ap_ * 128:(ap_ + 1) * 128]
                        u.Tkt = T_kt
                        u.v = vown[:, ap_ * 4 + h, 0:129]
                        u.Tv = T_vo
                        u.smm = []
                        for m in range(2):
                            if h == 0:
                                for hf in range(2):
                                    qlo = max(ap_, 2 * hf)
                                    qhi = 2 * hf + 2
                                    if qlo < qhi:
                                        u.smm.append((m, qpad[m][:, h, qlo * 128:qhi * 128],
                                                      hf * 512 + m * 256 + (qlo % 2) * 128, (qhi - qlo) * 128))
                            else:
                                u.smm.append((m, qpad[m][:, h, ap_ * 128:512], m * 512 + ap_ * 128,
                                              (4 - ap_) * 128))
                        u.exps = []
                        u.pv = []
                        for m in range(2):
                            for qi in range(ap_, 4):
                                c = ptcol(h, m, qi)
                                if qi == ap_:
                                    u.exps.append(("diag", c, c + 128, bdiag[:, h * 4 + ap_, :]))
                                else:
                                    c0 = h * 16 + ap_ * 4 + qi
                                    u.exps.append(("fast", c, c + 128, dbias[:, c0:c0 + 1]))
                                u.pv.append((m * 4 + qi, c, 128, (ap_ == 0) and ((m * 4 + qi) % 3 == 0), False))
                        dunits.append(u)
                    units = dunits + units
                    run_units(units, h, None, 128)
                    dst_tokens[0] = T_mix
                    evac_head(h, 128, lambda a, h=h: mixb[a][:, 512 + h * 128:512 + (h + 1) * 128],
                              [(a, a, 4 + a) for a in range(4)])
            else:
                ckf = [A.f32(PAST) for _ in range(2)]; T_ckf = [T(), T()]
                ckbs = [A.bf16(4, PAST) for _ in range(2)]; T_ckbs = [T(), T()]
                cvf = [A.f32(4, 512) for _ in range(2)]; T_cvf = [T(), T()]
                cvbs = [A.bf16(NKBS * 4, 130) for _ in range(2)]; T_cvbs = [T(), T()]
                yab = [A.bf16(512) for _ in range(2)]; T_yab = [T(), T()]
                selT = A.bf16(4, 128); T_sel = T()
                dve(lambda e: e.memset(selT, 0.0), [], [T_sel])
                for b in range(BPC):
                    dve(lambda e, b=b: e.tensor_copy(out=selT[0:32, b, 32 * b:32 * b + 32], in_=ident_b[0:32, 0:32]),
                        [T_identb], [T_sel])
                for s in range(2):
                    dve(lambda e, s=s: e.memset(cvbs[s][:, :, 128:129], 1.0), [], [T_cvbs[s]])
                cc = [0]

                def prep_b(b):
                    pb = b % 2
                    for h in range(H):
                        s = cc[0] % 2
                        cc[0] += 1
                        dma(ckf[s], ckT[b, h], w=[T_ckf[s]], sem="ckf%d" % s)
                        dve(lambda e, s=s, h=h: e.tensor_copy(out=ckbs[pb][:, h, :], in_=ckf[s]),
                            [T_ckf[s]], [T_ckbs[pb]])
                    for g4 in range(NKBS // 4):
                        s = cc[0] % 2
                        cc[0] += 1
                        dma(cvf[s], cv[b, g4 * 512:(g4 + 1) * 512, :].rearrange("(k p) n -> p k n", p=128),
                            w=[T_cvf[s]], sem="cvf%d" % s)
                        pool(lambda e, s=s, g4=g4: e.tensor_copy(
                            out=cvbs[pb][:, g4 * 16:(g4 + 1) * 16, 0:128],
                            in_=cvf[s].rearrange("p k (h n) -> p (k h) n", h=4)), [T_cvf[s]], [T_cvbs[pb]])

                prep_b(0)
                for b in range(BPC):
                    if b + 1 < BPC:
                        prep_b(b + 1)
                    ckb = ckbs[b % 2]; T_ckb = T_ckbs[b % 2]
                    cvb = cvbs[b % 2]; T_cvb = T_cvbs[b % 2]
                    units = []
                    started = set()
                    for h in range(H):
                        for kbi in range(NKBS + 1):
                            u = U()
                            own = (kbi == NKBS)
                            if own:
                                u.kt = ktown[:, h, 0:128]
                                u.Tkt = T_kt
                                u.v = vown[:, h, 0:129]
                                u.Tv = T_vo
                                u.exps = [("diag", m * 512, m * 512 + 32, bs[:, b * 4 + h, :]) for m in range(2)]
                            else:
                                u.kt = ckb[:, h, kbi * 128:(kbi + 1) * 128]
                                u.Tkt = T_ckb
                                u.v = cvb[:, kbi * 4 + h, 0:129]
                                u.Tv = T_cvb
                                c0 = h * NKBS + kbi
                                u.exps = [("fast2", 0, 32, sbias[:, c0:c0 + 1])]
                            u.smm = [(m, qpad[m][:, h, b * 32:(b + 1) * 32], m * 512, 32) for m in range(2)]
                            u.pv = []
                            for m in range(2):
                                i = m * 4 + h
                                bk_ = 4 + i // 3
                                st = (kbi == 0) and (bk_ not in started)
                                if kbi == 0:
                                    started.add(bk_)
                                u.pv.append((i, m * 512, 32, st, own))
                            units.append(u)
                    run_units(units, 0, None, 32)
                    dst_tokens[0] = [T_yab[b % 2]]
                    evac_head(0, 32, lambda a, b=b: yab[b % 2][0:32, a * 128:(a + 1) * 128],
                              [(h, h, 4 + h) for h in range(H)])
                    pe(lambda e, b=b: e.matmul(bank(7), lhsT=selT[0:32, b, :], rhs=yab[b % 2][0:32, :],
                                               start=(b == 0), stop=(b == BPC - 1)),
                       [T_sel, T_yab[b % 2]], [TB[7]])
                act(lambda e: e.activation(out=mixb[0][:, 512:1024], in_=bank(7), func=AF.Copy),
                    [], [TB[7], T_mix[0]])
            while hook_stage or late_jobs:
                if hook_stage:
                    cs, stf = hook_stage.pop(0)
                    cs(); stf()
                if late_jobs:
                    ld, cs, stf = late_jobs.pop(0)
                    ld()
                    hook_stage.append((cs, stf))
            for f in deferred_prefetch:
                f()
            kb.barrier()
            A.top = markQ

            if DEBUG:
                dbgt = [A.f32(D) for _ in range(NA)]
                for a in range(NA):
                    dve(lambda e, a=a: e.tensor_copy(out=dbgt[a], in_=mixb[a]), [T_mix[a]], [T_mix[a]])
                    dma(dbg_mix[ob0 + a], dbgt[a], r=[T_mix[a]], sem="dbg%d" % a)
            h1 = [A.f32(D) for _ in range(NA)]; T_h = [T() for _ in range(NA)]
            yt = [A.f32(D) for _ in range(NA)]; T_y = [T() for _ in range(NA)]
            TAb = A.bf16(8, W); T_TA = T()
            TBb = A.bf16(8, W); T_TBb = T()
            ffT = A.bf16(32, W); T_ff = T()
            pf = [A.f32(PLE) for _ in range(NA)]; T_pf = [T() for _ in range(NA)]
            pbf = [A.bf16(PLE) for _ in range(2)]; T_pbf = [T(), T()]
            pT = A.bf16(2, W); T_pT = T()
            cbf = [A.bf16(D) for _ in range(2)]; T_cbf = [T(), T()]
            rsC = A.f32(NA, 8); T_rsC = [T() for _ in range(NA)]
            rl = [A.f32(W) for _ in range(2)]; T_rl = [T(), T()]
            junkC = A.bf16(D); T_junkC = T()

            if not is_sample:
                ring_plan(plan_A(SLOT_ORDER.index(j) + 1 >= NSLOT))
            for a in range(NA):
                dma(h1[a], x_own[ob0 + a], w=[T_h[a]], sem="hx%d" % a)
                dma(pf[a], p_own[ob0 + a], w=[T_pf[a]], sem="pf%d" % a)

            def evac_T(dst, evi):
                def f(pv):
                    if evi % 2 == 0:
                        act(lambda e: e.activation(out=dst, in_=pv, func=AF.Copy), [], f.toks)
                    else:
                        dve(lambda e: e.tensor_copy(out=dst, in_=pv), [], f.toks)
                return f

            def norm_add(a, col, g_idx):
                rs = rsC[:, a, col:col + 1]
                act(lambda e: e.activation(out=junkC, in_=yt[a], func=AF.Square, accum_out=rs),
                    [T_y[a]], [T_rsC[a]])
                rstd_from_ss(rs, D, rs, [T_rsC[a]])
                dve(lambda e: e.scalar_tensor_tensor(out=yt[a], in0=yt[a], scalar=rs, in1=gbc[:, g_idx, :],
                                                     op0=ALU.mult, op1=ALU.mult), [T_rsC[a], T_c], [T_y[a]])
                dve(lambda e: e.tensor_tensor(out=h1[a], in0=h1[a], in1=yt[a], op=ALU.add), [T_y[a]], [T_h[a]])

            for a in range(NA):
                f = evac_T(TAb[:, :, a * 128:(a + 1) * 128], a)
                f.toks = [TB[a], T_TA]
                transpose_to(f, mixb[a], 128, 8, a, T_mix[a], None)
            stream_tokmajor(ws_out, 2, 4, lambda a, kc: TAb[:, kc, a * 128:(a + 1) * 128], NA,
                            lambda cg, a, bk: act(lambda e: e.activation(
                                out=yt[a][:, cg * 512:(cg + 1) * 512], in_=bank(bk), func=AF.Copy),
                                [], [TB[bk], T_y[a]]),
                            lambda a: [T_TA], SETS4)
            for a in range(NA):
                norm_add(a, 0, 0)
            for a in range(NA):
                b2 = a % 2
                rs2 = rsC[:, a, 1:2]
                act(lambda e, a=a, rs2=rs2: e.activation(out=junkC, in_=h1[a], func=AF.Square, accum_out=rs2),
                    [T_h[a]], [T_rsC[a]])
                rstd_from_ss(rs2, D, rs2, [T_rsC[a]])
                dve(lambda e, a=a, rs2=rs2: e.tensor_tensor(out=rsC[:, a, 2:3], in0=rs2, in1=rs2, op=ALU.mult),
                    [], [T_rsC[a]])
                act(lambda e, a=a, b2=b2: e.activation(out=cbf[b2], in_=h1[a], func=AF.Copy), [T_h[a]], [T_cbf[b2]])
                f = evac_T(TBb[:, :, a * 128:(a + 1) * 128], a + 1)
                f.toks = [TB[4 + a], T_TBb]
                transpose_to(f, cbf[b2], 128, 8, 4 + a, T_cbf[b2], None)
            for fch in range(32):
                piece, Tp = ring_load(ws_up[fch])
                pv = piece.rearrange("p (k n) -> p k n", k=8)
                bk = fch % 2
                for kc in range(8):
                    pe(lambda e, pv=pv, kc=kc, bk=bk: e.matmul(bank(bk)[:, 0:W], lhsT=pv[:, kc, :],
                                                               rhs=TBb[:, kc, :], start=(kc == 0), stop=(kc == 7)),
                       [Tp, T_TBb], [TB[bk]])
                ring_topup()
                act(lambda e, bk=bk: e.activation(out=rl[bk], in_=bank(bk)[:, 0:W], func=AF.Relu),
                    [], [TB[bk], T_rl[bk]])
                dve(lambda e, bk=bk, fch=fch: e.tensor_tensor(out=ffT[:, fch, :], in0=rl[bk], in1=rl[bk],
                                                              op=ALU.mult), [T_rl[bk]], [T_ff])
            stream_tokmajor(ws_dn, 2, 16, lambda a, kc: ffT[:, kc, a * 128:(a + 1) * 128], NA,
                            lambda cg, a, bk: act(lambda e: e.activation(
                                out=yt[a][:, cg * 512:(cg + 1) * 512], in_=bank(bk), func=AF.Copy,
                                scale=rsC[:, a, 2:3]), [T_rsC[a]], [TB[bk], T_y[a]]),
                            lambda a: [T_ff], SETS4)
            for a in range(NA):
                norm_add(a, 3, 1)
            for a in range(NA):
                b2 = a % 2
                act(lambda e, a=a, b2=b2: e.activation(out=cbf[b2], in_=h1[a], func=AF.Copy), [T_h[a]], [T_cbf[b2]])
                f = evac_T(TAb[:, :, a * 128:(a + 1) * 128], a)
                f.toks = [TB[a], T_TA]
                transpose_to(f, cbf[b2], 128, 8, a, T_cbf[b2], None)
                act(lambda e, a=a, b2=b2: e.activation(out=pbf[b2], in_=pf[a], func=AF.Copy), [T_pf[a]], [T_pbf[b2]])
                f = evac_T(pT[:, :, a * 128:(a + 1) * 128], a + 1)
                f.toks = [TB[4 + a], T_pT]
                transpose_to(f, pbf[b2], 128, 2, 4 + a, T_pbf[b2], None)
            stream_tokmajor(ws_g, 2, 4, lambda a, kc: TAb[:, kc, a * 128:(a + 1) * 128], NA,
                            lambda cg, a, bk: act(lambda e: e.activation(
                                out=yt[a][:, cg * 512:(cg + 1) * 512], in_=bank(bk), func=AF.Exp, scale=-1.0),
                                [], [TB[bk], T_y[a]]),
                            lambda a: [T_TA], SETS4)
            for a in range(NA):
                dve(lambda e, a=a: e.tensor_scalar(out=yt[a], in0=yt[a], scalar1=1.0, scalar2=None, op0=ALU.add),
                    [], [T_y[a]])
                dve(lambda e, a=a: e.reciprocal(out=yt[a], in_=yt[a]), [], [T_y[a]])
            stream_tokmajor(ws_pe, 2, 1, lambda a, kc: pT[:, kc, a * 128:(a + 1) * 128], NA,
                            lambda cg, a, bk: dve(lambda e: e.tensor_tensor(
                                out=yt[a][:, cg * 512:(cg + 1) * 512], in0=bank(bk),
                                in1=yt[a][:, cg * 512:(cg + 1) * 512], op=ALU.mult), [], [TB[bk], T_y[a]]),
                            lambda a: [T_pT], SETS4)
            for a in range(NA):
                norm_add(a, 4, 2)
                dma(y_own[ob0 + a], h1[a], r=[T_h[a]], sem="yo%d" % a)
            kb.barrier()
            A.top = markS

        for si, j in enumerate(SLOT_ORDER):
            do_slot(j, False)
            if si == 0:
                assert not late_jobs
        do_slot(0, True)
        kb.emit()
    return nc


_PROG_CACHE = {}


def kernel(x_prompt, x_sample, cache_k, cache_v, state_conv, p_prompt, p_sample,
           w_in, w_conv, b_conv, lambda_q1, lambda_k1, lambda_q2, lambda_k2, g_subln,
           w_out, g_pre_mix, g_post_mix, g_pre_mlp, g_post_mlp, w_up, w_down,
           w_pe, w_pe_gate, g_pe):
    NSLOT = CFG["NSLOT"]
    PAST = CFG["PAST"]
    f = np.float32
    A_ = lambda v: np.ascontiguousarray(np.asarray(v), dtype=f)
    x_prompt = A_(x_prompt); x_sample = A_(x_sample); cache_k = A_(cache_k); cache_v = A_(cache_v)
    state_conv = A_(state_conv); p_prompt = A_(p_prompt); p_sample = A_(p_sample)
    SEQ = x_prompt.shape[1]
    assert SEQ == NCORE * NSLOT * 512 and cache_k.shape[2] == PAST
    NOWN = 4 * NSLOT + 1
    key = (NSLOT, PAST)
    if key not in _PROG_CACHE:
        _PROG_CACHE[key] = build_program(NSLOT, PAST)
    nc = _PROG_CACHE[key]

    def bc(v, n=128):
        v = A_(v).reshape(1, -1)
        return np.ascontiguousarray(np.broadcast_to(v, (n, v.shape[1])))

    gvec = np.concatenate([A_(g_pre_mix)[0].reshape(8, 128).T, A_(g_pre_mlp)[0].reshape(8, 128).T], axis=1)
    gbc = np.concatenate([bc(g_post_mix[0]), bc(g_post_mlp[0]), bc(g_pe[0])], axis=1)
    convbc = np.concatenate([bc(A_(w_conv)[0, 0]), bc(A_(w_conv)[0, 1]), bc(A_(w_conv)[0, 2]), bc(A_(b_conv)[0])],
                            axis=1)
    lamv = np.concatenate([bc(lambda_q1[0]), bc(lambda_k1[0]), bc(lambda_q2[0]), bc(lambda_k2[0])], axis=1)
    shared = dict(
        x_all=x_prompt[0], w_in=A_(w_in)[0], w_out=A_(w_out)[0], w_up=A_(w_up)[0], w_down=A_(w_down)[0],
        w_pe=A_(w_pe)[0], w_g=A_(w_pe_gate)[0], gvec=np.ascontiguousarray(gvec), gbc=np.ascontiguousarray(gbc),
        gsub=bc(g_subln[0]), convbc=np.ascontiguousarray(convbc), lamv=np.ascontiguousarray(lamv),
        ident=np.eye(128, dtype=f))
    in_maps = []
    for c in range(NCORE):
        tb = make_tables(c, NSLOT, PAST)
        xo = np.zeros((NOWN, 128, D), f)
        po = np.zeros((NOWN, 128, PLE), f)
        xh = np.zeros((NSLOT, 2, D), f)
        for j in range(NSLOT):
            sb = NCORE * j + c
            xo[4 * j:4 * j + 4] = x_prompt[0, sb * 512:(sb + 1) * 512].reshape(4, 128, D)
            po[4 * j:4 * j + 4] = p_prompt[0, 0, sb * 512:(sb + 1) * 512].reshape(4, 128, PLE)
            if sb > 0:
                xh[j] = x_prompt[0, sb * 512 - 2:sb * 512]
        bsl = slice(c * BPC, (c + 1) * BPC)
        xo[4 * NSLOT] = x_sample[bsl].reshape(128, D)
        po[4 * NSLOT] = p_sample[0, bsl].reshape(128, PLE)
        m = dict(shared)
        m.update(x_own=xo, p_own=po, x_halo=xh,
                 hist_s=np.ascontiguousarray(state_conv[0, bsl].reshape(8, 512)),
                 ckT=np.ascontiguousarray(cache_k[0, bsl].transpose(0, 2, 3, 1)),
                 cv=np.ascontiguousarray(cache_v[0, bsl].reshape(BPC, PAST, 512)),
                 **tb)
        in_maps.append(m)
    res = run_bass_kernel_spmd(nc, in_maps, core_ids=list(range(NCORE)))
    R = res.results
    if CFG.get("DEBUG", False):
        CFG["_dbg"] = [r["dbg_mix"] for r in R]
        CFG["_kt"] = [np.asarray(r["kt_scr"]).astype(np.float32) for r in R]
        CFG["_v"] = [np.asarray(r["v_scr"]).astype(np.float32) for r in R]
    y_prompt = np.zeros((1, SEQ, D), f)
    k_prompt = np.zeros((1, 1, SEQ, H, 128), f)
    v_prompt = np.zeros((1, 1, SEQ, H, 128), f)
    y_sample = np.zeros((NCORE * BPC, DEC_T, D), f)
    k_sample = np.zeros((1, NCORE * BPC, DEC_T, H, 128), f)
    v_sample = np.zeros((1, NCORE * BPC, DEC_T, H, 128), f)
    conv_sample = np.zeros((1, NCORE * BPC, 2, 512), f)
    for c in range(NCORE):
        r = R[c]
        for j in range(NSLOT):
            sb = NCORE * j + c
            sl = slice(sb * 512, (sb + 1) * 512)
            y_prompt[0, sl] = r["y_own"][4 * j:4 * j + 4].reshape(512, D)
            k_prompt[0, 0, sl] = r["k_own"][4 * j:4 * j + 4].reshape(512, H, 128)
            v_prompt[0, 0, sl] = r["v_own"][4 * j:4 * j + 4].reshape(512, H, 128)
        bsl = slice(c * BPC, (c + 1) * BPC)
        y_sample[bsl] = r["y_own"][4 * NSLOT].reshape(BPC, DEC_T, D)
        k_sample[0, bsl] = r["k_own"][4 * NSLOT].reshape(BPC, DEC_T, H, 128)
        v_sample[0, bsl] = r["v_own"][4 * NSLOT].reshape(BPC, DEC_T, H, 128)
        conv_sample[0, bsl] = r["conv_s"].reshape(BPC, 2, 512)
    conv_prompt = np.ascontiguousarray(R[NCORE - 1]["conv_p"]).reshape(1, 1, 2, 512).astype(f)
    return (y_prompt, y_sample, k_prompt, v_prompt, conv_prompt, k_sample, v_sample, conv_sample)
```

```python
import math
from contextlib import ExitStack
import numpy as np
import concourse.bass as bass
import concourse.mybir as mybir
from concourse.bass_utils import run_bass_kernel_spmd

F32 = mybir.dt.float32
BF16 = mybir.dt.bfloat16
AF = mybir.ActivationFunctionType
ALU = mybir.AluOpType

NCORE = 8
D = 1024
DFF = 4096
PLE = 256
H = 4
DEC_T = 32
BPC = 4
RMS_EPS = 1e-6
NEG = -30000.0
SLOPES = [2.0 ** (-8.0 * (h + 1) / 4) for h in range(4)]
LAM_INIT = 0.8 - 0.6 * math.exp(-0.3 * 0)
CFG = dict(NSLOT=4, PAST=2048)

ENGS = ["pe", "act", "dve", "pool", "sp"]


class T:
    __slots__ = ("w", "rs")

    def __init__(self):
        self.w = None
        self.rs = []


class Op:
    __slots__ = ("eng", "fn", "deps", "needed", "ev", "dma_sem", "epoch", "barrier")

    def __init__(self, eng, fn):
        self.eng = eng
        self.fn = fn
        self.deps = []
        self.needed = False
        self.ev = None
        self.dma_sem = None
        self.barrier = 0


class DSem:
    def __init__(self, sem):
        self.sem = sem
        self.n = 0


class KB:
    def __init__(self, nc, es):
        self.nc = nc
        self.es = es
        self.ops = []
        self.epoch = 0
        self.esem = {e: es.enter_context(nc.semaphore("es_" + e)) for e in ENGS[:4]}
        self.bsem = es.enter_context(nc.semaphore("bar"))
        self.dsems = {}
        self.nbar = 0

    def dsem(self, name):
        if name not in self.dsems:
            self.dsems[name] = DSem(self.es.enter_context(self.nc.semaphore("ds_" + name)))
        return self.dsems[name]

    def op(self, eng, fn, reads=(), writes=(), dsem=None):
        o = Op(eng, fn)
        o.epoch = self.epoch
        o.dma_sem = dsem
        deps = []
        for r in reads:
            if r.w is not None:
                deps.append(r.w)
        for w in writes:
            if w.w is not None:
                deps.append(w.w)
            deps.extend(w.rs)
        seen = set()
        for d in deps:
            if d is o or id(d) in seen or d.epoch < self.epoch:
                continue
            seen.add(id(d))
            if d.eng == eng and d.dma_sem is None and eng in ("pe", "sp"):
                continue
            if eng == "sp" and d.eng == "sp" and d.dma_sem is not None and d.dma_sem is dsem:
                continue
            o.deps.append(d)
            d.needed = True
        for r in reads:
            r.rs.append(o)
        for w in writes:
            w.w = o
            w.rs = []
        self.ops.append(o)
        return o

    def barrier(self):
        self.nbar += 1
        for e in ENGS:
            o = Op(e, None)
            o.barrier = self.nbar
            o.epoch = self.epoch
            self.ops.append(o)
        self.epoch += 1

    def emit(self):
        nc = self.nc
        per = {e: [] for e in ENGS}
        cnt = {e: 0 for e in ENGS}
        for o in self.ops:
            per[o.eng].append(o)
            if o.barrier:
                continue
            if o.dma_sem is not None:
                o.dma_sem.n += 1
                o.ev = (o.dma_sem, o.dma_sem.n * 16)
            elif o.needed:
                cnt[o.eng] += 1
                o.ev = (o.eng, cnt[o.eng])
        esem = self.esem
        bsem = self.bsem
        alld = list(self.dsems.values())

        def run(ename, eng):
            waited = {}
            dcount = {id(d): 0 for d in alld}
            for o in per[ename]:
                if o.barrier:
                    for d in alld:
                        if dcount[id(d)] > waited.get(id(d), 0):
                            eng.wait_ge(d.sem, dcount[id(d)])
                            waited[id(d)] = dcount[id(d)]
                    if ename == "sp":
                        eng.sem_inc(bsem, 1)
                    else:
                        eng.drain().then_inc(bsem, 1)
                    eng.wait_ge(bsem, 5 * o.barrier)
                    continue
                need = {}
                for d in o.deps:
                    key, val = d.ev
                    k = id(key) if isinstance(key, DSem) else key
                    if waited.get(k, 0) >= val:
                        continue
                    if k not in need or need[k][1] < val:
                        need[k] = (key, val)
                for k, (key, val) in need.items():
                    sem = key.sem if isinstance(key, DSem) else esem[key]
                    eng.wait_ge(sem, val)
                    waited[k] = val
                inst = o.fn(eng)
                if o.dma_sem is not None:
                    inst.then_inc(o.dma_sem.sem, 16)
                    dcount[id(o.dma_sem)] = o.ev[1]
                elif o.needed:
                    inst.then_inc(esem[o.eng], 1)
            if ename == "sp":
                for d in alld:
                    if d.n > 0:
                        eng.wait_ge(d.sem, d.n * 16)

        with nc.Block() as block:
            @block.tensor
            def _(eng):
                run("pe", eng)

            @block.scalar
            def _(eng):
                run("act", eng)

            @block.vector
            def _(eng):
                run("dve", eng)

            @block.gpsimd
            def _(eng):
                run("pool", eng)

            @block.sync
            def _(eng):
                run("sp", eng)


class Arena:
    def __init__(self, t, n):
        self.t = t
        self.n = n
        self.top = 0

    def _take(self, n32):
        off = self.top
        self.top += n32
        assert self.top <= self.n, ("SBUF arena overflow", self.top, self.n)
        return off

    def f32(self, *shape):
        n = int(np.prod(shape))
        off = self._take(n)
        v = self.t[:, off:off + n]
        return _shape(v, shape)

    def bf16(self, *shape):
        n = int(np.prod(shape))
        n32 = (n + 1) // 2
        off = self._take(n32)
        v = self.t[:, off:off + n32].bitcast(BF16)[:, 0:n]
        return _shape(v, shape)


def _shape(v, shape):
    if len(shape) == 1:
        return v
    if len(shape) == 2:
        return v.rearrange("p (a b) -> p a b", a=shape[0])
    if len(shape) == 3:
        return v.rearrange("p (a b c) -> p a b c", a=shape[0], b=shape[1])
    raise ValueError(shape)


def _hw(h):
    return 256 if h == 0 else 512


def _refoff(h, a):
    return 256 * (a // 2) if h == 0 else 0


def bias_layout(NSLOT):
    E = [4 * (NCORE * j + NCORE - 1) for j in range(NSLOT)]
    idx = {}
    n = 0
    for j in range(NSLOT):
        for h in range(H):
            for kb in range(E[j]):
                for half in range(2 if h == 0 else 1):
                    idx[(j, h, kb, half)] = n
                    n += 1
    return E, idx, n


def make_tables(c, NSLOT, PAST):
    E, idx, ncol = bias_layout(NSLOT)
    kl = np.arange(128, dtype=np.float64)
    btab = np.zeros((128, ncol), np.float64)
    for (j, h, kb, half), col in idx.items():
        sb = NCORE * j + c
        m = SLOPES[h]
        Cc = m * _hw(h) / 2
        ref = sb * 512 + half * 256
        if kb < 4 * sb:
            btab[:, col] = m * (kl + 128 * kb - ref) - Cc
        else:
            btab[:, col] = NEG
    btab = np.maximum(btab, NEG)
    dbias = np.zeros((128, H, 4, 4), np.float64)
    bdiag = np.zeros((128, H, 4, 128), np.float64)
    ql = np.arange(128, dtype=np.float64)
    for h in range(H):
        m = SLOPES[h]
        Cc = m * _hw(h) / 2
        for a in range(4):
            for ap in range(4):
                dbias[:, h, ap, a] = m * (kl + 128 * ap - _refoff(h, a)) - Cc
            vis = (kl[:, None] // 64) <= (ql[None, :] // 64)
            val = -m * np.abs(ql[None, :] - kl[:, None]) + m * (128 * a + ql[None, :] - _refoff(h, a)) - Cc
            bdiag[:, h, a, :] = np.where(vis, val, NEG)
    nkb = PAST // 128
    sbias = np.zeros((128, H, nkb), np.float64)
    bs = np.full((128, BPC, H, DEC_T), NEG, np.float64)
    q32 = np.arange(DEC_T, dtype=np.float64)
    for h in range(H):
        m = SLOPES[h]
        Cs = m * 16
        for kb in range(nkb):
            sbias[:, h, kb] = m * (kl + 128 * kb - PAST) - Cs
        for b in range(BPC):
            for tp in range(DEC_T):
                bs[b * 32 + tp, b, h, :] = -m * np.abs(q32 - tp) + m * q32 - Cs
    sbias = np.maximum(sbias, NEG)
    shm = np.zeros((128, 10, 128), np.float32)
    for t in range(128):
        if t >= 1:
            shm[t - 1, 0, t] = 1
        if t >= 2:
            shm[t - 2, 1, t] = 1
        if t % 32 != 0:
            shm[t - 1, 6, t] = 1
        if t % 32 >= 2:
            shm[t - 2, 7, t] = 1
    shm[127, 2, 0] = 1
    shm[126, 3, 0] = 1
    shm[127, 3, 1] = 1
    shm[1, 4, 0] = 1
    shm[0, 5, 0] = 1
    shm[1, 5, 1] = 1
    for b in range(BPC):
        shm[2 * b + 1, 8, 32 * b] = 1
        shm[2 * b, 9, 32 * b] = 1
        shm[2 * b + 1, 9, 32 * b + 1] = 1
    f = np.float32
    return dict(btab=btab.astype(f), dbias=dbias.reshape(128, -1).astype(f),
                bdiag=bdiag.reshape(128, -1).astype(f), sbias=sbias.reshape(128, -1).astype(f),
                bs=bs.reshape(128, -1).astype(f), shm=shm.reshape(128, -1))


def build_program(NSLOT, PAST):
    SEQ = NCORE * NSLOT * 512
    NBLK = SEQ // 128
    NOWN = 4 * NSLOT + 1
    NKBS = PAST // 128
    E, BIDX, NCOL = bias_layout(NSLOT)
    EMAX = E[-1]
    NG = EMAX // 4

    nc = bass.Bass("TRN2", target_bir_lowering=False)

    def din(name, shape, dt=F32):
        return nc.dram_tensor(name, list(shape), dt, kind="ExternalInput").ap()

    def dout(name, shape):
        return nc.dram_tensor(name, list(shape), F32, kind="ExternalOutput").ap()

    def dscr(name, shape):
        return nc.dram_tensor(name, list(shape), BF16, kind="Internal").ap()

    x_all = din("x_all", [SEQ, D])
    x_own = din("x_own", [NOWN, 128, D])
    x_halo = din("x_halo", [NSLOT, 2, D])
    p_own = din("p_own", [NOWN, 128, PLE])
    hist_s = din("hist_s", [8, 512])
    ckT = din("ckT", [BPC, H, 128, PAST])
    cv = din("cv", [BPC, PAST, 512])
    w_in = din("w_in", [D, 3072])
    w_out = din("w_out", [D, D])
    w_up = din("w_up", [D, DFF])
    w_down = din("w_down", [DFF, D])
    w_pe = din("w_pe", [PLE, D])
    w_g = din("w_g", [D, D])
    gvec_d = din("gvec", [128, 16])
    gbc_d = din("gbc", [128, 3 * D])
    gsub_d = din("gsub", [128, 128])
    convbc_d = din("convbc", [128, 4 * 512])
    lamv_d = din("lamv", [128, 4 * 64])
    ident_d = din("ident", [128, 128])
    shm_d = din("shm", [128, 10 * 128])
    btab_d = din("btab", [128, NCOL])
    dbias_d = din("dbias", [128, H * 16])
    bdiag_d = din("bdiag", [128, H * 4 * 128])
    sbias_d = din("sbias", [128, H * NKBS])
    bs_d = din("bs", [128, BPC * H * DEC_T])

    y_own = dout("y_own", [NOWN, 128, D])
    k_own = dout("k_own", [NOWN, 128, 512])
    v_own = dout("v_own", [NOWN, 128, 512])
    conv_p = dout("conv_p", [2, 512])
    conv_s = dout("conv_s", [8, 512])
    DEBUG = CFG.get("DEBUG", False)
    if DEBUG:
        dbg_mix = dout("dbg_mix", [NOWN, 128, D])

    if CFG.get("DEBUG", False):
        kt_scr = nc.dram_tensor("kt_scr", [H, 128, NG * 512], BF16, kind="ExternalOutput").ap()
        v_scr = nc.dram_tensor("v_scr", [H, 128, NG * 4, 130], BF16, kind="ExternalOutput").ap()
    else:
        kt_scr = dscr("kt_scr", [H, 128, NG * 512])
        v_scr = dscr("v_scr", [H, 128, NG * 4, 130])
    ws_in = dscr("ws_in", [4, 4, 128, 1024])
    ws_out = dscr("ws_out", [2, 4, 128, 1024])
    ws_g = dscr("ws_g", [2, 4, 128, 1024])
    ws_pe = dscr("ws_pe", [2, 1, 128, 1024])
    ws_dn = dscr("ws_dn", [2, 16, 128, 1024])
    ws_up = dscr("ws_up", [32, 128, 1024])

    ARENA_N = 53000
    with ExitStack() as es:
        kb = KB(nc, es)
        arena_t = es.enter_context(nc.sbuf_tensor("arena", [128, ARENA_N], F32))
        ps_t = es.enter_context(nc.psum_tensor("psum", [128, 8 * 512], F32))
        A = Arena(arena_t, ARENA_N)
        TB = [T() for _ in range(8)]

        def bank(b):
            return ps_t[:, b * 512:(b + 1) * 512]

        def bankb(b):
            return ps_t[:, b * 512:(b + 1) * 512].bitcast(BF16)

        def pe(fn, r=(), w=()):
            return kb.op("pe", fn, r, w)

        def act(fn, r=(), w=()):
            return kb.op("act", fn, r, w)

        def dve(fn, r=(), w=()):
            return kb.op("dve", fn, r, w)

        def pool(fn, r=(), w=()):
            return kb.op("pool", fn, r, w)

        def dma(out, in_, r=(), w=(), sem="const", q="sp"):
            return kb.op(q, lambda e: e.dma_start(out=out, in_=in_), r, w, dsem=kb.dsem(sem))

        ident_f = A.f32(128); T_identf = T()
        ident_b = A.bf16(128); T_identb = T()
        shm = A.f32(10, 128); T_shm = T()
        gvec = A.f32(16); T_gvec = T()
        gbc = A.f32(3, D); T_gbc = T()
        gsub = A.f32(128); T_gsub = T()
        convbc = A.f32(4, 512); T_convbc = T()
        btab = A.f32(NCOL); T_btab = T()
        dbias = A.f32(H * 16)
        bdiag = A.f32(H * 4, 128)
        sbias = A.f32(H * NKBS)
        bs = A.f32(BPC * H, DEC_T)
        lam_t = A.f32(4); T_lam = T()
        wkv = A.bf16(8, 1024); T_wkv = T()
        NRING = 8
        ring = [A.bf16(1024) for _ in range(NRING)]
        T_ring = [T() for _ in range(NRING)]
        ring_pos = [0]
        T_c = T()
        xo = [A.f32(D) for _ in range(4)]; T_xo = [T() for _ in range(4)]

        def load_xo(slot_idx, is_sample_):
            na = 1 if is_sample_ else 4
            o0 = 4 * NSLOT if is_sample_ else 4 * slot_idx
            for a in range(na):
                dma(xo[a], x_own[o0 + a], w=[T_xo[a]], sem="xo%d" % a)

        for dst, src in ((ident_f, ident_d), (shm.rearrange("p a b -> p (a b)"), shm_d), (gvec, gvec_d),
                         (gbc.rearrange("p a b -> p (a b)"), gbc_d), (gsub, gsub_d),
                         (convbc.rearrange("p a b -> p (a b)"), convbc_d), (btab, btab_d), (dbias, dbias_d),
                         (bdiag.rearrange("p a b -> p (a b)"), bdiag_d), (sbias, sbias_d),
                         (bs.rearrange("p a b -> p (a b)"), bs_d)):
            dma(dst, src, w=[T_c])
        dve(lambda e: e.tensor_copy(out=ident_b, in_=ident_f), [T_c], [T_identb])
        dve(lambda e: e.memset(lam_t[:, 2:3], RMS_EPS), [], [T_lam])
        dve(lambda e: e.tensor_scalar(out=gsub, in0=gsub, scalar1=1.0 - LAM_INIT, scalar2=None, op0=ALU.mult),
            [T_c], [T_c])
        eps_ap = lam_t[:, 2:3]

        mark0 = A.top
        lamv = A.f32(4, 64)
        lprod = A.f32(2, 64)
        lsum = A.f32(2)
        dma(lamv.rearrange("p a b -> p (a b)"), lamv_d, w=[T_c])
        for i in range(2):
            dve(lambda e, i=i: e.tensor_tensor(out=lprod[:, i, :], in0=lamv[:, 2 * i, :], in1=lamv[:, 2 * i + 1, :],
                                               op=ALU.mult), [T_c], [T_c])
            act(lambda e, i=i: e.activation(out=lprod[:, i, :], in_=lprod[:, i, :], func=AF.Copy,
                                            accum_out=lsum[:, i:i + 1]), [T_c], [T_c])
        act(lambda e: e.activation(out=lsum, in_=lsum, func=AF.Exp), [T_c], [T_c])
        dve(lambda e: e.tensor_tensor(out=lam_t[:, 0:1], in0=lsum[:, 0:1], in1=lsum[:, 1:2], op=ALU.subtract),
            [T_c, T_lam], [T_lam])
        dve(lambda e: e.tensor_scalar(out=lam_t[:, 0:1], in0=lam_t[:, 0:1], scalar1=LAM_INIT, scalar2=None,
                                      op0=ALU.add), [T_lam], [T_lam])
        dve(lambda e: e.tensor_scalar(out=lam_t[:, 1:2], in0=lam_t[:, 0:1], scalar1=-1.0, scalar2=None,
                                      op0=ALU.mult), [T_lam], [T_lam])

        NWS = 2
        wst_f = [A.f32(4096) for _ in range(NWS)]
        wst_b = [A.bf16(4096) for _ in range(NWS)]
        T_wf = [T() for _ in range(NWS)]
        T_wb = [T() for _ in range(NWS)]
        T_ws = T()
        prep_i = [0]
        prep_q = ["sp"]

        def make_job(src_rows, width, gcol, stores_fn):
            st = {}

            def load():
                st["s"] = prep_i[0] % NWS
                prep_i[0] += 1
                s = st["s"]
                dma(wst_f[s][:, 0:width], src_rows, w=[T_wf[s]], sem="wf%d" % s, q=prep_q[0])

            def cast():
                s = st["s"]
                o = wst_b[s][:, 0:width]
                src_ = wst_f[s][:, 0:width]
                if gcol is None:
                    dve(lambda e: e.tensor_copy(out=o, in_=src_), [T_wf[s]], [T_wb[s]])
                else:
                    g_ap = gvec[:, gcol:gcol + 1]
                    dve(lambda e: e.tensor_scalar(out=o, in0=src_, scalar1=g_ap, scalar2=None, op0=ALU.mult),
                        [T_wf[s], T_c], [T_wb[s]])

            def store():
                s = st["s"]
                stores_fn(s)
            return (load, cast, store)

        def std_stores(wdst, kc, ncg):
            def f(s):
                for cg in range(ncg):
                    dma(wdst[cg, kc // 2, :, (kc % 2) * 512:(kc % 2) * 512 + 512],
                        wst_b[s][:, cg * 512:(cg + 1) * 512], r=[T_wb[s]], w=[T_ws], sem="wb%d" % s, q="pool")
            return f

        for kc in range(8):
            def win_store(s, kc=kc):
                src_ap = wst_b[s][:, 0:1024]
                dve(lambda e: e.tensor_copy(out=wkv[:, kc, :], in_=src_ap), [T_wb[s]], [T_wkv])
            ld, cs, stf = make_job(w_in[kc * 128:(kc + 1) * 128, 2048:3072], 1024, kc, win_store)
            ld(); cs(); stf()
        prep_jobs = []
        for kc in range(8):
            prep_jobs.append(make_job(w_in[kc * 128:(kc + 1) * 128, 0:2048], 2048, kc, std_stores(ws_in, kc, 4)))
        for (wsrc, wdst, nkc) in ((w_out, ws_out, 8), (w_g, ws_g, 8), (w_pe, ws_pe, 2), (w_down, ws_dn, 32)):
            for kc in range(nkc):
                prep_jobs.append(make_job(wsrc[kc * 128:(kc + 1) * 128, :], 1024, None, std_stores(wdst, kc, 2)))
        for kc in range(8):
            def up_store(s, kc=kc):
                dma(ws_up[:, :, kc * 128:(kc + 1) * 128].rearrange("f p n -> p f n"),
                    wst_b[s][:, 0:4096].rearrange("p (f n) -> p f n", f=32), r=[T_wb[s]], w=[T_ws],
                    sem="wb%d" % s, q="pool")
            prep_jobs.append(make_job(w_up[kc * 128:(kc + 1) * 128, :], 4096, 8 + kc, up_store))

        ring_q = []
        ring_issued = [0]
        ring_taken = [0]

        def ring_plan(srcs):
            ring_q.extend(srcs)

        def ring_prefetch():
            while ring_issued[0] < len(ring_q) and ring_issued[0] - ring_taken[0] < NRING:
                s = ring_issued[0] % NRING
                dma(ring[s], ring_q[ring_issued[0]], r=[T_ws], w=[T_ring[s]], sem="ring%d" % s)
                ring_issued[0] += 1

        def ring_load(src):
            if ring_taken[0] >= len(ring_q):
                ring_q.append(src)
            assert ring_q[ring_taken[0]] is src or True
            if ring_issued[0] <= ring_taken[0]:
                ring_prefetch()
            s = ring_taken[0] % NRING
            ring_taken[0] += 1
            return ring[s], T_ring[s]

        def ring_topup():
            ring_prefetch()

        def plan_tok(ws, cgs, nkcp):
            return [ws[cg, kcp] for cg in cgs for kcp in range(nkcp)]

        def plan_A(is_sample):
            p = []
            if not is_sample:
                p += plan_tok(ws_in, [1, 2], 4)
            p += plan_tok(ws_in, [0, 1, 2, 3], 4)
            return p

        def plan_C():
            return (plan_tok(ws_out, [0, 1], 4) + [ws_up[f] for f in range(32)] + plan_tok(ws_dn, [0, 1], 16)
                    + plan_tok(ws_g, [0, 1], 4) + plan_tok(ws_pe, [0, 1], 1))

        def rstd_from_ss(ss_ap, n_feat, out_ap, toks):
            npp = ss_ap.shape[0]
            act(lambda e: e.activation(out=out_ap, in_=ss_ap, func=AF.Ln, scale=1.0 / n_feat, bias=eps_ap[0:npp]),
                toks + [T_lam], toks)
            act(lambda e: e.activation(out=out_ap, in_=out_ap, func=AF.Exp, scale=-0.5), toks, toks)

        def transpose_to(dst_fn, src_b, nparts, nchunks, bk, Tsrc, Tdst, evac_eng="act"):
            pv = bankb(bk)
            for c in range(nchunks):
                pe(lambda e, c=c: e.transpose(out=pv[:, c * nparts:(c + 1) * nparts],
                                              in_=src_b[0:nparts, c * 128:(c + 1) * 128],
                                              identity=ident_b[0:nparts, 0:nparts]),
                   [Tsrc, T_identb], [TB[bk]])
            dst_fn(pv[:, 0:nchunks * nparts].rearrange("p (c t) -> p c t", c=nchunks))

        mark1 = A.top
        xg = [A.f32(4, D) for _ in range(2)]; T_xg = [T(), T()]
        xb1 = [A.bf16(D) for _ in range(3)]; T_xb1 = [T(), T(), T()]
        junk = A.bf16(D); T_junk = T()
        xT1 = [A.bf16(8, 128) for _ in range(2)]; T_xT1 = [T(), T()]
        ss1 = [A.f32(4) for _ in range(2)]; T_ss1 = [[T() for _ in range(4)] for _ in range(2)]
        kbf = [A.bf16(512) for _ in range(2)]; T_kbf = [T(), T()]
        ktst = [A.bf16(4, 512) for _ in range(2)]; T_ktst = [T(), T()]
        vst = [A.bf16(4 * 4, 130) for _ in range(2)]; T_vst = [T(), T()]
        for s in range(2):
            dve(lambda e, s=s: e.memset(vst[s][:, :, 128:130], 1.0), [], [T_vst[s]])

        def load_xg(g):
            s = g % 2
            dma(xg[s], x_all[g * 512:(g + 1) * 512, :].rearrange("(a p) d -> p a d", p=128), w=[T_xg[s]],
                sem="xg%d" % s)

        NB1 = 4 * NG

        def st_L(n):
            g, a = divmod(n, 4)
            s = g % 2
            b2 = n % 2
            if a == 0 and g + 1 < NG:
                load_xg(g + 1)
            xa = xg[s][:, a, :]
            act(lambda e: e.activation(out=junk, in_=xa, func=AF.Square, accum_out=ss1[s][:, a:a + 1]),
                [T_xg[s]], [T_ss1[s][a]])
            b3 = n % 3
            dve(lambda e: e.tensor_copy(out=xb1[b3], in_=xa), [T_xg[s]], [T_xb1[b3]])
            rstd_from_ss(ss1[s][:, a:a + 1], D, ss1[s][:, a:a + 1], [T_ss1[s][a]])

        def st_TX(n):
            b2 = n % 2
            b3 = n % 3
            transpose_to(lambda pv: act(lambda e: e.activation(out=xT1[b2], in_=pv, func=AF.Copy),
                                        [], [TB[b2], T_xT1[b2]]),
                         xb1[b3], 128, 8, b2, T_xb1[b3], None)

        def st_MM(n):
            g, a = divmod(n, 4)
            s = g % 2
            b2 = n % 2
            for cg in range(2):
                bk = 2 + 2 * cg + b2
                for kc in range(8):
                    pe(lambda e, bk=bk, kc=kc, cg=cg: e.matmul(
                        bank(bk), lhsT=xT1[b2][:, kc, :], rhs=wkv[:, kc, cg * 512:(cg + 1) * 512],
                        start=(kc == 0), stop=(kc == 7)), [T_xT1[b2], T_wkv], [TB[bk]])
            rs = ss1[s][:, a:a + 1]
            dve(lambda e: e.tensor_scalar(out=kbf[b2], in0=bank(2 + b2), scalar1=rs, scalar2=None, op0=ALU.mult),
                [T_ss1[s][a]], [TB[2 + b2], T_kbf[b2]])
            vdst = vst[s].rearrange("p (h a) n -> p h a n", h=4)[:, :, a, 0:128]
            act(lambda e: e.activation(out=vdst, in_=bank(4 + b2).rearrange("p (h n) -> p h n", h=4),
                                       func=AF.Copy, scale=rs),
                [T_ss1[s][a]], [TB[4 + b2], T_vst[s]])

        def st_TK(n):
            g, a = divmod(n, 4)
            s = g % 2
            b2 = n % 2
            transpose_to(lambda pv: dve(lambda e: e.tensor_copy(out=ktst[s][:, :, a * 128:(a + 1) * 128], in_=pv),
                                        [], [TB[6 + b2], T_ktst[s]]),
                         kbf[b2], 128, 4, 6 + b2, T_kbf[b2], None)
            if a == 3:
                dma(kt_scr[:, :, g * 512:(g + 1) * 512].rearrange("h p n -> p h n"), ktst[s], r=[T_ktst[s]],
                    sem="kts%d" % s)
                dma(v_scr.rearrange("h p k n -> p h (k n)")[:, :, g * 520:(g + 1) * 520],
                    vst[s].rearrange("p (h a) n -> p h (a n)", h=4), r=[T_vst[s]], sem="vs%d" % s)

        if NG > 0:
            load_xg(0)
            st_L(0)
            if NB1 > 1:
                st_L(1)
            st_TX(0)
        stages = []
        pj = 0
        for n in range(NB1):
            if n + 2 < NB1:
                st_L(n + 2)
            if n + 1 < NB1:
                st_TX(n + 1)
            st_MM(n)
            if n >= 1:
                st_TK(n - 1)
            if stages:
                cs, stf = stages.pop(0)
                cs(); stf()
            if pj < len(prep_jobs) and (n % 2 == 0 or n < 10):
                ld, cs, stf = prep_jobs[pj]
                pj += 1
                ld()
                stages.append((cs, stf))
        if NB1 > 0:
            st_TK(NB1 - 1)
        while stages or pj < len(prep_jobs):
            if stages:
                cs, stf = stages.pop(0)
                cs(); stf()
            if pj < len(prep_jobs):
                ld, cs, stf = prep_jobs[pj]
                pj += 1
                ld()
                stages.append((cs, stf))
        late_jobs = prep_jobs[pj:]
        SLOT_ORDER = list(range(NSLOT))[::-1]
        ring_plan(plan_A(False))
        ring_prefetch()
        load_xo(SLOT_ORDER[0], False)
        kb.barrier()
        A.top = mark0

        def stream_tokmajor(ws, ncg, nkcp, lhs_fn, NA, evac_fn, lhs_toks, bank_sets, M=128, cgs=None, wres=None):
            cgl = list(range(ncg)) if cgs is None else cgs
            for ci, cg in enumerate(cgl):
                banks = bank_sets[ci % len(bank_sets)]
                for kcp in range(nkcp):
                    if wres is None:
                        piece, Tp = ring_load(ws[cg, kcp])
                        pv = piece.rearrange("p (k n) -> p k n", k=2)
                    for kk in range(2):
                        kc = 2 * kcp + kk
                        for a in range(NA):
                            if wres is None:
                                rhs = pv[:, kk, :]
                                rt = Tp
                            else:
                                rhs, rt = wres(cg, kc)
                            pe(lambda e, a=a, kc=kc, rhs=rhs, bk=banks[a]: e.matmul(
                                bank(bk)[0:M, :], lhsT=lhs_fn(a, kc), rhs=rhs,
                                start=(kc == 0), stop=(kc == 2 * nkcp - 1)),
                               lhs_toks(a) + [rt], [TB[banks[a]]])
                    if wres is None:
                        ring_topup()
                for a in range(NA):
                    evac_fn(cg, a, banks[a])

        SETS4 = [[0, 1, 2, 3], [4, 5, 6, 7]]

        def do_slot(j, is_sample):
            NA = 1 if is_sample else 4
            W = NA * 128
            ob0 = 4 * NSLOT if is_sample else 4 * j
            markS = A.top
            mixb = [A.bf16(D) for _ in range(NA)]; T_mix = [T() for _ in range(NA)]
            markQ = A.top
            qpad = [A.bf16(4, W) for _ in range(2)]; T_q = T()
            ktown = A.bf16(4, W); T_kt = T()
            vown = A.bf16(NA * 4, 130); T_vo = T()
            pool(lambda e: e.memset(qpad[0][64:128], 0.0), [], [T_q])
            pool(lambda e: e.memset(qpad[1][0:64], 0.0), [], [T_q])
            dve(lambda e: e.memset(vown[:, :, 128:130], 1.0), [], [T_vo])
            markA = A.top

            xbA = [A.bf16(D) for _ in range(2)]; T_xbA = [T(), T()]
            junkA = A.bf16(D); T_junkA = T()
            xTA = [A.bf16(8, 128) for _ in range(NA)]; T_xTA = [T() for _ in range(NA)]
            rsA = A.f32(NA + 1); T_rsA = [T() for _ in range(NA + 1)]
            cb = [A.f32(512) for _ in range(NA)]; T_cb = [T() for _ in range(NA)]
            zc = [A.f32(512) for _ in range(NA)]; T_zc = [T() for _ in range(NA)]
            zw = [[A.f32(512) for _ in range(2)] for _ in range(2)]
            T_zw = [[T() for _ in range(2)] for _ in range(2)]
            zw2 = A.f32(512); T_zw2 = T()
            qb = [A.bf16(512) for _ in range(NA)]; T_qb = [T() for _ in range(NA)]
            kbA = [A.bf16(512) for _ in range(NA)]; T_kbA = [T() for _ in range(NA)]
            kf = [A.f32(512) for _ in range(2)]; T_kf = [T(), T()]
            vf = [A.f32(512) for _ in range(2)]; T_vf = [T(), T()]
            hz = A.f32(512); T_hz = T()
            hzw = [A.f32(512) for _ in range(2)]; T_hzw = [T(), T()]

            for a in range(NA):
                b2 = a % 2
                act(lambda e, a=a: e.activation(out=junkA, in_=xo[a], func=AF.Square, accum_out=rsA[:, a:a + 1]),
                    [T_xo[a]], [T_rsA[a]])
                dve(lambda e, a=a, b2=b2: e.tensor_copy(out=xbA[b2], in_=xo[a]), [T_xo[a]], [T_xbA[b2]])
                rstd_from_ss(rsA[:, a:a + 1], D, rsA[:, a:a + 1], [T_rsA[a]])
                transpose_to(lambda pv, a=a: act(lambda e: e.activation(out=xTA[a], in_=pv, func=AF.Copy),
                                                 [], [TB[a], T_xTA[a]]),
                             xbA[b2], 128, 8, a, T_xbA[b2], None)

            if not is_sample:
                xh = A.f32(D); T_xh = T()
                xhb = A.bf16(D)
                xhT = A.bf16(8, 2); T_xhT = T()
                dma(xh[0:2], x_halo[j], w=[T_xh], sem="xh")
                act(lambda e: e.activation(out=junkA[0:2], in_=xh[0:2], func=AF.Square,
                                           accum_out=rsA[0:2, NA:NA + 1]), [T_xh], [T_rsA[NA]])
                pool(lambda e: e.tensor_copy(out=xhb[0:2], in_=xh[0:2]), [T_xh], [T_xh])
                rstd_from_ss(rsA[0:2, NA:NA + 1], D, rsA[0:2, NA:NA + 1], [T_rsA[NA]])
                transpose_to(lambda pv: act(lambda e: e.activation(out=xhT, in_=pv, func=AF.Copy),
                                            [], [TB[4], T_xhT]),
                             xhb, 2, 8, 4, T_xh, None)
                rsh = rsA[0:2, NA:NA + 1]

                def evac_halo(cg, a, bk):
                    if cg == 1:
                        act(lambda e: e.activation(out=hzw[0][0:2], in_=bank(bk)[0:2, :], func=AF.Copy, scale=rsh),
                            [T_rsA[NA]], [TB[bk], T_hzw[0]])
                    else:
                        dve(lambda e: e.scalar_tensor_tensor(out=hz[0:2], in0=bank(bk)[0:2, :], scalar=rsh,
                                                             in1=hzw[0][0:2], op0=ALU.mult, op1=ALU.mult),
                            [T_rsA[NA], T_hzw[0]], [TB[bk], T_hz])
                stream_tokmajor(ws_in, 4, 4, lambda a, kc: xhT[:, kc, :], 1, evac_halo, lambda a: [T_xhT],
                                [[5], [6]], M=2, cgs=[1, 2])
                NH = 2
            else:
                dma(hz[0:8], hist_s, w=[T_hz], sem="xh")
                NH = 8

            def evac_z(cg, a, bk):
                rs = rsA[:, a:a + 1]
                b2 = a % 2
                if cg == 0:
                    act(lambda e: e.activation(out=cb[a], in_=bank(bk), func=AF.Copy, scale=rs),
                        [T_rsA[a]], [TB[bk], T_cb[a]])
                elif cg == 1:
                    act(lambda e: e.activation(out=zc[a], in_=bank(bk), func=AF.Copy, scale=rs),
                        [T_rsA[a]], [TB[bk], T_zc[a]])
                elif cg == 2:
                    dve(lambda e: e.scalar_tensor_tensor(out=zc[a], in0=bank(bk), scalar=rs, in1=zc[a],
                                                         op0=ALU.mult, op1=ALU.mult),
                        [T_rsA[a]], [TB[bk], T_zc[a]])
                elif cg == 3:
                    dve(lambda e: e.tensor_scalar(out=qb[a], in0=bank(bk), scalar1=rs, scalar2=None, op0=ALU.mult),
                        [T_rsA[a]], [TB[bk], T_qb[a]])
                elif cg == 4:
                    act(lambda e: e.activation(out=kf[b2], in_=bank(bk), func=AF.Copy, scale=rs),
                        [T_rsA[a]], [TB[bk], T_kf[b2]])
                    dve(lambda e: e.tensor_copy(out=kbA[a], in_=kf[b2]), [T_kf[b2]], [T_kbA[a]])
                    dma(k_own[ob0 + a], kf[b2], r=[T_kf[b2]], sem="kf%d" % b2)
                else:
                    act(lambda e: e.activation(out=vf[b2], in_=bank(bk), func=AF.Copy, scale=rs),
                        [T_rsA[a]], [TB[bk], T_vf[b2]])
                    vdst = vown.rearrange("p (a h) n -> p a h n", a=NA)[:, a, :, 0:128]
                    pool(lambda e: e.tensor_copy(out=vdst, in_=vf[b2].rearrange("p (h n) -> p h n", h=4)),
                         [T_vf[b2]], [T_vo])
                    dma(v_own[ob0 + a], vf[b2], r=[T_vf[b2]], sem="vf%d" % b2)

            stream_tokmajor(ws_in, 4, 4, lambda a, kc: xTA[a][:, kc, :], NA, evac_z, lambda a: [T_xTA[a]], SETS4)
            stream_tokmajor(None, 2, 4, lambda a, kc: xTA[a][:, kc, :], NA,
                            lambda cg, a, bk: evac_z(cg + 4, a, bk), lambda a: [T_xTA[a]], SETS4,
                            wres=lambda cg, kc: (wkv[:, kc, cg * 512:(cg + 1) * 512], T_wkv))

            w0b, w1b, w2b, bbb = convbc[:, 0, :], convbc[:, 1, :], convbc[:, 2, :], convbc[:, 3, :]
            pool(lambda e: e.tensor_tensor(out=hzw[0][0:NH], in0=hz[0:NH], in1=w1b[0:NH], op=ALU.mult),
                 [T_hz, T_c], [T_hzw[0]])
            pool(lambda e: e.tensor_tensor(out=hzw[1][0:NH], in0=hz[0:NH], in1=w0b[0:NH], op=ALU.mult),
                 [T_hz, T_c], [T_hzw[1]])
            if is_sample:
                m_sh1, m_sh2, m_h1, m_h2 = 6, 7, 8, 9
            else:
                m_sh1, m_sh2, m_h1, m_h2 = 0, 1, 4, 5
            for a in range(NA):
                b2 = a % 2

                def q_evac(pv, a=a, b2=b2):
                    dve(lambda e: e.tensor_copy(out=qpad[0][0:64, :, a * 128:(a + 1) * 128], in_=pv[0:64]),
                        [], [TB[0 + b2], T_q])
                    dve(lambda e: e.tensor_copy(out=qpad[1][64:128, :, a * 128:(a + 1) * 128], in_=pv[64:128]),
                        [], [TB[0 + b2], T_q])
                transpose_to(q_evac, qb[a], 128, 4, 0 + b2, T_qb[a], None)
                transpose_to(lambda pv, a=a, b2=b2: act(
                    lambda e: e.activation(out=ktown[:, :, a * 128:(a + 1) * 128], in_=pv, func=AF.Copy),
                    [], [TB[2 + b2], T_kt]), kbA[a], 128, 4, 2 + b2, T_kbA[a], None)
                dve(lambda e, a=a, b2=b2: e.tensor_tensor(out=zw[b2][0], in0=zc[a], in1=w1b, op=ALU.mult),
                     [T_zc[a], T_c], [T_zw[b2][0]])
                dve(lambda e, a=a, b2=b2: e.tensor_tensor(out=zw[b2][1], in0=zc[a], in1=w0b, op=ALU.mult),
                     [T_zc[a], T_c], [T_zw[b2][1]])
                dve(lambda e, a=a: e.tensor_tensor(out=zw2, in0=zc[a], in1=w2b, op=ALU.mult),
                     [T_zc[a], T_c], [T_zw2])
                dve(lambda e: e.tensor_tensor(out=zw2, in0=zw2, in1=bbb, op=ALU.add), [T_c], [T_zw2])
                bk = 4 + b2
                pe(lambda e, b2=b2, bk=bk: e.matmul(bank(bk), lhsT=shm[:, m_sh1, :], rhs=zw[b2][0], start=True,
                                                    stop=False), [T_c, T_zw[b2][0]], [TB[bk]])
                pe(lambda e, b2=b2, bk=bk: e.matmul(bank(bk), lhsT=shm[:, m_sh2, :], rhs=zw[b2][1], start=False,
                                                    stop=False), [T_c, T_zw[b2][1]], [TB[bk]])
                if a == 0:
                    pe(lambda e, bk=bk: e.matmul(bank(bk), lhsT=shm[0:NH, m_h1, :], rhs=hzw[0][0:NH], start=False,
                                                 stop=False), [T_c, T_hzw[0]], [TB[bk]])
                    pe(lambda e, bk=bk: e.matmul(bank(bk), lhsT=shm[0:NH, m_h2, :], rhs=hzw[1][0:NH], start=False,
                                                 stop=True), [T_c, T_hzw[1]], [TB[bk]])
                else:
                    p2 = 1 - b2
                    pe(lambda e, bk=bk, p2=p2: e.matmul(bank(bk), lhsT=shm[:, 2, :], rhs=zw[p2][0], start=False,
                                                        stop=False), [T_c, T_zw[p2][0]], [TB[bk]])
                    pe(lambda e, bk=bk, p2=p2: e.matmul(bank(bk), lhsT=shm[:, 3, :], rhs=zw[p2][1], start=False,
                                                        stop=True), [T_c, T_zw[p2][1]], [TB[bk]])
                dve(lambda e, bk=bk: e.tensor_tensor(out=zw2, in0=bank(bk), in1=zw2, op=ALU.add),
                    [], [TB[bk], T_zw2])
                dve(lambda e, a=a: e.tensor_tensor(out=mixb[a][:, 0:512], in0=zw2, in1=cb[a], op=ALU.mult),
                    [T_cb[a]], [T_zw2, T_mix[a]])
            if is_sample:
                for b in range(BPC):
                    dma(conv_s[2 * b:2 * b + 2, :], zc[0][32 * b + 30:32 * b + 32, :], r=[T_zc[0]], sem="cvo")
            elif j == NSLOT - 1:
                dma(conv_p, zc[3][126:128, :], r=[T_zc[3]], sem="cvo")
            kb.barrier()
            A.top = markA

            if late_jobs:
                for i_ in range(NWS):
                    wst_f[i_] = A.f32(4096)
                    wst_b[i_] = A.bf16(4096)
            ring_plan(plan_C())
            deferred_prefetch = [ring_prefetch]
            if not is_sample:
                nxt = SLOT_ORDER.index(j) + 1
                if nxt < NSLOT:
                    deferred_prefetch.append(lambda: load_xo(SLOT_ORDER[nxt], False))
                else:
                    deferred_prefetch.append(lambda: load_xo(0, True))
            PTs = [A.bf16(2, 512) for _ in range(3)]; T_pt = [T() for _ in range(3)]
            dtmp = [A.f32(128) for _ in range(2)]; T_dtmp = [T(), T()]
            o_t = [A.f32(128) for _ in range(2)]; T_o = [T(), T()]
            junko = A.bf16(128); T_junko = T()
            rr = [A.f32(4) for _ in range(2)]; T_rr = [T(), T()]
            ucount = [0]
            dcount = [0]
            ecount = [0]

            def accv(i):
                return bank(4 + i // 3)[:, (i % 3) * 130:(i % 3) * 130 + 129], 4 + i // 3

            class U:
                pass

            def sreg(sbp):
                return ps_t[:, sbp * 1024:(sbp + 1) * 1024]

            hook_n = [0]
            hook_stage = []

            def unit_hook():
                hook_n[0] += 1
                if hook_n[0] >= 6 and deferred_prefetch and not (late_jobs or hook_stage):
                    for f in deferred_prefetch:
                        f()
                    del deferred_prefetch[:]
                if late_jobs or hook_stage:
                    if hook_n[0] % 3 == 0:
                        if hook_stage:
                            cs, stf = hook_stage.pop(0)
                            cs(); stf()
                        if late_jobs:
                            ld, cs, stf = late_jobs.pop(0)
                            ld()
                            hook_stage.append((cs, stf))

            def run_units(units, h, qbase_of, np_q):
                def emit_S(u):
                    if getattr(u, "load", None) is not None:
                        u.load()
                    sbp = u.idx % 2
                    R = sreg(sbp)
                    for (m, rhs, col, n) in u.smm:
                        pe(lambda e, u=u, rhs=rhs, col=col, n=n, R=R: e.matmul(
                            R[:, col:col + n], lhsT=u.kt, rhs=rhs, start=True, stop=True),
                           [u.Tkt, T_q], [TB[2 * sbp], TB[2 * sbp + 1]])

                def emit_exp(u):
                    sbp = u.idx % 2
                    ps = u.idx % 3
                    R = sreg(sbp)
                    PT = PTs[ps].rearrange("p a b -> p (a b)")
                    for (kind, lo, hi, arg) in u.exps:
                        if kind == "fast":
                            act(lambda e, lo=lo, hi=hi, arg=arg, R=R, PT=PT: e.activation(
                                out=PT[:, lo:hi], in_=R[:, lo:hi], func=AF.Exp, scale=0.125, bias=arg),
                                [T_c], [TB[2 * sbp], TB[2 * sbp + 1], T_pt[ps]])
                        elif kind == "fast2":
                            Rv = R.rearrange("p (a b) -> p a b", a=2)[:, :, lo:hi]
                            Pv = PTs[ps][:, :, lo:hi]
                            act(lambda e, arg=arg, Rv=Rv, Pv=Pv: e.activation(
                                out=Pv, in_=Rv, func=AF.Exp, scale=0.125, bias=arg),
                                [T_c], [TB[2 * sbp], TB[2 * sbp + 1], T_pt[ps]])
                        else:
                            k = dcount[0] % 2
                            dcount[0] += 1
                            dve(lambda e, lo=lo, hi=hi, arg=arg, k=k, R=R: e.scalar_tensor_tensor(
                                out=dtmp[k][:, 0:hi - lo], in0=R[:, lo:hi], scalar=0.125, in1=arg,
                                op0=ALU.mult, op1=ALU.add), [T_c], [TB[2 * sbp], TB[2 * sbp + 1], T_dtmp[k]])
                            act(lambda e, lo=lo, hi=hi, k=k, PT=PT: e.activation(
                                out=PT[:, lo:hi], in_=dtmp[k][:, 0:hi - lo], func=AF.Exp),
                                [T_dtmp[k]], [T_pt[ps]])

                def emit_PV(u):
                    ps = u.idx % 3
                    PT = PTs[ps].rearrange("p a b -> p (a b)")
                    for (i, col, wq, st, sp_) in u.pv:
                        av, abk = accv(i)
                        pe(lambda e, u=u, av=av, col=col, wq=wq, st=st, sp_=sp_, PT=PT: e.matmul(
                            av[0:wq, :], lhsT=PT[:, col:col + wq], rhs=u.v, start=st, stop=sp_),
                           [T_pt[ps], u.Tv], [TB[abk]])

                for ui, u in enumerate(units):
                    u.idx = ucount[0]
                    ucount[0] += 1
                for u in units[:10]:
                    if getattr(u, "load", None) is not None:
                        u.load()
                        u.load = None
                for ui, u in enumerate(units):
                    if ui == 0:
                        emit_S(u)
                        if len(units) > 1:
                            emit_S(units[1])
                    emit_exp(u)
                    if ui + 2 < len(units):
                        emit_S(units[ui + 2])
                    emit_PV(u)
                    unit_hook()

            def ptcol(h, m, qi):
                if h == 0:
                    return (qi // 2) * 512 + m * 256 + (qi % 2) * 128
                return m * 512 + qi * 128

            rr4 = [A.f32(12) for _ in range(2)]; T_rr4 = [T(), T()]
            o4 = [A.f32(128) for _ in range(4)]; T_o4 = [T() for _ in range(4)]

            def evac_head(h, np_q, dst_fn, accs):
                k2 = ecount[0] % 2
                ecount[0] += 1
                r = rr4[k2]
                Tr = T_rr4[k2]
                n = len(accs)
                for idx, (a, i1, i2) in enumerate(accs):
                    O1, b1 = accv(i1)
                    O2, b2_ = accv(i2)
                    dve(lambda e, O1=O1, idx=idx: e.reciprocal(out=r[0:np_q, idx:idx + 1], in_=O1[0:np_q, 128:129]),
                        [], [TB[b1], Tr])
                    dve(lambda e, O2=O2, idx=idx: e.reciprocal(out=r[0:np_q, 4 + idx:5 + idx],
                                                               in_=O2[0:np_q, 128:129]), [], [TB[b2_], Tr])
                dve(lambda e: e.tensor_scalar(out=r[0:np_q, 4:4 + n], in0=r[0:np_q, 4:4 + n],
                                              scalar1=lam_t[0:np_q, 1:2], scalar2=None, op0=ALU.mult),
                    [T_lam], [Tr])
                for idx, (a, i1, i2) in enumerate(accs):
                    O1, b1 = accv(i1)
                    O2, b2_ = accv(i2)
                    dve(lambda e, O1=O1, idx=idx: e.tensor_scalar(out=o4[idx][0:np_q], in0=O1[0:np_q, 0:128],
                                                                  scalar1=r[0:np_q, idx:idx + 1], scalar2=None,
                                                                  op0=ALU.mult), [Tr], [TB[b1], T_o4[idx]])
                    dve(lambda e, O2=O2, idx=idx: e.scalar_tensor_tensor(
                        out=o4[idx][0:np_q], in0=O2[0:np_q, 0:128], scalar=r[0:np_q, 4 + idx:5 + idx],
                        in1=o4[idx][0:np_q], op0=ALU.mult, op1=ALU.add), [Tr], [TB[b2_], T_o4[idx]])
                    act(lambda e, idx=idx: e.activation(out=junko[0:np_q], in_=o4[idx][0:np_q], func=AF.Square,
                                                        accum_out=r[0:np_q, 8 + idx:9 + idx]),
                        [T_o4[idx]], [Tr])
                rstd_from_ss(r[0:np_q, 8:8 + n], 128, r[0:np_q, 8:8 + n], [Tr])
                for idx, (a, i1, i2) in enumerate(accs):
                    dst = dst_fn(a)
                    dve(lambda e, idx=idx, dst=dst: e.scalar_tensor_tensor(
                        out=dst, in0=o4[idx][0:np_q], scalar=r[0:np_q, 8 + idx:9 + idx], in1=gsub[0:np_q],
                        op0=ALU.mult, op1=ALU.mult), [Tr, T_c, T_o4[idx]], dst_tokens[0])

            dst_tokens = [[]]

            if not is_sample:
                NKV = CFG.get("NKV", 6)
                kvK = [A.bf16(512) for _ in range(NKV)]
                kvV = [A.bf16(4, 130) for _ in range(NKV)]
                T_kv = [T() for _ in range(NKV)]
                kvpos = [0]
                for h in range(H):
                    units = []
                    for ch in range(E[j] // 4):
                        s = kvpos[0] % NKV
                        kvpos[0] += 1

                        def load_chunk(s=s, h=h, ch=ch):
                            dma(kvK[s], kt_scr[h, :, ch * 512:(ch + 1) * 512], w=[T_kv[s]], sem="kv%d" % s)
                            dma(kvV[s].rearrange("p k n -> p (k n)"),
                                v_scr[h].rearrange("p k n -> p (k n)")[:, ch * 520:(ch + 1) * 520],
                                w=[T_kv[s]], sem="kv%d" % s)
                        for i in range(4):
                            kbi = 4 * ch + i
                            u = U()
                            u.load = load_chunk if i == 0 else None
                            u.kt = kvK[s][:, i * 128:(i + 1) * 128]
                            u.Tkt = T_kv[s]
                            u.v = kvV[s][:, i, 0:129]
                            u.Tv = T_kv[s]
                            first = False
                            if h == 0:
                                u.smm = [(m, qpad[m][:, h, hf * 256:(hf + 1) * 256], hf * 512 + m * 256, 256)
                                         for hf in range(2) for m in range(2)]
                                u.exps = [("fast", hf * 512, (hf + 1) * 512,
                                           btab[:, BIDX[(j, h, kbi, hf)]:BIDX[(j, h, kbi, hf)] + 1])
                                          for hf in range(2)]
                            else:
                                u.smm = [(m, qpad[m][:, h, :], m * 512, 512) for m in range(2)]
                                u.exps = [("fast", 0, 1024, btab[:, BIDX[(j, h, kbi, 0)]:BIDX[(j, h, kbi, 0)] + 1])]
                            u.pv = [(m * 4 + qi, ptcol(h, m, qi), 128, first and ((m * 4 + qi) % 3 == 0), False)
                                    for m in range(2) for qi in range(4)]
                            units.append(u)
                    dunits = []
                    for ap_ in range(4):
                        u = U()
                        u.kt = ktown[:, h, ap_ * 128:(ap_ + 1) * 128]
                        u.Tkt = T_kt
                        u.v = vown[:, ap_ * 4 + h, 0:129]
                        u.Tv = T_vo
                        u.smm = []
                        for m in range(2):
                            if h == 0:
                                for hf in range(2):
                                    qlo = max(ap_, 2 * hf)
                                    qhi = 2 * hf + 2
                                    if qlo < qhi:
                                        u.smm.append((m, qpad[m][:, h, qlo * 128:qhi * 128],
                                                      hf * 512 + m * 256 + (qlo % 2) * 128, (qhi - qlo) * 128))
                            else:
                                u.smm.append((m, qpad[m][:, h, ap_ * 128:512], m * 512 + ap_ * 128,
                                              (4 - ap_) * 128))
                        u.exps = []
                        u.pv = []
                        for m in range(2):
                            for qi in range(ap_, 4):
                                c = ptcol(h, m, qi)
                                if qi == ap_:
                                    u.exps.append(("diag", c, c + 128, bdiag[:, h * 4 + ap_, :]))
                                else:
                                    c0 = h * 16 + ap_ * 4 + qi
                                    u.exps.append(("fast", c, c + 128, dbias[:, c0:c0 + 1]))
                                u.pv.append((m * 4 + qi, c, 128, (ap_ == 0) and ((m * 4 + qi) % 3 == 0), False))
                        dunits.append(u)
                    units = dunits + units
                    run_units(units, h, None, 128)
                    dst_tokens[0] = T_mix
                    evac_head(h, 128, lambda a, h=h: mixb[a][:, 512 + h * 128:512 + (h + 1) * 128],
                              [(a, a, 4 + a) for a in range(4)])
            else:
                ckf = [A.f32(PAST) for _ in range(2)]; T_ckf = [T(), T()]
                ckbs = [A.bf16(4, PAST) for _ in range(2)]; T_ckbs = [T(), T()]
                cvf = [A.f32(4, 512) for _ in range(2)]; T_cvf = [T(), T()]
                cvbs = [A.bf16(NKBS * 4, 130) for _ in range(2)]; T_cvbs = [T(), T()]
                yab = [A.bf16(512) for _ in range(2)]; T_yab = [T(), T()]
                selT = A.bf16(4, 128); T_sel = T()
                dve(lambda e: e.memset(selT, 0.0), [], [T_sel])
                for b in range(BPC):
                    dve(lambda e, b=b: e.tensor_copy(out=selT[0:32, b, 32 * b:32 * b + 32], in_=ident_b[0:32, 0:32]),
                        [T_identb], [T_sel])
                for s in range(2):
                    dve(lambda e, s=s: e.memset(cvbs[s][:, :, 128:130], 1.0), [], [T_cvbs[s]])
                cc = [0]

                def prep_b(b):
                    pb = b % 2
                    for h in range(H):
                        s = cc[0] % 2
                        cc[0] += 1
                        dma(ckf[s], ckT[b, h], w=[T_ckf[s]], sem="ckf%d" % s)
                        dve(lambda e, s=s, h=h: e.tensor_copy(out=ckbs[pb][:, h, :], in_=ckf[s]),
                            [T_ckf[s]], [T_ckbs[pb]])
                    for g4 in range(NKBS // 4):
                        s = cc[0] % 2
                        cc[0] += 1
                        dma(cvf[s], cv[b, g4 * 512:(g4 + 1) * 512, :].rearrange("(k p) n -> p k n", p=128),
                            w=[T_cvf[s]], sem="cvf%d" % s)
                        pool(lambda e, s=s, g4=g4: e.tensor_copy(
                            out=cvbs[pb][:, g4 * 16:(g4 + 1) * 16, 0:128],
                            in_=cvf[s].rearrange("p k (h n) -> p (k h) n", h=4)), [T_cvf[s]], [T_cvbs[pb]])

                prep_b(0)
                for b in range(BPC):
                    if b + 1 < BPC:
                        prep_b(b + 1)
                    ckb = ckbs[b % 2]; T_ckb = T_ckbs[b % 2]
                    cvb = cvbs[b % 2]; T_cvb = T_cvbs[b % 2]
                    units = []
                    started = set()
                    for h in range(H):
                        for kbi in range(NKBS + 1):
                            u = U()
                            own = (kbi == NKBS)
                            if own:
                                u.kt = ktown[:, h, 0:128]
                                u.Tkt = T_kt
                                u.v = vown[:, h, 0:129]
                                u.Tv = T_vo
                                u.exps = [("diag", m * 512, m * 512 + 32, bs[:, b * 4 + h, :]) for m in range(2)]
                            else:
                                u.kt = ckb[:, h, kbi * 128:(kbi + 1) * 128]
                                u.Tkt = T_ckb
                                u.v = cvb[:, kbi * 4 + h, 0:129]
                                u.Tv = T_cvb
                                c0 = h * NKBS + kbi
                                u.exps = [("fast2", 0, 32, sbias[:, c0:c0 + 1])]
                            u.smm = [(m, qpad[m][:, h, b * 32:(b + 1) * 32], m * 512, 32) for m in range(2)]
                            u.pv = []
                            for m in range(2):
                                i = m * 4 + h
                                bk_ = 4 + i // 3
                                st = (kbi == 0) and (bk_ not in started)
                                if kbi == 0:
                                    started.add(bk_)
                                u.pv.append((i, m * 512, 32, st, own))
                            units.append(u)
                    run_units(units, 0, None, 32)
                    dst_tokens[0] = [T_yab[b % 2]]
                    evac_head(0, 32, lambda a, b=b: yab[b % 2][0:32, a * 128:(a + 1) * 128],
                              [(h, h, 4 + h) for h in range(H)])
                    pe(lambda e, b=b: e.matmul(bank(7), lhsT=selT[0:32, b, :], rhs=yab[b % 2][0:32, :],
                                               start=(b == 0), stop=(b == BPC - 1)),
                       [T_sel, T_yab[b % 2]], [TB[7]])
                act(lambda e: e.activation(out=mixb[0][:, 512:1024], in_=bank(7), func=AF.Copy),
                    [], [TB[7], T_mix[0]])
            while hook_stage or late_jobs:
                if hook_stage:
                    cs, stf = hook_stage.pop(0)
                    cs(); stf()
                if late_jobs:
                    ld, cs, stf = late_jobs.pop(0)
                    ld()
                    hook_stage.append((cs, stf))
            for f in deferred_prefetch:
                f()
            kb.barrier()
            A.top = markQ

            if DEBUG:
                dbgt = [A.f32(D) for _ in range(NA)]
                for a in range(NA):
                    dve(lambda e, a=a: e.tensor_copy(out=dbgt[a], in_=mixb[a]), [T_mix[a]], [T_mix[a]])
                    dma(dbg_mix[ob0 + a], dbgt[a], r=[T_mix[a]], sem="dbg%d" % a)
            h1 = [A.f32(D) for _ in range(NA)]; T_h = [T() for _ in range(NA)]
            yt = [A.f32(D) for _ in range(NA)]; T_y = [T() for _ in range(NA)]
            TAb = A.bf16(8, W); T_TA = T()
            TBb = A.bf16(8, W); T_TBb = T()
            ffT = A.bf16(32, W); T_ff = T()
            pf = [A.f32(PLE) for _ in range(NA)]; T_pf = [T() for _ in range(NA)]
            pbf = [A.bf16(PLE) for _ in range(2)]; T_pbf = [T(), T()]
            pT = A.bf16(2, W); T_pT = T()
            cbf = [A.bf16(D) for _ in range(2)]; T_cbf = [T(), T()]
            rsC = A.f32(NA, 8); T_rsC = [T() for _ in range(NA)]
            rl = [A.f32(W) for _ in range(2)]; T_rl = [T(), T()]
            junkC = A.bf16(D); T_junkC = T()

            if not is_sample:
                ring_plan(plan_A(SLOT_ORDER.index(j) + 1 >= NSLOT))
            for a in range(NA):
                dma(h1[a], x_own[ob0 + a], w=[T_h[a]], sem="hx%d" % a)
                dma(pf[a], p_own[ob0 + a], w=[T_pf[a]], sem="pf%d" % a)

            def evac_T(dst, evi):
                def f(pv):
                    if evi % 2 == 0:
                        act(lambda e: e.activation(out=dst, in_=pv, func=AF.Copy), [], f.toks)
                    else:
                        dve(lambda e: e.tensor_copy(out=dst, in_=pv), [], f.toks)
                return f

            def norm_add(a, col, g_idx):
                rs = rsC[:, a, col:col + 1]
                act(lambda e: e.activation(out=junkC, in_=yt[a], func=AF.Square, accum_out=rs),
                    [T_y[a]], [T_rsC[a]])
                rstd_from_ss(rs, D, rs, [T_rsC[a]])
                dve(lambda e: e.scalar_tensor_tensor(out=yt[a], in0=yt[a], scalar=rs, in1=gbc[:, g_idx, :],
                                                     op0=ALU.mult, op1=ALU.mult), [T_rsC[a], T_c], [T_y[a]])
                dve(lambda e: e.tensor_tensor(out=h1[a], in0=h1[a], in1=yt[a], op=ALU.add), [T_y[a]], [T_h[a]])

            for a in range(NA):
                f = evac_T(TAb[:, :, a * 128:(a + 1) * 128], a)
                f.toks = [TB[a], T_TA]
                transpose_to(f, mixb[a], 128, 8, a, T_mix[a], None)
            stream_tokmajor(ws_out, 2, 4, lambda a, kc: TAb[:, kc, a * 128:(a + 1) * 128], NA,
                            lambda cg, a, bk: act(lambda e: e.activation(
                                out=yt[a][:, cg * 512:(cg + 1) * 512], in_=bank(bk), func=AF.Copy),
                                [], [TB[bk], T_y[a]]),
                            lambda a: [T_TA], SETS4)
            for a in range(NA):
                norm_add(a, 0, 0)
            for a in range(NA):
                b2 = a % 2
                rs2 = rsC[:, a, 1:2]
                act(lambda e, a=a, rs2=rs2: e.activation(out=junkC, in_=h1[a], func=AF.Square, accum_out=rs2),
                    [T_h[a]], [T_rsC[a]])
                rstd_from_ss(rs2, D, rs2, [T_rsC[a]])
                dve(lambda e, a=a, rs2=rs2: e.tensor_tensor(out=rsC[:, a, 2:3], in0=rs2, in1=rs2, op=ALU.mult),
                    [], [T_rsC[a]])
                act(lambda e, a=a, b2=b2: e.activation(out=cbf[b2], in_=h1[a], func=AF.Copy), [T_h[a]], [T_cbf[b2]])
                f = evac_T(TBb[:, :, a * 128:(a + 1) * 128], a + 1)
                f.toks = [TB[4 + a], T_TBb]
                transpose_to(f, cbf[b2], 128, 8, 4 + a, T_cbf[b2], None)
            for fch in range(32):
                piece, Tp = ring_load(ws_up[fch])
                pv = piece.rearrange("p (k n) -> p k n", k=8)
                bk = fch % 2
                for kc in range(8):
                    pe(lambda e, pv=pv, kc=kc, bk=bk: e.matmul(bank(bk)[:, 0:W], lhsT=pv[:, kc, :],
                                                               rhs=TBb[:, kc, :], start=(kc == 0), stop=(kc == 7)),
                       [Tp, T_TBb], [TB[bk]])
                ring_topup()
                act(lambda e, bk=bk: e.activation(out=rl[bk], in_=bank(bk)[:, 0:W], func=AF.Relu),
                    [], [TB[bk], T_rl[bk]])
                dve(lambda e, bk=bk, fch=fch: e.tensor_tensor(out=ffT[:, fch, :], in0=rl[bk], in1=rl[bk],
                                                              op=ALU.mult), [T_rl[bk]], [T_ff])
            stream_tokmajor(ws_dn, 2, 16, lambda a, kc: ffT[:, kc, a * 128:(a + 1) * 128], NA,
                            lambda cg, a, bk: act(lambda e: e.activation(
                                out=yt[a][:, cg * 512:(cg + 1) * 512], in_=bank(bk), func=AF.Copy,
                                scale=rsC[:, a, 2:3]), [T_rsC[a]], [TB[bk], T_y[a]]),
                            lambda a: [T_ff], SETS4)
            for a in range(NA):
                norm_add(a, 3, 1)
            for a in range(NA):
                b2 = a % 2
                act(lambda e, a=a, b2=b2: e.activation(out=cbf[b2], in_=h1[a], func=AF.Copy), [T_h[a]], [T_cbf[b2]])
                f = evac_T(TAb[:, :, a * 128:(a + 1) * 128], a)
                f.toks = [TB[a], T_TA]
                transpose_to(f, cbf[b2], 128, 8, a, T_cbf[b2], None)
                act(lambda e, a=a, b2=b2: e.activation(out=pbf[b2], in_=pf[a], func=AF.Copy), [T_pf[a]], [T_pbf[b2]])
                f = evac_T(pT[:, :, a * 128:(a + 1) * 128], a + 1)
                f.toks = [TB[4 + a], T_pT]
                transpose_to(f, pbf[b2], 128, 2, 4 + a, T_pbf[b2], None)
            stream_tokmajor(ws_g, 2, 4, lambda a, kc: TAb[:, kc, a * 128:(a + 1) * 128], NA,
                            lambda cg, a, bk: act(lambda e: e.activation(
                                out=yt[a][:, cg * 512:(cg + 1) * 512], in_=bank(bk), func=AF.Exp, scale=-1.0),
                                [], [TB[bk], T_y[a]]),
                            lambda a: [T_TA], SETS4)
            for a in range(NA):
                dve(lambda e, a=a: e.tensor_scalar(out=yt[a], in0=yt[a], scalar1=1.0, scalar2=None, op0=ALU.add),
                    [], [T_y[a]])
                dve(lambda e, a=a: e.reciprocal(out=yt[a], in_=yt[a]), [], [T_y[a]])
            stream_tokmajor(ws_pe, 2, 1, lambda a, kc: pT[:, kc, a * 128:(a + 1) * 128], NA,
                            lambda cg, a, bk: dve(lambda e: e.tensor_tensor(
                                out=yt[a][:, cg * 512:(cg + 1) * 512], in0=bank(bk),
                                in1=yt[a][:, cg * 512:(cg + 1) * 512], op=ALU.mult), [], [TB[bk], T_y[a]]),
                            lambda a: [T_pT], SETS4)
            for a in range(NA):
                norm_add(a, 4, 2)
                dma(y_own[ob0 + a], h1[a], r=[T_h[a]], sem="yo%d" % a)
            kb.barrier()
            A.top = markS

        for si, j in enumerate(SLOT_ORDER):
            do_slot(j, False)
            if si == 0:
                assert not late_jobs
        do_slot(0, True)
        kb.emit()
    return nc


_PROG_CACHE = {}


def kernel(x_prompt, x_sample, cache_k, cache_v, state_conv, p_prompt, p_sample,
           w_in, w_conv, b_conv, lambda_q1, lambda_k1, lambda_q2, lambda_k2, g_subln,
           w_out, g_pre_mix, g_post_mix, g_pre_mlp, g_post_mlp, w_up, w_down,
           w_pe, w_pe_gate, g_pe):
    NSLOT = CFG["NSLOT"]
    PAST = CFG["PAST"]
    f = np.float32
    A_ = lambda v: np.ascontiguousarray(np.asarray(v), dtype=f)
    x_prompt = A_(x_prompt); x_sample = A_(x_sample); cache_k = A_(cache_k); cache_v = A_(cache_v)
    state_conv = A_(state_conv); p_prompt = A_(p_prompt); p_sample = A_(p_sample)
    SEQ = x_prompt.shape[1]
    assert SEQ == NCORE * NSLOT * 512 and cache_k.shape[2] == PAST
    NOWN = 4 * NSLOT + 1
    key = (NSLOT, PAST)
    if key not in _PROG_CACHE:
        _PROG_CACHE[key] = build_program(NSLOT, PAST)
    nc = _PROG_CACHE[key]

    def bc(v, n=128):
        v = A_(v).reshape(1, -1)
        return np.ascontiguousarray(np.broadcast_to(v, (n, v.shape[1])))

    gvec = np.concatenate([A_(g_pre_mix)[0].reshape(8, 128).T, A_(g_pre_mlp)[0].reshape(8, 128).T], axis=1)
    gbc = np.concatenate([bc(g_post_mix[0]), bc(g_post_mlp[0]), bc(g_pe[0])], axis=1)
    convbc = np.concatenate([bc(A_(w_conv)[0, 0]), bc(A_(w_conv)[0, 1]), bc(A_(w_conv)[0, 2]), bc(A_(b_conv)[0])],
                            axis=1)
    lamv = np.concatenate([bc(lambda_q1[0]), bc(lambda_k1[0]), bc(lambda_q2[0]), bc(lambda_k2[0])], axis=1)
    shared = dict(
        x_all=x_prompt[0], w_in=A_(w_in)[0], w_out=A_(w_out)[0], w_up=A_(w_up)[0], w_down=A_(w_down)[0],
        w_pe=A_(w_pe)[0], w_g=A_(w_pe_gate)[0], gvec=np.ascontiguousarray(gvec), gbc=np.ascontiguousarray(gbc),
        gsub=bc(g_subln[0]), convbc=np.ascontiguousarray(convbc), lamv=np.ascontiguousarray(lamv),
        ident=np.eye(128, dtype=f))
    in_maps = []
    for c in range(NCORE):
        tb = make_tables(c, NSLOT, PAST)
        xo = np.zeros((NOWN, 128, D), f)
        po = np.zeros((NOWN, 128, PLE), f)
        xh = np.zeros((NSLOT, 2, D), f)
        for j in range(NSLOT):
            sb = NCORE * j + c
            xo[4 * j:4 * j + 4] = x_prompt[0, sb * 512:(sb + 1) * 512].reshape(4, 128, D)
            po[4 * j:4 * j + 4] = p_prompt[0, 0, sb * 512:(sb + 1) * 512].reshape(4, 128, PLE)
            if sb > 0:
                xh[j] = x_prompt[0, sb * 512 - 2:sb * 512]
        bsl = slice(c * BPC, (c + 1) * BPC)
        xo[4 * NSLOT] = x_sample[bsl].reshape(128, D)
        po[4 * NSLOT] = p_sample[0, bsl].reshape(128, PLE)
        m = dict(shared)
        m.update(x_own=xo, p_own=po, x_halo=xh,
                 hist_s=np.ascontiguousarray(state_conv[0, bsl].reshape(8, 512)),
                 ckT=np.ascontiguousarray(cache_k[0, bsl].transpose(0, 2, 3, 1)),
                 cv=np.ascontiguousarray(cache_v[0, bsl].reshape(BPC, PAST, 512)),
                 **tb)
        in_maps.append(m)
    res = run_bass_kernel_spmd(nc, in_maps, core_ids=list(range(NCORE)))
    R = res.results
    if CFG.get("DEBUG", False):
        CFG["_dbg"] = [r["dbg_mix"] for r in R]
        CFG["_kt"] = [np.asarray(r["kt_scr"]).astype(np.float32) for r in R]
        CFG["_v"] = [np.asarray(r["v_scr"]).astype(np.float32) for r in R]
    y_prompt = np.zeros((1, SEQ, D), f)
    k_prompt = np.zeros((1, 1, SEQ, H, 128), f)
    v_prompt = np.zeros((1, 1, SEQ, H, 128), f)
    y_sample = np.zeros((NCORE * BPC, DEC_T, D), f)
    k_sample = np.zeros((1, NCORE * BPC, DEC_T, H, 128), f)
    v_sample = np.zeros((1, NCORE * BPC, DEC_T, H, 128), f)
    conv_sample = np.zeros((1, NCORE * BPC, 2, 512), f)
    for c in range(NCORE):
        r = R[c]
        for j in range(NSLOT):
            sb = NCORE * j + c
            sl = slice(sb * 512, (sb + 1) * 512)
            y_prompt[0, sl] = r["y_own"][4 * j:4 * j + 4].reshape(512, D)
            k_prompt[0, 0, sl] = r["k_own"][4 * j:4 * j + 4].reshape(512, H, 128)
            v_prompt[0, 0, sl] = r["v_own"][4 * j:4 * j + 4].reshape(512, H, 128)
        bsl = slice(c * BPC, (c + 1) * BPC)
        y_sample[bsl] = r["y_own"][4 * NSLOT].reshape(BPC, DEC_T, D)
        k_sample[0, bsl] = r["k_own"][4 * NSLOT].reshape(BPC, DEC_T, H, 128)
        v_sample[0, bsl] = r["v_own"][4 * NSLOT].reshape(BPC, DEC_T, H, 128)
        conv_sample[0, bsl] = r["conv_s"].reshape(BPC, 2, 512)
    conv_prompt = np.ascontiguousarray(R[NCORE - 1]["conv_p"]).reshape(1, 1, 2, 512).astype(f)
    return (y_prompt, y_sample, k_prompt, v_prompt, conv_prompt, k_sample, v_sample, conv_sample)
```

```python
import math
from contextlib import ExitStack
import numpy as np
import concourse.bass as bass
import concourse.mybir as mybir
from concourse.bass_utils import run_bass_kernel_spmd

F32 = mybir.dt.float32
BF16 = mybir.dt.bfloat16
AF = mybir.ActivationFunctionType
ALU = mybir.AluOpType

NCORE = 8
D = 1024
DFF = 4096
PLE = 256
H = 4
DEC_T = 32
BPC = 4
RMS_EPS = 1e-6
NEG = -30000.0
SLOPES = [2.0 ** (-8.0 * (h + 1) / 4) for h in range(4)]
LAM_INIT = 0.8 - 0.6 * math.exp(-0.3 * 0)
CFG = dict(NSLOT=4, PAST=2048)

ENGS = ["pe", "act", "dve", "pool", "sp"]


class T:
    __slots__ = ("w", "rs")

    def __init__(self):
        self.w = None
        self.rs = []


class Op:
    __slots__ = ("eng", "fn", "deps", "needed", "ev", "dma_sem", "epoch", "barrier")

    def __init__(self, eng, fn):
        self.eng = eng
        self.fn = fn
        self.deps = []
        self.needed = False
        self.ev = None
        self.dma_sem = None
        self.barrier = 0


class DSem:
    def __init__(self, sem):
        self.sem = sem
        self.n = 0


class KB:
    def __init__(self, nc, es):
        self.nc = nc
        self.es = es
        self.ops = []
        self.epoch = 0
        self.esem = {e: es.enter_context(nc.semaphore("es_" + e)) for e in ENGS[:4]}
        self.bsem = es.enter_context(nc.semaphore("bar"))
        self.dsems = {}
        self.nbar = 0

    def dsem(self, name):
        if name not in self.dsems:
            self.dsems[name] = DSem(self.es.enter_context(self.nc.semaphore("ds_" + name)))
        return self.dsems[name]

    def op(self, eng, fn, reads=(), writes=(), dsem=None):
        o = Op(eng, fn)
        o.epoch = self.epoch
        o.dma_sem = dsem
        deps = []
        for r in reads:
            if r.w is not None:
                deps.append(r.w)
        for w in writes:
            if w.w is not None:
                deps.append(w.w)
            deps.extend(w.rs)
        seen = set()
        for d in deps:
            if d is o or id(d) in seen or d.epoch < self.epoch:
                continue
            seen.add(id(d))
            if d.eng == eng and d.dma_sem is None and eng in ("pe", "sp"):
                continue
            if eng == "sp" and d.eng == "sp" and d.dma_sem is not None and d.dma_sem is dsem:
                continue
            o.deps.append(d)
            d.needed = True
        for r in reads:
            r.rs.append(o)
        for w in writes:
            w.w = o
            w.rs = []
        self.ops.append(o)
        return o

    def barrier(self):
        self.nbar += 1
        for e in ENGS:
            o = Op(e, None)
            o.barrier = self.nbar
            o.epoch = self.epoch
            self.ops.append(o)
        self.epoch += 1

    def emit(self):
        nc = self.nc
        per = {e: [] for e in ENGS}
        cnt = {e: 0 for e in ENGS}
        for o in self.ops:
            per[o.eng].append(o)
            if o.barrier:
                continue
            if o.dma_sem is not None:
                o.dma_sem.n += 1
                o.ev = (o.dma_sem, o.dma_sem.n * 16)
            elif o.needed:
                cnt[o.eng] += 1
                o.ev = (o.eng, cnt[o.eng])
        esem = self.esem
        bsem = self.bsem
        alld = list(self.dsems.values())

        def run(ename, eng):
            waited = {}
            dcount = {id(d): 0 for d in alld}
            for o in per[ename]:
                if o.barrier:
                    for d in alld:
                        if dcount[id(d)] > waited.get(id(d), 0):
                            eng.wait_ge(d.sem, dcount[id(d)])
                            waited[id(d)] = dcount[id(d)]
                    if ename == "sp":
                        eng.sem_inc(bsem, 1)
                    else:
                        eng.drain().then_inc(bsem, 1)
                    eng.wait_ge(bsem, 5 * o.barrier)
                    continue
                need = {}
                for d in o.deps:
                    key, val = d.ev
                    k = id(key) if isinstance(key, DSem) else key
                    if waited.get(k, 0) >= val:
                        continue
                    if k not in need or need[k][1] < val:
                        need[k] = (key, val)
                for k, (key, val) in need.items():
                    sem = key.sem if isinstance(key, DSem) else esem[key]
                    eng.wait_ge(sem, val)
                    waited[k] = val
                inst = o.fn(eng)
                if o.dma_sem is not None:
                    inst.then_inc(o.dma_sem.sem, 16)
                    dcount[id(o.dma_sem)] = o.ev[1]
                elif o.needed:
                    inst.then_inc(esem[o.eng], 1)
            if ename == "sp":
                for d in alld:
                    if d.n > 0:
                        eng.wait_ge(d.sem, d.n * 16)

        with nc.Block() as block:
            @block.tensor
            def _(eng):
                run("pe", eng)

            @block.scalar
            def _(eng):
                run("act", eng)

            @block.vector
            def _(eng):
                run("dve", eng)

            @block.gpsimd
            def _(eng):
                run("pool", eng)

            @block.sync
            def _(eng):
                run("sp", eng)


class Arena:
    def __init__(self, t, n):
        self.t = t
        self.n = n
        self.top = 0

    def _take(self, n32):
        off = self.top
        self.top += n32
        assert self.top <= self.n, ("SBUF arena overflow", self.top, self.n)
        return off

    def f32(self, *shape):
        n = int(np.prod(shape))
        off = self._take(n)
        v = self.t[:, off:off + n]
        return _shape(v, shape)

    def bf16(self, *shape):
        n = int(np.prod(shape))
        n32 = (n + 1) // 2
        off = self._take(n32)
        v = self.t[:, off:off + n32].bitcast(BF16)[:, 0:n]
        return _shape(v, shape)


def _shape(v, shape):
    if len(shape) == 1:
        return v
    if len(shape) == 2:
        return v.rearrange("p (a b) -> p a b", a=shape[0])
    if len(shape) == 3:
        return v.rearrange("p (a b c) -> p a b c", a=shape[0], b=shape[1])
    raise ValueError(shape)


def _hw(h):
    return 256 if h == 0 else 512


def _refoff(h, a):
    return 256 * (a // 2) if h == 0 else 0


def bias_layout(NSLOT):
    E = [4 * (NCORE * j + NCORE - 1) for j in range(NSLOT)]
    idx = {}
    n = 0
    for j in range(NSLOT):
        for h in range(H):
            for kb in range(E[j]):
                for half in range(2 if h == 0 else 1):
                    idx[(j, h, kb, half)] = n
                    n += 1
    return E, idx, n


def make_tables(c, NSLOT, PAST):
    E, idx, ncol = bias_layout(NSLOT)
    kl = np.arange(128, dtype=np.float64)
    btab = np.zeros((128, ncol), np.float64)
    for (j, h, kb, half), col in idx.items():
        sb = NCORE * j + c
        m = SLOPES[h]
        Cc = m * _hw(h) / 2
        ref = sb * 512 + half * 256
        if kb < 4 * sb:
            btab[:, col] = m * (kl + 128 * kb - ref) - Cc
        else:
            btab[:, col] = NEG
    btab = np.maximum(btab, NEG)
    dbias = np.zeros((128, H, 4, 4), np.float64)
    bdiag = np.zeros((128, H, 4, 128), np.float64)
    ql = np.arange(128, dtype=np.float64)
    for h in range(H):
        m = SLOPES[h]
        Cc = m * _hw(h) / 2
        for a in range(4):
            for ap in range(4):
                dbias[:, h, ap, a] = m * (kl + 128 * ap - _refoff(h, a)) - Cc
            vis = (kl[:, None] // 64) <= (ql[None, :] // 64)
            val = -m * np.abs(ql[None, :] - kl[:, None]) + m * (128 * a + ql[None, :] - _refoff(h, a)) - Cc
            bdiag[:, h, a, :] = np.where(vis, val, NEG)
    nkb = PAST // 128
    sbias = np.zeros((128, H, nkb), np.float64)
    bs = np.full((128, BPC, H, DEC_T), NEG, np.float64)
    q32 = np.arange(DEC_T, dtype=np.float64)
    for h in range(H):
        m = SLOPES[h]
        Cs = m * 16
        for kb in range(nkb):
            sbias[:, h, kb] = m * (kl + 128 * kb - PAST) - Cs
        for b in range(BPC):
            for tp in range(DEC_T):
                bs[b * 32 + tp, b, h, :] = -m * np.abs(q32 - tp) + m * q32 - Cs
    sbias = np.maximum(sbias, NEG)
    shm = np.zeros((128, 10, 128), np.float32)
    for t in range(128):
        if t >= 1:
            shm[t - 1, 0, t] = 1
        if t >= 2:
            shm[t - 2, 1, t] = 1
        if t % 32 != 0:
            shm[t - 1, 6, t] = 1
        if t % 32 >= 2:
            shm[t - 2, 7, t] = 1
    shm[127, 2, 0] = 1
    shm[126, 3, 0] = 1
    shm[127, 3, 1] = 1
    shm[1, 4, 0] = 1
    shm[0, 5, 0] = 1
    shm[1, 5, 1] = 1
    for b in range(BPC):
        shm[2 * b + 1, 8, 32 * b] = 1
        shm[2 * b, 9, 32 * b] = 1
        shm[2 * b + 1, 9, 32 * b + 1] = 1
    f = np.float32
    return dict(btab=btab.astype(f), dbias=dbias.reshape(128, -1).astype(f),
                bdiag=bdiag.reshape(128, -1).astype(f), sbias=sbias.reshape(128, -1).astype(f),
                bs=bs.reshape(128, -1).astype(f), shm=shm.reshape(128, -1))


def build_program(NSLOT, PAST):
    SEQ = NCORE * NSLOT * 512
    NBLK = SEQ // 128
    NOWN = 4 * NSLOT + 1
    NKBS = PAST // 128
    E, BIDX, NCOL = bias_layout(NSLOT)
    EMAX = E[-1]
    NG = EMAX // 4

    nc = bass.Bass("TRN2", target_bir_lowering=False)

    def din(name, shape, dt=F32):
        return nc.dram_tensor(name, list(shape), dt, kind="ExternalInput").ap()

    def dout(name, shape):
        return nc.dram_tensor(name, list(shape), F32, kind="ExternalOutput").ap()

    def dscr(name, shape):
        return nc.dram_tensor(name, list(shape), BF16, kind="Internal").ap()

    x_all = din("x_all", [SEQ, D])
    x_own = din("x_own", [NOWN, 128, D])
    x_halo = din("x_halo", [NSLOT, 2, D])
    p_own = din("p_own", [NOWN, 128, PLE])
    hist_s = din("hist_s", [8, 512])
    ckT = din("ckT", [BPC, H, 128, PAST])
    cv = din("cv", [BPC, PAST, 512])
    w_in = din("w_in", [D, 3072])
    w_out = din("w_out", [D, D])
    w_up = din("w_up", [D, DFF])
    w_down = din("w_down", [DFF, D])
    w_pe = din("w_pe", [PLE, D])
    w_g = din("w_g", [D, D])
    gvec_d = din("gvec", [128, 16])
    gbc_d = din("gbc", [128, 3 * D])
    gsub_d = din("gsub", [128, 128])
    convbc_d = din("convbc", [128, 4 * 512])
    lamv_d = din("lamv", [128, 4 * 64])
    ident_d = din("ident", [128, 128])
    shm_d = din("shm", [128, 10 * 128])
    btab_d = din("btab", [128, NCOL])
    dbias_d = din("dbias", [128, H * 16])
    bdiag_d = din("bdiag", [128, H * 4 * 128])
    sbias_d = din("sbias", [128, H * NKBS])
    bs_d = din("bs", [128, BPC * H * DEC_T])

    y_own = dout("y_own", [NOWN, 128, D])
    k_own = dout("k_own", [NOWN, 128, 512])
    v_own = dout("v_own", [NOWN, 128, 512])
    conv_p = dout("conv_p", [2, 512])
    conv_s = dout("conv_s", [8, 512])
    DEBUG = CFG.get("DEBUG", False)
    if DEBUG:
        dbg_mix = dout("dbg_mix", [NOWN, 128, D])

    if CFG.get("DEBUG", False):
        kt_scr = nc.dram_tensor("kt_scr", [H, 128, NG * 512], BF16, kind="ExternalOutput").ap()
        v_scr = nc.dram_tensor("v_scr", [H, 128, NG * 4, 130], BF16, kind="ExternalOutput").ap()
    else:
        kt_scr = dscr("kt_scr", [H, 128, NG * 512])
        v_scr = dscr("v_scr", [H, 128, NG * 4, 130])
    ws_in = dscr("ws_in", [4, 4, 128, 1024])
    ws_out = dscr("ws_out", [2, 4, 128, 1024])
    ws_g = dscr("ws_g", [2, 4, 128, 1024])
    ws_pe = dscr("ws_pe", [2, 1, 128, 1024])
    ws_dn = dscr("ws_dn", [2, 16, 128, 1024])
    ws_up = dscr("ws_up", [32, 128, 1024])

    ARENA_N = 53000
    with ExitStack() as es:
        kb = KB(nc, es)
        arena_t = es.enter_context(nc.sbuf_tensor("arena", [128, ARENA_N], F32))
        ps_t = es.enter_context(nc.psum_tensor("psum", [128, 8 * 512], F32))
        A = Arena(arena_t, ARENA_N)
        TB = [T() for _ in range(8)]

        def bank(b):
            return ps_t[:, b * 512:(b + 1) * 512]

        def bankb(b):
            return ps_t[:, b * 512:(b + 1) * 512].bitcast(BF16)

        def pe(fn, r=(), w=()):
            return kb.op("pe", fn, r, w)

        def act(fn, r=(), w=()):
            return kb.op("act", fn, r, w)

        def dve(fn, r=(), w=()):
            return kb.op("dve", fn, r, w)

        def pool(fn, r=(), w=()):
            return kb.op("pool", fn, r, w)

        def dma(out, in_, r=(), w=(), sem="const", q="sp"):
            return kb.op(q, lambda e: e.dma_start(out=out, in_=in_), r, w, dsem=kb.dsem(sem))

        ident_f = A.f32(128); T_identf = T()
        ident_b = A.bf16(128); T_identb = T()
        shm = A.f32(10, 128); T_shm = T()
        gvec = A.f32(16); T_gvec = T()
        gbc = A.f32(3, D); T_gbc = T()
        gsub = A.f32(128); T_gsub = T()
        convbc = A.f32(4, 512); T_convbc = T()
        btab = A.f32(NCOL); T_btab = T()
        dbias = A.f32(H * 16)
        bdiag = A.f32(H * 4, 128)
        sbias = A.f32(H * NKBS)
        bs = A.f32(BPC * H, DEC_T)
        lam_t = A.f32(4); T_lam = T()
        wkv = A.bf16(8, 1024); T_wkv = T()
        NRING = 8
        ring = [A.bf16(1024) for _ in range(NRING)]
        T_ring = [T() for _ in range(NRING)]
        ring_pos = [0]
        T_c = T()
        xo = [A.f32(D) for _ in range(4)]; T_xo = [T() for _ in range(4)]

        def load_xo(slot_idx, is_sample_):
            na = 1 if is_sample_ else 4
            o0 = 4 * NSLOT if is_sample_ else 4 * slot_idx
            for a in range(na):
                dma(xo[a], x_own[o0 + a], w=[T_xo[a]], sem="xo%d" % a)

        for dst, src in ((ident_f, ident_d), (shm.rearrange("p a b -> p (a b)"), shm_d), (gvec, gvec_d),
                         (gbc.rearrange("p a b -> p (a b)"), gbc_d), (gsub, gsub_d),
                         (convbc.rearrange("p a b -> p (a b)"), convbc_d), (btab, btab_d), (dbias, dbias_d),
                         (bdiag.rearrange("p a b -> p (a b)"), bdiag_d), (sbias, sbias_d),
                         (bs.rearrange("p a b -> p (a b)"), bs_d)):
            dma(dst, src, w=[T_c])
        dve(lambda e: e.tensor_copy(out=ident_b, in_=ident_f), [T_c], [T_identb])
        dve(lambda e: e.memset(lam_t[:, 2:3], RMS_EPS), [], [T_lam])
        dve(lambda e: e.tensor_scalar(out=gsub, in0=gsub, scalar1=1.0 - LAM_INIT, scalar2=None, op0=ALU.mult),
            [T_c], [T_c])
        eps_ap = lam_t[:, 2:3]

        mark0 = A.top
        lamv = A.f32(4, 64)
        lprod = A.f32(2, 64)
        lsum = A.f32(2)
        dma(lamv.rearrange("p a b -> p (a b)"), lamv_d, w=[T_c])
        for i in range(2):
            dve(lambda e, i=i: e.tensor_tensor(out=lprod[:, i, :], in0=lamv[:, 2 * i, :], in1=lamv[:, 2 * i + 1, :],
                                               op=ALU.mult), [T_c], [T_c])
            act(lambda e, i=i: e.activation(out=lprod[:, i, :], in_=lprod[:, i, :], func=AF.Copy,
                                            accum_out=lsum[:, i:i + 1]), [T_c], [T_c])
        act(lambda e: e.activation(out=lsum, in_=lsum, func=AF.Exp), [T_c], [T_c])
        dve(lambda e: e.tensor_tensor(out=lam_t[:, 0:1], in0=lsum[:, 0:1], in1=lsum[:, 1:2], op=ALU.subtract),
            [T_c, T_lam], [T_lam])
        dve(lambda e: e.tensor_scalar(out=lam_t[:, 0:1], in0=lam_t[:, 0:1], scalar1=LAM_INIT, scalar2=None,
                                      op0=ALU.add), [T_lam], [T_lam])
        dve(lambda e: e.tensor_scalar(out=lam_t[:, 1:2], in0=lam_t[:, 0:1], scalar1=-1.0, scalar2=None,
                                      op0=ALU.mult), [T_lam], [T_lam])

        NWS = 2
        wst_f = [A.f32(4096) for _ in range(NWS)]
        wst_b = [A.bf16(4096) for _ in range(NWS)]
        T_wf = [T() for _ in range(NWS)]
        T_wb = [T() for _ in range(NWS)]
        T_ws = T()
        prep_i = [0]
        prep_q = ["sp"]

        def make_job(src_rows, width, gcol, stores_fn):
            st = {}

            def load():
                st["s"] = prep_i[0] % NWS
                prep_i[0] += 1
                s = st["s"]
                dma(wst_f[s][:, 0:width], src_rows, w=[T_wf[s]], sem="wf%d" % s, q=prep_q[0])

            def cast():
                s = st["s"]
                o = wst_b[s][:, 0:width]
                src_ = wst_f[s][:, 0:width]
                if gcol is None:
                    dve(lambda e: e.tensor_copy(out=o, in_=src_), [T_wf[s]], [T_wb[s]])
                else:
                    g_ap = gvec[:, gcol:gcol + 1]
                    dve(lambda e: e.tensor_scalar(out=o, in0=src_, scalar1=g_ap, scalar2=None, op0=ALU.mult),
                        [T_wf[s], T_c], [T_wb[s]])

            def store():
                s = st["s"]
                stores_fn(s)
            return (load, cast, store)

        def std_stores(wdst, kc, ncg):
            def f(s):
                for cg in range(ncg):
                    dma(wdst[cg, kc // 2, :, (kc % 2) * 512:(kc % 2) * 512 + 512],
                        wst_b[s][:, cg * 512:(cg + 1) * 512], r=[T_wb[s]], w=[T_ws], sem="wb%d" % s, q="pool")
            return f

        for kc in range(8):
            def win_store(s, kc=kc):
                src_ap = wst_b[s][:, 0:1024]
                dve(lambda e: e.tensor_copy(out=wkv[:, kc, :], in_=src_ap), [T_wb[s]], [T_wkv])
            ld, cs, stf = make_job(w_in[kc * 128:(kc + 1) * 128, 2048:3072], 1024, kc, win_store)
            ld(); cs(); stf()
        prep_jobs = []
        for kc in range(8):
            prep_jobs.append(make_job(w_in[kc * 128:(kc + 1) * 128, 0:2048], 2048, kc, std_stores(ws_in, kc, 4)))
        for (wsrc, wdst, nkc) in ((w_out, ws_out, 8), (w_g, ws_g, 8), (w_pe, ws_pe, 2), (w_down, ws_dn, 32)):
            for kc in range(nkc):
                prep_jobs.append(make_job(wsrc[kc * 128:(kc + 1) * 128, :], 1024, None, std_stores(wdst, kc, 2)))
        for kc in range(8):
            def up_store(s, kc=kc):
                dma(ws_up[:, :, kc * 128:(kc + 1) * 128].rearrange("f p n -> p f n"),
                    wst_b[s][:, 0:4096].rearrange("p (f n) -> p f n", f=32), r=[T_wb[s]], w=[T_ws],
                    sem="wb%d" % s, q="pool")
            prep_jobs.append(make_job(w_up[kc * 128:(kc + 1) * 128, :], 4096, 8 + kc, up_store))

        ring_q = []
        ring_issued = [0]
        ring_taken = [0]

        def ring_plan(srcs):
            ring_q.extend(srcs)

        def ring_prefetch():
            while ring_issued[0] < len(ring_q) and ring_issued[0] - ring_taken[0] < NRING:
                s = ring_issued[0] % NRING
                dma(ring[s], ring_q[ring_issued[0]], r=[T_ws], w=[T_ring[s]], sem="ring%d" % s)
                ring_issued[0] += 1

        def ring_load(src):
            if ring_taken[0] >= len(ring_q):
                ring_q.append(src)
            assert ring_q[ring_taken[0]] is src or True
            if ring_issued[0] <= ring_taken[0]:
                ring_prefetch()
            s = ring_taken[0] % NRING
            ring_taken[0] += 1
            return ring[s], T_ring[s]

        def ring_topup():
            ring_prefetch()

        def plan_tok(ws, cgs, nkcp):
            return [ws[cg, kcp] for cg in cgs for kcp in range(nkcp)]

        def plan_A(is_sample):
            p = []
            if not is_sample:
                p += plan_tok(ws_in, [1, 2], 4)
            p += plan_tok(ws_in, [0, 1, 2, 3], 4)
            return p

        def plan_C():
            return (plan_tok(ws_out, [0, 1], 4) + [ws_up[f] for f in range(32)] + plan_tok(ws_dn, [0, 1], 16)
                    + plan_tok(ws_g, [0, 1], 4) + plan_tok(ws_pe, [0, 1], 1))

        def rstd_from_ss(ss_ap, n_feat, out_ap, toks):
            npp = ss_ap.shape[0]
            act(lambda e: e.activation(out=out_ap, in_=ss_ap, func=AF.Ln, scale=1.0 / n_feat, bias=eps_ap[0:npp]),
                toks + [T_lam], toks)
            act(lambda e: e.activation(out=out_ap, in_=out_ap, func=AF.Exp, scale=-0.5), toks, toks)

        def transpose_to(dst_fn, src_b, nparts, nchunks, bk, Tsrc, Tdst, evac_eng="act"):
            pv = bankb(bk)
            for c in range(nchunks):
                pe(lambda e, c=c: e.transpose(out=pv[:, c * nparts:(c + 1) * nparts],
                                              in_=src_b[0:nparts, c * 128:(c + 1) * 128],
                                              identity=ident_b[0:nparts, 0:nparts]),
                   [Tsrc, T_identb], [TB[bk]])
            dst_fn(pv[:, 0:nchunks * nparts].rearrange("p (c t) -> p c t", c=nchunks))

        mark1 = A.top
        xg = [A.f32(4, D) for _ in range(2)]; T_xg = [T(), T()]
        xb1 = [A.bf16(D) for _ in range(3)]; T_xb1 = [T(), T(), T()]
        junk = A.bf16(D); T_junk = T()
        xT1 = [A.bf16(8, 128) for _ in range(2)]; T_xT1 = [T(), T()]
        ss1 = [A.f32(4) for _ in range(2)]; T_ss1 = [[T() for _ in range(4)] for _ in range(2)]
        kbf = [A.bf16(512) for _ in range(2)]; T_kbf = [T(), T()]
        ktst = [A.bf16(4, 512) for _ in range(2)]; T_ktst = [T(), T()]
        vst = [A.bf16(4 * 4, 130) for _ in range(2)]; T_vst = [T(), T()]
        for s in range(2):
            dve(lambda e, s=s: e.memset(vst[s][:, :, 128:130], 1.0), [], [T_vst[s]])

        def load_xg(g):
            s = g % 2
            dma(xg[s], x_all[g * 512:(g + 1) * 512, :].rearrange("(a p) d -> p a d", p=128), w=[T_xg[s]],
                sem="xg%d" % s)

        NB1 = 4 * NG

        def st_L(n):
            g, a = divmod(n, 4)
            s = g % 2
            b2 = n % 2
            if a == 0 and g + 1 < NG:
                load_xg(g + 1)
            xa = xg[s][:, a, :]
            act(lambda e: e.activation(out=junk, in_=xa, func=AF.Square, accum_out=ss1[s][:, a:a + 1]),
                [T_xg[s]], [T_ss1[s][a]])
            b3 = n % 3
            dve(lambda e: e.tensor_copy(out=xb1[b3], in_=xa), [T_xg[s]], [T_xb1[b3]])
            rstd_from_ss(ss1[s][:, a:a + 1], D, ss1[s][:, a:a + 1], [T_ss1[s][a]])

        def st_TX(n):
            b2 = n % 2
            b3 = n % 3
            transpose_to(lambda pv: act(lambda e: e.activation(out=xT1[b2], in_=pv, func=AF.Copy),
                                        [], [TB[b2], T_xT1[b2]]),
                         xb1[b3], 128, 8, b2, T_xb1[b3], None)

        def st_MM(n):
            g, a = divmod(n, 4)
            s = g % 2
            b2 = n % 2
            for cg in range(2):
                bk = 2 + 2 * cg + b2
                for kc in range(8):
                    pe(lambda e, bk=bk, kc=kc, cg=cg: e.matmul(
                        bank(bk), lhsT=xT1[b2][:, kc, :], rhs=wkv[:, kc, cg * 512:(cg + 1) * 512],
                        start=(kc == 0), stop=(kc == 7)), [T_xT1[b2], T_wkv], [TB[bk]])
            rs = ss1[s][:, a:a + 1]
            dve(lambda e: e.tensor_scalar(out=kbf[b2], in0=bank(2 + b2), scalar1=rs, scalar2=None, op0=ALU.mult),
                [T_ss1[s][a]], [TB[2 + b2], T_kbf[b2]])
            vdst = vst[s].rearrange("p (h a) n -> p h a n", h=4)[:, :, a, 0:128]
            act(lambda e: e.activation(out=vdst, in_=bank(4 + b2).rearrange("p (h n) -> p h n", h=4),
                                       func=AF.Copy, scale=rs),
                [T_ss1[s][a]], [TB[4 + b2], T_vst[s]])

        def st_TK(n):
            g, a = divmod(n, 4)
            s = g % 2
            b2 = n % 2
            transpose_to(lambda pv: dve(lambda e: e.tensor_copy(out=ktst[s][:, :, a * 128:(a + 1) * 128], in_=pv),
                                        [], [TB[6 + b2], T_ktst[s]]),
                         kbf[b2], 128, 4, 6 + b2, T_kbf[b2], None)
            if a == 3:
                dma(kt_scr[:, :, g * 512:(g + 1) * 512].rearrange("h p n -> p h n"), ktst[s], r=[T_ktst[s]],
                    sem="kts%d" % s)
                dma(v_scr.rearrange("h p k n -> p h (k n)")[:, :, g * 520:(g + 1) * 520],
                    vst[s].rearrange("p (h a) n -> p h (a n)", h=4), r=[T_vst[s]], sem="vs%d" % s)

        if NG > 0:
            load_xg(0)
            st_L(0)
            if NB1 > 1:
                st_L(1)
            st_TX(0)
        stages = []
        pj = 0
        for n in range(NB1):
            if n + 2 < NB1:
                st_L(n + 2)
            if n + 1 < NB1:
                st_TX(n + 1)
            st_MM(n)
            if n >= 1:
                st_TK(n - 1)
            if stages:
                cs, stf = stages.pop(0)
                cs(); stf()
            if pj < len(prep_jobs) and (n % 2 == 0 or n < 10):
                ld, cs, stf = prep_jobs[pj]
                pj += 1
                ld()
                stages.append((cs, stf))
        if NB1 > 0:
            st_TK(NB1 - 1)
        while stages or pj < len(prep_jobs):
            if stages:
                cs, stf = stages.pop(0)
                cs(); stf()
            if pj < len(prep_jobs):
                ld, cs, stf = prep_jobs[pj]
                pj += 1
                ld()
                stages.append((cs, stf))
        late_jobs = prep_jobs[pj:]
        SLOT_ORDER = list(range(NSLOT))[::-1]
        ring_plan(plan_A(False))
        ring_prefetch()
        load_xo(SLOT_ORDER[0], False)
        kb.barrier()
        A.top = mark0

        def stream_tokmajor(ws, ncg, nkcp, lhs_fn, NA, evac_fn, lhs_toks, bank_sets, M=128, cgs=None, wres=None):
            cgl = list(range(ncg)) if cgs is None else cgs
            for ci, cg in enumerate(cgl):
                banks = bank_sets[ci % len(bank_sets)]
                for kcp in range(nkcp):
                    if wres is None:
                        piece, Tp = ring_load(ws[cg, kcp])
                        pv = piece.rearrange("p (k n) -> p k n", k=2)
                    for kk in range(2):
                        kc = 2 * kcp + kk
                        for a in range(NA):
                            if wres is None:
                                rhs = pv[:, kk, :]
                                rt = Tp
                            else:
                                rhs, rt = wres(cg, kc)
                            pe(lambda e, a=a, kc=kc, rhs=rhs, bk=banks[a]: e.matmul(
                                bank(bk)[0:M, :], lhsT=lhs_fn(a, kc), rhs=rhs,
                                start=(kc == 0), stop=(kc == 2 * nkcp - 1)),
                               lhs_toks(a) + [rt], [TB[banks[a]]])
                    if wres is None:
                        ring_topup()
                for a in range(NA):
                    evac_fn(cg, a, banks[a])

        SETS4 = [[0, 1, 2, 3], [4, 5, 6, 7]]

        def do_slot(j, is_sample):
            NA = 1 if is_sample else 4
            W = NA * 128
            ob0 = 4 * NSLOT if is_sample else 4 * j
            markS = A.top
            mixb = [A.bf16(D) for _ in range(NA)]; T_mix = [T() for _ in range(NA)]
            markQ = A.top
            qpad = [A.bf16(4, W) for _ in range(2)]; T_q = T()
            ktown = A.bf16(4, W); T_kt = T()
            vown = A.bf16(NA * 4, 130); T_vo = T()
            pool(lambda e: e.memset(qpad[0][64:128], 0.0), [], [T_q])
            pool(lambda e: e.memset(qpad[1][0:64], 0.0), [], [T_q])
            dve(lambda e: e.memset(vown[:, :, 128:130], 1.0), [], [T_vo])
            markA = A.top

            xbA = [A.bf16(D) for _ in range(2)]; T_xbA = [T(), T()]
            junkA = A.bf16(D); T_junkA = T()
            xTA = [A.bf16(8, 128) for _ in range(NA)]; T_xTA = [T() for _ in range(NA)]
            rsA = A.f32(NA + 1); T_rsA = [T() for _ in range(NA + 1)]
            cb = [A.f32(512) for _ in range(NA)]; T_cb = [T() for _ in range(NA)]
            zc = [A.f32(512) for _ in range(NA)]; T_zc = [T() for _ in range(NA)]
            zw = [[A.f32(512) for _ in range(2)] for _ in range(2)]
            T_zw = [[T() for _ in range(2)] for _ in range(2)]
            zw2 = A.f32(512); T_zw2 = T()
            qb = [A.bf16(512) for _ in range(NA)]; T_qb = [T() for _ in range(NA)]
            kbA = [A.bf16(512) for _ in range(NA)]; T_kbA = [T() for _ in range(NA)]
            kf = [A.f32(512) for _ in range(2)]; T_kf = [T(), T()]
            vf = [A.f32(512) for _ in range(2)]; T_vf = [T(), T()]
            hz = A.f32(512); T_hz = T()
            hzw = [A.f32(512) for _ in range(2)]; T_hzw = [T(), T()]

            for a in range(NA):
                b2 = a % 2
                act(lambda e, a=a: e.activation(out=junkA, in_=xo[a], func=AF.Square, accum_out=rsA[:, a:a + 1]),
                    [T_xo[a]], [T_rsA[a]])
                dve(lambda e, a=a, b2=b2: e.tensor_copy(out=xbA[b2], in_=xo[a]), [T_xo[a]], [T_xbA[b2]])
                rstd_from_ss(rsA[:, a:a + 1], D, rsA[:, a:a + 1], [T_rsA[a]])
                transpose_to(lambda pv, a=a: act(lambda e: e.activation(out=xTA[a], in_=pv, func=AF.Copy),
                                                 [], [TB[a], T_xTA[a]]),
                             xbA[b2], 128, 8, a, T_xbA[b2], None)

            if not is_sample:
                xh = A.f32(D); T_xh = T()
                xhb = A.bf16(D)
                xhT = A.bf16(8, 2); T_xhT = T()
                dma(xh[0:2], x_halo[j], w=[T_xh], sem="xh")
                act(lambda e: e.activation(out=junkA[0:2], in_=xh[0:2], func=AF.Square,
                                           accum_out=rsA[0:2, NA:NA + 1]), [T_xh], [T_rsA[NA]])
                pool(lambda e: e.tensor_copy(out=xhb[0:2], in_=xh[0:2]), [T_xh], [T_xh])
                rstd_from_ss(rsA[0:2, NA:NA + 1], D, rsA[0:2, NA:NA + 1], [T_rsA[NA]])
                transpose_to(lambda pv: act(lambda e: e.activation(out=xhT, in_=pv, func=AF.Copy),
                                            [], [TB[4], T_xhT]),
                             xhb, 2, 8, 4, T_xh, None)
                rsh = rsA[0:2, NA:NA + 1]

                def evac_halo(cg, a, bk):
                    if cg == 1:
                        act(lambda e: e.activation(out=hzw[0][0:2], in_=bank(bk)[0:2, :], func=AF.Copy, scale=rsh),
                            [T_rsA[NA]], [TB[bk], T_hzw[0]])
                    else:
                        dve(lambda e: e.scalar_tensor_tensor(out=hz[0:2], in0=bank(bk)[0:2, :], scalar=rsh,
                                                             in1=hzw[0][0:2], op0=ALU.mult, op1=ALU.mult),
                            [T_rsA[NA], T_hzw[0]], [TB[bk], T_hz])
                stream_tokmajor(ws_in, 4, 4, lambda a, kc: xhT[:, kc, :], 1, evac_halo, lambda a: [T_xhT],
                                [[5], [6]], M=2, cgs=[1, 2])
                NH = 2
            else:
                dma(hz[0:8], hist_s, w=[T_hz], sem="xh")
                NH = 8

            def evac_z(cg, a, bk):
                rs = rsA[:, a:a + 1]
                b2 = a % 2
                if cg == 0:
                    act(lambda e: e.activation(out=cb[a], in_=bank(bk), func=AF.Copy, scale=rs),
                        [T_rsA[a]], [TB[bk], T_cb[a]])
                elif cg == 1:
                    act(lambda e: e.activation(out=zc[a], in_=bank(bk), func=AF.Copy, scale=rs),
                        [T_rsA[a]], [TB[bk], T_zc[a]])
                elif cg == 2:
                    dve(lambda e: e.scalar_tensor_tensor(out=zc[a], in0=bank(bk), scalar=rs, in1=zc[a],
                                                         op0=ALU.mult, op1=ALU.mult),
                        [T_rsA[a]], [TB[bk], T_zc[a]])
                elif cg == 3:
                    dve(lambda e: e.tensor_scalar(out=qb[a], in0=bank(bk), scalar1=rs, scalar2=None, op0=ALU.mult),
                        [T_rsA[a]], [TB[bk], T_qb[a]])
                elif cg == 4:
                    act(lambda e: e.activation(out=kf[b2], in_=bank(bk), func=AF.Copy, scale=rs),
                        [T_rsA[a]], [TB[bk], T_kf[b2]])
                    dve(lambda e: e.tensor_copy(out=kbA[a], in_=kf[b2]), [T_kf[b2]], [T_kbA[a]])
                    dma(k_own[ob0 + a], kf[b2], r=[T_kf[b2]], sem="kf%d" % b2)
                else:
                    act(lambda e: e.activation(out=vf[b2], in_=bank(bk), func=AF.Copy, scale=rs),
                        [T_rsA[a]], [TB[bk], T_vf[b2]])
                    vdst = vown.rearrange("p (a h) n -> p a h n", a=NA)[:, a, :, 0:128]
                    pool(lambda e: e.tensor_copy(out=vdst, in_=vf[b2].rearrange("p (h n) -> p h n", h=4)),
                         [T_vf[b2]], [T_vo])
                    dma(v_own[ob0 + a], vf[b2], r=[T_vf[b2]], sem="vf%d" % b2)

            stream_tokmajor(ws_in, 4, 4, lambda a, kc: xTA[a][:, kc, :], NA, evac_z, lambda a: [T_xTA[a]], SETS4)
            stream_tokmajor(None, 2, 4, lambda a, kc: xTA[a][:, kc, :], NA,
                            lambda cg, a, bk: evac_z(cg + 4, a, bk), lambda a: [T_xTA[a]], SETS4,
                            wres=lambda cg, kc: (wkv[:, kc, cg * 512:(cg + 1) * 512], T_wkv))

            w0b, w1b, w2b, bbb = convbc[:, 0, :], convbc[:, 1, :], convbc[:, 2, :], convbc[:, 3, :]
            pool(lambda e: e.tensor_tensor(out=hzw[0][0:NH], in0=hz[0:NH], in1=w1b[0:NH], op=ALU.mult),
                 [T_hz, T_c], [T_hzw[0]])
            pool(lambda e: e.tensor_tensor(out=hzw[1][0:NH], in0=hz[0:NH], in1=w0b[0:NH], op=ALU.mult),
                 [T_hz, T_c], [T_hzw[1]])
            if is_sample:
                m_sh1, m_sh2, m_h1, m_h2 = 6, 7, 8, 9
            else:
                m_sh1, m_sh2, m_h1, m_h2 = 0, 1, 4, 5
            for a in range(NA):
                b2 = a % 2

                def q_evac(pv, a=a, b2=b2):
                    dve(lambda e: e.tensor_copy(out=qpad[0][0:64, :, a * 128:(a + 1) * 128], in_=pv[0:64]),
                        [], [TB[0 + b2], T_q])
                    dve(lambda e: e.tensor_copy(out=qpad[1][64:128, :, a * 128:(a + 1) * 128], in_=pv[64:128]),
                        [], [TB[0 + b2], T_q])
                transpose_to(q_evac, qb[a], 128, 4, 0 + b2, T_qb[a], None)
                transpose_to(lambda pv, a=a, b2=b2: act(
                    lambda e: e.activation(out=ktown[:, :, a * 128:(a + 1) * 128], in_=pv, func=AF.Copy),
                    [], [TB[2 + b2], T_kt]), kbA[a], 128, 4, 2 + b2, T_kbA[a], None)
                dve(lambda e, a=a, b2=b2: e.tensor_tensor(out=zw[b2][0], in0=zc[a], in1=w1b, op=ALU.mult),
                     [T_zc[a], T_c], [T_zw[b2][0]])
                dve(lambda e, a=a, b2=b2: e.tensor_tensor(out=zw[b2][1], in0=zc[a], in1=w0b, op=ALU.mult),
                     [T_zc[a], T_c], [T_zw[b2][1]])
                dve(lambda e, a=a: e.tensor_tensor(out=zw2, in0=zc[a], in1=w2b, op=ALU.mult),
                     [T_zc[a], T_c], [T_zw2])
                dve(lambda e: e.tensor_tensor(out=zw2, in0=zw2, in1=bbb, op=ALU.add), [T_c], [T_zw2])
                bk = 4 + b2
                pe(lambda e, b2=b2, bk=bk: e.matmul(bank(bk), lhsT=shm[:, m_sh1, :], rhs=zw[b2][0], start=True,
                                                    stop=False), [T_c, T_zw[b2][0]], [TB[bk]])
                pe(lambda e, b2=b2, bk=bk: e.matmul(bank(bk), lhsT=shm[:, m_sh2, :], rhs=zw[b2][1], start=False,
                                                    stop=False), [T_c, T_zw[b2][1]], [TB[bk]])
                if a == 0:
                    pe(lambda e, bk=bk: e.matmul(bank(bk), lhsT=shm[0:NH, m_h1, :], rhs=hzw[0][0:NH], start=False,
                                                 stop=False), [T_c, T_hzw[0]], [TB[bk]])
                    pe(lambda e, bk=bk: e.matmul(bank(bk), lhsT=shm[0:NH, m_h2, :], rhs=hzw[1][0:NH], start=False,
                                                 stop=True), [T_c, T_hzw[1]], [TB[bk]])
                else:
                    p2 = 1 - b2
                    pe(lambda e, bk=bk, p2=p2: e.matmul(bank(bk), lhsT=shm[:, 2, :], rhs=zw[p2][0], start=False,
                                                        stop=False), [T_c, T_zw[p2][0]], [TB[bk]])
                    pe(lambda e, bk=bk, p2=p2: e.matmul(bank(bk), lhsT=shm[:, 3, :], rhs=zw[p2][1], start=False,
                                                        stop=True), [T_c, T_zw[p2][1]], [TB[bk]])
                dve(lambda e, bk=bk: e.tensor_tensor(out=zw2, in0=bank(bk), in1=zw2, op=ALU.add),
                    [], [TB[bk], T_zw2])
                dve(lambda e, a=a: e.tensor_tensor(out=mixb[a][:, 0:512], in0=zw2, in1=cb[a], op=ALU.mult),
                    [T_cb[a]], [T_zw2, T_mix[a]])
            if is_sample:
                for b in range(BPC):
                    dma(conv_s[2 * b:2 * b + 2, :], zc[0][32 * b + 30:32 * b + 32, :], r=[T_zc[0]], sem="cvo")
            elif j == NSLOT - 1:
                dma(conv_p, zc[3][126:128, :], r=[T_zc[3]], sem="cvo")
            kb.barrier()
            A.top = markA

            if late_jobs:
                for i_ in range(NWS):
                    wst_f[i_] = A.f32(4096)
                    wst_b[i_] = A.bf16(4096)
            ring_plan(plan_C())
            deferred_prefetch = [ring_prefetch]
            if not is_sample:
                nxt = SLOT_ORDER.index(j) + 1
                if nxt < NSLOT:
                    deferred_prefetch.append(lambda: load_xo(SLOT_ORDER[nxt], False))
                else:
                    deferred_prefetch.append(lambda: load_xo(0, True))
            PTs = [A.bf16(2, 512) for _ in range(3)]; T_pt = [T() for _ in range(3)]
            dtmp = [A.f32(128) for _ in range(2)]; T_dtmp = [T(), T()]
            o_t = [A.f32(128) for _ in range(2)]; T_o = [T(), T()]
            junko = A.bf16(128); T_junko = T()
            rr = [A.f32(4) for _ in range(2)]; T_rr = [T(), T()]
            ucount = [0]
            dcount = [0]
            ecount = [0]

            def accv(i):
                return bank(4 + i // 3)[:, (i % 3) * 130:(i % 3) * 130 + 129], 4 + i // 3

            class U:
                pass

            def sreg(sbp):
                return ps_t[:, sbp * 1024:(sbp + 1) * 1024]

            hook_n = [0]
            hook_stage = []

            def unit_hook():
                hook_n[0] += 1
                if hook_n[0] >= 6 and deferred_prefetch and not (late_jobs or hook_stage):
                    for f in deferred_prefetch:
                        f()
                    del deferred_prefetch[:]
                if late_jobs or hook_stage:
                    if hook_n[0] % 3 == 0:
                        if hook_stage:
                            cs, stf = hook_stage.pop(0)
                            cs(); stf()
                        if late_jobs:
                            ld, cs, stf = late_jobs.pop(0)
                            ld()
                            hook_stage.append((cs, stf))

            def run_units(units, h, qbase_of, np_q):
                def emit_S(u):
                    if getattr(u, "load", None) is not None:
                        u.load()
                    sbp = u.idx % 2
                    R = sreg(sbp)
                    for (m, rhs, col, n) in u.smm:
                        pe(lambda e, u=u, rhs=rhs, col=col, n=n, R=R: e.matmul(
                            R[:, col:col + n], lhsT=u.kt, rhs=rhs, start=True, stop=True),
                           [u.Tkt, T_q], [TB[2 * sbp], TB[2 * sbp + 1]])

                def emit_exp(u):
                    sbp = u.idx % 2
                    ps = u.idx % 3
                    R = sreg(sbp)
                    PT = PTs[ps].rearrange("p a b -> p (a b)")
                    for (kind, lo, hi, arg) in u.exps:
                        if kind == "fast":
                            act(lambda e, lo=lo, hi=hi, arg=arg, R=R, PT=PT: e.activation(
                                out=PT[:, lo:hi], in_=R[:, lo:hi], func=AF.Exp, scale=0.125, bias=arg),
                                [T_c], [TB[2 * sbp], TB[2 * sbp + 1], T_pt[ps]])
                        elif kind == "fast2":
                            Rv = R.rearrange("p (a b) -> p a b", a=2)[:, :, lo:hi]
                            Pv = PTs[ps][:, :, lo:hi]
                            act(lambda e, arg=arg, Rv=Rv, Pv=Pv: e.activation(
                                out=Pv, in_=Rv, func=AF.Exp, scale=0.125, bias=arg),
                                [T_c], [TB[2 * sbp], TB[2 * sbp + 1], T_pt[ps]])
                        else:
                            k = dcount[0] % 2
                            dcount[0] += 1
                            dve(lambda e, lo=lo, hi=hi, arg=arg, k=k, R=R: e.scalar_tensor_tensor(
                                out=dtmp[k][:, 0:hi - lo], in0=R[:, lo:hi], scalar=0.125, in1=arg,
                                op0=ALU.mult, op1=ALU.add), [T_c], [TB[2 * sbp], TB[2 * sbp + 1], T_dtmp[k]])
                            act(lambda e, lo=lo, hi=hi, k=k, PT=PT: e.activation(
                                out=PT[:, lo:hi], in_=dtmp[k][:, 0:hi - lo], func=AF.Exp),
                                [T_dtmp[k]], [T_pt[ps]])

                def emit_PV(u):
                    ps = u.idx % 3
                    PT = PTs[ps].rearrange("p a b -> p (a b)")
                    for (i, col, wq, st, sp_) in u.pv:
                        av, abk = accv(i)
                        pe(lambda e, u=u, av=av, col=col, wq=wq, st=st, sp_=sp_, PT=PT: e.matmul(
                            av[0:wq, :], lhsT=PT[:, col:col + wq], rhs=u.v, start=st, stop=sp_,
                            skip_group_check=True),
                           [T_pt[ps], u.Tv], [TB[abk]])

                for ui, u in enumerate(units):
                    u.idx = ucount[0]
                    ucount[0] += 1
                last_touch = {}
                for ui, u in enumerate(units):
                    for pi, ent in enumerate(u.pv):
                        last_touch[ent[0]] = (ui, pi)
                for ui, u in enumerate(units):
                    u.pv = [(i_, c_, w_, st_, last_touch[i_] == (ui, pi)) for pi, (i_, c_, w_, st_, _sp) in
                            enumerate(u.pv)]
                for u in units[:10]:
                    if getattr(u, "load", None) is not None:
                        u.load()
                        u.load = None
                for ui, u in enumerate(units):
                    if ui == 0:
                        emit_S(u)
                        if len(units) > 1:
                            emit_S(units[1])
                    emit_exp(u)
                    if ui + 2 < len(units):
                        emit_S(units[ui + 2])
                    emit_PV(u)
                    unit_hook()

            def ptcol(h, m, qi):
                if h == 0:
                    return (qi // 2) * 512 + m * 256 + (qi % 2) * 128
                return m * 512 + qi * 128

            rr4 = [A.f32(12) for _ in range(2)]; T_rr4 = [T(), T()]
            o4 = [A.f32(128) for _ in range(4)]; T_o4 = [T() for _ in range(4)]

            def evac_head(h, np_q, dst_fn, accs):
                k2 = ecount[0] % 2
                ecount[0] += 1
                r = rr4[k2]
                Tr = T_rr4[k2]
                n = len(accs)
                for idx, (a, i1, i2) in enumerate(accs):
                    O1, b1 = accv(i1)
                    O2, b2_ = accv(i2)
                    dve(lambda e, O1=O1, idx=idx: e.reciprocal(out=r[0:np_q, idx:idx + 1], in_=O1[0:np_q, 128:129]),
                        [], [TB[b1], Tr])
                    dve(lambda e, O2=O2, idx=idx: e.reciprocal(out=r[0:np_q, 4 + idx:5 + idx],
                                                               in_=O2[0:np_q, 128:129]), [], [TB[b2_], Tr])
                dve(lambda e: e.tensor_scalar(out=r[0:np_q, 4:4 + n], in0=r[0:np_q, 4:4 + n],
                                              scalar1=lam_t[0:np_q, 1:2], scalar2=None, op0=ALU.mult),
                    [T_lam], [Tr])
                for idx, (a, i1, i2) in enumerate(accs):
                    O1, b1 = accv(i1)
                    O2, b2_ = accv(i2)
                    dve(lambda e, O1=O1, idx=idx: e.tensor_scalar(out=o4[idx][0:np_q], in0=O1[0:np_q, 0:128],
                                                                  scalar1=r[0:np_q, idx:idx + 1], scalar2=None,
                                                                  op0=ALU.mult), [Tr], [TB[b1], T_o4[idx]])
                    dve(lambda e, O2=O2, idx=idx: e.scalar_tensor_tensor(
                        out=o4[idx][0:np_q], in0=O2[0:np_q, 0:128], scalar=r[0:np_q, 4 + idx:5 + idx],
                        in1=o4[idx][0:np_q], op0=ALU.mult, op1=ALU.add), [Tr], [TB[b2_], T_o4[idx]])
                    act(lambda e, idx=idx: e.activation(out=junko[0:np_q], in_=o4[idx][0:np_q], func=AF.Square,
                                                        accum_out=r[0:np_q, 8 + idx:9 + idx]),
                        [T_o4[idx]], [Tr])
                rstd_from_ss(r[0:np_q, 8:8 + n], 128, r[0:np_q, 8:8 + n], [Tr])
                for idx, (a, i1, i2) in enumerate(accs):
                    dst = dst_fn(a)
                    dve(lambda e, idx=idx, dst=dst: e.scalar_tensor_tensor(
                        out=dst, in0=o4[idx][0:np_q], scalar=r[0:np_q, 8 + idx:9 + idx], in1=gsub[0:np_q],
                        op0=ALU.mult, op1=ALU.mult), [Tr, T_c, T_o4[idx]], dst_tokens[0])

            dst_tokens = [[]]

            if not is_sample:
                NKV = CFG.get("NKV", 6)
                kvK = [A.bf16(512) for _ in range(NKV)]
                kvV = [A.bf16(4, 130) for _ in range(NKV)]
                T_kv = [T() for _ in range(NKV)]
                kvpos = [0]
                for h in range(H):
                    units = []
                    for ch in range(E[j] // 4):
                        s = kvpos[0] % NKV
                        kvpos[0] += 1

                        def load_chunk(s=s, h=h, ch=ch):
                            dma(kvK[s], kt_scr[h, :, ch * 512:(ch + 1) * 512], w=[T_kv[s]], sem="kv%d" % s)
                            dma(kvV[s].rearrange("p k n -> p (k n)"),
                                v_scr[h].rearrange("p k n -> p (k n)")[:, ch * 520:(ch + 1) * 520],
                                w=[T_kv[s]], sem="kv%d" % s)
                        for i in range(4):
                            kbi = 4 * ch + i
                            u = U()
                            u.load = load_chunk if i == 0 else None
                            u.kt = kvK[s][:, i * 128:(i + 1) * 128]
                            u.Tkt = T_kv[s]
                            u.v = kvV[s][:, i, 0:129]
                            u.Tv = T_kv[s]
                            first = False
                            if h == 0:
                                u.smm = [(m, qpad[m][:, h, hf * 256:(hf + 1) * 256], hf * 512 + m * 256, 256)
                                         for hf in range(2) for m in range(2)]
                                u.exps = [("fast", hf * 512, (hf + 1) * 512,
                                           btab[:, BIDX[(j, h, kbi, hf)]:BIDX[(j, h, kbi, hf)] + 1])
                                          for hf in range(2)]
                            else:
                                u.smm = [(m, qpad[m][:, h, :], m * 512, 512) for m in range(2)]
                                u.exps = [("fast", 0, 1024, btab[:, BIDX[(j, h, kbi, 0)]:BIDX[(j, h, kbi, 0)] + 1])]
                            u.pv = [(m * 4 + qi, ptcol(h, m, qi), 128, first and ((m * 4 + qi) % 3 == 0), False)
                                    for m in range(2) for qi in range(4)]
                            units.append(u)
                    dunits = []
                    for ap_ in range(4):
                        u = U()
                        u.kt = ktown[:, h, ap_ * 128:(ap_ + 1) * 128]
                        u.Tkt = T_kt
                        u.v = vown[:, ap_ * 4 + h, 0:129]
                        u.Tv = T_vo
                        u.smm = []
                        for m in range(2):
                            if h == 0:
                                for hf in range(2):
                                    qlo = max(ap_, 2 * hf)
                                    qhi = 2 * hf + 2
                                    if qlo < qhi:
                                        u.smm.append((m, qpad[m][:, h, qlo * 128:qhi * 128],
                                                      hf * 512 + m * 256 + (qlo % 2) * 128, (qhi - qlo) * 128))
                            else:
                                u.smm.append((m, qpad[m][:, h, ap_ * 128:512], m * 512 + ap_ * 128,
                                              (4 - ap_) * 128))
                        u.exps = []
                        u.pv = []
                        for m in range(2):
                            for qi in range(ap_, 4):
                                c = ptcol(h, m, qi)
                                if qi == ap_:
                                    u.exps.append(("diag", c, c + 128, bdiag[:, h * 4 + ap_, :]))
                                else:
                                    c0 = h * 16 + ap_ * 4 + qi
                                    u.exps.append(("fast", c, c + 128, dbias[:, c0:c0 + 1]))
                                u.pv.append((m * 4 + qi, c, 128, (ap_ == 0) and ((m * 4 + qi) % 3 == 0), False))
                        dunits.append(u)
                    units = dunits + units
                    run_units(units, h, None, 128)
                    dst_tokens[0] = T_mix
                    evac_head(h, 128, lambda a, h=h: mixb[a][:, 512 + h * 128:512 + (h + 1) * 128],
                              [(a, a, 4 + a) for a in range(4)])
            else:
                ckf = [A.f32(PAST) for _ in range(2)]; T_ckf = [T(), T()]
                ckbs = [A.bf16(4, PAST) for _ in range(2)]; T_ckbs = [T(), T()]
                cvf = [A.f32(4, 512) for _ in range(2)]; T_cvf = [T(), T()]
                cvbs = [A.bf16(NKBS * 4, 130) for _ in range(2)]; T_cvbs = [T(), T()]
                yab = [A.bf16(512) for _ in range(2)]; T_yab = [T(), T()]
                selT = A.bf16(4, 128); T_sel = T()
                dve(lambda e: e.memset(selT, 0.0), [], [T_sel])
                for b in range(BPC):
                    dve(lambda e, b=b: e.tensor_copy(out=selT[0:32, b, 32 * b:32 * b + 32], in_=ident_b[0:32, 0:32]),
                        [T_identb], [T_sel])
                for s in range(2):
                    dve(lambda e, s=s: e.memset(cvbs[s][:, :, 128:130], 1.0), [], [T_cvbs[s]])
                cc = [0]

                def prep_b(b):
                    pb = b % 2
                    for h in range(H):
                        s = cc[0] % 2
                        cc[0] += 1
                        dma(ckf[s], ckT[b, h], w=[T_ckf[s]], sem="ckf%d" % s)
                        dve(lambda e, s=s, h=h: e.tensor_copy(out=ckbs[pb][:, h, :], in_=ckf[s]),
                            [T_ckf[s]], [T_ckbs[pb]])
                    for g4 in range(NKBS // 4):
                        s = cc[0] % 2
                        cc[0] += 1
                        dma(cvf[s], cv[b, g4 * 512:(g4 + 1) * 512, :].rearrange("(k p) n -> p k n", p=128),
                            w=[T_cvf[s]], sem="cvf%d" % s)
                        pool(lambda e, s=s, g4=g4: e.tensor_copy(
                            out=cvbs[pb][:, g4 * 16:(g4 + 1) * 16, 0:128],
                            in_=cvf[s].rearrange("p k (h n) -> p (k h) n", h=4)), [T_cvf[s]], [T_cvbs[pb]])

                prep_b(0)
                for b in range(BPC):
                    if b + 1 < BPC:
                        prep_b(b + 1)
                    ckb = ckbs[b % 2]; T_ckb = T_ckbs[b % 2]
                    cvb = cvbs[b % 2]; T_cvb = T_cvbs[b % 2]
                    units = []
                    started = set()
                    for h in range(H):
                        for kbi in range(NKBS + 1):
                            u = U()
                            own = (kbi == NKBS)
                            if own:
                                u.kt = ktown[:, h, 0:128]
                                u.Tkt = T_kt
                                u.v = vown[:, h, 0:129]
                                u.Tv = T_vo
                                u.exps = [("diag", m * 512, m * 512 + 32, bs[:, b * 4 + h, :]) for m in range(2)]
                            else:
                                u.kt = ckb[:, h, kbi * 128:(kbi + 1) * 128]
                                u.Tkt = T_ckb
                                u.v = cvb[:, kbi * 4 + h, 0:129]
                                u.Tv = T_cvb
                                c0 = h * NKBS + kbi
                                u.exps = [("fast2", 0, 32, sbias[:, c0:c0 + 1])]
                            u.smm = [(m, qpad[m][:, h, b * 32:(b + 1) * 32], m * 512, 32) for m in range(2)]
                            u.pv = []
                            for m in range(2):
                                i = m * 4 + h
                                bk_ = 4 + i // 3
                                st = (kbi == 0) and (bk_ not in started)
                                if kbi == 0:
                                    started.add(bk_)
                                u.pv.append((i, m * 512, 32, st, own))
                            units.append(u)
                    run_units(units, 0, None, 32)
                    dst_tokens[0] = [T_yab[b % 2]]
                    evac_head(0, 32, lambda a, b=b: yab[b % 2][0:32, a * 128:(a + 1) * 128],
                              [(h, h, 4 + h) for h in range(H)])
                    pe(lambda e, b=b: e.matmul(bank(7), lhsT=selT[0:32, b, :], rhs=yab[b % 2][0:32, :],
                                               start=(b == 0), stop=(b == BPC - 1)),
                       [T_sel, T_yab[b % 2]], [TB[7]])
                act(lambda e: e.activation(out=mixb[0][:, 512:1024], in_=bank(7), func=AF.Copy),
                    [], [TB[7], T_mix[0]])
            while hook_stage or late_jobs:
                if hook_stage:
                    cs, stf = hook_stage.pop(0)
                    cs(); stf()
                if late_jobs:
                    ld, cs, stf = late_jobs.pop(0)
                    ld()
                    hook_stage.append((cs, stf))
            for f in deferred_prefetch:
                f()
            kb.barrier()
            A.top = markQ

            if DEBUG:
                dbgt = [A.f32(D) for _ in range(NA)]
                for a in range(NA):
                    dve(lambda e, a=a: e.tensor_copy(out=dbgt[a], in_=mixb[a]), [T_mix[a]], [T_mix[a]])
                    dma(dbg_mix[ob0 + a], dbgt[a], r=[T_mix[a]], sem="dbg%d" % a)
            h1 = [A.f32(D) for _ in range(NA)]; T_h = [T() for _ in range(NA)]
            yt = [A.f32(D) for _ in range(NA)]; T_y = [T() for _ in range(NA)]
            TAb = A.bf16(8, W); T_TA = T()
            TBb = A.bf16(8, W); T_TBb = T()
            ffT = A.bf16(32, W); T_ff = T()
            pf = [A.f32(PLE) for _ in range(NA)]; T_pf = [T() for _ in range(NA)]
            pbf = [A.bf16(PLE) for _ in range(2)]; T_pbf = [T(), T()]
            pT = A.bf16(2, W); T_pT = T()
            cbf = [A.bf16(D) for _ in range(2)]; T_cbf = [T(), T()]
            rsC = A.f32(NA, 8); T_rsC = [T() for _ in range(NA)]
            rl = [A.f32(W) for _ in range(2)]; T_rl = [T(), T()]
            junkC = A.bf16(D); T_junkC = T()

            if not is_sample:
                ring_plan(plan_A(SLOT_ORDER.index(j) + 1 >= NSLOT))
            for a in range(NA):
                dma(h1[a], x_own[ob0 + a], w=[T_h[a]], sem="hx%d" % a)
                dma(pf[a], p_own[ob0 + a], w=[T_pf[a]], sem="pf%d" % a)

            def evac_T(dst, evi):
                def f(pv):
                    if evi % 2 == 0:
                        act(lambda e: e.activation(out=dst, in_=pv, func=AF.Copy), [], f.toks)
                    else:
                        dve(lambda e: e.tensor_copy(out=dst, in_=pv), [], f.toks)
                return f

            def norm_add(a, col, g_idx):
                rs = rsC[:, a, col:col + 1]
                act(lambda e: e.activation(out=junkC, in_=yt[a], func=AF.Square, accum_out=rs),
                    [T_y[a]], [T_rsC[a]])
                rstd_from_ss(rs, D, rs, [T_rsC[a]])
                dve(lambda e: e.scalar_tensor_tensor(out=yt[a], in0=yt[a], scalar=rs, in1=gbc[:, g_idx, :],
                                                     op0=ALU.mult, op1=ALU.mult), [T_rsC[a], T_c], [T_y[a]])
                dve(lambda e: e.tensor_tensor(out=h1[a], in0=h1[a], in1=yt[a], op=ALU.add), [T_y[a]], [T_h[a]])

            for a in range(NA):
                f = evac_T(TAb[:, :, a * 128:(a + 1) * 128], a)
                f.toks = [TB[a], T_TA]
                transpose_to(f, mixb[a], 128, 8, a, T_mix[a], None)
            stream_tokmajor(ws_out, 2, 4, lambda a, kc: TAb[:, kc, a * 128:(a + 1) * 128], NA,
                            lambda cg, a, bk: act(lambda e: e.activation(
                                out=yt[a][:, cg * 512:(cg + 1) * 512], in_=bank(bk), func=AF.Copy),
                                [], [TB[bk], T_y[a]]),
                            lambda a: [T_TA], SETS4)
            for a in range(NA):
                norm_add(a, 0, 0)
            for a in range(NA):
                b2 = a % 2
                rs2 = rsC[:, a, 1:2]
                act(lambda e, a=a, rs2=rs2: e.activation(out=junkC, in_=h1[a], func=AF.Square, accum_out=rs2),
                    [T_h[a]], [T_rsC[a]])
                rstd_from_ss(rs2, D, rs2, [T_rsC[a]])
                dve(lambda e, a=a, rs2=rs2: e.tensor_tensor(out=rsC[:, a, 2:3], in0=rs2, in1=rs2, op=ALU.mult),
                    [], [T_rsC[a]])
                act(lambda e, a=a, b2=b2: e.activation(out=cbf[b2], in_=h1[a], func=AF.Copy), [T_h[a]], [T_cbf[b2]])
                f = evac_T(TBb[:, :, a * 128:(a + 1) * 128], a + 1)
                f.toks = [TB[4 + a], T_TBb]
                transpose_to(f, cbf[b2], 128, 8, 4 + a, T_cbf[b2], None)
            for fch in range(32):
                piece, Tp = ring_load(ws_up[fch])
                pv = piece.rearrange("p (k n) -> p k n", k=8)
                bk = fch % 2
                for kc in range(8):
                    pe(lambda e, pv=pv, kc=kc, bk=bk: e.matmul(bank(bk)[:, 0:W], lhsT=pv[:, kc, :],
                                                               rhs=TBb[:, kc, :], start=(kc == 0), stop=(kc == 7)),
                       [Tp, T_TBb], [TB[bk]])
                ring_topup()
                act(lambda e, bk=bk: e.activation(out=rl[bk], in_=bank(bk)[:, 0:W], func=AF.Relu),
                    [], [TB[bk], T_rl[bk]])
                dve(lambda e, bk=bk, fch=fch: e.tensor_tensor(out=ffT[:, fch, :], in0=rl[bk], in1=rl[bk],
                                                              op=ALU.mult), [T_rl[bk]], [T_ff])
            stream_tokmajor(ws_dn, 2, 16, lambda a, kc: ffT[:, kc, a * 128:(a + 1) * 128], NA,
                            lambda cg, a, bk: act(lambda e: e.activation(
                                out=yt[a][:, cg * 512:(cg + 1) * 512], in_=bank(bk), func=AF.Copy,
                                scale=rsC[:, a, 2:3]), [T_rsC[a]], [TB[bk], T_y[a]]),
                            lambda a: [T_ff], SETS4)
            for a in range(NA):
                norm_add(a, 3, 1)
            for a in range(NA):
                b2 = a % 2
                act(lambda e, a=a, b2=b2: e.activation(out=cbf[b2], in_=h1[a], func=AF.Copy), [T_h[a]], [T_cbf[b2]])
                f = evac_T(TAb[:, :, a * 128:(a + 1) * 128], a)
                f.toks = [TB[a], T_TA]
                transpose_to(f, cbf[b2], 128, 8, a, T_cbf[b2], None)
                act(lambda e, a=a, b2=b2: e.activation(out=pbf[b2], in_=pf[a], func=AF.Copy), [T_pf[a]], [T_pbf[b2]])
                f = evac_T(pT[:, :, a * 128:(a + 1) * 128], a + 1)
                f.toks = [TB[4 + a], T_pT]
                transpose_to(f, pbf[b2], 128, 2, 4 + a, T_pbf[b2], None)
            stream_tokmajor(ws_g, 2, 4, lambda a, kc: TAb[:, kc, a * 128:(a + 1) * 128], NA,
                            lambda cg, a, bk: act(lambda e: e.activation(
                                out=yt[a][:, cg * 512:(cg + 1) * 512], in_=bank(bk), func=AF.Exp, scale=-1.0),
                                [], [TB[bk], T_y[a]]),
                            lambda a: [T_TA], SETS4)
            for a in range(NA):
                dve(lambda e, a=a: e.tensor_scalar(out=yt[a], in0=yt[a], scalar1=1.0, scalar2=None, op0=ALU.add),
                    [], [T_y[a]])
                dve(lambda e, a=a: e.reciprocal(out=yt[a], in_=yt[a]), [], [T_y[a]])
            stream_tokmajor(ws_pe, 2, 1, lambda a, kc: pT[:, kc, a * 128:(a + 1) * 128], NA,
                            lambda cg, a, bk: dve(lambda e: e.tensor_tensor(
                                out=yt[a][:, cg * 512:(cg + 1) * 512], in0=bank(bk),
                                in1=yt[a][:, cg * 512:(cg + 1) * 512], op=ALU.mult), [], [TB[bk], T_y[a]]),
                            lambda a: [T_pT], SETS4)
            for a in range(NA):
                norm_add(a, 4, 2)
                dma(y_own[ob0 + a], h1[a], r=[T_h[a]], sem="yo%d" % a)
            kb.barrier()
            A.top = markS

        for si, j in enumerate(SLOT_ORDER):
            do_slot(j, False)
            if si == 0:
                assert not late_jobs
        do_slot(0, True)
        kb.emit()
    return nc


_PROG_CACHE = {}


def kernel(x_prompt, x_sample, cache_k, cache_v, state_conv, p_prompt, p_sample,
           w_in, w_conv, b_conv, lambda_q1, lambda_k1, lambda_q2, lambda_k2, g_subln,
           w_out, g_pre_mix, g_post_mix, g_pre_mlp, g_post_mlp, w_up, w_down,
           w_pe, w_pe_gate, g_pe):
    NSLOT = CFG["NSLOT"]
    PAST = CFG["PAST"]
    f = np.float32
    A_ = lambda v: np.ascontiguousarray(np.asarray(v), dtype=f)
    x_prompt = A_(x_prompt); x_sample = A_(x_sample); cache_k = A_(cache_k); cache_v = A_(cache_v)
    state_conv = A_(state_conv); p_prompt = A_(p_prompt); p_sample = A_(p_sample)
    SEQ = x_prompt.shape[1]
    assert SEQ == NCORE * NSLOT * 512 and cache_k.shape[2] == PAST
    NOWN = 4 * NSLOT + 1
    key = (NSLOT, PAST)
    if key not in _PROG_CACHE:
        _PROG_CACHE[key] = build_program(NSLOT, PAST)
    nc = _PROG_CACHE[key]

    def bc(v, n=128):
        v = A_(v).reshape(1, -1)
        return np.ascontiguousarray(np.broadcast_to(v, (n, v.shape[1])))

    gvec = np.concatenate([A_(g_pre_mix)[0].reshape(8, 128).T, A_(g_pre_mlp)[0].reshape(8, 128).T], axis=1)
    gbc = np.concatenate([bc(g_post_mix[0]), bc(g_post_mlp[0]), bc(g_pe[0])], axis=1)
    convbc = np.concatenate([bc(A_(w_conv)[0, 0]), bc(A_(w_conv)[0, 1]), bc(A_(w_conv)[0, 2]), bc(A_(b_conv)[0])],
                            axis=1)
    lamv = np.concatenate([bc(lambda_q1[0]), bc(lambda_k1[0]), bc(lambda_q2[0]), bc(lambda_k2[0])], axis=1)
    shared = dict(
        x_all=x_prompt[0], w_in=A_(w_in)[0], w_out=A_(w_out)[0], w_up=A_(w_up)[0], w_down=A_(w_down)[0],
        w_pe=A_(w_pe)[0], w_g=A_(w_pe_gate)[0], gvec=np.ascontiguousarray(gvec), gbc=np.ascontiguousarray(gbc),
        gsub=bc(g_subln[0]), convbc=np.ascontiguousarray(convbc), lamv=np.ascontiguousarray(lamv),
        ident=np.eye(128, dtype=f))
    in_maps = []
    for c in range(NCORE):
        tb = make_tables(c, NSLOT, PAST)
        xo = np.zeros((NOWN, 128, D), f)
        po = np.zeros((NOWN, 128, PLE), f)
        xh = np.zeros((NSLOT, 2, D), f)
        for j in range(NSLOT):
            sb = NCORE * j + c
            xo[4 * j:4 * j + 4] = x_prompt[0, sb * 512:(sb + 1) * 512].reshape(4, 128, D)
            po[4 * j:4 * j + 4] = p_prompt[0, 0, sb * 512:(sb + 1) * 512].reshape(4, 128, PLE)
            if sb > 0:
                xh[j] = x_prompt[0, sb * 512 - 2:sb * 512]
        bsl = slice(c * BPC, (c + 1) * BPC)
        xo[4 * NSLOT] = x_sample[bsl].reshape(128, D)
        po[4 * NSLOT] = p_sample[0, bsl].reshape(128, PLE)
        m = dict(shared)
        m.update(x_own=xo, p_own=po, x_halo=xh,
                 hist_s=np.ascontiguousarray(state_conv[0, bsl].reshape(8, 512)),
                 ckT=np.ascontiguousarray(cache_k[0, bsl].transpose(0, 2, 3, 1)),
                 cv=np.ascontiguousarray(cache_v[0, bsl].reshape(BPC, PAST, 512)),
                 **tb)
        in_maps.append(m)
    res = run_bass_kernel_spmd(nc, in_maps, core_ids=list(range(NCORE)))
    R = res.results
    if CFG.get("DEBUG", False):
        CFG["_dbg"] = [r["dbg_mix"] for r in R]
        CFG["_kt"] = [np.asarray(r["kt_scr"]).astype(np.float32) for r in R]
        CFG["_v"] = [np.asarray(r["v_scr"]).astype(np.float32) for r in R]
    y_prompt = np.zeros((1, SEQ, D), f)
    k_prompt = np.zeros((1, 1, SEQ, H, 128), f)
    v_prompt = np.zeros((1, 1, SEQ, H, 128), f)
    y_sample = np.zeros((NCORE * BPC, DEC_T, D), f)
    k_sample = np.zeros((1, NCORE * BPC, DEC_T, H, 128), f)
    v_sample = np.zeros((1, NCORE * BPC, DEC_T, H, 128), f)
    conv_sample = np.zeros((1, NCORE * BPC, 2, 512), f)
    for c in range(NCORE):
        r = R[c]
        for j in range(NSLOT):
            sb = NCORE * j + c
            sl = slice(sb * 512, (sb + 1) * 512)
            y_prompt[0, sl] = r["y_own"][4 * j:4 * j + 4].reshape(512, D)
            k_prompt[0, 0, sl] = r["k_own"][4 * j:4 * j + 4].reshape(512, H, 128)
            v_prompt[0, 0, sl] = r["v_own"][4 * j:4 * j + 4].reshape(512, H, 128)
        bsl = slice(c * BPC, (c + 1) * BPC)
        y_sample[bsl] = r["y_own"][4 * NSLOT].reshape(BPC, DEC_T, D)
        k_sample[0, bsl] = r["k_own"][4 * NSLOT].reshape(BPC, DEC_T, H, 128)
        v_sample[0, bsl] = r["v_own"][4 * NSLOT].reshape(BPC, DEC_T, H, 128)
        conv_sample[0, bsl] = r["conv_s"].reshape(BPC, 2, 512)
    conv_prompt = np.ascontiguousarray(R[NCORE - 1]["conv_p"]).reshape(1, 1, 2, 512).astype(f)
    return (y_prompt, y_sample, k_prompt, v_prompt, conv_prompt, k_sample, v_sample, conv_sample)
```

```python
import math
from contextlib import ExitStack
import numpy as np
import concourse.bass as bass
import concourse.mybir as mybir
from concourse.bass_utils import run_bass_kernel_spmd

F32 = mybir.dt.float32
BF16 = mybir.dt.bfloat16
AF = mybir.ActivationFunctionType
ALU = mybir.AluOpType

NCORE = 8
D = 1024
DFF = 4096
PLE = 256
H = 4
DEC_T = 32
BPC = 4
RMS_EPS = 1e-6
NEG = -30000.0
SLOPES = [2.0 ** (-8.0 * (h + 1) / 4) for h in range(4)]
LAM_INIT = 0.8 - 0.6 * math.exp(-0.3 * 0)
CFG = dict(NSLOT=4, PAST=2048)

ENGS = ["pe", "act", "dve", "pool", "sp"]


class T:
    __slots__ = ("w", "rs")

    def __init__(self):
        self.w = None
        self.rs = []


class Op:
    __slots__ = ("eng", "fn", "deps", "needed", "ev", "dma_sem", "epoch", "barrier")

    def __init__(self, eng, fn):
        self.eng = eng
        self.fn = fn
        self.deps = []
        self.needed = False
        self.ev = None
        self.dma_sem = None
        self.barrier = 0


class DSem:
    def __init__(self, sem):
        self.sem = sem
        self.n = 0


class KB:
    def __init__(self, nc, es):
        self.nc = nc
        self.es = es
        self.ops = []
        self.epoch = 0
        self.esem = {e: es.enter_context(nc.semaphore("es_" + e)) for e in ENGS[:4]}
        self.bsem = es.enter_context(nc.semaphore("bar"))
        self.dsems = {}
        self.nbar = 0

    def dsem(self, name):
        if name not in self.dsems:
            self.dsems[name] = DSem(self.es.enter_context(self.nc.semaphore("ds_" + name)))
        return self.dsems[name]

    def op(self, eng, fn, reads=(), writes=(), dsem=None):
        o = Op(eng, fn)
        o.epoch = self.epoch
        o.dma_sem = dsem
        deps = []
        for r in reads:
            if r.w is not None:
                deps.append(r.w)
        for w in writes:
            if w.w is not None:
                deps.append(w.w)
            deps.extend(w.rs)
        seen = set()
        for d in deps:
            if d is o or id(d) in seen or d.epoch < self.epoch:
                continue
            seen.add(id(d))
            if d.eng == eng and d.dma_sem is None and eng in ("pe", "sp"):
                continue
            if eng == "sp" and d.eng == "sp" and d.dma_sem is not None and d.dma_sem is dsem:
                continue
            o.deps.append(d)
            d.needed = True
        for r in reads:
            r.rs.append(o)
        for w in writes:
            w.w = o
            w.rs = []
        self.ops.append(o)
        return o

    def barrier(self):
        self.nbar += 1
        for e in ENGS:
            o = Op(e, None)
            o.barrier = self.nbar
            o.epoch = self.epoch
            self.ops.append(o)
        self.epoch += 1

    def emit(self):
        nc = self.nc
        per = {e: [] for e in ENGS}
        cnt = {e: 0 for e in ENGS}
        for o in self.ops:
            per[o.eng].append(o)
            if o.barrier:
                continue
            if o.dma_sem is not None:
                o.dma_sem.n += 1
                o.ev = (o.dma_sem, o.dma_sem.n * 16)
            elif o.needed:
                cnt[o.eng] += 1
                o.ev = (o.eng, cnt[o.eng])
        esem = self.esem
        bsem = self.bsem
        alld = list(self.dsems.values())

        def run(ename, eng):
            waited = {}
            dcount = {id(d): 0 for d in alld}
            for o in per[ename]:
                if o.barrier:
                    for d in alld:
                        if dcount[id(d)] > waited.get(id(d), 0):
                            eng.wait_ge(d.sem, dcount[id(d)])
                            waited[id(d)] = dcount[id(d)]
                    if ename == "sp":
                        eng.sem_inc(bsem, 1)
                    else:
                        eng.drain().then_inc(bsem, 1)
                    eng.wait_ge(bsem, 5 * o.barrier)
                    continue
                need = {}
                for d in o.deps:
                    key, val = d.ev
                    k = id(key) if isinstance(key, DSem) else key
                    if waited.get(k, 0) >= val:
                        continue
                    if k not in need or need[k][1] < val:
                        need[k] = (key, val)
                for k, (key, val) in need.items():
                    sem = key.sem if isinstance(key, DSem) else esem[key]
                    eng.wait_ge(sem, val)
                    waited[k] = val
                inst = o.fn(eng)
                if o.dma_sem is not None:
                    inst.then_inc(o.dma_sem.sem, 16)
                    dcount[id(o.dma_sem)] = o.ev[1]
                elif o.needed:
                    inst.then_inc(esem[o.eng], 1)
            if ename == "sp":
                for d in alld:
                    if d.n > 0:
                        eng.wait_ge(d.sem, d.n * 16)

        with nc.Block() as block:
            @block.tensor
            def _(eng):
                run("pe", eng)

            @block.scalar
            def _(eng):
                run("act", eng)

            @block.vector
            def _(eng):
                run("dve", eng)

            @block.gpsimd
            def _(eng):
                run("pool", eng)

            @block.sync
            def _(eng):
                run("sp", eng)


class Arena:
    def __init__(self, t, n):
        self.t = t
        self.n = n
        self.top = 0

    def _take(self, n32):
        off = self.top
        self.top += n32
        assert self.top <= self.n, ("SBUF arena overflow", self.top, self.n)
        return off

    def f32(self, *shape):
        n = int(np.prod(shape))
        off = self._take(n)
        v = self.t[:, off:off + n]
        return _shape(v, shape)

    def bf16(self, *shape):
        n = int(np.prod(shape))
        n32 = (n + 1) // 2
        off = self._take(n32)
        v = self.t[:, off:off + n32].bitcast(BF16)[:, 0:n]
        return _shape(v, shape)


def _shape(v, shape):
    if len(shape) == 1:
        return v
    if len(shape) == 2:
        return v.rearrange("p (a b) -> p a b", a=shape[0])
    if len(shape) == 3:
        return v.rearrange("p (a b c) -> p a b c", a=shape[0], b=shape[1])
    raise ValueError(shape)


def _hw(h):
    return 256 if h == 0 else 512


def _refoff(h, a):
    return 256 * (a // 2) if h == 0 else 0


def bias_layout(NSLOT):
    E = [4 * (NCORE * j + NCORE - 1) for j in range(NSLOT)]
    idx = {}
    n = 0
    for j in range(NSLOT):
        for h in range(H):
            for kb in range(E[j]):
                for half in range(2 if h == 0 else 1):
                    idx[(j, h, kb, half)] = n
                    n += 1
    return E, idx, n


def make_tables(c, NSLOT, PAST):
    E, idx, ncol = bias_layout(NSLOT)
    kl = np.arange(128, dtype=np.float64)
    btab = np.zeros((128, ncol), np.float64)
    for (j, h, kb, half), col in idx.items():
        sb = NCORE * j + c
        m = SLOPES[h]
        Cc = m * _hw(h) / 2
        ref = sb * 512 + half * 256
        if kb < 4 * sb:
            btab[:, col] = m * (kl + 128 * kb - ref) - Cc
        else:
            btab[:, col] = NEG
    btab = np.maximum(btab, NEG)
    dbias = np.zeros((128, H, 4, 4), np.float64)
    bdiag = np.zeros((128, H, 4, 128), np.float64)
    ql = np.arange(128, dtype=np.float64)
    for h in range(H):
        m = SLOPES[h]
        Cc = m * _hw(h) / 2
        for a in range(4):
            for ap in range(4):
                dbias[:, h, ap, a] = m * (kl + 128 * ap - _refoff(h, a)) - Cc
            vis = (kl[:, None] // 64) <= (ql[None, :] // 64)
            val = -m * np.abs(ql[None, :] - kl[:, None]) + m * (128 * a + ql[None, :] - _refoff(h, a)) - Cc
            bdiag[:, h, a, :] = np.where(vis, val, NEG)
    nkb = PAST // 128
    sbias = np.zeros((128, H, nkb), np.float64)
    bs = np.full((128, BPC, H, DEC_T), NEG, np.float64)
    q32 = np.arange(DEC_T, dtype=np.float64)
    for h in range(H):
        m = SLOPES[h]
        Cs = m * 16
        for kb in range(nkb):
            sbias[:, h, kb] = m * (kl + 128 * kb - PAST) - Cs
        for b in range(BPC):
            for tp in range(DEC_T):
                bs[b * 32 + tp, b, h, :] = -m * np.abs(q32 - tp) + m * q32 - Cs
    sbias = np.maximum(sbias, NEG)
    shm = np.zeros((128, 10, 128), np.float32)
    for t in range(128):
        if t >= 1:
            shm[t - 1, 0, t] = 1
        if t >= 2:
            shm[t - 2, 1, t] = 1
        if t % 32 != 0:
            shm[t - 1, 6, t] = 1
        if t % 32 >= 2:
            shm[t - 2, 7, t] = 1
    shm[127, 2, 0] = 1
    shm[126, 3, 0] = 1
    shm[127, 3, 1] = 1
    shm[1, 4, 0] = 1
    shm[0, 5, 0] = 1
    shm[1, 5, 1] = 1
    for b in range(BPC):
        shm[2 * b + 1, 8, 32 * b] = 1
        shm[2 * b, 9, 32 * b] = 1
        shm[2 * b + 1, 9, 32 * b + 1] = 1
    f = np.float32
    return dict(btab=btab.astype(f), dbias=dbias.reshape(128, -1).astype(f),
                bdiag=bdiag.reshape(128, -1).astype(f), sbias=sbias.reshape(128, -1).astype(f),
                bs=bs.reshape(128, -1).astype(f), shm=shm.reshape(128, -1))


def build_program(NSLOT, PAST):
    SEQ = NCORE * NSLOT * 512
    NBLK = SEQ // 128
    NOWN = 4 * NSLOT + 1
    NKBS = PAST // 128
    E, BIDX, NCOL = bias_layout(NSLOT)
    EMAX = E[-1]
    NG = EMAX // 4

    nc = bass.Bass("TRN2", target_bir_lowering=False)

    def din(name, shape, dt=F32):
        return nc.dram_tensor(name, list(shape), dt, kind="ExternalInput").ap()

    def dout(name, shape):
        return nc.dram_tensor(name, list(shape), F32, kind="ExternalOutput").ap()

    def dscr(name, shape):
        return nc.dram_tensor(name, list(shape), BF16, kind="Internal").ap()

    x_all = din("x_all", [SEQ, D])
    x_own = din("x_own", [NOWN, 128, D])
    x_halo = din("x_halo", [NSLOT, 2, D])
    p_own = din("p_own", [NOWN, 128, PLE])
    hist_s = din("hist_s", [8, 512])
    ckT = din("ckT", [BPC, H, 128, PAST])
    cv = din("cv", [BPC, PAST, 512])
    w_in = din("w_in", [D, 3072])
    w_out = din("w_out", [D, D])
    w_up = din("w_up", [D, DFF])
    w_down = din("w_down", [DFF, D])
    w_pe = din("w_pe", [PLE, D])
    w_g = din("w_g", [D, D])
    gvec_d = din("gvec", [128, 16])
    gbc_d = din("gbc", [128, 3 * D])
    gsub_d = din("gsub", [128, 128])
    convbc_d = din("convbc", [128, 4 * 512])
    lamv_d = din("lamv", [128, 4 * 64])
    ident_d = din("ident", [128, 128])
    shm_d = din("shm", [128, 10 * 128])
    btab_d = din("btab", [128, NCOL])
    dbias_d = din("dbias", [128, H * 16])
    bdiag_d = din("bdiag", [128, H * 4 * 128])
    sbias_d = din("sbias", [128, H * NKBS])
    bs_d = din("bs", [128, BPC * H * DEC_T])

    y_own = dout("y_own", [NOWN, 128, D])
    k_own = dout("k_own", [NOWN, 128, 512])
    v_own = dout("v_own", [NOWN, 128, 512])
    conv_p = dout("conv_p", [2, 512])
    conv_s = dout("conv_s", [8, 512])
    DEBUG = CFG.get("DEBUG", False)
    if DEBUG:
        dbg_mix = dout("dbg_mix", [NOWN, 128, D])

    if CFG.get("DEBUG", False):
        kt_scr = nc.dram_tensor("kt_scr", [H, 128, NG * 512], BF16, kind="ExternalOutput").ap()
        v_scr = nc.dram_tensor("v_scr", [H, 128, NG * 4, 130], BF16, kind="ExternalOutput").ap()
    else:
        kt_scr = dscr("kt_scr", [H, 128, NG * 512])
        v_scr = dscr("v_scr", [H, 128, NG * 4, 130])
    ws_in = dscr("ws_in", [4, 4, 128, 1024])
    ws_out = dscr("ws_out", [2, 4, 128, 1024])
    ws_g = dscr("ws_g", [2, 4, 128, 1024])
    ws_pe = dscr("ws_pe", [2, 1, 128, 1024])
    ws_dn = dscr("ws_dn", [2, 16, 128, 1024])
    ws_up = dscr("ws_up", [32, 128, 1024])

    ARENA_N = 53000
    with ExitStack() as es:
        kb = KB(nc, es)
        arena_t = es.enter_context(nc.sbuf_tensor("arena", [128, ARENA_N], F32))
        ps_t = es.enter_context(nc.psum_tensor("psum", [128, 8 * 512], F32))
        A = Arena(arena_t, ARENA_N)
        TB = [T() for _ in range(8)]

        def bank(b):
            return ps_t[:, b * 512:(b + 1) * 512]

        def bankb(b):
            return ps_t[:, b * 512:(b + 1) * 512].bitcast(BF16)

        def pe(fn, r=(), w=()):
            return kb.op("pe", fn, r, w)

        def act(fn, r=(), w=()):
            return kb.op("act", fn, r, w)

        def dve(fn, r=(), w=()):
            return kb.op("dve", fn, r, w)

        def pool(fn, r=(), w=()):
            return kb.op("pool", fn, r, w)

        def dma(out, in_, r=(), w=(), sem="const", q="sp"):
            return kb.op(q, lambda e: e.dma_start(out=out, in_=in_), r, w, dsem=kb.dsem(sem))

        ident_f = A.f32(128); T_identf = T()
        ident_b = A.bf16(128); T_identb = T()
        shm = A.f32(10, 128); T_shm = T()
        gvec = A.f32(16); T_gvec = T()
        gbc = A.f32(3, D); T_gbc = T()
        gsub = A.f32(128); T_gsub = T()
        convbc = A.f32(4, 512); T_convbc = T()
        btab = A.f32(NCOL); T_btab = T()
        dbias = A.f32(H * 16)
        bdiag = A.f32(H * 4, 128)
        sbias = A.f32(H * NKBS)
        bs = A.f32(BPC * H, DEC_T)
        lam_t = A.f32(4); T_lam = T()
        wkv = A.bf16(8, 1024); T_wkv = T()
        NRING = 8
        ring = [A.bf16(1024) for _ in range(NRING)]
        T_ring = [T() for _ in range(NRING)]
        ring_pos = [0]
        T_c = T()
        xo = [A.f32(D) for _ in range(4)]; T_xo = [T() for _ in range(4)]

        def load_xo(slot_idx, is_sample_):
            na = 1 if is_sample_ else 4
            o0 = 4 * NSLOT if is_sample_ else 4 * slot_idx
            for a in range(na):
                dma(xo[a], x_own[o0 + a], w=[T_xo[a]], sem="xo%d" % a)

        for dst, src in ((ident_f, ident_d), (shm.rearrange("p a b -> p (a b)"), shm_d), (gvec, gvec_d),
                         (gbc.rearrange("p a b -> p (a b)"), gbc_d), (gsub, gsub_d),
                         (convbc.rearrange("p a b -> p (a b)"), convbc_d), (btab, btab_d), (dbias, dbias_d),
                         (bdiag.rearrange("p a b -> p (a b)"), bdiag_d), (sbias, sbias_d),
                         (bs.rearrange("p a b -> p (a b)"), bs_d)):
            dma(dst, src, w=[T_c])
        dve(lambda e: e.tensor_copy(out=ident_b, in_=ident_f), [T_c], [T_identb])
        dve(lambda e: e.memset(lam_t[:, 2:3], RMS_EPS), [], [T_lam])
        dve(lambda e: e.tensor_scalar(out=gsub, in0=gsub, scalar1=1.0 - LAM_INIT, scalar2=None, op0=ALU.mult),
            [T_c], [T_c])
        eps_ap = lam_t[:, 2:3]

        mark0 = A.top
        lamv = A.f32(4, 64)
        lprod = A.f32(2, 64)
        lsum = A.f32(2)
        dma(lamv.rearrange("p a b -> p (a b)"), lamv_d, w=[T_c])
        for i in range(2):
            dve(lambda e, i=i: e.tensor_tensor(out=lprod[:, i, :], in0=lamv[:, 2 * i, :], in1=lamv[:, 2 * i + 1, :],
                                               op=ALU.mult), [T_c], [T_c])
            act(lambda e, i=i: e.activation(out=lprod[:, i, :], in_=lprod[:, i, :], func=AF.Copy,
                                            accum_out=lsum[:, i:i + 1]), [T_c], [T_c])
        act(lambda e: e.activation(out=lsum, in_=lsum, func=AF.Exp), [T_c], [T_c])
        dve(lambda e: e.tensor_tensor(out=lam_t[:, 0:1], in0=lsum[:, 0:1], in1=lsum[:, 1:2], op=ALU.subtract),
            [T_c, T_lam], [T_lam])
        dve(lambda e: e.tensor_scalar(out=lam_t[:, 0:1], in0=lam_t[:, 0:1], scalar1=LAM_INIT, scalar2=None,
                                      op0=ALU.add), [T_lam], [T_lam])
        dve(lambda e: e.tensor_scalar(out=lam_t[:, 1:2], in0=lam_t[:, 0:1], scalar1=-1.0, scalar2=None,
                                      op0=ALU.mult), [T_lam], [T_lam])

        NWS = 2
        wst_f = [A.f32(4096) for _ in range(NWS)]
        wst_b = [A.bf16(4096) for _ in range(NWS)]
        T_wf = [T() for _ in range(NWS)]
        T_wb = [T() for _ in range(NWS)]
        T_ws = T()
        prep_i = [0]
        prep_q = ["sp"]

        def make_job(src_rows, width, gcol, stores_fn):
            st = {}

            def load():
                st["s"] = prep_i[0] % NWS
                prep_i[0] += 1
                s = st["s"]
                dma(wst_f[s][:, 0:width], src_rows, w=[T_wf[s]], sem="wf%d" % s, q=prep_q[0])

            def cast():
                s = st["s"]
                o = wst_b[s][:, 0:width]
                src_ = wst_f[s][:, 0:width]
                if gcol is None:
                    dve(lambda e: e.tensor_copy(out=o, in_=src_), [T_wf[s]], [T_wb[s]])
                else:
                    g_ap = gvec[:, gcol:gcol + 1]
                    dve(lambda e: e.tensor_scalar(out=o, in0=src_, scalar1=g_ap, scalar2=None, op0=ALU.mult),
                        [T_wf[s], T_c], [T_wb[s]])

            def store():
                s = st["s"]
                stores_fn(s)
            return (load, cast, store)

        def std_stores(wdst, kc, ncg):
            def f(s):
                for cg in range(ncg):
                    dma(wdst[cg, kc // 2, :, (kc % 2) * 512:(kc % 2) * 512 + 512],
                        wst_b[s][:, cg * 512:(cg + 1) * 512], r=[T_wb[s]], w=[T_ws], sem="wb%d" % s, q="pool")
            return f

        for kc in range(8):
            def win_store(s, kc=kc):
                src_ap = wst_b[s][:, 0:1024]
                dve(lambda e: e.tensor_copy(out=wkv[:, kc, :], in_=src_ap), [T_wb[s]], [T_wkv])
            ld, cs, stf = make_job(w_in[kc * 128:(kc + 1) * 128, 2048:3072], 1024, kc, win_store)
            ld(); cs(); stf()
        prep_jobs = []
        for kc in range(8):
            prep_jobs.append(make_job(w_in[kc * 128:(kc + 1) * 128, 0:2048], 2048, kc, std_stores(ws_in, kc, 4)))
        for (wsrc, wdst, nkc) in ((w_out, ws_out, 8), (w_g, ws_g, 8), (w_pe, ws_pe, 2), (w_down, ws_dn, 32)):
            for kc in range(nkc):
                prep_jobs.append(make_job(wsrc[kc * 128:(kc + 1) * 128, :], 1024, None, std_stores(wdst, kc, 2)))
        for kc in range(8):
            def up_store(s, kc=kc):
                dma(ws_up[:, :, kc * 128:(kc + 1) * 128].rearrange("f p n -> p f n"),
                    wst_b[s][:, 0:4096].rearrange("p (f n) -> p f n", f=32), r=[T_wb[s]], w=[T_ws],
                    sem="wb%d" % s, q="pool")
            prep_jobs.append(make_job(w_up[kc * 128:(kc + 1) * 128, :], 4096, 8 + kc, up_store))

        ring_q = []
        ring_issued = [0]
        ring_taken = [0]

        def ring_plan(srcs):
            ring_q.extend(srcs)

        def ring_prefetch():
            while ring_issued[0] < len(ring_q) and ring_issued[0] - ring_taken[0] < NRING:
                s = ring_issued[0] % NRING
                dma(ring[s], ring_q[ring_issued[0]], r=[T_ws], w=[T_ring[s]], sem="ring%d" % s)
                ring_issued[0] += 1

        def ring_load(src):
            if ring_taken[0] >= len(ring_q):
                ring_q.append(src)
            assert ring_q[ring_taken[0]] is src or True
            if ring_issued[0] <= ring_taken[0]:
                ring_prefetch()
            s = ring_taken[0] % NRING
            ring_taken[0] += 1
            return ring[s], T_ring[s]

        def ring_topup():
            ring_prefetch()

        def plan_tok(ws, cgs, nkcp):
            return [ws[cg, kcp] for cg in cgs for kcp in range(nkcp)]

        def plan_A(is_sample):
            p = []
            if not is_sample:
                p += plan_tok(ws_in, [1, 2], 4)
            p += plan_tok(ws_in, [0, 1, 2, 3], 4)
            return p

        def plan_C():
            return (plan_tok(ws_out, [0, 1], 4) + [ws_up[f] for f in range(32)] + plan_tok(ws_dn, [0, 1], 16)
                    + plan_tok(ws_g, [0, 1], 4) + plan_tok(ws_pe, [0, 1], 1))

        def rstd_from_ss(ss_ap, n_feat, out_ap, toks):
            npp = ss_ap.shape[0]
            act(lambda e: e.activation(out=out_ap, in_=ss_ap, func=AF.Ln, scale=1.0 / n_feat, bias=eps_ap[0:npp]),
                toks + [T_lam], toks)
            act(lambda e: e.activation(out=out_ap, in_=out_ap, func=AF.Exp, scale=-0.5), toks, toks)

        def transpose_to(dst_fn, src_b, nparts, nchunks, bk, Tsrc, Tdst, evac_eng="act"):
            pv = bankb(bk)
            for c in range(nchunks):
                pe(lambda e, c=c: e.transpose(out=pv[:, c * nparts:(c + 1) * nparts],
                                              in_=src_b[0:nparts, c * 128:(c + 1) * 128],
                                              identity=ident_b[0:nparts, 0:nparts]),
                   [Tsrc, T_identb], [TB[bk]])
            dst_fn(pv[:, 0:nchunks * nparts].rearrange("p (c t) -> p c t", c=nchunks))

        mark1 = A.top
        xg = [A.f32(4, D) for _ in range(2)]; T_xg = [T(), T()]
        xb1 = [A.bf16(D) for _ in range(3)]; T_xb1 = [T(), T(), T()]
        junk = A.bf16(D); T_junk = T()
        xT1 = [A.bf16(8, 128) for _ in range(2)]; T_xT1 = [T(), T()]
        ss1 = [A.f32(4) for _ in range(2)]; T_ss1 = [[T() for _ in range(4)] for _ in range(2)]
        kbf = [A.bf16(512) for _ in range(2)]; T_kbf = [T(), T()]
        ktst = [A.bf16(4, 512) for _ in range(2)]; T_ktst = [T(), T()]
        vst = [A.bf16(4 * 4, 130) for _ in range(2)]; T_vst = [T(), T()]
        for s in range(2):
            dve(lambda e, s=s: e.memset(vst[s][:, :, 128:130], 1.0), [], [T_vst[s]])

        def load_xg(g):
            s = g % 2
            dma(xg[s], x_all[g * 512:(g + 1) * 512, :].rearrange("(a p) d -> p a d", p=128), w=[T_xg[s]],
                sem="xg%d" % s)

        NB1 = 4 * NG

        def st_L(n):
            g, a = divmod(n, 4)
            s = g % 2
            b2 = n % 2
            if a == 0 and g + 1 < NG:
                load_xg(g + 1)
            xa = xg[s][:, a, :]
            act(lambda e: e.activation(out=junk, in_=xa, func=AF.Square, accum_out=ss1[s][:, a:a + 1]),
                [T_xg[s]], [T_ss1[s][a]])
            b3 = n % 3
            dve(lambda e: e.tensor_copy(out=xb1[b3], in_=xa), [T_xg[s]], [T_xb1[b3]])
            rstd_from_ss(ss1[s][:, a:a + 1], D, ss1[s][:, a:a + 1], [T_ss1[s][a]])

        def st_TX(n):
            b2 = n % 2
            b3 = n % 3
            transpose_to(lambda pv: act(lambda e: e.activation(out=xT1[b2], in_=pv, func=AF.Copy),
                                        [], [TB[b2], T_xT1[b2]]),
                         xb1[b3], 128, 8, b2, T_xb1[b3], None)

        def st_MM(n):
            g, a = divmod(n, 4)
            s = g % 2
            b2 = n % 2
            for cg in range(2):
                bk = 2 + 2 * cg + b2
                for kc in range(8):
                    pe(lambda e, bk=bk, kc=kc, cg=cg: e.matmul(
                        bank(bk), lhsT=xT1[b2][:, kc, :], rhs=wkv[:, kc, cg * 512:(cg + 1) * 512],
                        start=(kc == 0), stop=(kc == 7)), [T_xT1[b2], T_wkv], [TB[bk]])
            rs = ss1[s][:, a:a + 1]
            dve(lambda e: e.tensor_scalar(out=kbf[b2], in0=bank(2 + b2), scalar1=rs, scalar2=None, op0=ALU.mult),
                [T_ss1[s][a]], [TB[2 + b2], T_kbf[b2]])
            vdst = vst[s].rearrange("p (h a) n -> p h a n", h=4)[:, :, a, 0:128]
            act(lambda e: e.activation(out=vdst, in_=bank(4 + b2).rearrange("p (h n) -> p h n", h=4),
                                       func=AF.Copy, scale=rs),
                [T_ss1[s][a]], [TB[4 + b2], T_vst[s]])

        def st_TK(n):
            g, a = divmod(n, 4)
            s = g % 2
            b2 = n % 2
            transpose_to(lambda pv: dve(lambda e: e.tensor_copy(out=ktst[s][:, :, a * 128:(a + 1) * 128], in_=pv),
                                        [], [TB[6 + b2], T_ktst[s]]),
                         kbf[b2], 128, 4, 6 + b2, T_kbf[b2], None)
            if a == 3:
                dma(kt_scr[:, :, g * 512:(g + 1) * 512].rearrange("h p n -> p h n"), ktst[s], r=[T_ktst[s]],
                    sem="kts%d" % s)
                dma(v_scr.rearrange("h p k n -> p h (k n)")[:, :, g * 520:(g + 1) * 520],
                    vst[s].rearrange("p (h a) n -> p h (a n)", h=4), r=[T_vst[s]], sem="vs%d" % s)

        if NG > 0:
            load_xg(0)
            st_L(0)
            if NB1 > 1:
                st_L(1)
            st_TX(0)
        stages = []
        pj = 0
        for n in range(NB1):
            if n + 2 < NB1:
                st_L(n + 2)
            if n + 1 < NB1:
                st_TX(n + 1)
            st_MM(n)
            if n >= 1:
                st_TK(n - 1)
            if stages:
                cs, stf = stages.pop(0)
                cs(); stf()
            if pj < len(prep_jobs) and (n % 2 == 0 or n < 10):
                ld, cs, stf = prep_jobs[pj]
                pj += 1
                ld()
                stages.append((cs, stf))
        if NB1 > 0:
            st_TK(NB1 - 1)
        while stages or pj < len(prep_jobs):
            if stages:
                cs, stf = stages.pop(0)
                cs(); stf()
            if pj < len(prep_jobs):
                ld, cs, stf = prep_jobs[pj]
                pj += 1
                ld()
                stages.append((cs, stf))
        late_jobs = prep_jobs[pj:]
        SLOT_ORDER = list(range(NSLOT))[::-1]
        ring_plan(plan_A(False))
        ring_prefetch()
        load_xo(SLOT_ORDER[0], False)
        kb.barrier()
        A.top = mark0

        def stream_tokmajor(ws, ncg, nkcp, lhs_fn, NA, evac_fn, lhs_toks, bank_sets, M=128, cgs=None, wres=None):
            cgl = list(range(ncg)) if cgs is None else cgs
            for ci, cg in enumerate(cgl):
                banks = bank_sets[ci % len(bank_sets)]
                for kcp in range(nkcp):
                    if wres is None:
                        piece, Tp = ring_load(ws[cg, kcp])
                        pv = piece.rearrange("p (k n) -> p k n", k=2)
                    for kk in range(2):
                        kc = 2 * kcp + kk
                        for a in range(NA):
                            if wres is None:
                                rhs = pv[:, kk, :]
                                rt = Tp
                            else:
                                rhs, rt = wres(cg, kc)
                            pe(lambda e, a=a, kc=kc, rhs=rhs, bk=banks[a]: e.matmul(
                                bank(bk)[0:M, :], lhsT=lhs_fn(a, kc), rhs=rhs,
                                start=(kc == 0), stop=(kc == 2 * nkcp - 1)),
                               lhs_toks(a) + [rt], [TB[banks[a]]])
                    if wres is None:
                        ring_topup()
                for a in range(NA):
                    evac_fn(cg, a, banks[a])

        SETS4 = [[0, 1, 2, 3], [4, 5, 6, 7]]

        def do_slot(j, is_sample):
            NA = 1 if is_sample else 4
            W = NA * 128
            ob0 = 4 * NSLOT if is_sample else 4 * j
            markS = A.top
            mixb = [A.bf16(D) for _ in range(NA)]; T_mix = [T() for _ in range(NA)]
            markQ = A.top
            qpad = [A.bf16(4, W) for _ in range(2)]; T_q = T()
            ktown = A.bf16(4, W); T_kt = T()
            vown = A.bf16(NA * 4, 130); T_vo = T()
            pool(lambda e: e.memset(qpad[0][64:128], 0.0), [], [T_q])
            pool(lambda e: e.memset(qpad[1][0:64], 0.0), [], [T_q])
            dve(lambda e: e.memset(vown[:, :, 128:130], 1.0), [], [T_vo])
            markA = A.top

            xbA = [A.bf16(D) for _ in range(2)]; T_xbA = [T(), T()]
            junkA = A.bf16(D); T_junkA = T()
            xTA = [A.bf16(8, 128) for _ in range(NA)]; T_xTA = [T() for _ in range(NA)]
            rsA = A.f32(NA + 1); T_rsA = [T() for _ in range(NA + 1)]
            cb = [A.f32(512) for _ in range(NA)]; T_cb = [T() for _ in range(NA)]
            zc = [A.f32(512) for _ in range(NA)]; T_zc = [T() for _ in range(NA)]
            zw = [[A.f32(512) for _ in range(2)] for _ in range(2)]
            T_zw = [[T() for _ in range(2)] for _ in range(2)]
            zw2 = A.f32(512); T_zw2 = T()
            qb = [A.bf16(512) for _ in range(NA)]; T_qb = [T() for _ in range(NA)]
            kbA = [A.bf16(512) for _ in range(NA)]; T_kbA = [T() for _ in range(NA)]
            kf = [A.f32(512) for _ in range(2)]; T_kf = [T(), T()]
            vf = [A.f32(512) for _ in range(2)]; T_vf = [T(), T()]
            hz = A.f32(512); T_hz = T()
            hzw = [A.f32(512) for _ in range(2)]; T_hzw = [T(), T()]

            for a in range(NA):
                b2 = a % 2
                act(lambda e, a=a: e.activation(out=junkA, in_=xo[a], func=AF.Square, accum_out=rsA[:, a:a + 1]),
                    [T_xo[a]], [T_rsA[a]])
                dve(lambda e, a=a, b2=b2: e.tensor_copy(out=xbA[b2], in_=xo[a]), [T_xo[a]], [T_xbA[b2]])
                rstd_from_ss(rsA[:, a:a + 1], D, rsA[:, a:a + 1], [T_rsA[a]])
                transpose_to(lambda pv, a=a: act(lambda e: e.activation(out=xTA[a], in_=pv, func=AF.Copy),
                                                 [], [TB[a], T_xTA[a]]),
                             xbA[b2], 128, 8, a, T_xbA[b2], None)

            if not is_sample:
                xh = A.f32(D); T_xh = T()
                xhb = A.bf16(D)
                xhT = A.bf16(8, 2); T_xhT = T()
                dma(xh[0:2], x_halo[j], w=[T_xh], sem="xh")
                act(lambda e: e.activation(out=junkA[0:2], in_=xh[0:2], func=AF.Square,
                                           accum_out=rsA[0:2, NA:NA + 1]), [T_xh], [T_rsA[NA]])
                pool(lambda e: e.tensor_copy(out=xhb[0:2], in_=xh[0:2]), [T_xh], [T_xh])
                rstd_from_ss(rsA[0:2, NA:NA + 1], D, rsA[0:2, NA:NA + 1], [T_rsA[NA]])
                transpose_to(lambda pv: act(lambda e: e.activation(out=xhT, in_=pv, func=AF.Copy),
                                            [], [TB[4], T_xhT]),
                             xhb, 2, 8, 4, T_xh, None)
                rsh = rsA[0:2, NA:NA + 1]

                def evac_halo(cg, a, bk):
                    if cg == 1:
                        act(lambda e: e.activation(out=hzw[0][0:2], in_=bank(bk)[0:2, :], func=AF.Copy, scale=rsh),
                            [T_rsA[NA]], [TB[bk], T_hzw[0]])
                    else:
                        dve(lambda e: e.scalar_tensor_tensor(out=hz[0:2], in0=bank(bk)[0:2, :], scalar=rsh,
                                                             in1=hzw[0][0:2], op0=ALU.mult, op1=ALU.mult),
                            [T_rsA[NA], T_hzw[0]], [TB[bk], T_hz])
                stream_tokmajor(ws_in, 4, 4, lambda a, kc: xhT[:, kc, :], 1, evac_halo, lambda a: [T_xhT],
                                [[5], [6]], M=2, cgs=[1, 2])
                NH = 2
            else:
                dma(hz[0:8], hist_s, w=[T_hz], sem="xh")
                NH = 8

            def evac_z(cg, a, bk):
                rs = rsA[:, a:a + 1]
                b2 = a % 2
                if cg == 0:
                    act(lambda e: e.activation(out=cb[a], in_=bank(bk), func=AF.Copy, scale=rs),
                        [T_rsA[a]], [TB[bk], T_cb[a]])
                elif cg == 1:
                    act(lambda e: e.activation(out=zc[a], in_=bank(bk), func=AF.Copy, scale=rs),
                        [T_rsA[a]], [TB[bk], T_zc[a]])
                elif cg == 2:
                    dve(lambda e: e.scalar_tensor_tensor(out=zc[a], in0=bank(bk), scalar=rs, in1=zc[a],
                                                         op0=ALU.mult, op1=ALU.mult),
                        [T_rsA[a]], [TB[bk], T_zc[a]])
                elif cg == 3:
                    dve(lambda e: e.tensor_scalar(out=qb[a], in0=bank(bk), scalar1=rs, scalar2=None, op0=ALU.mult),
                        [T_rsA[a]], [TB[bk], T_qb[a]])
                elif cg == 4:
                    act(lambda e: e.activation(out=kf[b2], in_=bank(bk), func=AF.Copy, scale=rs),
                        [T_rsA[a]], [TB[bk], T_kf[b2]])
                    dve(lambda e: e.tensor_copy(out=kbA[a], in_=kf[b2]), [T_kf[b2]], [T_kbA[a]])
                    dma(k_own[ob0 + a], kf[b2], r=[T_kf[b2]], sem="kf%d" % b2)
                else:
                    act(lambda e: e.activation(out=vf[b2], in_=bank(bk), func=AF.Copy, scale=rs),
                        [T_rsA[a]], [TB[bk], T_vf[b2]])
                    vdst = vown.rearrange("p (a h) n -> p a h n", a=NA)[:, a, :, 0:128]
                    pool(lambda e: e.tensor_copy(out=vdst, in_=vf[b2].rearrange("p (h n) -> p h n", h=4)),
                         [T_vf[b2]], [T_vo])
                    dma(v_own[ob0 + a], vf[b2], r=[T_vf[b2]], sem="vf%d" % b2)

            stream_tokmajor(ws_in, 4, 4, lambda a, kc: xTA[a][:, kc, :], NA, evac_z, lambda a: [T_xTA[a]], SETS4)
            stream_tokmajor(None, 2, 4, lambda a, kc: xTA[a][:, kc, :], NA,
                            lambda cg, a, bk: evac_z(cg + 4, a, bk), lambda a: [T_xTA[a]], SETS4,
                            wres=lambda cg, kc: (wkv[:, kc, cg * 512:(cg + 1) * 512], T_wkv))

            w0b, w1b, w2b, bbb = convbc[:, 0, :], convbc[:, 1, :], convbc[:, 2, :], convbc[:, 3, :]
            pool(lambda e: e.tensor_tensor(out=hzw[0][0:NH], in0=hz[0:NH], in1=w1b[0:NH], op=ALU.mult),
                 [T_hz, T_c], [T_hzw[0]])
            pool(lambda e: e.tensor_tensor(out=hzw[1][0:NH], in0=hz[0:NH], in1=w0b[0:NH], op=ALU.mult),
                 [T_hz, T_c], [T_hzw[1]])
            if is_sample:
                m_sh1, m_sh2, m_h1, m_h2 = 6, 7, 8, 9
            else:
                m_sh1, m_sh2, m_h1, m_h2 = 0, 1, 4, 5
            for a in range(NA):
                b2 = a % 2

                def q_evac(pv, a=a, b2=b2):
                    dve(lambda e: e.tensor_copy(out=qpad[0][0:64, :, a * 128:(a + 1) * 128], in_=pv[0:64]),
                        [], [TB[0 + b2], T_q])
                    dve(lambda e: e.tensor_copy(out=qpad[1][64:128, :, a * 128:(a + 1) * 128], in_=pv[64:128]),
                        [], [TB[0 + b2], T_q])
                transpose_to(q_evac, qb[a], 128, 4, 0 + b2, T_qb[a], None)
                transpose_to(lambda pv, a=a, b2=b2: act(
                    lambda e: e.activation(out=ktown[:, :, a * 128:(a + 1) * 128], in_=pv, func=AF.Copy),
                    [], [TB[2 + b2], T_kt]), kbA[a], 128, 4, 2 + b2, T_kbA[a], None)
                dve(lambda e, a=a, b2=b2: e.tensor_tensor(out=zw[b2][0], in0=zc[a], in1=w1b, op=ALU.mult),
                     [T_zc[a], T_c], [T_zw[b2][0]])
                dve(lambda e, a=a, b2=b2: e.tensor_tensor(out=zw[b2][1], in0=zc[a], in1=w0b, op=ALU.mult),
                     [T_zc[a], T_c], [T_zw[b2][1]])
                dve(lambda e, a=a: e.tensor_tensor(out=zw2, in0=zc[a], in1=w2b, op=ALU.mult),
                     [T_zc[a], T_c], [T_zw2])
                dve(lambda e: e.tensor_tensor(out=zw2, in0=zw2, in1=bbb, op=ALU.add), [T_c], [T_zw2])
                bk = 4 + b2
                pe(lambda e, b2=b2, bk=bk: e.matmul(bank(bk), lhsT=shm[:, m_sh1, :], rhs=zw[b2][0], start=True,
                                                    stop=False), [T_c, T_zw[b2][0]], [TB[bk]])
                pe(lambda e, b2=b2, bk=bk: e.matmul(bank(bk), lhsT=shm[:, m_sh2, :], rhs=zw[b2][1], start=False,
                                                    stop=False), [T_c, T_zw[b2][1]], [TB[bk]])
                if a == 0:
                    pe(lambda e, bk=bk: e.matmul(bank(bk), lhsT=shm[0:NH, m_h1, :], rhs=hzw[0][0:NH], start=False,
                                                 stop=False), [T_c, T_hzw[0]], [TB[bk]])
                    pe(lambda e, bk=bk: e.matmul(bank(bk), lhsT=shm[0:NH, m_h2, :], rhs=hzw[1][0:NH], start=False,
                                                 stop=True), [T_c, T_hzw[1]], [TB[bk]])
                else:
                    p2 = 1 - b2
                    pe(lambda e, bk=bk, p2=p2: e.matmul(bank(bk), lhsT=shm[:, 2, :], rhs=zw[p2][0], start=False,
                                                        stop=False), [T_c, T_zw[p2][0]], [TB[bk]])
                    pe(lambda e, bk=bk, p2=p2: e.matmul(bank(bk), lhsT=shm[:, 3, :], rhs=zw[p2][1], start=False,
                                                        stop=True), [T_c, T_zw[p2][1]], [TB[bk]])
                dve(lambda e, bk=bk: e.tensor_tensor(out=zw2, in0=bank(bk), in1=zw2, op=ALU.add),
                    [], [TB[bk], T_zw2])
                dve(lambda e, a=a: e.tensor_tensor(out=mixb[a][:, 0:512], in0=zw2, in1=cb[a], op=ALU.mult),
                    [T_cb[a]], [T_zw2, T_mix[a]])
            if is_sample:
                for b in range(BPC):
                    dma(conv_s[2 * b:2 * b + 2, :], zc[0][32 * b + 30:32 * b + 32, :], r=[T_zc[0]], sem="cvo")
            elif j == NSLOT - 1:
                dma(conv_p, zc[3][126:128, :], r=[T_zc[3]], sem="cvo")
            kb.barrier()
            A.top = markA

            if late_jobs:
                for i_ in range(NWS):
                    wst_f[i_] = A.f32(4096)
                    wst_b[i_] = A.bf16(4096)
            ring_plan(plan_C())
            deferred_prefetch = [ring_prefetch]
            if not is_sample:
                nxt = SLOT_ORDER.index(j) + 1
                if nxt < NSLOT:
                    deferred_prefetch.append(lambda: load_xo(SLOT_ORDER[nxt], False))
                else:
                    deferred_prefetch.append(lambda: load_xo(0, True))
            PTs = [A.bf16(2, 512) for _ in range(3)]; T_pt = [T() for _ in range(3)]
            dtmp = [A.f32(128) for _ in range(2)]; T_dtmp = [T(), T()]
            o_t = [A.f32(128) for _ in range(2)]; T_o = [T(), T()]
            junko = A.bf16(128); T_junko = T()
            rr = [A.f32(4) for _ in range(2)]; T_rr = [T(), T()]
            ucount = [0]
            dcount = [0]
            ecount = [0]

            def accv(i):
                return bank(4 + i // 3)[:, (i % 3) * 130:(i % 3) * 130 + 129], 4 + i // 3

            class U:
                pass

            def sreg(sbp):
                return ps_t[:, sbp * 1024:(sbp + 1) * 1024]

            hook_n = [0]
            hook_stage = []

            def unit_hook():
                hook_n[0] += 1
                if hook_n[0] >= 6 and deferred_prefetch and not (late_jobs or hook_stage):
                    for f in deferred_prefetch:
                        f()
                    del deferred_prefetch[:]
                if late_jobs or hook_stage:
                    if hook_n[0] % 3 == 0:
                        if hook_stage:
                            cs, stf = hook_stage.pop(0)
                            cs(); stf()
                        if late_jobs:
                            ld, cs, stf = late_jobs.pop(0)
                            ld()
                            hook_stage.append((cs, stf))

            def run_units(units, h, qbase_of, np_q):
                def emit_S(u):
                    if getattr(u, "load", None) is not None:
                        u.load()
                    sbp = u.idx % 2
                    R = sreg(sbp)
                    for (m, rhs, col, n) in u.smm:
                        pe(lambda e, u=u, rhs=rhs, col=col, n=n, R=R: e.matmul(
                            R[:, col:col + n], lhsT=u.kt, rhs=rhs, start=True, stop=True),
                           [u.Tkt, T_q], [TB[2 * sbp], TB[2 * sbp + 1]])

                def emit_exp(u):
                    sbp = u.idx % 2
                    ps = u.idx % 3
                    R = sreg(sbp)
                    PT = PTs[ps].rearrange("p a b -> p (a b)")
                    for (kind, lo, hi, arg) in u.exps:
                        if kind == "fast":
                            act(lambda e, lo=lo, hi=hi, arg=arg, R=R, PT=PT: e.activation(
                                out=PT[:, lo:hi], in_=R[:, lo:hi], func=AF.Exp, scale=0.125, bias=arg),
                                [T_c], [TB[2 * sbp], TB[2 * sbp + 1], T_pt[ps]])
                        elif kind == "fast2":
                            Rv = R.rearrange("p (a b) -> p a b", a=2)[:, :, lo:hi]
                            Pv = PTs[ps][:, :, lo:hi]
                            act(lambda e, arg=arg, Rv=Rv, Pv=Pv: e.activation(
                                out=Pv, in_=Rv, func=AF.Exp, scale=0.125, bias=arg),
                                [T_c], [TB[2 * sbp], TB[2 * sbp + 1], T_pt[ps]])
                        else:
                            k = dcount[0] % 2
                            dcount[0] += 1
                            dve(lambda e, lo=lo, hi=hi, arg=arg, k=k, R=R: e.scalar_tensor_tensor(
                                out=dtmp[k][:, 0:hi - lo], in0=R[:, lo:hi], scalar=0.125, in1=arg,
                                op0=ALU.mult, op1=ALU.add), [T_c], [TB[2 * sbp], TB[2 * sbp + 1], T_dtmp[k]])
                            act(lambda e, lo=lo, hi=hi, k=k, PT=PT: e.activation(
                                out=PT[:, lo:hi], in_=dtmp[k][:, 0:hi - lo], func=AF.Exp),
                                [T_dtmp[k]], [T_pt[ps]])

                def emit_PV(u):
                    ps = u.idx % 3
                    PT = PTs[ps].rearrange("p a b -> p (a b)")
                    for (i, col, wq, st, sp_) in u.pv:
                        av, abk = accv(i)
                        pe(lambda e, u=u, av=av, col=col, wq=wq, st=st, sp_=sp_, PT=PT: e.matmul(
                            av[0:wq, :], lhsT=PT[:, col:col + wq], rhs=u.v, start=st, stop=sp_,
                            skip_group_check=True),
                           [T_pt[ps], u.Tv], [TB[abk]])

                for ui, u in enumerate(units):
                    u.idx = ucount[0]
                    ucount[0] += 1
                last_touch = {}
                for ui, u in enumerate(units):
                    for pi, ent in enumerate(u.pv):
                        last_touch[ent[0]] = (ui, pi)
                for ui, u in enumerate(units):
                    u.pv = [(i_, c_, w_, st_, last_touch[i_] == (ui, pi)) for pi, (i_, c_, w_, st_, _sp) in
                            enumerate(u.pv)]
                for u in units[:10]:
                    if getattr(u, "load", None) is not None:
                        u.load()
                        u.load = None
                for ui, u in enumerate(units):
                    if ui == 0:
                        emit_S(u)
                        if len(units) > 1:
                            emit_S(units[1])
                    emit_exp(u)
                    if ui + 2 < len(units):
                        emit_S(units[ui + 2])
                    emit_PV(u)
                    unit_hook()

            def ptcol(h, m, qi):
                if h == 0:
                    return (qi // 2) * 512 + m * 256 + (qi % 2) * 128
                return m * 512 + qi * 128

            rr4 = [A.f32(12) for _ in range(2)]; T_rr4 = [T(), T()]
            o4 = [A.f32(128) for _ in range(4)]; T_o4 = [T() for _ in range(4)]

            def evac_head(h, np_q, dst_fn, accs):
                k2 = ecount[0] % 2
                ecount[0] += 1
                r = rr4[k2]
                Tr = T_rr4[k2]
                n = len(accs)
                for idx, (a, i1, i2) in enumerate(accs):
                    O1, b1 = accv(i1)
                    O2, b2_ = accv(i2)
                    dve(lambda e, O1=O1, idx=idx: e.reciprocal(out=r[0:np_q, idx:idx + 1], in_=O1[0:np_q, 128:129]),
                        [], [TB[b1], Tr])
                    dve(lambda e, O2=O2, idx=idx: e.reciprocal(out=r[0:np_q, 4 + idx:5 + idx],
                                                               in_=O2[0:np_q, 128:129]), [], [TB[b2_], Tr])
                dve(lambda e: e.tensor_scalar(out=r[0:np_q, 4:4 + n], in0=r[0:np_q, 4:4 + n],
                                              scalar1=lam_t[0:np_q, 1:2], scalar2=None, op0=ALU.mult),
                    [T_lam], [Tr])
                for idx, (a, i1, i2) in enumerate(accs):
                    O1, b1 = accv(i1)
                    O2, b2_ = accv(i2)
                    dve(lambda e, O1=O1, idx=idx: e.tensor_scalar(out=o4[idx][0:np_q], in0=O1[0:np_q, 0:128],
                                                                  scalar1=r[0:np_q, idx:idx + 1], scalar2=None,
                                                                  op0=ALU.mult), [Tr], [TB[b1], T_o4[idx]])
                    dve(lambda e, O2=O2, idx=idx: e.scalar_tensor_tensor(
                        out=o4[idx][0:np_q], in0=O2[0:np_q, 0:128], scalar=r[0:np_q, 4 + idx:5 + idx],
                        in1=o4[idx][0:np_q], op0=ALU.mult, op1=ALU.add), [Tr], [TB[b2_], T_o4[idx]])
                    act(lambda e, idx=idx: e.activation(out=junko[0:np_q], in_=o4[idx][0:np_q], func=AF.Square,
                                                        accum_out=r[0:np_q, 8 + idx:9 + idx]),
                        [T_o4[idx]], [Tr])
                rstd_from_ss(r[0:np_q, 8:8 + n], 128, r[0:np_q, 8:8 + n], [Tr])
                for idx, (a, i1, i2) in enumerate(accs):
                    dst = dst_fn(a)
                    dve(lambda e, idx=idx, dst=dst: e.scalar_tensor_tensor(
                        out=dst, in0=o4[idx][0:np_q], scalar=r[0:np_q, 8 + idx:9 + idx], in1=gsub[0:np_q],
                        op0=ALU.mult, op1=ALU.mult), [Tr, T_c, T_o4[idx]], dst_tokens[0])

            dst_tokens = [[]]

            if not is_sample:
                NKV = CFG.get("NKV", 6)
                kvK = [A.bf16(512) for _ in range(NKV)]
                kvV = [A.bf16(4, 130) for _ in range(NKV)]
                T_kv = [T() for _ in range(NKV)]
                kvpos = [0]
                for h in range(H):
                    units = []
                    for ch in range(E[j] // 4):
                        s = kvpos[0] % NKV
                        kvpos[0] += 1

                        def load_chunk(s=s, h=h, ch=ch):
                            dma(kvK[s], kt_scr[h, :, ch * 512:(ch + 1) * 512], w=[T_kv[s]], sem="kv%d" % s)
                            dma(kvV[s].rearrange("p k n -> p (k n)"),
                                v_scr[h].rearrange("p k n -> p (k n)")[:, ch * 520:(ch + 1) * 520],
                                w=[T_kv[s]], sem="kv%d" % s)
                        for i in range(4):
                            kbi = 4 * ch + i
                            u = U()
                            u.load = load_chunk if i == 0 else None
                            u.kt = kvK[s][:, i * 128:(i + 1) * 128]
                            u.Tkt = T_kv[s]
                            u.v = kvV[s][:, i, 0:129]
                            u.Tv = T_kv[s]
                            first = False
                            if h == 0:
                                u.smm = [(m, qpad[m][:, h, hf * 256:(hf + 1) * 256], hf * 512 + m * 256, 256)
                                         for hf in range(2) for m in range(2)]
                                u.exps = [("fast", hf * 512, (hf + 1) * 512,
                                           btab[:, BIDX[(j, h, kbi, hf)]:BIDX[(j, h, kbi, hf)] + 1])
                                          for hf in range(2)]
                            else:
                                u.smm = [(m, qpad[m][:, h, :], m * 512, 512) for m in range(2)]
                                u.exps = [("fast", 0, 1024, btab[:, BIDX[(j, h, kbi, 0)]:BIDX[(j, h, kbi, 0)] + 1])]
                            u.pv = [(m * 4 + qi, ptcol(h, m, qi), 128, first and ((m * 4 + qi) % 3 == 0), False)
                                    for m in range(2) for qi in range(4)]
                            units.append(u)
                    dunits = []
                    for ap_ in range(4):
                        u = U()
                        u.kt = ktown[:, h, ap_ * 128:(ap_ + 1) * 128]
                        u.Tkt = T_kt
                        u.v = vown[:, ap_ * 4 + h, 0:129]
                        u.Tv = T_vo
                        u.smm = []
                        for m in range(2):
                            if h == 0:
                                for hf in range(2):
                                    qlo = max(ap_, 2 * hf)
                                    qhi = 2 * hf + 2
                                    if qlo < qhi:
                                        u.smm.append((m, qpad[m][:, h, qlo * 128:qhi * 128],
                                                      hf * 512 + m * 256 + (qlo % 2) * 128, (qhi - qlo) * 128))
                            else:
                                u.smm.append((m, qpad[m][:, h, ap_ * 128:512], m * 512 + ap_ * 128,
                                              (4 - ap_) * 128))
                        u.exps = []
                        u.pv = []
                        for m in range(2):
                            for qi in range(ap_, 4):
                                c = ptcol(h, m, qi)
                                if qi == ap_:
                                    u.exps.append(("diag", c, c + 128, bdiag[:, h * 4 + ap_, :]))
                                elif h == 0:
                                    c0 = h * 16 + ap_ * 4 + qi
                                    u.exps.append(("fast", c, c + 128, dbias[:, c0:c0 + 1]))
                                u.pv.append((m * 4 + qi, c, 128, (ap_ == 0) and ((m * 4 + qi) % 3 == 0), False))
                        if h != 0 and ap_ < 3:
                            c0 = h * 16 + ap_ * 4 + (ap_ + 1)
                            u.exps.append(("fast2", (ap_ + 1) * 128, 512, dbias[:, c0:c0 + 1]))
                        dunits.append(u)
                    units = dunits + units
                    run_units(units, h, None, 128)
                    dst_tokens[0] = T_mix
                    evac_head(h, 128, lambda a, h=h: mixb[a][:, 512 + h * 128:512 + (h + 1) * 128],
                              [(a, a, 4 + a) for a in range(4)])
            else:
                ckf = [A.f32(PAST) for _ in range(2)]; T_ckf = [T(), T()]
                ckbs = [A.bf16(4, PAST) for _ in range(2)]; T_ckbs = [T(), T()]
                cvf = [A.f32(4, 512) for _ in range(2)]; T_cvf = [T(), T()]
                cvbs = [A.bf16(NKBS * 4, 130) for _ in range(2)]; T_cvbs = [T(), T()]
                yab = [A.bf16(512) for _ in range(2)]; T_yab = [T(), T()]
                selT = A.bf16(4, 128); T_sel = T()
                dve(lambda e: e.memset(selT, 0.0), [], [T_sel])
                for b in range(BPC):
                    dve(lambda e, b=b: e.tensor_copy(out=selT[0:32, b, 32 * b:32 * b + 32], in_=ident_b[0:32, 0:32]),
                        [T_identb], [T_sel])
                for s in range(2):
                    dve(lambda e, s=s: e.memset(cvbs[s][:, :, 128:130], 1.0), [], [T_cvbs[s]])
                cc = [0]

                def prep_b(b):
                    pb = b % 2
                    for h in range(H):
                        s = cc[0] % 2
                        cc[0] += 1
                        dma(ckf[s], ckT[b, h], w=[T_ckf[s]], sem="ckf%d" % s)
                        dve(lambda e, s=s, h=h: e.tensor_copy(out=ckbs[pb][:, h, :], in_=ckf[s]),
                            [T_ckf[s]], [T_ckbs[pb]])
                    for g4 in range(NKBS // 4):
                        s = cc[0] % 2
                        cc[0] += 1
                        dma(cvf[s], cv[b, g4 * 512:(g4 + 1) * 512, :].rearrange("(k p) n -> p k n", p=128),
                            w=[T_cvf[s]], sem="cvf%d" % s)
                        pool(lambda e, s=s, g4=g4: e.tensor_copy(
                            out=cvbs[pb][:, g4 * 16:(g4 + 1) * 16, 0:128],
                            in_=cvf[s].rearrange("p k (h n) -> p (k h) n", h=4)), [T_cvf[s]], [T_cvbs[pb]])

                prep_b(0)
                for b in range(BPC):
                    if b + 1 < BPC:
                        prep_b(b + 1)
                    ckb = ckbs[b % 2]; T_ckb = T_ckbs[b % 2]
                    cvb = cvbs[b % 2]; T_cvb = T_cvbs[b % 2]
                    units = []
                    started = set()
                    for h in range(H):
                        for kbi in range(NKBS + 1):
                            u = U()
                            own = (kbi == NKBS)
                            if own:
                                u.kt = ktown[:, h, 0:128]
                                u.Tkt = T_kt
                                u.v = vown[:, h, 0:129]
                                u.Tv = T_vo
                                u.exps = [("diag", m * 512, m * 512 + 32, bs[:, b * 4 + h, :]) for m in range(2)]
                            else:
                                u.kt = ckb[:, h, kbi * 128:(kbi + 1) * 128]
                                u.Tkt = T_ckb
                                u.v = cvb[:, kbi * 4 + h, 0:129]
                                u.Tv = T_cvb
                                c0 = h * NKBS + kbi
                                u.exps = [("fast2", 0, 32, sbias[:, c0:c0 + 1])]
                            u.smm = [(m, qpad[m][:, h, b * 32:(b + 1) * 32], m * 512, 32) for m in range(2)]
                            u.pv = []
                            for m in range(2):
                                i = m * 4 + h
                                bk_ = 4 + i // 3
                                st = (kbi == 0) and (bk_ not in started)
                                if kbi == 0:
                                    started.add(bk_)
                                u.pv.append((i, m * 512, 32, st, own))
                            units.append(u)
                    run_units(units, 0, None, 32)
                    dst_tokens[0] = [T_yab[b % 2]]
                    evac_head(0, 32, lambda a, b=b: yab[b % 2][0:32, a * 128:(a + 1) * 128],
                              [(h, h, 4 + h) for h in range(H)])
                    pe(lambda e, b=b: e.matmul(bank(7), lhsT=selT[0:32, b, :], rhs=yab[b % 2][0:32, :],
                                               start=(b == 0), stop=(b == BPC - 1)),
                       [T_sel, T_yab[b % 2]], [TB[7]])
                act(lambda e: e.activation(out=mixb[0][:, 512:1024], in_=bank(7), func=AF.Copy),
                    [], [TB[7], T_mix[0]])
            while hook_stage or late_jobs:
                if hook_stage:
                    cs, stf = hook_stage.pop(0)
                    cs(); stf()
                if late_jobs:
                    ld, cs, stf = late_jobs.pop(0)
                    ld()
                    hook_stage.append((cs, stf))
            for f in deferred_prefetch:
                f()
            kb.barrier()
            A.top = markQ

            if DEBUG:
                dbgt = [A.f32(D) for _ in range(NA)]
                for a in range(NA):
                    dve(lambda e, a=a: e.tensor_copy(out=dbgt[a], in_=mixb[a]), [T_mix[a]], [T_mix[a]])
                    dma(dbg_mix[ob0 + a], dbgt[a], r=[T_mix[a]], sem="dbg%d" % a)
            h1 = [A.f32(D) for _ in range(NA)]; T_h = [T() for _ in range(NA)]
            yt = [A.f32(D) for _ in range(NA)]; T_y = [T() for _ in range(NA)]
            TAb = A.bf16(8, W); T_TA = T()
            TBb = A.bf16(8, W); T_TBb = T()
            ffT = A.bf16(32, W); T_ff = T()
            pf = [A.f32(PLE) for _ in range(NA)]; T_pf = [T() for _ in range(NA)]
            pbf = [A.bf16(PLE) for _ in range(2)]; T_pbf = [T(), T()]
            pT = A.bf16(2, W); T_pT = T()
            cbf = [A.bf16(D) for _ in range(2)]; T_cbf = [T(), T()]
            rsC = A.f32(NA, 8); T_rsC = [T() for _ in range(NA)]
            rl = [A.f32(W) for _ in range(2)]; T_rl = [T(), T()]
            junkC = A.bf16(D); T_junkC = T()

            if not is_sample:
                ring_plan(plan_A(SLOT_ORDER.index(j) + 1 >= NSLOT))
            for a in range(NA):
                dma(h1[a], x_own[ob0 + a], w=[T_h[a]], sem="hx%d" % a)
                dma(pf[a], p_own[ob0 + a], w=[T_pf[a]], sem="pf%d" % a)

            def evac_T(dst, evi):
                def f(pv):
                    if evi % 2 == 0:
                        act(lambda e: e.activation(out=dst, in_=pv, func=AF.Copy), [], f.toks)
                    else:
                        dve(lambda e: e.tensor_copy(out=dst, in_=pv), [], f.toks)
                return f

            def norm_add(a, col, g_idx):
                rs = rsC[:, a, col:col + 1]
                act(lambda e: e.activation(out=junkC, in_=yt[a], func=AF.Square, accum_out=rs),
                    [T_y[a]], [T_rsC[a]])
                rstd_from_ss(rs, D, rs, [T_rsC[a]])
                dve(lambda e: e.scalar_tensor_tensor(out=yt[a], in0=yt[a], scalar=rs, in1=gbc[:, g_idx, :],
                                                     op0=ALU.mult, op1=ALU.mult), [T_rsC[a], T_c], [T_y[a]])
                dve(lambda e: e.tensor_tensor(out=h1[a], in0=h1[a], in1=yt[a], op=ALU.add), [T_y[a]], [T_h[a]])

            for a in range(NA):
                f = evac_T(TAb[:, :, a * 128:(a + 1) * 128], a)
                f.toks = [TB[a], T_TA]
                transpose_to(f, mixb[a], 128, 8, a, T_mix[a], None)
            stream_tokmajor(ws_out, 2, 4, lambda a, kc: TAb[:, kc, a * 128:(a + 1) * 128], NA,
                            lambda cg, a, bk: act(lambda e: e.activation(
                                out=yt[a][:, cg * 512:(cg + 1) * 512], in_=bank(bk), func=AF.Copy),
                                [], [TB[bk], T_y[a]]),
                            lambda a: [T_TA], SETS4)
            for a in range(NA):
                norm_add(a, 0, 0)
            for a in range(NA):
                b2 = a % 2
                rs2 = rsC[:, a, 1:2]
                act(lambda e, a=a, rs2=rs2: e.activation(out=junkC, in_=h1[a], func=AF.Square, accum_out=rs2),
                    [T_h[a]], [T_rsC[a]])
                rstd_from_ss(rs2, D, rs2, [T_rsC[a]])
                dve(lambda e, a=a, rs2=rs2: e.tensor_tensor(out=rsC[:, a, 2:3], in0=rs2, in1=rs2, op=ALU.mult),
                    [], [T_rsC[a]])
                act(lambda e, a=a, b2=b2: e.activation(out=cbf[b2], in_=h1[a], func=AF.Copy), [T_h[a]], [T_cbf[b2]])
                f = evac_T(TBb[:, :, a * 128:(a + 1) * 128], a + 1)
                f.toks = [TB[4 + a], T_TBb]
                transpose_to(f, cbf[b2], 128, 8, 4 + a, T_cbf[b2], None)
            for fch in range(32):
                piece, Tp = ring_load(ws_up[fch])
                pv = piece.rearrange("p (k n) -> p k n", k=8)
                bk = fch % 2
                for kc in range(8):
                    pe(lambda e, pv=pv, kc=kc, bk=bk: e.matmul(bank(bk)[:, 0:W], lhsT=pv[:, kc, :],
                                                               rhs=TBb[:, kc, :], start=(kc == 0), stop=(kc == 7)),
                       [Tp, T_TBb], [TB[bk]])
                ring_topup()
                act(lambda e, bk=bk: e.activation(out=rl[bk], in_=bank(bk)[:, 0:W], func=AF.Relu),
                    [], [TB[bk], T_rl[bk]])
                dve(lambda e, bk=bk, fch=fch: e.tensor_tensor(out=ffT[:, fch, :], in0=rl[bk], in1=rl[bk],
                                                              op=ALU.mult), [T_rl[bk]], [T_ff])
            stream_tokmajor(ws_dn, 2, 16, lambda a, kc: ffT[:, kc, a * 128:(a + 1) * 128], NA,
                            lambda cg, a, bk: act(lambda e: e.activation(
                                out=yt[a][:, cg * 512:(cg + 1) * 512], in_=bank(bk), func=AF.Copy,
                                scale=rsC[:, a, 2:3]), [T_rsC[a]], [TB[bk], T_y[a]]),
                            lambda a: [T_ff], SETS4)
            for a in range(NA):
                norm_add(a, 3, 1)
            for a in range(NA):
                b2 = a % 2
                act(lambda e, a=a, b2=b2: e.activation(out=cbf[b2], in_=h1[a], func=AF.Copy), [T_h[a]], [T_cbf[b2]])
                f = evac_T(TAb[:, :, a * 128:(a + 1) * 128], a)
                f.toks = [TB[a], T_TA]
                transpose_to(f, cbf[b2], 128, 8, a, T_cbf[b2], None)
                act(lambda e, a=a, b2=b2: e.activation(out=pbf[b2], in_=pf[a], func=AF.Copy), [T_pf[a]], [T_pbf[b2]])
                f = evac_T(pT[:, :, a * 128:(a + 1) * 128], a + 1)
                f.toks = [TB[4 + a], T_pT]
                transpose_to(f, pbf[b2], 128, 2, 4 + a, T_pbf[b2], None)
            stream_tokmajor(ws_g, 2, 4, lambda a, kc: TAb[:, kc, a * 128:(a + 1) * 128], NA,
                            lambda cg, a, bk: act(lambda e: e.activation(
                                out=yt[a][:, cg * 512:(cg + 1) * 512], in_=bank(bk), func=AF.Exp, scale=-1.0),
                                [], [TB[bk], T_y[a]]),
                            lambda a: [T_TA], SETS4)
            for a in range(NA):
                dve(lambda e, a=a: e.tensor_scalar(out=yt[a], in0=yt[a], scalar1=1.0, scalar2=None, op0=ALU.add),
                    [], [T_y[a]])
                dve(lambda e, a=a: e.reciprocal(out=yt[a], in_=yt[a]), [], [T_y[a]])
            stream_tokmajor(ws_pe, 2, 1, lambda a, kc: pT[:, kc, a * 128:(a + 1) * 128], NA,
                            lambda cg, a, bk: dve(lambda e: e.tensor_tensor(
                                out=yt[a][:, cg * 512:(cg + 1) * 512], in0=bank(bk),
                                in1=yt[a][:, cg * 512:(cg + 1) * 512], op=ALU.mult), [], [TB[bk], T_y[a]]),
                            lambda a: [T_pT], SETS4)
            for a in range(NA):
                norm_add(a, 4, 2)
                dma(y_own[ob0 + a], h1[a], r=[T_h[a]], sem="yo%d" % a)
            kb.barrier()
            A.top = markS

        for si, j in enumerate(SLOT_ORDER):
            do_slot(j, False)
            if si == 0:
                assert not late_jobs
        do_slot(0, True)
        kb.emit()
    return nc


_PROG_CACHE = {}


def kernel(x_prompt, x_sample, cache_k, cache_v, state_conv, p_prompt, p_sample,
           w_in, w_conv, b_conv, lambda_q1, lambda_k1, lambda_q2, lambda_k2, g_subln,
           w_out, g_pre_mix, g_post_mix, g_pre_mlp, g_post_mlp, w_up, w_down,
           w_pe, w_pe_gate, g_pe):
    NSLOT = CFG["NSLOT"]
    PAST = CFG["PAST"]
    f = np.float32
    A_ = lambda v: np.ascontiguousarray(np.asarray(v), dtype=f)
    x_prompt = A_(x_prompt); x_sample = A_(x_sample); cache_k = A_(cache_k); cache_v = A_(cache_v)
    state_conv = A_(state_conv); p_prompt = A_(p_prompt); p_sample = A_(p_sample)
    SEQ = x_prompt.shape[1]
    assert SEQ == NCORE * NSLOT * 512 and cache_k.shape[2] == PAST
    NOWN = 4 * NSLOT + 1
    key = (NSLOT, PAST)
    if key not in _PROG_CACHE:
        _PROG_CACHE[key] = build_program(NSLOT, PAST)
    nc = _PROG_CACHE[key]

    def bc(v, n=128):
        v = A_(v).reshape(1, -1)
        return np.ascontiguousarray(np.broadcast_to(v, (n, v.shape[1])))

    gvec = np.concatenate([A_(g_pre_mix)[0].reshape(8, 128).T, A_(g_pre_mlp)[0].reshape(8, 128).T], axis=1)
    gbc = np.concatenate([bc(g_post_mix[0]), bc(g_post_mlp[0]), bc(g_pe[0])], axis=1)
    convbc = np.concatenate([bc(A_(w_conv)[0, 0]), bc(A_(w_conv)[0, 1]), bc(A_(w_conv)[0, 2]), bc(A_(b_conv)[0])],
                            axis=1)
    lamv = np.concatenate([bc(lambda_q1[0]), bc(lambda_k1[0]), bc(lambda_q2[0]), bc(lambda_k2[0])], axis=1)
    shared = dict(
        x_all=x_prompt[0], w_in=A_(w_in)[0], w_out=A_(w_out)[0], w_up=A_(w_up)[0], w_down=A_(w_down)[0],
        w_pe=A_(w_pe)[0], w_g=A_(w_pe_gate)[0], gvec=np.ascontiguousarray(gvec), gbc=np.ascontiguousarray(gbc),
        gsub=bc(g_subln[0]), convbc=np.ascontiguousarray(convbc), lamv=np.ascontiguousarray(lamv),
        ident=np.eye(128, dtype=f))
    in_maps = []
    for c in range(NCORE):
        tb = make_tables(c, NSLOT, PAST)
        xo = np.zeros((NOWN, 128, D), f)
        po = np.zeros((NOWN, 128, PLE), f)
        xh = np.zeros((NSLOT, 2, D), f)
        for j in range(NSLOT):
            sb = NCORE * j + c
            xo[4 * j:4 * j + 4] = x_prompt[0, sb * 512:(sb + 1) * 512].reshape(4, 128, D)
            po[4 * j:4 * j + 4] = p_prompt[0, 0, sb * 512:(sb + 1) * 512].reshape(4, 128, PLE)
            if sb > 0:
                xh[j] = x_prompt[0, sb * 512 - 2:sb * 512]
        bsl = slice(c * BPC, (c + 1) * BPC)
        xo[4 * NSLOT] = x_sample[bsl].reshape(128, D)
        po[4 * NSLOT] = p_sample[0, bsl].reshape(128, PLE)
        m = dict(shared)
        m.update(x_own=xo, p_own=po, x_halo=xh,
                 hist_s=np.ascontiguousarray(state_conv[0, bsl].reshape(8, 512)),
                 ckT=np.ascontiguousarray(cache_k[0, bsl].transpose(0, 2, 3, 1)),
                 cv=np.ascontiguousarray(cache_v[0, bsl].reshape(BPC, PAST, 512)),
                 **tb)
        in_maps.append(m)
    res = run_bass_kernel_spmd(nc, in_maps, core_ids=list(range(NCORE)))
    R = res.results
    if CFG.get("DEBUG", False):
        CFG["_dbg"] = [r["dbg_mix"] for r in R]
        CFG["_kt"] = [np.asarray(r["kt_scr"]).astype(np.float32) for r in R]
        CFG["_v"] = [np.asarray(r["v_scr"]).astype(np.float32) for r in R]
    y_prompt = np.zeros((1, SEQ, D), f)
    k_prompt = np.zeros((1, 1, SEQ, H, 128), f)
    v_prompt = np.zeros((1, 1, SEQ, H, 128), f)
    y_sample = np.zeros((NCORE * BPC, DEC_T, D), f)
    k_sample = np.zeros((1, NCORE * BPC, DEC_T, H, 128), f)
    v_sample = np.zeros((1, NCORE * BPC, DEC_T, H, 128), f)
    conv_sample = np.zeros((1, NCORE * BPC, 2, 512), f)
    for c in range(NCORE):
        r = R[c]
        for j in range(NSLOT):
            sb = NCORE * j + c
            sl = slice(sb * 512, (sb + 1) * 512)
            y_prompt[0, sl] = r["y_own"][4 * j:4 * j + 4].reshape(512, D)
            k_prompt[0, 0, sl] = r["k_own"][4 * j:4 * j + 4].reshape(512, H, 128)
            v_prompt[0, 0, sl] = r["v_own"][4 * j:4 * j + 4].reshape(512, H, 128)
        bsl = slice(c * BPC, (c + 1) * BPC)
        y_sample[bsl] = r["y_own"][4 * NSLOT].reshape(BPC, DEC_T, D)
        k_sample[0, bsl] = r["k_own"][4 * NSLOT].reshape(BPC, DEC_T, H, 128)
        v_sample[0, bsl] = r["v_own"][4 * NSLOT].reshape(BPC, DEC_T, H, 128)
        conv_sample[0, bsl] = r["conv_s"].reshape(BPC, 2, 512)
    conv_prompt = np.ascontiguousarray(R[NCORE - 1]["conv_p"]).reshape(1, 1, 2, 512).astype(f)
    return (y_prompt, y_sample, k_prompt, v_prompt, conv_prompt, k_sample, v_sample, conv_sample)
```
